# Optimizing a Trainium2 kernel written in Bass

```python
import jax, jax.numpy as jnp
from jax import lax
import numpy as np

D_MODEL = 2048
BATCH = 32
SEQ = 256
DEPTH = 1
DEC_BATCH = 2
DEC_SEQ = 4096
PAST_LEN = 512

GRID_W = 64
MIX_WIDTH = D_MODEL
LRU_WIDTH = MIX_WIDTH // 2
LRU_HEADS = 8
LRU_HEAD_DIM = LRU_WIDTH // LRU_HEADS
LRU_C = 8.0
CONV_WIDTH = 4
CONV_LEFT = 2
RWKV_WIDTH = MIX_WIDTH - LRU_WIDTH
HEAD_SIZE = 64
RWKV_HEADS = RWKV_WIDTH // HEAD_SIZE
DECAY_LORA = 64
AAA_LORA = 64
GATE_LORA = 160
RWKV_IN_WIDTH = 3 * RWKV_WIDTH + DECAY_LORA + AAA_LORA + GATE_LORA
IN_WIDTH = 2 * LRU_WIDTH + RWKV_IN_WIDTH
FFN_HIDDEN = -(-8 * D_MODEL // (3 * 256)) * 256
RMS_EPS = 1e-6
GN_EPS = 64e-5

kernel_name = 'hybrid_rglru_rwkv7_prefix_diffusion_step'


def rms_norm(x, g):
    x32 = x.astype(jnp.float32)
    y = x32 * lax.rsqrt(jnp.mean(x32 * x32, axis=-1, keepdims=True) + RMS_EPS)
    return (y * g.astype(jnp.float32)).astype(x.dtype)


def shift_context(p):
    half = p.shape[-1] // 2
    prev = jnp.pad(p[..., :half], ((0, 0), (1, 0), (0, 0)))[:, :-1]
    nxt = jnp.pad(p[..., half:], ((0, 0), (0, 1), (0, 0)))[:, 1:]
    return jnp.concatenate([prev, nxt], axis=-1)


def shift_grid(p):
    B, T, C = p.shape
    rows = T // GRID_W
    q = C // 4
    g = p.reshape(B, rows, GRID_W, C)
    left = jnp.pad(g[..., :q], ((0, 0), (0, 0), (1, 0), (0, 0)))[:, :, :-1]
    right = jnp.pad(g[..., q:2 * q], ((0, 0), (0, 0), (0, 1), (0, 0)))[:, :, 1:]
    up = jnp.pad(g[..., 2 * q:3 * q], ((0, 0), (1, 0), (0, 0), (0, 0)))[:, :-1]
    down = jnp.pad(g[..., 3 * q:], ((0, 0), (0, 1), (0, 0), (0, 0)))[:, 1:]
    return jnp.concatenate([left, right, up, down], axis=-1).reshape(B, T, C)


def conv_centred(x, w, b):
    T = x.shape[1]
    xp = jnp.pad(x, ((0, 0), (CONV_LEFT, CONV_WIDTH - 1 - CONV_LEFT), (0, 0)))
    y = xp[:, 0:T] * w[0]
    for k in range(1, CONV_WIDTH):
        y = y + xp[:, k:k + T] * w[k]
    return y + b


def _linear_combine(e1, e2):
    a1, b1 = e1
    a2, b2 = e2
    return a1 * a2, a2 * b1 + b2


def rglru(x, wr, br, wi, bi, lam, h0, reverse):
    B, T, W = x.shape
    xf = x.astype(jnp.float32)
    xh = xf.reshape(B, T, LRU_HEADS, LRU_HEAD_DIM)
    r = jax.nn.sigmoid(jnp.einsum('bthi,hij->bthj', xh, wr.astype(jnp.float32)) + br).reshape(B, T, W)
    i = jax.nn.sigmoid(jnp.einsum('bthi,hij->bthj', xh, wi.astype(jnp.float32)) + bi).reshape(B, T, W)
    log_a = -LRU_C * r * jax.nn.softplus(-lam.astype(jnp.float32))
    a = jnp.exp(log_a)
    b = jnp.sqrt(-jnp.expm1(2.0 * log_a)) * (i * xf)
    if reverse:
        a, b = jnp.flip(a, 1), jnp.flip(b, 1)
    a_cum, b_cum = lax.associative_scan(_linear_combine, (a, b), axis=1)
    hs = a_cum * h0.astype(jnp.float32)[:, None, :] + b_cum
    h_fin = hs[:, -1]
    if reverse:
        hs = jnp.flip(hs, 1)
    return hs, h_fin


def wkv7_scan(r, decay, kk, a, k, v, S0, reverse):
    xs = tuple(jnp.moveaxis(t, 1, 0) for t in (r, decay, kk, a, k, v))

    def step(S, inp):
        r_t, w_t, kk_t, a_t, k_t, v_t = inp
        sa = jnp.einsum('bhvk,bhk->bhv', S, -kk_t)
        S = (S * w_t[:, :, None, :] + sa[..., None] * (kk_t * a_t)[:, :, None, :]
             + v_t[..., None] * k_t[:, :, None, :])
        return S, jnp.einsum('bhvk,bhk->bhv', S, r_t)

    S_fin, ys = lax.scan(step, S0.astype(jnp.float32), xs, reverse=reverse)
    return jnp.moveaxis(ys, 0, 1), S_fin


def rwkv7(pr, shift_fn, S0, L):
    dt = pr.dtype
    B, T, _ = pr.shape
    pr = pr + L['rwkv_mu'] * (shift_fn(pr) - pr)
    s1 = RWKV_WIDTH
    r, k, v, wd, ad, gd = jnp.split(pr, [s1, 2 * s1, 3 * s1, 3 * s1 + DECAY_LORA, 3 * s1 + DECAY_LORA + AAA_LORA], axis=-1)

    def heads(t):
        return t.astype(jnp.float32).reshape(B, T, RWKV_HEADS, HEAD_SIZE)

    kk = heads(k * L['rwkv_k_k'])
    kk = kk * lax.rsqrt(jnp.maximum(jnp.sum(kk * kk, axis=-1, keepdims=True), 1e-24))
    g = jax.nn.sigmoid(gd) @ L['rwkv_g_up']
    wl = jnp.tanh(wd)
    rh, vh = heads(r), heads(v)
    ys, finals = [], []
    for d, rev in ((0, False), (1, True)):
        w_log = -jax.nn.softplus(-(L['rwkv_w0'][d] + wl @ L['rwkv_w_up'][d])) - 0.5
        decay = jnp.exp(-jnp.exp(heads(w_log)))
        a_flat = jax.nn.sigmoid(L['rwkv_a0'][d] + ad @ L['rwkv_a_up'][d])
        k_d = heads(k * (1.0 + (a_flat - 1.0) * L['rwkv_k_a']))
        y_d, S_d = wkv7_scan(rh, decay, kk, heads(a_flat), k_d, vh, S0[:, d], rev)
        ys.append(y_d)
        finals.append(S_d)
    bonus = jnp.sum(rh * heads(k) * L['rwkv_r_k'].astype(jnp.float32), axis=-1, keepdims=True) * vh
    y = ys[0] + ys[1] + bonus
    mean = jnp.mean(y, axis=-1, keepdims=True)
    var = jnp.mean(jnp.square(y - mean), axis=-1, keepdims=True)
    y = ((y - mean) * lax.rsqrt(var + GN_EPS)).reshape(B, T, RWKV_WIDTH)
    y = y * L['rwkv_ln_w'] + L['rwkv_ln_b']
    out = (y * g.astype(jnp.float32)).astype(dt)
    return out, jnp.stack([finals[0], finals[1]], axis=1)


def mixer(h, shift_fn, lru_h0, wkv_S0, L):
    dt = h.dtype
    proj = h @ L['w_in']
    xl, gl, pr = jnp.split(proj, [LRU_WIDTH, 2 * LRU_WIDTH], axis=-1)
    xc = conv_centred(xl, L['lru_conv_w'], L['lru_conv_b'])
    hs_f, hf = rglru(xc, L['lru_wr'][0], L['lru_br'][0], L['lru_wi'][0], L['lru_bi'][0], L['lru_lambda'][0], lru_h0[:, 0], False)
    hs_b, hb = rglru(xc, L['lru_wr'][1], L['lru_br'][1], L['lru_wi'][1], L['lru_bi'][1], L['lru_lambda'][1], lru_h0[:, 1], True)
    lru_out = ((hs_f + hs_b) * jax.nn.gelu(gl.astype(jnp.float32))).astype(dt)
    rwkv_out, wkv_fin = rwkv7(pr, shift_fn, wkv_S0, L)
    out = jnp.concatenate([lru_out, rwkv_out], axis=-1) @ L['w_out']
    return out, jnp.stack([hf, hb], axis=1), wkv_fin


def block(x, mod, shift_fn, lru_h0, wkv_S0, L):
    shift_m, scale_m, gate_m, shift_f, scale_f, gate_f = jnp.split(mod, 6, axis=-1)
    h = rms_norm(x, L['norm_mix_pre']) * (1.0 + scale_m) + shift_m
    out, lru_fin, wkv_fin = mixer(h, shift_fn, lru_h0, wkv_S0, L)
    x = x + gate_m * rms_norm(out, L['norm_mix_post'])
    h = rms_norm(x, L['norm_ffn_pre']) * (1.0 + scale_f) + shift_f
    gate, up = jnp.split(h @ L['ffn_w_gu'], 2, axis=-1)
    f = (jax.nn.silu(gate) * up) @ L['ffn_w_down']
    x = x + gate_f * rms_norm(f, L['norm_ffn_post'])
    return x, lru_fin, wkv_fin


def setup_inputs(seed: int = 0) -> dict:
    key = jax.random.key(seed)
    ks = jax.random.split(key, 40)
    f32 = jnp.float32
    D = D_MODEL

    def nrm(k, shape, scale):
        return jax.random.normal(k, shape, f32) * scale

    u = jax.random.uniform(ks[13], (DEPTH, 2, LRU_WIDTH), f32, minval=0.9, maxval=0.999)
    s = u ** (1.0 / LRU_C)
    lru_lambda = jnp.log(s) - jnp.log1p(-s)
    return {
        'x_prompt': nrm(ks[0], (BATCH, SEQ, D), 1.0),
        'x_sample': nrm(ks[1], (DEC_BATCH, DEC_SEQ, D), 1.0),
        'state_lru': nrm(ks[2], (DEC_BATCH, DEPTH, 2, LRU_WIDTH), 1.0),
        'state_wkv': nrm(ks[3], (DEC_BATCH, DEPTH, 2, RWKV_HEADS, HEAD_SIZE, HEAD_SIZE), 0.5),
        'c': nrm(ks[4], (DEC_BATCH, D), 1.0),
        'c_ctx': nrm(ks[5], (D,), 1.0),
        'norm_mix_pre': 1.0 + nrm(ks[6], (DEPTH, D), 0.1),
        'norm_mix_post': 1.0 + nrm(ks[7], (DEPTH, D), 0.1),
        'norm_ffn_pre': 1.0 + nrm(ks[8], (DEPTH, D), 0.1),
        'norm_ffn_post': 1.0 + nrm(ks[9], (DEPTH, D), 0.1),
        'w_mod': nrm(ks[10], (DEPTH, D, 6 * D), 0.5 * D ** -0.5),
        'b_mod': nrm(ks[11], (DEPTH, 6 * D), 0.02),
        'w_in': nrm(ks[12], (DEPTH, D, IN_WIDTH), D ** -0.5),
        'lru_conv_w': nrm(ks[14], (DEPTH, CONV_WIDTH, LRU_WIDTH), CONV_WIDTH ** -0.5),
        'lru_conv_b': nrm(ks[15], (DEPTH, LRU_WIDTH), 0.02),
        'lru_wr': nrm(ks[16], (DEPTH, 2, LRU_HEADS, LRU_HEAD_DIM, LRU_HEAD_DIM), LRU_HEAD_DIM ** -0.5),
        'lru_br': nrm(ks[17], (DEPTH, 2, LRU_HEADS, LRU_HEAD_DIM), 0.1),
        'lru_wi': nrm(ks[18], (DEPTH, 2, LRU_HEADS, LRU_HEAD_DIM, LRU_HEAD_DIM), LRU_HEAD_DIM ** -0.5),
        'lru_bi': nrm(ks[19], (DEPTH, 2, LRU_HEADS, LRU_HEAD_DIM), 0.1),
        'lru_lambda': lru_lambda,
        'rwkv_mu': jax.random.uniform(ks[20], (DEPTH, RWKV_IN_WIDTH), f32),
        'rwkv_w0': jax.random.uniform(ks[21], (DEPTH, 2, RWKV_WIDTH), f32, minval=-6.0, maxval=-1.0),
        'rwkv_w_up': nrm(ks[22], (DEPTH, 2, DECAY_LORA, RWKV_WIDTH), DECAY_LORA ** -0.5),
        'rwkv_a0': nrm(ks[23], (DEPTH, 2, RWKV_WIDTH), 0.5),
        'rwkv_a_up': nrm(ks[24], (DEPTH, 2, AAA_LORA, RWKV_WIDTH), AAA_LORA ** -0.5),
        'rwkv_g_up': nrm(ks[25], (DEPTH, GATE_LORA, RWKV_WIDTH), GATE_LORA ** -0.5),
        'rwkv_k_k': 0.85 + nrm(ks[26], (DEPTH, RWKV_WIDTH), 0.1),
        'rwkv_k_a': 1.0 + nrm(ks[27], (DEPTH, RWKV_WIDTH), 0.1),
        'rwkv_r_k': nrm(ks[28], (DEPTH, RWKV_HEADS, HEAD_SIZE), 0.1),
        'rwkv_ln_w': 1.0 + nrm(ks[29], (DEPTH, RWKV_WIDTH), 0.1),
        'rwkv_ln_b': nrm(ks[30], (DEPTH, RWKV_WIDTH), 0.02),
        'w_out': nrm(ks[31], (DEPTH, MIX_WIDTH, D), MIX_WIDTH ** -0.5),
        'ffn_w_gu': nrm(ks[32], (DEPTH, D, 2 * FFN_HIDDEN), D ** -0.5),
        'ffn_w_down': nrm(ks[33], (DEPTH, FFN_HIDDEN, D), FFN_HIDDEN ** -0.5),
    }


def reference(x_prompt, x_sample, state_lru, state_wkv, c, c_ctx,
              norm_mix_pre, norm_mix_post, norm_ffn_pre, norm_ffn_post, w_mod, b_mod, w_in,
              lru_conv_w, lru_conv_b, lru_wr, lru_br, lru_wi, lru_bi, lru_lambda,
              rwkv_mu, rwkv_w0, rwkv_w_up, rwkv_a0, rwkv_a_up, rwkv_g_up, rwkv_k_k, rwkv_k_a, rwkv_r_k,
              rwkv_ln_w, rwkv_ln_b, w_out, ffn_w_gu, ffn_w_down):
    y_p = x_prompt
    y_s = x_sample
    Bp = x_prompt.shape[0]
    new_lru, new_wkv = [], []
    for l in range(DEPTH):
        L = {
            'norm_mix_pre': norm_mix_pre[l], 'norm_mix_post': norm_mix_post[l],
            'norm_ffn_pre': norm_ffn_pre[l], 'norm_ffn_post': norm_ffn_post[l],
            'w_in': w_in[l], 'w_out': w_out[l],
            'lru_conv_w': lru_conv_w[l], 'lru_conv_b': lru_conv_b[l],
            'lru_wr': lru_wr[l], 'lru_br': lru_br[l], 'lru_wi': lru_wi[l], 'lru_bi': lru_bi[l],
            'lru_lambda': lru_lambda[l],
            'rwkv_mu': rwkv_mu[l], 'rwkv_w0': rwkv_w0[l], 'rwkv_w_up': rwkv_w_up[l],
            'rwkv_a0': rwkv_a0[l], 'rwkv_a_up': rwkv_a_up[l], 'rwkv_g_up': rwkv_g_up[l],
            'rwkv_k_k': rwkv_k_k[l], 'rwkv_k_a': rwkv_k_a[l], 'rwkv_r_k': rwkv_r_k[l],
            'rwkv_ln_w': rwkv_ln_w[l], 'rwkv_ln_b': rwkv_ln_b[l],
            'ffn_w_gu': ffn_w_gu[l], 'ffn_w_down': ffn_w_down[l],
        }
        mod_ctx = (jax.nn.silu(c_ctx) @ w_mod[l] + b_mod[l])[None, None, :]
        mod_lat = (jax.nn.silu(c) @ w_mod[l] + b_mod[l])[:, None, :]
        lru_zero = jnp.zeros((Bp, 2, LRU_WIDTH), jnp.float32)
        wkv_zero = jnp.zeros((Bp, 2, RWKV_HEADS, HEAD_SIZE, HEAD_SIZE), jnp.float32)
        y_p, lru_ctx, wkv_ctx = block(y_p, mod_ctx, shift_context, lru_zero, wkv_zero, L)
        new_lru.append(lru_ctx.astype(x_prompt.dtype))
        new_wkv.append(wkv_ctx.astype(x_prompt.dtype))
        y_s, _, _ = block(y_s, mod_lat, shift_grid, state_lru[:, l], state_wkv[:, l], L)
    new_state_lru = jnp.stack(new_lru, axis=1)
    new_state_wkv = jnp.stack(new_wkv, axis=1)
    return (y_p, y_s, new_state_lru, new_state_wkv)
```

```python
import contextlib
import numpy as np
import ml_dtypes
import concourse.bass as bass
import concourse.mybir as mybir
from concourse.bass_utils import run_bass_kernel_spmd

F32 = mybir.dt.float32
BF16 = mybir.dt.bfloat16
AF = mybir.ActivationFunctionType
ALU = mybir.AluOpType
AX = mybir.AxisListType

D = 2048
NT = 4096
NSEG = 16
SEG = 256
NTILE = NT // 128
LW = 1024
RW = 1024
PRW = 3360
INW = 5408
FH = 5632
DEBUG = False
DEBUG_SS = False
REFINE = True
SAME_ENGINE_SYNC = True
NOSYNC_ENGS = ("pe",)


class Buf:
    __slots__ = ("name", "w", "r", "dsem", "dcnt")

    def __init__(self, name):
        self.name = name
        self.w = None
        self.r = []
        self.dsem = {}
        self.dcnt = {}


class Sched:
    ENG = ("pe", "dve", "act", "pool", "sp")

    def __init__(self, nc):
        self.nc = nc
        self.prog = {e: [] for e in self.ENG}
        self.cnt = {e: 0 for e in self.ENG}
        self.waited = {e: {} for e in self.ENG}
        self.sems = {}
        self.stack = []
        self.dma_state = {}
        self.epoch = 0
        self.ekey = {}
        for e in ("pe", "dve", "act", "pool"):
            self.ekey[e] = "E_" + e + "_0"
            self._mksem(self.ekey[e])
        self.finals = []
        self.free_dsems = {}
        self.stage_bufs = []
        self.ndsem = 0

    def keep(self):
        self.stage_bufs = []

    def _mksem(self, key):
        cm = self.nc.semaphore(key)
        h = cm.__enter__()
        self.stack.append(cm)
        self.sems[key] = h
        return h

    def _deps(self, eng, reads, writes):
        best = {}
        own = self.ekey.get(eng, "none")
        wd = self.waited[eng]

        def add(dep):
            k, v = dep
            if k == own and (not SAME_ENGINE_SYNC or eng in NOSYNC_ENGS):
                return
            if wd.get(k, 0) >= v:
                return
            if best.get(k, 0) < v:
                best[k] = v
        for b in reads:
            if b.w is not None:
                add(b.w)
        for b in writes:
            if b.w is not None:
                add(b.w)
            for d in b.r:
                add(d)
        waits = []
        for k, v in best.items():
            wd[k] = v
            waits.append((k, v))
        return waits

    def _mark(self, me, reads, writes):
        for b in reads:
            b.r.append(me)
            if len(b.r) > 64:
                mx = {}
                for k, v in b.r:
                    if mx.get(k, 0) < v:
                        mx[k] = v
                b.r = list(mx.items())
        for b in writes:
            b.w = me
            b.r = []

    def op(self, eng, fn, reads=(), writes=()):
        waits = self._deps(eng, reads, writes)
        self.cnt[eng] += 1
        me = (self.ekey[eng], self.cnt[eng])
        self.prog[eng].append((waits, [fn], (me[0], 1)))
        self._mark(me, reads, writes)
        return me

    def dma(self, eng, fns, reads=(), writes=(), sembuf=None, final=False):
        if not isinstance(fns, (list, tuple)):
            fns = [fns]
        if sembuf is None:
            sembuf = writes[0] if writes else reads[0]
        cls = eng
        if sembuf.dsem.get(cls) is None:
            pool_ = self.free_dsems.setdefault(cls, [])
            if pool_:
                sembuf.dsem[cls], sembuf.dcnt[cls] = pool_.pop()
            else:
                self.ndsem += 1
                sembuf.dsem[cls] = "D_%d" % self.ndsem
                sembuf.dcnt[cls] = 0
                self._mksem(sembuf.dsem[cls])
            self.stage_bufs.append((sembuf, cls))
        waits = self._deps(eng, reads, writes)
        dk, dc = sembuf.dsem[cls], sembuf.dcnt[cls]
        if dc > 0 and self.waited[eng].get(dk, 0) < dc:
            self.waited[eng][dk] = dc
            waits.append((dk, dc))
        sembuf.dcnt[cls] = dc + 16 * len(fns)
        me = (dk, sembuf.dcnt[cls])
        self.prog[eng].append((waits, list(fns), (me[0], 16)))
        self._mark(me, reads, writes)
        if final:
            self.finals.append(me)
        return me

    def barrier(self, bufs=()):
        deps = [(self.ekey[e], self.cnt[e]) for e in ("pe", "dve", "act", "pool") if self.cnt[e] > 0]
        for k in self.sems:
            if k.startswith("D_"):
                pass
        for (b, cls) in self.stage_bufs:
            if b.dsem.get(cls) is not None and b.dcnt[cls] > 0:
                deps.append((b.dsem[cls], b.dcnt[cls]))
        for e in self.ENG:
            waits = []
            for (k, v) in deps:
                if k == self.ekey.get(e):
                    continue
                if self.waited[e].get(k, 0) < v:
                    self.waited[e][k] = v
                    waits.append((k, v))
            if waits:
                self.prog[e].append((waits, [], None))
        for (b, cls) in self.stage_bufs:
            if b.dcnt[cls] < 20000:
                self.free_dsems.setdefault(cls, []).append((b.dsem[cls], b.dcnt[cls]))
            b.dsem[cls] = None
        self.stage_bufs = []
        self.epoch += 1
        for e in ("pe", "dve", "act", "pool"):
            self.ekey[e] = "E_%s_%d" % (e, self.epoch)
            self._mksem(self.ekey[e])
            self.cnt[e] = 0

    def emit(self):
        nc = self.nc
        engmap = {"pe": "tensor", "dve": "vector", "act": "scalar", "pool": "gpsimd", "sp": "sync"}
        finals = {}
        for k, v in self.finals:
            finals[k] = max(finals.get(k, 0), v)
        with nc.Block() as block:
            for e in self.ENG:
                prog = self.prog[e]

                def body(eng, prog=prog, e=e):
                    for (waits, fns, inc) in prog:
                        for (k, v) in waits:
                            eng.wait_ge(self.sems[k], v)
                        for fn in fns:
                            ins = fn(eng)
                            ins.then_inc(self.sems[inc[0]], inc[1])
                    if e == "sp":
                        for k, v in finals.items():
                            eng.wait_ge(self.sems[k], v)
                getattr(block, engmap[e])(body)

    def close(self):
        for cm in reversed(self.stack):
            cm.__exit__(None, None, None)


class Ctx:
    def __init__(self, nc):
        self.nc = nc
        self.S = Sched(nc)
        self.uid = 0
        self.rr = 0

    def sb(self, es, name, shape, dt):
        self.uid += 1
        t = es.enter_context(self.nc.sbuf_tensor("%s_%d" % (name, self.uid), list(shape), dt))
        return t, Buf("%s_%d" % (name, self.uid))

    def ps(self, es, name, shape, dt):
        self.uid += 1
        t = es.enter_context(self.nc.psum_tensor("%s_%d" % (name, self.uid), list(shape), dt))
        return t, Buf("%s_%d" % (name, self.uid))

    def ring(self, es, name, n, shape, dt):
        return [self.sb(es, "%s%d" % (name, i), shape, dt) for i in range(n)]


def build_program():
    nc = bass.Bass("TRN2", target_bir_lowering=False)
    C = Ctx(nc)
    S = C.S

    def din(name, shape, dt=F32):
        return nc.dram_tensor(name, list(shape), dt, kind="ExternalInput").ap()

    def dout(name, shape, dt=F32):
        return nc.dram_tensor(name, list(shape), dt, kind="ExternalOutput").ap()

    def dscr(name, shape, dt=F32):
        kind = "ExternalOutput" if DEBUG else "Internal"
        return nc.dram_tensor(name, list(shape), dt, kind=kind).ap()

    x = din("x", [NT, D])
    cvec = din("cvec", [128, 16])
    cmcol = din("cmcol", [128, 1])
    h0lru = din("h0lru", [128, 8, 2])
    s0wkv = din("s0wkv", [64, 2, 16, 64])
    w_mod = din("w_mod", [D, 6 * D])
    b_modT = din("b_modT", [128, 96])
    ncols = din("ncols", [128, 4, 16])
    w_in = din("w_in", [D, INW])
    convc = din("convc", [128, 8, 5])
    lru_wr = din("lru_wr", [2, 8, 128, 128])
    lru_wi = din("lru_wi", [2, 8, 128, 128])
    lru_bc = din("lru_bc", [128, 2, 8, 2])
    lru_lam = din("lru_lam", [128, 2, 8])
    rows = din("rows", [1, 3360 * 5 + 1024 * 9])
    rmask = din("rmask", [128, 4])
    w_up = din("w_up", [2, 64, RW])
    a_up = din("a_up", [2, 64, RW])
    g_up = din("g_up", [160, RW])
    w_out = din("w_out", [D, D])
    w_gu = din("w_gu", [D, 2 * FH])
    w_down = din("w_down", [FH, D])
    ident_in = din("ident", [128, 128])
    tri_in = din("tri", [128, 2, 128])
    gmask_in = din("gmask", [128, 2, 512])
    y_out = dout("y", [NT, D])
    lru_fin = dout("lru_fin", [128, 8, 2, 16])
    wkv_fin = dout("wkv_fin", [16, 2, 64, 16, 64])
    xlglT = dscr("xlglT", [2048, NT])
    projTok = dscr("projTok", [NT, PRW])
    mixT = dscr("mixT", [2048, NT], BF16)
    prep = {n: dscr("prep_" + n, [NT, RW]) for n in ("R", "KK", "V", "KD0", "KD1", "BD0", "BD1", "LD0", "LD1", "G", "BON")}
    YD = [dscr("YF", [NT, RW]), dscr("YB", [NT, RW])]
    X1 = dscr("X1", [NT, D])
    h2R = dscr("h2R", [NTILE, 128, 16, 128], BF16)
    mixR = dscr("mixR", [NTILE, 128, 8, 128], BF16)
    actT = dscr("actT", [FH, NT], BF16)
    FD = dscr("FD", [NT, D])
    DBGSS = dscr("DBGSS", [128, 64])

    def tb(name):
        return [Buf("%s_t%d" % (name, i)) for i in range(NTILE)]
    B_xlgl = [[Buf("xlgl_%d_%d" % (c, g)) for g in range(4)] for c in range(16)]
    B_proj = tb("proj")
    B_mixT = tb("mixT")
    B_mixL = [Buf("mixL%d" % h) for h in range(8)]
    B_prep = {n: tb("prep" + n) for n in prep}
    B_Y = [tb("YF"), tb("YB")]
    B_X1 = tb("X1")
    B_h2T = tb("h2T")
    B_actT = [[Buf("actT_%d_%d" % (f, g)) for g in range(4)] for f in range(44)]
    B_FD = tb("FD")

    ROWOFF = {}
    o = 0
    for n, w in (("mu", 3360), ("cmM1", 3360), ("cmP1", 3360), ("cmU", 3360), ("cmD", 3360),
                 ("w00", 1024), ("w01", 1024), ("a00", 1024), ("a01", 1024), ("kk", 1024), ("ka", 1024),
                 ("rk", 1024), ("lnw", 1024), ("lnb", 1024)):
        ROWOFF[n] = (o, w)
        o += w

    def rowap(n, lo=0, hi=None):
        o0, w = ROWOFF[n]
        if hi is None:
            hi = w
        return rows[0:1, o0 + lo:o0 + hi].broadcast_to([128, hi - lo])

    with contextlib.ExitStack() as es0:
        ident, b_ident = C.sb(es0, "ident", [128, 128], F32)
        identb, b_identb = C.sb(es0, "identb", [128, 128], BF16)
        ones, b_ones = C.sb(es0, "ones", [128, 128], F32)
        tri, b_tri = C.sb(es0, "tri", [128, 2, 128], F32)
        gmask, b_gmask = C.sb(es0, "gmask", [128, 2, 512], F32)
        cm, b_cm = C.sb(es0, "cm", [128, 1], F32)
        modT, b_modT_ = C.sb(es0, "modT", [128, 96], F32)
        ncl, b_ncl = C.sb(es0, "ncl", [128, 4, 16], F32)
        Am, b_Am = C.sb(es0, "Am", [128, 16], F32)
        Af, b_Af = C.sb(es0, "Af", [128, 16], F32)
        G1r, b_G1r = C.sb(es0, "G1r", [128, D], F32)
        G2r, b_G2r = C.sb(es0, "G2r", [128, D], F32)
        S.dma("sp", lambda e: e.dma_start(out=ident[:], in_=ident_in), writes=[b_ident])
        S.dma("sp", lambda e: e.dma_start(out=tri[:], in_=tri_in), writes=[b_tri])
        S.dma("sp", lambda e: e.dma_start(out=gmask[:], in_=gmask_in), writes=[b_gmask])
        S.dma("sp", lambda e: e.dma_start(out=cm[:], in_=cmcol), writes=[b_cm])
        S.dma("sp", lambda e: e.dma_start(out=ncl[:], in_=ncols), writes=[b_ncl])
        S.op("dve", lambda e: e.tensor_copy(out=identb[:], in_=ident[:]), reads=[b_ident], writes=[b_identb])
        S.op("dve", lambda e: e.memset(ones[:], 1.0), writes=[b_ones])
        S.keep()

        cast_engs = ["pool", "dve", "act"]

        def cast(out_ap, in_ap, reads, writes, eng=None):
            if eng is None:
                eng = cast_engs[C.rr % 3]
                C.rr += 1
            if eng == "act":
                S.op("act", lambda e: e.copy(out=out_ap, in_=in_ap), reads=reads, writes=writes)
            else:
                S.op(eng, lambda e: e.tensor_copy(out=out_ap, in_=in_ap), reads=reads, writes=writes)

        def rstd_from_ss(ss, b_ss, n, eps, eng_tmp):
            S.op("dve", lambda e: e.tensor_scalar(out=ss, in0=ss, scalar1=1.0 / n, scalar2=eps, op0=ALU.mult, op1=ALU.add), reads=[b_ss], writes=[b_ss])
            S.op("act", lambda e: e.activation(out=ss, in_=ss, func=AF.Sqrt), reads=[b_ss], writes=[b_ss])
            S.op("dve", lambda e: e.reciprocal(out=ss, in_=ss), reads=[b_ss], writes=[b_ss])

        with contextlib.ExitStack() as es:
            cv, b_cv = C.sb(es, "cv", [128, 16], F32)
            sc, b_sc = C.sb(es, "sc", [128, 16, 2], F32)
            bmt, b_bmt = C.sb(es, "bmt", [128, 96], F32)
            wm = C.ring(es, "wm", 2, [128, 16, 512], F32)
            pmod, b_pmod = C.ps(es, "pmod", [128, 96, 2], F32)
            pbc, b_pbc = C.ps(es, "pbc", [128, 512], F32)
            dg, b_dg = C.sb(es, "dg", [128, 128], F32)
            gc, b_gc = C.sb(es, "gc", [128, 2, 16], F32)
            S.dma("sp", lambda e: e.dma_start(out=cv[:], in_=cvec), writes=[b_cv])
            S.dma("sp", lambda e: e.dma_start(out=bmt[:], in_=b_modT), writes=[b_bmt])
            S.op("act", lambda e: e.activation(out=sc[:, :, 0], in_=cv[:], func=AF.Silu), reads=[b_cv], writes=[b_sc])
            S.op("act", lambda e: e.activation(out=sc[:, :, 1], in_=cv[:], func=AF.Silu), reads=[b_cv], writes=[b_sc])
            wmv = w_mod.rearrange("(dc p) f -> p dc f", p=128)
            for j in range(24):
                wt, bw = wm[j % 2]
                S.dma("sp", [lambda e, j=j, wt=wt, q=q: e.dma_start(out=wt[:, q * 4:(q + 1) * 4, :], in_=wmv[:, q * 4:(q + 1) * 4, j * 512:(j + 1) * 512]) for q in range(4)], writes=[bw])
                for f in range(4):
                    fc = j * 4 + f
                    for dc in range(16):
                        S.op("pe", lambda e, wt=wt, f=f, dc=dc, fc=fc: e.matmul(pmod[:, fc, :], lhsT=wt[:, dc, f * 128:(f + 1) * 128], rhs=sc[:, dc, :], start=(dc == 0), stop=(dc == 15)),
                             reads=[bw, b_sc], writes=[b_pmod])
            S.op("dve", lambda e: e.tensor_tensor(out=modT[:], in0=pmod[:, :, 0], in1=bmt[:], op=ALU.add), reads=[b_pmod, b_bmt], writes=[b_modT_])
            S.op("dve", lambda e: e.scalar_tensor_tensor(out=Am[:], in0=modT[:, 16:32], scalar=1.0, in1=ncl[:, 0, :], op0=ALU.add, op1=ALU.mult), reads=[b_modT_, b_ncl], writes=[b_Am])
            S.op("dve", lambda e: e.scalar_tensor_tensor(out=Af[:], in0=modT[:, 64:80], scalar=1.0, in1=ncl[:, 2, :], op0=ALU.add, op1=ALU.mult), reads=[b_modT_, b_ncl], writes=[b_Af])
            S.op("dve", lambda e: e.tensor_tensor(out=gc[:, 0, :], in0=modT[:, 32:48], in1=ncl[:, 1, :], op=ALU.mult), reads=[b_modT_, b_ncl], writes=[b_gc])
            S.op("dve", lambda e: e.tensor_tensor(out=gc[:, 1, :], in0=modT[:, 80:96], in1=ncl[:, 3, :], op=ALU.mult), reads=[b_modT_, b_ncl], writes=[b_gc])
            for which, (Gr, bGr) in enumerate(((G1r, b_G1r), (G2r, b_G2r))):
                for c in range(16):
                    S.op("dve", lambda e, which=which, c=c: e.tensor_scalar(out=dg[:], in0=ident[:], scalar1=gc[:, which, c:c + 1], scalar2=None, op0=ALU.mult), reads=[b_ident, b_gc], writes=[b_dg])
                    S.op("pe", lambda e: e.matmul(pbc[:, 0:128], lhsT=ones[:], rhs=dg[:], start=True, stop=True), reads=[b_ones, b_dg], writes=[b_pbc])
                    S.op("act", lambda e, Gr=Gr, c=c: e.copy(out=Gr[:, c * 128:(c + 1) * 128], in_=pbc[:, 0:128]), reads=[b_pbc], writes=[bGr])
            S.barrier([b for _, b in wm] + [b_cv, b_bmt])

        def load_w(stage_ring, wt, bw, src, kch, ncol, k0=0):
            srcv = src.rearrange("(kc p) f -> p kc f", p=128)
            for q in range(0, kch, 4):
                n = min(4, kch - q)
                st, bst = stage_ring[C.uid % len(stage_ring)]
                C.uid += 1
                S.dma("sp", lambda e, st=st, q=q, n=n: e.dma_start(out=st[:, 0:n, 0:ncol], in_=srcv[:, k0 + q:k0 + q + n, :]), writes=[bst])
                cast(wt[:, q:q + n, 0:ncol], st[:, 0:n, 0:ncol], [bst], [bw])

        def norm_transpose(xt, b_xt, hT_dst, b_hT, col0, Acol, shcol, bshc, es_bufs):
            junk, b_junk, ss, b_ss, xn, b_xn, pT, b_pT = es_bufs
            S.op("act", lambda e: e.activation(out=junk[:], in_=xt[:], func=AF.Square, accum_out=ss[:]), reads=[b_xt], writes=[b_junk, b_ss])
            rstd_from_ss(ss[:], b_ss, D, 1e-6, None)
            S.op("dve", lambda e: e.tensor_scalar(out=xn[:], in0=xt[:], scalar1=ss[:, 0:1], scalar2=None, op0=ALU.mult), reads=[b_xt, b_ss], writes=[b_xn])
            for c in range(16):
                S.op("pe", lambda e, c=c: e.transpose(out=pT[:, c * 128:(c + 1) * 128], in_=xn[:, c * 128:(c + 1) * 128], identity=identb[:]), reads=[b_xn, b_identb], writes=[b_pT])
            for c in range(16):
                S.op("act", lambda e, c=c: e.activation(out=hT_dst[:, c, col0:col0 + 128], in_=pT[:, c * 128:(c + 1) * 128], func=AF.Identity, scale=Acol[:, c:c + 1], bias=shcol[:, c:c + 1]),
                     reads=[b_pT] + bshc, writes=[b_hT])

        with contextlib.ExitStack() as es:
            xts = C.ring(es, "xt", 2, [128, D], F32)
            junk, b_junk = C.sb(es, "junk", [128, D], BF16)
            ss, b_ss = C.sb(es, "ss", [128, 1], F32)
            xn, b_xn = C.sb(es, "xn", [128, D], BF16)
            pT, b_pT = C.ps(es, "pT", [128, D], BF16)
            hTs = C.ring(es, "hT", 2, [128, 16, 1024], BF16)
            wst = C.ring(es, "wst", 4, [128, 4, 512], F32)
            wbs = C.ring(es, "wb", 2, [128, 16, 512], BF16)
            pmm = [C.ps(es, "pmm%d" % i, [128, 512], F32) for i in range(4)]
            ost = C.ring(es, "ost", 4, [128, 512], F32)
            nbufs = (junk, b_junk, ss, b_ss, xn, b_xn, pT, b_pT)
            oi = 0

            def norm_tile(g, t):
                tile = g * 8 + t
                hT, b_hT = hTs[g % 2]
                xt, b_xt = xts[tile % 2]
                S.dma("sp", lambda e, xt=xt, tile=tile: e.dma_start(out=xt[:], in_=x[tile * 128:(tile + 1) * 128, :]), writes=[b_xt])
                norm_transpose(xt, b_xt, hT, b_hT, t * 128, Am, modT[:, 0:16], [b_Am, b_modT_], nbufs)
            for t in range(8):
                norm_tile(0, t)
            wi = 0
            for g in range(4):
                hT, b_hT = hTs[g % 2]
                nxt = list(range(8)) if g < 3 else []
                for j in range(4):
                    wt, bw = wbs[wi % 2]
                    wi += 1
                    load_w(wst, wt, bw, w_in[:, j * 512:(j + 1) * 512], 16, 512)
                    for f in range(4):
                        fc = j * 4 + f
                        for tt in range(2):
                            pm, bpm = pmm[oi % 4]
                            o_t, bo = ost[oi % 4]
                            oi += 1
                            for dc in range(16):
                                S.op("pe", lambda e, pm=pm, wt=wt, f=f, dc=dc, tt=tt, hT=hT: e.matmul(pm[:], lhsT=wt[:, dc, f * 128:(f + 1) * 128], rhs=hT[:, dc, tt * 512:(tt + 1) * 512], start=(dc == 0), stop=(dc == 15)),
                                     reads=[bw, b_hT], writes=[bpm])
                            cast(o_t[:], pm[:], [bpm], [bo], eng=("act" if oi % 2 else "dve"))
                            S.dma("pool", lambda e, o_t=o_t, fc=fc, g=g, tt=tt: e.dma_start(out=xlglT[fc * 128:(fc + 1) * 128, g * 1024 + tt * 512:g * 1024 + (tt + 1) * 512], in_=o_t[:]),
                                  reads=[bo], writes=[B_xlgl[fc][g]], sembuf=bo)
                    if nxt:
                        norm_tile(g + 1, nxt.pop(0))
                for ct in range(7):
                    ncol = 512 if ct < 6 else PRW - 6 * 512
                    wt, bw = wbs[wi % 2]
                    wi += 1
                    load_w(wst, wt, bw, w_in[:, 2048 + ct * 512:2048 + ct * 512 + ncol], 16, ncol)
                    for t in range(8):
                        tile = g * 8 + t
                        pm, bpm = pmm[oi % 4]
                        o_t, bo = ost[oi % 4]
                        oi += 1
                        for dc in range(16):
                            S.op("pe", lambda e, pm=pm, wt=wt, dc=dc, t=t, ncol=ncol, hT=hT: e.matmul(pm[:, 0:ncol], lhsT=hT[:, dc, t * 128:(t + 1) * 128], rhs=wt[:, dc, 0:ncol], start=(dc == 0), stop=(dc == 15)),
                                 reads=[bw, b_hT], writes=[bpm])
                        cast(o_t[:, 0:ncol], pm[:, 0:ncol], [bpm], [bo], eng=("act" if oi % 2 else "dve"))
                        S.dma("pool", lambda e, o_t=o_t, tile=tile, ct=ct, ncol=ncol: e.dma_start(out=projTok[tile * 128:(tile + 1) * 128, ct * 512:ct * 512 + ncol], in_=o_t[:, 0:ncol]),
                              reads=[bo], writes=[B_proj[tile]], sembuf=bo)
                    if nxt:
                        norm_tile(g + 1, nxt.pop(0))
                while nxt:
                    norm_tile(g + 1, nxt.pop(0))
            S.barrier([b for _, b in xts] + [b for _, b in wst] + [b for _, b in ost])

        with contextlib.ExitStack() as es:
            X, bX = C.sb(es, "X", [128, NT], F32)
            GL, bGL = C.sb(es, "GL", [128, NT], F32)
            XC, bXC = C.sb(es, "XC", [128, NT], F32)
            XCB, bXCB = C.sb(es, "XCB", [128, NT], BF16)
            R1, bR1 = C.sb(es, "R1", [128, NT], F32)
            I1, bI1 = C.sb(es, "I1", [128, NT], F32)
            E1, bE1 = C.sb(es, "E1", [128, NT], F32)
            E2, bE2 = C.sb(es, "E2", [128, NT], F32)
            OB, bOB = C.sb(es, "OB", [128, NT], BF16)
            cc, bcc = C.sb(es, "cc", [128, 8, 5], F32)
            ccm, bccm = C.sb(es, "ccm", [128, 8, 5], F32)
            lbc, blbc = C.sb(es, "lbc", [128, 2, 8, 2], F32)
            lam, blam = C.sb(es, "lam", [128, 2, 8], F32)
            spc, bspc = C.sb(es, "spc", [128, 2, 8, 2], F32)
            tmpc, btmpc = C.sb(es, "tmpc", [128, 16], F32)
            z2, bz2 = C.sb(es, "z2", [128, 16], F32)
            pz, bpz = C.sb(es, "pz", [128, 16], F32)
            h0, bh0 = C.sb(es, "h0", [128, 8, 2], F32)
            fin, bfin = C.sb(es, "fin", [128, 8, 2, 16], F32)
            gst = C.ring(es, "gst", 2, [128, 128], F32)
            gwb = C.ring(es, "gwb", 4, [128, 128], BF16)
            pg = [C.ps(es, "pg%d" % i, [128, 512], F32) for i in range(4)]
            S.dma("sp", lambda e: e.dma_start(out=cc[:], in_=convc), writes=[bcc])
            S.dma("sp", lambda e: e.dma_start(out=lbc[:], in_=lru_bc), writes=[blbc])
            S.dma("sp", lambda e: e.dma_start(out=lam[:], in_=lru_lam), writes=[blam])
            S.dma("sp", lambda e: e.dma_start(out=h0[:], in_=h0lru), writes=[bh0])
            S.op("dve", lambda e: e.tensor_scalar(out=ccm[:], in0=cc[:], scalar1=cm[:, 0:1], scalar2=None, op0=ALU.mult), reads=[bcc, b_cm], writes=[bccm])
            lamf = lam[:].rearrange("p a b -> p (a b)")
            S.op("dve", lambda e: e.tensor_scalar(out=z2[:], in0=lamf, scalar1=-1.0, scalar2=None, op0=ALU.mult), reads=[blam], writes=[bz2])
            S.op("dve", lambda e: e.tensor_tensor(out=tmpc[:], in0=lamf, in1=z2[:], op=ALU.max), reads=[blam, bz2], writes=[btmpc])
            S.op("act", lambda e: e.activation(out=tmpc[:], in_=tmpc[:], func=AF.Exp, scale=-1.0), reads=[btmpc], writes=[btmpc])
            S.op("dve", lambda e: e.tensor_scalar(out=z2[:], in0=tmpc[:], scalar1=2.0, scalar2=None, op0=ALU.add), reads=[btmpc], writes=[bz2])
            S.op("dve", lambda e: e.reciprocal(out=z2[:], in_=z2[:]), reads=[bz2], writes=[bz2])
            S.op("dve", lambda e: e.tensor_tensor(out=tmpc[:], in0=tmpc[:], in1=z2[:], op=ALU.mult), reads=[btmpc, bz2], writes=[btmpc])
            S.op("dve", lambda e: e.tensor_tensor(out=z2[:], in0=tmpc[:], in1=tmpc[:], op=ALU.mult), reads=[btmpc], writes=[bz2])
            S.op("dve", lambda e: e.memset(pz[:], 1.0 / 13.0), writes=[bpz])
            for coef in (1.0 / 11, 1.0 / 9, 1.0 / 7, 1.0 / 5, 1.0 / 3, 1.0):
                S.op("dve", lambda e: e.tensor_tensor(out=pz[:], in0=pz[:], in1=z2[:], op=ALU.mult), reads=[bpz, bz2], writes=[bpz])
                S.op("dve", lambda e, coef=coef: e.tensor_scalar(out=pz[:], in0=pz[:], scalar1=float(coef), scalar2=None, op0=ALU.add), reads=[bpz], writes=[bpz])
            S.op("dve", lambda e: e.tensor_tensor(out=pz[:], in0=pz[:], in1=tmpc[:], op=ALU.mult), reads=[bpz, btmpc], writes=[bpz])
            S.op("dve", lambda e: e.tensor_scalar(out=z2[:], in0=lamf, scalar1=-1.0, scalar2=0.0, op0=ALU.mult, op1=ALU.max), reads=[blam], writes=[bz2])
            S.op("dve", lambda e: e.scalar_tensor_tensor(out=pz[:], in0=pz[:], scalar=2.0, in1=z2[:], op0=ALU.mult, op1=ALU.add), reads=[bpz, bz2], writes=[bpz])
            spf = spc[:].rearrange("p a b c -> p (a b) c")
            S.op("dve", lambda e: e.tensor_scalar(out=spf[:, :, 0], in0=pz[:], scalar1=-8.0, scalar2=None, op0=ALU.mult), reads=[bpz], writes=[bspc])
            S.op("dve", lambda e: e.tensor_scalar(out=spf[:, :, 1], in0=pz[:], scalar1=-16.0, scalar2=None, op0=ALU.mult), reads=[bpz], writes=[bspc])

            def seg(ap):
                return ap.rearrange("p (s t) -> p s t", t=SEG)
            gi = 0
            for h in range(8):
                S.dma("sp", [lambda e, h=h, g=g: e.dma_start(out=X[:, g * 1024:(g + 1) * 1024], in_=xlglT[h * 128:(h + 1) * 128, g * 1024:(g + 1) * 1024]) for g in range(4)],
                      reads=[B_xlgl[h][g] for g in range(4)], writes=[bX])
                S.dma("sp", [lambda e, h=h, g=g: e.dma_start(out=GL[:, g * 1024:(g + 1) * 1024], in_=xlglT[(8 + h) * 128:(9 + h) * 128, g * 1024:(g + 1) * 1024]) for g in range(4)],
                      reads=[B_xlgl[8 + h][g] for g in range(4)], writes=[bGL])
                S.op("dve", lambda e, h=h: e.tensor_scalar(out=XC[:], in0=X[:], scalar1=cc[:, h, 2:3], scalar2=cc[:, h, 4:5], op0=ALU.mult, op1=ALU.add), reads=[bX, bcc], writes=[bXC])
                Xs, XCs = seg(X[:]), seg(XC[:])
                for (tap, dlt) in ((0, -2), (1, -1), (3, 1)):
                    if dlt < 0:
                        o_v, i_v = XCs[:, :, -dlt:], Xs[:, :, :SEG + dlt]
                    else:
                        o_v, i_v = XCs[:, :, :SEG - dlt], Xs[:, :, dlt:]
                    S.op("dve", lambda e, h=h, tap=tap, o_v=o_v, i_v=i_v: e.scalar_tensor_tensor(out=o_v, in0=i_v, scalar=cc[:, h, tap:tap + 1], in1=o_v, op0=ALU.mult, op1=ALU.add), reads=[bX, bXC, bcc], writes=[bXC])
                fix = ((1, XCs[:, 1:, 0:1], Xs[:, :15, 255:256]), (0, XCs[:, 1:, 0:1], Xs[:, :15, 254:255]), (0, XCs[:, 1:, 1:2], Xs[:, :15, 255:256]), (3, XCs[:, :15, 255:256], Xs[:, 1:, 0:1]))
                for (tap, o_v, i_v) in fix:
                    S.op("dve", lambda e, h=h, tap=tap, o_v=o_v, i_v=i_v: e.scalar_tensor_tensor(out=o_v, in0=i_v, scalar=ccm[:, h, tap:tap + 1], in1=o_v, op0=ALU.mult, op1=ALU.add), reads=[bX, bXC, bccm], writes=[bXC])
                S.op("act", lambda e: e.copy(out=XCB[:], in_=XC[:]), reads=[bXC], writes=[bXCB])
                HS = []
                for d in range(2):
                    Ebuf, bE = (E1, bE1) if d == 0 else (E2, bE2)
                    for (wsrc, dst, bdst, bi) in ((lru_wr, R1, bR1, 0), (lru_wi, I1, bI1, 1)):
                        gs, bgs = gst[gi % 2]
                        gw, bgw = gwb[gi % 4]
                        gi += 1
                        S.dma("sp", lambda e, gs=gs, wsrc=wsrc, d=d, h=h: e.dma_start(out=gs[:], in_=wsrc[d, h]), writes=[bgs])
                        cast(gw[:], gs[:], [bgs], [bgw], eng="pool")
                        for tt in range(8):
                            pm, bpm = pg[tt % 4]
                            S.op("pe", lambda e, pm=pm, gw=gw, tt=tt: e.matmul(pm[:], lhsT=gw[:], rhs=XCB[:, tt * 512:(tt + 1) * 512], start=True, stop=True), reads=[bgw, bXCB], writes=[bpm])
                            S.op("act", lambda e, pm=pm, dst=dst, tt=tt, d=d, h=h, bi=bi: e.activation(out=dst[:, tt * 512:(tt + 1) * 512], in_=pm[:], func=AF.Sigmoid, bias=lbc[:, d, h, bi:bi + 1]),
                                 reads=[bpm, blbc], writes=[bdst])
                    S.op("act", lambda e, Ebuf=Ebuf, d=d, h=h: e.activation(out=Ebuf[:], in_=R1[:], func=AF.Exp, scale=spc[:, d, h, 1:2]), reads=[bR1, bspc], writes=[bE])
                    S.op("act", lambda e, d=d, h=h: e.activation(out=R1[:], in_=R1[:], func=AF.Exp, scale=spc[:, d, h, 0:1]), reads=[bR1, bspc], writes=[bR1])
                    S.op("dve", lambda e, Ebuf=Ebuf: e.tensor_scalar(out=Ebuf[:], in0=Ebuf[:], scalar1=-1.0, scalar2=1.0, op0=ALU.mult, op1=ALU.add), reads=[bE], writes=[bE])
                    S.op("act", lambda e, Ebuf=Ebuf: e.activation(out=Ebuf[:], in_=Ebuf[:], func=AF.Sqrt), reads=[bE], writes=[bE])
                    S.op("pool", lambda e: e.tensor_tensor(out=I1[:], in0=I1[:], in1=XC[:], op=ALU.mult), reads=[bI1, bXC], writes=[bI1])
                    S.op("dve", lambda e, Ebuf=Ebuf: e.tensor_tensor(out=I1[:], in0=I1[:], in1=Ebuf[:], op=ALU.mult), reads=[bI1, bE], writes=[bI1])
                    As = seg(R1[:])
                    if d == 0:
                        S.op("dve", lambda e, As=As: e.tensor_scalar(out=As[:, 1:, 0:1], in0=As[:, 1:, 0:1], scalar1=cm[:, 0:1], scalar2=None, op0=ALU.mult), reads=[bR1, b_cm], writes=[bR1])
                        S.op("dve", lambda e, Ebuf=Ebuf, h=h: e.tensor_tensor_scan(out=Ebuf[:], data0=R1[:], data1=I1[:], initial=h0[:, h, 0:1], op0=ALU.mult, op1=ALU.add), reads=[bR1, bI1, bh0], writes=[bE])
                        S.op("pool", lambda e, Ebuf=Ebuf, h=h: e.tensor_copy(out=fin[:, h, 0, :], in_=seg(Ebuf[:])[:, :, 255]), reads=[bE], writes=[bfin])
                    else:
                        S.op("dve", lambda e, As=As: e.tensor_scalar(out=As[:, :15, 255:256], in0=As[:, :15, 255:256], scalar1=cm[:, 0:1], scalar2=None, op0=ALU.mult), reads=[bR1, b_cm], writes=[bR1])
                        S.op("dve", lambda e, Ebuf=Ebuf, h=h: e.tensor_tensor_scan(out=Ebuf[:, ::-1], data0=R1[:, ::-1], data1=I1[:, ::-1], initial=h0[:, h, 1:2], op0=ALU.mult, op1=ALU.add), reads=[bR1, bI1, bh0], writes=[bE])
                        S.op("pool", lambda e, Ebuf=Ebuf, h=h: e.tensor_copy(out=fin[:, h, 1, :], in_=seg(Ebuf[:])[:, :, 0]), reads=[bE], writes=[bfin])
                S.op("pool", lambda e: e.tensor_tensor(out=E1[:], in0=E1[:], in1=E2[:], op=ALU.add), reads=[bE1, bE2], writes=[bE1])
                S.op("act", lambda e: e.activation(out=R1[:], in_=GL[:], func=AF.Square), reads=[bGL], writes=[bR1])
                S.op("dve", lambda e: e.tensor_scalar(out=R1[:], in0=R1[:], scalar1=0.044715, scalar2=1.0, op0=ALU.mult, op1=ALU.add), reads=[bR1], writes=[bR1])
                S.op("dve", lambda e: e.tensor_tensor(out=R1[:], in0=R1[:], in1=GL[:], op=ALU.mult), reads=[bR1, bGL], writes=[bR1])
                S.op("act", lambda e: e.activation(out=R1[:], in_=R1[:], func=AF.Sigmoid, scale=1.5957691216057308), reads=[bR1], writes=[bR1])
                S.op("pool", lambda e: e.tensor_tensor(out=R1[:], in0=R1[:], in1=GL[:], op=ALU.mult), reads=[bR1, bGL], writes=[bR1])
                S.op("dve", lambda e: e.tensor_tensor(out=OB[:], in0=R1[:], in1=E1[:], op=ALU.mult), reads=[bR1, bE1], writes=[bOB])
                S.dma("pool", lambda e, h=h: e.dma_start(out=mixT[h * 128:(h + 1) * 128, :], in_=OB[:]), reads=[bOB], writes=[B_mixL[h]], sembuf=bOB)
            S.dma("pool", lambda e: e.dma_start(out=lru_fin, in_=fin[:]), reads=[bfin], sembuf=bfin, final=True)
            S.barrier([bX, bGL, bOB, bfin, bcc, blbc, blam, bh0] + [b for _, b in gst])

        with contextlib.ExitStack() as es:
            mu_r, bmu = C.sb(es, "mu_r", [128, PRW], F32)
            cM1, bcM1 = C.sb(es, "cM1", [128, 1680], F32)
            cP1, bcP1 = C.sb(es, "cP1", [128, 2520], F32)
            cU, bcU = C.sb(es, "cU", [128, 840], F32)
            cD, bcD = C.sb(es, "cD", [128, 840], F32)
            rws = {}
            for n in ("w00", "w01", "a00", "a01", "kk", "ka", "rk"):
                rws[n] = C.sb(es, "row_" + n, [128, RW], F32)
            omka, bomka = C.sb(es, "omka", [128, RW], F32)
            rm, brm = C.sb(es, "rm", [128, 4], F32)
            P0s = None
            SM1s = C.ring(es, "SM1", 1, [128, 1680], F32)
            SP1s = C.ring(es, "SP1", 1, [128, 2520], F32)
            SUs = C.ring(es, "SU", 1, [128, 840], F32)
            SDs = C.ring(es, "SD", 1, [128, 840], F32)
            Mx, bMx = C.sb(es, "Mx", [128, PRW], F32)
            lb, blb = C.sb(es, "lb", [128, 288], BF16)
            lT, blT = C.sb(es, "lT", [128, 4, 128], BF16)
            pTl, bpTl = C.ps(es, "pTl", [128, 4, 128], BF16)
            pz_, bpz_ = C.ps(es, "pzz", [128, 1024], F32)
            pa_, bpa_ = C.ps(es, "paa", [128, 1024], F32)
            lw = {}
            lwst, blwst = C.sb(es, "lwst", [128, RW], F32)
            for n in ("wu0", "wu1", "au0", "au1", "gu0", "gu1"):
                lw[n] = C.sb(es, "lw_" + n, [128, RW], BF16)
            outs = {n: C.ring(es, "o" + n, (2 if n in ("KD", "BD", "LD") else 1), [128, RW], F32) for n in ("KK", "KD", "BD", "LD", "G", "BON")}
            AD, bAD = C.sb(es, "AD", [128, RW], F32)
            t16, bt16 = C.sb(es, "t16", [128, 16], F32)
            t16b, bt16b = C.sb(es, "t16b", [128, 16], F32)
            tmpR, btmpR = C.sb(es, "tmpR", [128, RW], F32)

            S.dma("sp", lambda e: e.dma_start(out=mu_r[:], in_=rowap("mu")), writes=[bmu])
            S.dma("sp", lambda e: e.dma_start(out=rm[:], in_=rmask), writes=[brm])
            for (ct, bct, nm, lo, hi) in ((cM1, bcM1, "cmM1", 0, 1680), (cP1, bcP1, "cmP1", 840, 3360), (cU, bcU, "cmU", 1680, 2520), (cD, bcD, "cmD", 2520, 3360)):
                S.dma("sp", lambda e, ct=ct, nm=nm, lo=lo, hi=hi: e.dma_start(out=ct[:], in_=rowap(nm, lo, hi)), writes=[bct])
                S.op("dve", lambda e, ct=ct, lo=lo, hi=hi: e.tensor_tensor(out=ct[:], in0=ct[:], in1=mu_r[:, lo:hi], op=ALU.mult), reads=[bct, bmu], writes=[bct])
            S.op("dve", lambda e: e.tensor_scalar(out=mu_r[:], in0=mu_r[:], scalar1=-1.0, scalar2=1.0, op0=ALU.mult, op1=ALU.add), reads=[bmu, bcM1, bcP1, bcU, bcD], writes=[bmu])
            for n in rws:
                S.dma("sp", lambda e, n=n: e.dma_start(out=rws[n][0][:], in_=rowap(n)), writes=[rws[n][1]])
            S.op("dve", lambda e: e.tensor_scalar(out=omka[:], in0=rws["ka"][0][:], scalar1=-1.0, scalar2=1.0, op0=ALU.mult, op1=ALU.add), reads=[rws["ka"][1]], writes=[bomka])
            for (n, src, p0, p1) in (("wu0", w_up[0], 0, 64), ("wu1", w_up[1], 0, 64), ("au0", a_up[0], 64, 128), ("au1", a_up[1], 64, 128), ("gu0", g_up[0:128, :], 0, 128), ("gu1", g_up[128:160, :], 0, 32)):
                S.dma("sp", lambda e, src=src, p0=p0, p1=p1: e.dma_start(out=lwst[p0:p1, :], in_=src), writes=[blwst])
                S.op("dve", lambda e, n=n, p0=p0, p1=p1: e.tensor_copy(out=lw[n][0][p0:p1, :], in_=lwst[p0:p1, :]), reads=[blwst], writes=[lw[n][1]])

            def h3(ap):
                return ap.rearrange("p (h k) -> p h k", k=64)

            def bc16(ap):
                return ap.unsqueeze(2).broadcast_to([128, 16, 64])
            for tile in range(NTILE):
                par = tile % 2
                t0 = tile * 128
                P0, bP0 = Mx, bMx
                SM1, bSM1 = SM1s[0]
                SP1, bSP1 = SP1s[0]
                SU, bSU = SUs[0]
                SD, bSD = SDs[0]
                S.dma("sp", lambda e, P0=P0, t0=t0: e.dma_start(out=P0[:], in_=projTok[t0:t0 + 128, :]), reads=[B_proj[tile]], writes=[bP0])
                if tile == 0:
                    S.op("pool", lambda e, SM1=SM1: e.memset(SM1[:], 0.0), writes=[bSM1])
                    S.dma("sp", lambda e, SM1=SM1: e.dma_start(out=SM1[1:128, :], in_=projTok[0:127, 0:1680]), reads=[B_proj[0]], writes=[bSM1])
                else:
                    S.dma("sp", lambda e, SM1=SM1, t0=t0: e.dma_start(out=SM1[:], in_=projTok[t0 - 1:t0 + 127, 0:1680]), reads=[B_proj[tile - 1], B_proj[tile]], writes=[bSM1])
                if tile == NTILE - 1:
                    S.op("pool", lambda e, SP1=SP1: e.memset(SP1[:], 0.0), writes=[bSP1])
                    S.dma("sp", lambda e, SP1=SP1, t0=t0: e.dma_start(out=SP1[0:127, :], in_=projTok[t0 + 1:t0 + 128, 840:3360]), reads=[B_proj[tile]], writes=[bSP1])
                else:
                    S.dma("sp", lambda e, SP1=SP1, t0=t0: e.dma_start(out=SP1[:], in_=projTok[t0 + 1:t0 + 129, 840:3360]), reads=[B_proj[tile], B_proj[tile + 1]], writes=[bSP1])
                if tile == 0:
                    S.op("pool", lambda e, SU=SU: e.memset(SU[:], 0.0), writes=[bSU])
                    S.dma("sp", lambda e, SU=SU: e.dma_start(out=SU[64:128, :], in_=projTok[0:64, 1680:2520]), reads=[B_proj[0]], writes=[bSU])
                else:
                    S.dma("sp", lambda e, SU=SU, t0=t0: e.dma_start(out=SU[:], in_=projTok[t0 - 64:t0 + 64, 1680:2520]), reads=[B_proj[tile - 1], B_proj[tile]], writes=[bSU])
                if tile == NTILE - 1:
                    S.op("pool", lambda e, SD=SD: e.memset(SD[:], 0.0), writes=[bSD])
                    S.dma("sp", lambda e, SD=SD, t0=t0: e.dma_start(out=SD[0:64, :], in_=projTok[t0 + 64:t0 + 128, 2520:3360]), reads=[B_proj[tile]], writes=[bSD])
                else:
                    S.dma("sp", lambda e, SD=SD, t0=t0: e.dma_start(out=SD[:], in_=projTok[t0 + 64:t0 + 192, 2520:3360]), reads=[B_proj[tile], B_proj[tile + 1]], writes=[bSD])
                S.op("dve", lambda e: e.tensor_tensor(out=Mx[:], in0=Mx[:], in1=mu_r[:], op=ALU.mult), reads=[bMx, bmu], writes=[bMx])
                S.op("dve", lambda e, SM1=SM1, par=par: e.scalar_tensor_tensor(out=SM1[:], in0=SM1[:], scalar=rm[:, par:par + 1], in1=cM1[:], op0=ALU.mult, op1=ALU.mult), reads=[bSM1, brm, bcM1], writes=[bSM1])
                S.op("dve", lambda e, SM1=SM1: e.tensor_tensor(out=Mx[:, 0:1680], in0=Mx[:, 0:1680], in1=SM1[:], op=ALU.add), reads=[bSM1, bMx], writes=[bMx])
                S.op("dve", lambda e, SP1=SP1, par=par: e.scalar_tensor_tensor(out=SP1[:], in0=SP1[:], scalar=rm[:, 2 + par:3 + par], in1=cP1[:], op0=ALU.mult, op1=ALU.mult), reads=[bSP1, brm, bcP1], writes=[bSP1])
                S.op("dve", lambda e, SP1=SP1: e.tensor_tensor(out=Mx[:, 840:3360], in0=Mx[:, 840:3360], in1=SP1[:], op=ALU.add), reads=[bSP1, bMx], writes=[bMx])
                S.op("pool", lambda e, SU=SU: e.tensor_tensor(out=SU[:], in0=SU[:], in1=cU[:], op=ALU.mult), reads=[bSU, bcU], writes=[bSU])
                S.op("dve", lambda e, SU=SU: e.tensor_tensor(out=Mx[:, 1680:2520], in0=Mx[:, 1680:2520], in1=SU[:], op=ALU.add), reads=[bSU, bMx], writes=[bMx])
                S.op("pool", lambda e, SD=SD: e.tensor_tensor(out=SD[:], in0=SD[:], in1=cD[:], op=ALU.mult), reads=[bSD, bcD], writes=[bSD])
                S.op("dve", lambda e, SD=SD: e.tensor_tensor(out=Mx[:, 2520:3360], in0=Mx[:, 2520:3360], in1=SD[:], op=ALU.add), reads=[bSD, bMx], writes=[bMx])
                r_ap, k_ap, v_ap = Mx[:, 0:1024], Mx[:, 1024:2048], Mx[:, 2048:3072]
                S.dma("pool", lambda e, t0=t0: e.dma_start(out=prep["R"][t0:t0 + 128, :], in_=Mx[:, 0:1024]), reads=[bMx], writes=[B_prep["R"][tile]], sembuf=bMx)
                S.dma("pool", lambda e, t0=t0: e.dma_start(out=prep["V"][t0:t0 + 128, :], in_=Mx[:, 2048:3072]), reads=[bMx], writes=[B_prep["V"][tile]], sembuf=bMx)
                S.op("act", lambda e: e.activation(out=lb[:, 0:64], in_=Mx[:, 3072:3136], func=AF.Tanh), reads=[bMx], writes=[blb])
                S.op("act", lambda e: e.copy(out=lb[:, 64:128], in_=Mx[:, 3136:3200]), reads=[bMx], writes=[blb])
                S.op("act", lambda e: e.activation(out=lb[:, 128:288], in_=Mx[:, 3200:3360], func=AF.Sigmoid), reads=[bMx], writes=[blb])
                S.op("pe", lambda e: e.transpose(out=pTl[:, 0, :], in_=lb[:, 0:128], identity=identb[:]), reads=[blb, b_identb], writes=[bpTl])
                S.op("pe", lambda e: e.transpose(out=pTl[:, 1, :], in_=lb[:, 128:256], identity=identb[:]), reads=[blb, b_identb], writes=[bpTl])
                S.op("pe", lambda e: e.transpose(out=pTl[0:32, 2, :], in_=lb[:, 256:288], identity=identb[:]), reads=[blb, b_identb], writes=[bpTl])
                S.op("dve", lambda e: e.tensor_copy(out=lT[:, 0:2, :], in_=pTl[:, 0:2, :]), reads=[bpTl], writes=[blT])
                S.op("dve", lambda e: e.tensor_copy(out=lT[0:32, 2, :], in_=pTl[0:32, 2, :]), reads=[bpTl], writes=[blT])
                oG, boG = outs["G"][0]
                for hh in range(2):
                    S.op("pe", lambda e, hh=hh: e.matmul(pz_[:, hh * 512:(hh + 1) * 512], lhsT=lT[:, 1, :], rhs=lw["gu0"][0][:, hh * 512:(hh + 1) * 512], start=True, stop=False), reads=[blT, lw["gu0"][1]], writes=[bpz_])
                    S.op("pe", lambda e, hh=hh: e.matmul(pz_[:, hh * 512:(hh + 1) * 512], lhsT=lT[0:32, 2, :], rhs=lw["gu1"][0][0:32, hh * 512:(hh + 1) * 512], start=False, stop=True), reads=[blT, lw["gu1"][1]], writes=[bpz_])
                S.op("act", lambda e, oG=oG: e.copy(out=oG[:], in_=pz_[:]), reads=[bpz_], writes=[boG])
                S.dma("pool", lambda e, oG=oG, t0=t0: e.dma_start(out=prep["G"][t0:t0 + 128, :], in_=oG[:]), reads=[boG], writes=[B_prep["G"][tile]], sembuf=boG)
                oKK, boKK = outs["KK"][0]
                S.op("dve", lambda e, oKK=oKK: e.tensor_tensor(out=oKK[:], in0=k_ap, in1=rws["kk"][0][:], op=ALU.mult), reads=[bMx, rws["kk"][1]], writes=[boKK])
                S.op("act", lambda e, oKK=oKK: e.activation(out=tmpR[:], in_=oKK[:], func=AF.Square), reads=[boKK], writes=[btmpR])
                S.op("dve", lambda e: e.tensor_reduce(out=t16[:], in_=h3(tmpR[:]), axis=AX.X, op=ALU.add), reads=[btmpR], writes=[bt16])
                S.op("dve", lambda e: e.tensor_scalar(out=t16[:], in0=t16[:], scalar1=1e-24, scalar2=None, op0=ALU.max), reads=[bt16], writes=[bt16])
                S.op("act", lambda e: e.activation(out=t16[:], in_=t16[:], func=AF.Sqrt), reads=[bt16], writes=[bt16])
                S.op("dve", lambda e: e.reciprocal(out=t16[:], in_=t16[:]), reads=[bt16], writes=[bt16])
                S.op("dve", lambda e, oKK=oKK: e.tensor_tensor(out=h3(oKK[:]), in0=h3(oKK[:]), in1=bc16(t16[:]), op=ALU.mult), reads=[boKK, bt16], writes=[boKK])
                S.dma("pool", lambda e, oKK=oKK, t0=t0: e.dma_start(out=prep["KK"][t0:t0 + 128, :], in_=oKK[:]), reads=[boKK], writes=[B_prep["KK"][tile]], sembuf=boKK)
                oB, boB = outs["BON"][0]
                S.op("pool", lambda e: e.tensor_tensor(out=tmpR[:], in0=r_ap, in1=k_ap, op=ALU.mult), reads=[bMx], writes=[btmpR])
                S.op("dve", lambda e: e.tensor_tensor(out=tmpR[:], in0=tmpR[:], in1=rws["rk"][0][:], op=ALU.mult), reads=[btmpR, rws["rk"][1]], writes=[btmpR])
                S.op("dve", lambda e: e.tensor_reduce(out=t16b[:], in_=h3(tmpR[:]), axis=AX.X, op=ALU.add), reads=[btmpR], writes=[bt16b])
                S.op("dve", lambda e, oB=oB: e.tensor_tensor(out=h3(oB[:]), in0=h3(v_ap), in1=bc16(t16b[:]), op=ALU.mult), reads=[bMx, bt16b], writes=[boB])
                S.dma("pool", lambda e, oB=oB, t0=t0: e.dma_start(out=prep["BON"][t0:t0 + 128, :], in_=oB[:]), reads=[boB], writes=[B_prep["BON"][tile]], sembuf=boB)
                for d in range(2):
                    ds_ = str(d)
                    for hh in range(2):
                        S.op("pe", lambda e, hh=hh, ds_=ds_: e.matmul(pz_[:, hh * 512:(hh + 1) * 512], lhsT=lT[0:64, 0, :], rhs=lw["wu" + ds_][0][0:64, hh * 512:(hh + 1) * 512], start=True, stop=True), reads=[blT, lw["wu" + ds_][1]], writes=[bpz_])
                        S.op("pe", lambda e, hh=hh, ds_=ds_: e.matmul(pa_[:, hh * 512:(hh + 1) * 512], lhsT=lT[64:128, 0, :], rhs=lw["au" + ds_][0][64:128, hh * 512:(hh + 1) * 512], start=True, stop=True), reads=[blT, lw["au" + ds_][1]], writes=[bpa_])
                    oLD, boLD = outs["LD"][d]
                    S.op("dve", lambda e, oLD=oLD, ds_=ds_: e.tensor_tensor(out=oLD[:], in0=pz_[:], in1=rws["w0" + ds_][0][:], op=ALU.add), reads=[bpz_, rws["w0" + ds_][1]], writes=[boLD])
                    S.op("act", lambda e, oLD=oLD: e.activation(out=oLD[:], in_=oLD[:], func=AF.Sigmoid), reads=[boLD], writes=[boLD])
                    S.op("pool", lambda e, oLD=oLD: e.tensor_scalar(out=oLD[:], in0=oLD[:], scalar1=-0.6065306597126334, scalar2=None, op0=ALU.mult), reads=[boLD], writes=[boLD])
                    S.dma("pool", lambda e, oLD=oLD, t0=t0, ds_=ds_: e.dma_start(out=prep["LD" + ds_][t0:t0 + 128, :], in_=oLD[:]), reads=[boLD], writes=[B_prep["LD" + ds_][tile]], sembuf=boLD)
                    S.op("dve", lambda e, ds_=ds_: e.tensor_tensor(out=AD[:], in0=pa_[:], in1=rws["a0" + ds_][0][:], op=ALU.add), reads=[bpa_, rws["a0" + ds_][1]], writes=[bAD])
                    S.op("act", lambda e: e.activation(out=AD[:], in_=AD[:], func=AF.Sigmoid), reads=[bAD], writes=[bAD])
                    oBD, boBD = outs["BD"][d]
                    S.op("pool", lambda e, oBD=oBD, oKK=oKK: e.tensor_tensor(out=oBD[:], in0=oKK[:], in1=AD[:], op=ALU.mult), reads=[boKK, bAD], writes=[boBD])
                    S.dma("pool", lambda e, oBD=oBD, t0=t0, ds_=ds_: e.dma_start(out=prep["BD" + ds_][t0:t0 + 128, :], in_=oBD[:]), reads=[boBD], writes=[B_prep["BD" + ds_][tile]], sembuf=boBD)
                    oKD, boKD = outs["KD"][d]
                    S.op("dve", lambda e: e.tensor_tensor(out=tmpR[:], in0=AD[:], in1=rws["ka"][0][:], op=ALU.mult), reads=[bAD, rws["ka"][1]], writes=[btmpR])
                    S.op("pool", lambda e: e.tensor_tensor(out=tmpR[:], in0=tmpR[:], in1=omka[:], op=ALU.add), reads=[btmpR, bomka], writes=[btmpR])
                    S.op("dve", lambda e, oKD=oKD: e.tensor_tensor(out=oKD[:], in0=tmpR[:], in1=k_ap, op=ALU.mult), reads=[btmpR, bMx], writes=[boKD])
                    S.dma("pool", lambda e, oKD=oKD, t0=t0, ds_=ds_: e.dma_start(out=prep["KD" + ds_][t0:t0 + 128, :], in_=oKD[:]), reads=[boKD], writes=[B_prep["KD" + ds_][tile]], sembuf=boKD)
            allb = [bMx, blwst] + [b for r_ in (SM1s, SP1s, SUs, SDs) for _, b in r_] + [b for n in outs for _, b in outs[n]] + [rws[n][1] for n in rws] + [bmu, brm, bcM1, bcP1, bcU, bcD]
            S.barrier(allb)

        with contextlib.ExitStack() as es:
            names = ("R", "KK", "V", "KD", "BD", "LD")
            IN = {n: C.ring(es, "in" + n, 2, [128, RW], F32) for n in names}
            CL, bCL = C.sb(es, "CL", [128, RW], F32)
            TOT, bTOT = C.sb(es, "TOT", [128, RW], F32)
            EX, bEX = C.sb(es, "EX", [128, RW], F32)
            Ee, bEe = C.sb(es, "Ee", [128, RW], F32)
            SCb = {n: C.sb(es, "sc" + n, [128, RW], BF16) for n in ("kap", "rt", "bet", "kt", "khat", "bhat", "V")}
            pcl, bpcl = C.ps(es, "pcl", [128, 1024], F32)
            ptr, bptr = C.ps(es, "ptr", [128, 4, 512], BF16)
            pgr, bpgr = C.ps(es, "pgr", [128, 1024], F32)
            ptt, bptt = C.ps(es, "ptt", [128, 512], F32)
            pch, bpch = C.ps(es, "pch", [128, 512], F32)
            FT, bFT = C.sb(es, "FT", [64, 4, 512], BF16)
            GM, bGM = C.sb(es, "GM", [128, 4, 512], BF16)
            QQ, bQQ = C.sb(es, "QQ", [128, 4, 256], BF16)
            TTb, bTTb = C.sb(es, "TTb", [128, 4, 128], BF16)
            Xb, bXb = C.sb(es, "Xb", [128, 4, 64], BF16)
            Ub, bUb = C.sb(es, "Ub", [128, 4, 64], BF16)
            U0b, bU0b = C.sb(es, "U0b", [128, 4, 64], BF16)
            X32, bX32 = C.sb(es, "X32", [128, 4, 64], F32)
            U032, bU032 = C.sb(es, "U032", [128, 4, 64], F32)
            Yacc = C.ring(es, "Yacc", 2, [128, RW], F32)
            Ast, bAst = C.sb(es, "Ast", [64, 2, 16, 64], F32)
            Abf, bAbf = C.sb(es, "Abf", [64, 2, 16, 64], BF16)
            PCc, bPCc = C.sb(es, "PCc", [64, 16, 2], F32)
            S.dma("sp", lambda e: e.dma_start(out=Ast[:], in_=s0wkv), writes=[bAst])
            S.op("dve", lambda e: e.tensor_copy(out=Abf[:], in_=Ast[:]), reads=[bAst], writes=[bAbf])
            for step in range(NTILE):
                if step > 0 and step % 8 == 0:
                    S.barrier()
                for d in range(2):
                    c = step if d == 0 else NTILE - 1 - step
                    t0 = c * 128
                    slot = (step * 2 + d) % 2
                    cur = {}
                    for n in names:
                        tl, btl = IN[n][slot]
                        key = n + str(d) if n in ("KD", "BD", "LD") else n
                        S.dma("sp", lambda e, tl=tl, key=key, t0=t0: e.dma_start(out=tl[:], in_=prep[key][t0:t0 + 128, :]), reads=[B_prep[key][c]], writes=[btl])
                        cur[n] = (tl, btl)
                    LD, bLD = cur["LD"]
                    for hh in range(2):
                        S.op("pe", lambda e, hh=hh, d=d, LD=LD: e.matmul(pcl[:, hh * 512:(hh + 1) * 512], lhsT=tri[:, d, :], rhs=LD[:, hh * 512:(hh + 1) * 512], start=True, stop=True), reads=[b_tri, bLD], writes=[bpcl])
                    S.op("act", lambda e: e.copy(out=CL[:], in_=pcl[:]), reads=[bpcl], writes=[bCL])
                    for hh in range(2):
                        S.op("pe", lambda e, hh=hh, LD=LD: e.matmul(pcl[:, hh * 512:(hh + 1) * 512], lhsT=ones[:], rhs=LD[:, hh * 512:(hh + 1) * 512], start=True, stop=True), reads=[b_ones, bLD], writes=[bpcl])
                    S.op("dve", lambda e: e.tensor_tensor(out=TOT[:], in0=pcl[:], in1=CL[:], op=ALU.subtract), reads=[bpcl, bCL], writes=[bTOT])
                    S.op("pool", lambda e, LD=LD: e.tensor_tensor(out=EX[:], in0=CL[:], in1=LD[:], op=ALU.subtract), reads=[bCL, bLD], writes=[bEX])
                    for h in range(16):
                        S.op("pe", lambda e, h=h, LD=LD: e.matmul(pch[0:64, h * 2:h * 2 + 2], lhsT=LD[:, h * 64:(h + 1) * 64], rhs=ones[:, 0:2], start=True, stop=True), reads=[bLD, b_ones], writes=[bpch])
                    S.op("act", lambda e: e.activation(out=PCc[:].rearrange("k h t -> k (h t)"), in_=pch[0:64, 0:32], func=AF.Exp), reads=[bpch], writes=[bPCc])
                    R_, bR_ = cur["R"]
                    KK_, bKK_ = cur["KK"]
                    V_, bV_ = cur["V"]
                    KD_, bKD_ = cur["KD"]
                    BD_, bBD_ = cur["BD"]
                    S.op("act", lambda e: e.activation(out=Ee[:], in_=EX[:], func=AF.Exp), reads=[bEX], writes=[bEe])
                    S.op("dve", lambda e, KK_=KK_: e.tensor_tensor(out=SCb["kap"][0][:], in0=KK_[:], in1=Ee[:], op=ALU.mult), reads=[bKK_, bEe], writes=[SCb["kap"][1]])
                    S.op("act", lambda e: e.activation(out=Ee[:], in_=CL[:], func=AF.Exp), reads=[bCL, SCb["kap"][1]], writes=[bEe])
                    S.op("pool", lambda e, R_=R_: e.tensor_tensor(out=SCb["rt"][0][:], in0=R_[:], in1=Ee[:], op=ALU.mult), reads=[bR_, bEe], writes=[SCb["rt"][1]])
                    S.op("act", lambda e: e.activation(out=EX[:], in_=CL[:], func=AF.Exp, scale=-1.0), reads=[bCL, SCb["kap"][1]], writes=[bEX])
                    S.op("dve", lambda e, BD_=BD_: e.tensor_tensor(out=SCb["bet"][0][:], in0=BD_[:], in1=EX[:], op=ALU.mult), reads=[bBD_, bEX], writes=[SCb["bet"][1]])
                    S.op("pool", lambda e, KD_=KD_: e.tensor_tensor(out=SCb["kt"][0][:], in0=KD_[:], in1=EX[:], op=ALU.mult), reads=[bKD_, bEX], writes=[SCb["kt"][1]])
                    S.op("act", lambda e: e.activation(out=TOT[:], in_=TOT[:], func=AF.Exp), reads=[bTOT], writes=[bTOT])
                    S.op("dve", lambda e, KD_=KD_: e.tensor_tensor(out=SCb["khat"][0][:], in0=KD_[:], in1=TOT[:], op=ALU.mult), reads=[bKD_, bTOT], writes=[SCb["khat"][1]])
                    S.op("pool", lambda e, BD_=BD_: e.tensor_tensor(out=SCb["bhat"][0][:], in0=BD_[:], in1=TOT[:], op=ALU.mult), reads=[bBD_, bTOT], writes=[SCb["bhat"][1]])
                    S.op("act", lambda e, V_=V_: e.copy(out=SCb["V"][0][:], in_=V_[:]), reads=[bV_], writes=[SCb["V"][1]])
                    Ya, bYa = Yacc[slot]
                    for hg in range(4):
                        for j in range(4):
                            h = hg * 4 + j
                            for qi, n in enumerate(("kap", "rt", "bet", "kt")):
                                S.op("pe", lambda e, j=j, h=h, qi=qi, n=n: e.transpose(out=ptr[0:64, j, qi * 128:(qi + 1) * 128], in_=SCb[n][0][:, h * 64:(h + 1) * 64], identity=identb[:]),
                                     reads=[SCb[n][1], b_identb], writes=[bptr])
                        S.op("act", lambda e: e.copy(out=FT[:], in_=ptr[0:64, :, :]), reads=[bptr], writes=[bFT])
                        for j in range(4):
                            S.op("pe", lambda e, j=j: e.matmul(pgr[:, j * 256:(j + 1) * 256], lhsT=FT[:, j, 256:384], rhs=FT[:, j, 0:256], start=True, stop=True), reads=[bFT], writes=[bpgr])
                        gm4 = gmask[:, d, 0:256].unsqueeze(1).broadcast_to([128, 4, 256])
                        S.op("dve", lambda e, gm4=gm4: e.tensor_tensor(out=GM[:, :, 0:256], in0=pgr[:].rearrange("p (j w) -> p j w", w=256), in1=gm4, op=ALU.mult), reads=[bpgr, b_gmask], writes=[bGM])
                        for j in range(4):
                            S.op("pe", lambda e, j=j: e.matmul(pgr[:, j * 256:(j + 1) * 256], lhsT=FT[:, j, 384:512], rhs=FT[:, j, 0:256], start=True, stop=True), reads=[bFT, bGM], writes=[bpgr])
                        gm4b = gmask[:, d, 256:512].unsqueeze(1).broadcast_to([128, 4, 256])
                        S.op("dve", lambda e, gm4b=gm4b: e.tensor_tensor(out=GM[:, :, 256:512], in0=pgr[:].rearrange("p (j w) -> p j w", w=256), in1=gm4b, op=ALU.mult), reads=[bpgr, b_gmask], writes=[bGM])
                        S.op("pool", lambda e: e.tensor_copy(out=QQ[:, :, 0:128], in_=GM[:, :, 0:128]), reads=[bGM], writes=[bQQ])
                        ptb = ptr[:, :, 0:128]
                        for j in range(4):
                            S.op("pe", lambda e, j=j: e.transpose(out=ptr[:, j, 0:128], in_=GM[:, j, 0:128], identity=identb[:]), reads=[bGM, b_identb, bFT], writes=[bptr])
                        S.op("act", lambda e: e.copy(out=QQ[:, :, 128:256], in_=ptr[:, :, 0:128]), reads=[bptr], writes=[bQQ])
                        idb4 = identb[:].unsqueeze(1).broadcast_to([128, 4, 128])
                        S.op("dve", lambda e, idb4=idb4: e.tensor_tensor(out=TTb[:], in0=GM[:, :, 0:128], in1=idb4, op=ALU.add), reads=[bGM, b_identb], writes=[bTTb])
                        for lv in range(6):
                            last = (lv == 5)
                            for j in range(4):
                                if not last:
                                    S.op("pe", lambda e, j=j: e.matmul(pgr[:, j * 256:j * 256 + 128], lhsT=QQ[:, j, 128:256], rhs=QQ[:, j, 0:128], start=True, stop=True), reads=[bQQ], writes=[bpgr])
                                S.op("pe", lambda e, j=j: e.matmul(pgr[:, j * 256 + 128:(j + 1) * 256], lhsT=QQ[:, j, 0:128], rhs=QQ[:, j, 128:256], start=True, stop=True), reads=[bQQ], writes=[bpgr])
                            S.op("act", lambda e: e.copy(out=QQ[:], in_=pgr[:].rearrange("p (j w) -> p j w", w=256)), reads=[bpgr], writes=[bQQ])
                            for j in range(4):
                                S.op("pe", lambda e, j=j: e.matmul(ptt[:, j * 128:(j + 1) * 128], lhsT=QQ[:, j, 128:256], rhs=TTb[:, j, :], start=True, stop=True), reads=[bTTb, bQQ], writes=[bptt])
                            S.op("dve", lambda e: e.tensor_tensor(out=TTb[:], in0=ptt[:].rearrange("p (j w) -> p j w", w=128), in1=TTb[:], op=ALU.add), reads=[bptt, bTTb], writes=[bTTb])
                        Vb = SCb["V"][0]
                        for j in range(4):
                            h = hg * 4 + j
                            S.op("pe", lambda e, j=j, h=h, d=d: e.matmul(pch[:, j * 64:(j + 1) * 64], lhsT=FT[:, j, 0:128], rhs=Abf[:, d, h, :], start=True, stop=False), reads=[bFT, bAbf], writes=[bpch])
                            S.op("pe", lambda e, j=j, h=h: e.matmul(pch[:, j * 64:(j + 1) * 64], lhsT=GM[:, j, 256:384], rhs=Vb[:, h * 64:(h + 1) * 64], start=False, stop=True), reads=[bGM, SCb["V"][1]], writes=[bpch])
                        if REFINE:
                            pX = pch[:, 0:256].rearrange("p (j w) -> p j w", w=64)
                            pU = pch[:, 256:512].rearrange("p (j w) -> p j w", w=64)
                            S.op("act", lambda e: e.copy(out=Xb[:], in_=pX), reads=[bpch], writes=[bXb])
                            S.op("dve", lambda e: e.tensor_copy(out=X32[:], in_=pX), reads=[bpch, bXb], writes=[bX32])
                            for j in range(4):
                                S.op("pe", lambda e, j=j: e.matmul(pch[:, 256 + j * 64:256 + (j + 1) * 64], lhsT=TTb[:, j, :], rhs=Xb[:, j, :], start=True, stop=True), reads=[bTTb, bXb], writes=[bpch])
                            S.op("act", lambda e: e.copy(out=U0b[:], in_=pU), reads=[bpch], writes=[bU0b])
                            S.op("dve", lambda e: e.tensor_copy(out=U032[:], in_=pU), reads=[bpch, bU0b], writes=[bU032])
                            S.op("pool", lambda e: e.tensor_tensor(out=X32[:], in0=X32[:], in1=U032[:], op=ALU.subtract), reads=[bX32, bU032], writes=[bX32])
                            for j in range(4):
                                S.op("pe", lambda e, j=j: e.matmul(pch[:, j * 64:(j + 1) * 64], lhsT=GM[:, j, 0:128], rhs=U0b[:, j, :], start=True, stop=True), reads=[bGM, bU0b, bX32], writes=[bpch])
                            S.op("dve", lambda e: e.tensor_tensor(out=Xb[:], in0=pX, in1=X32[:], op=ALU.add), reads=[bpch, bX32], writes=[bXb])
                            for j in range(4):
                                S.op("pe", lambda e, j=j: e.matmul(pch[:, 256 + j * 64:256 + (j + 1) * 64], lhsT=TTb[:, j, :], rhs=Xb[:, j, :], start=True, stop=True), reads=[bTTb, bXb], writes=[bpch])
                            S.op("dve", lambda e: e.scalar_tensor_tensor(out=Ub[:], in0=pU, scalar=-1.0, in1=U032[:], op0=ALU.mult, op1=ALU.subtract), reads=[bpch, bU032], writes=[bUb])
                        else:
                            S.op("act", lambda e: e.copy(out=Xb[:], in_=pch[:, 0:256].rearrange("p (j w) -> p j w", w=64)), reads=[bpch], writes=[bXb])
                            for j in range(4):
                                S.op("pe", lambda e, j=j: e.matmul(pch[:, 256 + j * 64:256 + (j + 1) * 64], lhsT=TTb[:, j, :], rhs=Xb[:, j, :], start=True, stop=True), reads=[bTTb, bXb], writes=[bpch])
                            S.op("dve", lambda e: e.tensor_scalar(out=Ub[:], in0=pch[:, 256:512].rearrange("p (j w) -> p j w", w=64), scalar1=-1.0, scalar2=None, op0=ALU.mult), reads=[bpch], writes=[bUb])
                        for j in range(4):
                            h = hg * 4 + j
                            S.op("pe", lambda e, j=j, h=h, d=d: e.matmul(pch[:, j * 64:(j + 1) * 64], lhsT=FT[:, j, 128:256], rhs=Abf[:, d, h, :], start=True, stop=False), reads=[bFT, bAbf, bXb], writes=[bpch])
                            S.op("pe", lambda e, j=j, h=h: e.matmul(pch[:, j * 64:(j + 1) * 64], lhsT=GM[:, j, 384:512], rhs=Vb[:, h * 64:(h + 1) * 64], start=False, stop=False), reads=[bGM, SCb["V"][1]], writes=[bpch])
                            S.op("pe", lambda e, j=j: e.matmul(pch[:, j * 64:(j + 1) * 64], lhsT=GM[:, j, 128:256], rhs=Ub[:, j, :], start=False, stop=True), reads=[bGM, bUb], writes=[bpch])
                        S.op("act", lambda e, Ya=Ya, hg=hg: e.copy(out=Ya[:, hg * 256:(hg + 1) * 256], in_=pch[:, 0:256]), reads=[bpch], writes=[bYa])
                        for j in range(4):
                            h = hg * 4 + j
                            S.op("pe", lambda e, j=j, h=h: e.matmul(pch[0:64, 256 + j * 64:256 + (j + 1) * 64], lhsT=SCb["khat"][0][:, h * 64:(h + 1) * 64], rhs=Vb[:, h * 64:(h + 1) * 64], start=True, stop=False), reads=[SCb["khat"][1], SCb["V"][1], bUb], writes=[bpch])
                            S.op("pe", lambda e, j=j, h=h: e.matmul(pch[0:64, 256 + j * 64:256 + (j + 1) * 64], lhsT=SCb["bhat"][0][:, h * 64:(h + 1) * 64], rhs=Ub[:, j, :], start=False, stop=True), reads=[SCb["bhat"][1], bUb], writes=[bpch])
                        for j in range(4):
                            h = hg * 4 + j
                            S.op("dve", lambda e, j=j, h=h, d=d: e.scalar_tensor_tensor(out=Ast[:, d, h, :], in0=Ast[:, d, h, :], scalar=PCc[:, h, 0:1], in1=pch[0:64, 256 + j * 64:256 + (j + 1) * 64], op0=ALU.mult, op1=ALU.add),
                                 reads=[bAst, bPCc, bpch], writes=[bAst])
                        S.op("pool", lambda e, hg=hg, d=d: e.tensor_copy(out=Abf[:, d, hg * 4:(hg + 1) * 4, :], in_=Ast[:, d, hg * 4:(hg + 1) * 4, :]), reads=[bAst], writes=[bAbf])
                    S.dma("pool", lambda e, Ya=Ya, t0=t0, d=d: e.dma_start(out=YD[d][t0:t0 + 128, :], in_=Ya[:]), reads=[bYa], writes=[B_Y[d][c]], sembuf=bYa)
                    seg_end = (d == 0 and c % 2 == 1) or (d == 1 and c % 2 == 0)
                    if seg_end:
                        sidx = c // 2
                        S.dma("pool", lambda e, sidx=sidx, d=d: e.dma_start(out=wkv_fin[sidx, d], in_=Ast[:, d, :, :]), reads=[bAst], sembuf=bAst, final=True)
                        S.op("dve", lambda e, d=d: e.tensor_scalar(out=Ast[:, d, :, :], in0=Ast[:, d, :, :], scalar1=cm[0:64, 0:1], scalar2=None, op0=ALU.mult), reads=[bAst, b_cm], writes=[bAst])
                        S.op("pool", lambda e, d=d: e.tensor_copy(out=Abf[:, d, :, :], in_=Ast[:, d, :, :]), reads=[bAst], writes=[bAbf])
            S.barrier([bAst] + [b for n in names for _, b in IN[n]] + [b for _, b in Yacc])

        with contextlib.ExitStack() as es:
            lnw, blnw = C.sb(es, "lnw", [128, RW], F32)
            lnb, blnb = C.sb(es, "lnb", [128, RW], F32)
            S.dma("sp", lambda e: e.dma_start(out=lnw[:], in_=rowap("lnw")), writes=[blnw])
            S.dma("sp", lambda e: e.dma_start(out=lnb[:], in_=rowap("lnb")), writes=[blnb])
            rg = {n: C.ring(es, "e" + n, 2, [128, RW], F32) for n in ("YF", "YB", "BON", "G")}
            Ysq, bYsq = C.sb(es, "Ysq", [128, RW], F32)
            m16, bm16 = C.sb(es, "m16", [128, 16], F32)
            v16, bv16 = C.sb(es, "v16", [128, 16], F32)
            Ob, bOb = C.sb(es, "Ob", [128, RW], BF16)
            pTe, bpTe = C.ps(es, "pTe", [128, 8, 128], BF16)
            OTs = C.ring(es, "OT", 2, [128, 8, 128], BF16)

            def h3(ap):
                return ap.rearrange("p (h k) -> p h k", k=64)

            def bc16(ap):
                return ap.unsqueeze(2).broadcast_to([128, 16, 64])
            for tile in range(NTILE):
                t0 = tile * 128
                par = tile % 2
                YF_, bYF_ = rg["YF"][par]
                YB_, bYB_ = rg["YB"][par]
                BN_, bBN_ = rg["BON"][par]
                G_, bG_ = rg["G"][par]
                S.dma("sp", lambda e, YF_=YF_, t0=t0: e.dma_start(out=YF_[:], in_=YD[0][t0:t0 + 128, :]), reads=[B_Y[0][tile]], writes=[bYF_])
                S.dma("sp", lambda e, YB_=YB_, t0=t0: e.dma_start(out=YB_[:], in_=YD[1][t0:t0 + 128, :]), reads=[B_Y[1][tile]], writes=[bYB_])
                S.dma("sp", lambda e, BN_=BN_, t0=t0: e.dma_start(out=BN_[:], in_=prep["BON"][t0:t0 + 128, :]), reads=[B_prep["BON"][tile]], writes=[bBN_])
                S.dma("sp", lambda e, G_=G_, t0=t0: e.dma_start(out=G_[:], in_=prep["G"][t0:t0 + 128, :]), reads=[B_prep["G"][tile]], writes=[bG_])
                S.op("dve", lambda e, YF_=YF_, YB_=YB_: e.tensor_tensor(out=YF_[:], in0=YF_[:], in1=YB_[:], op=ALU.add), reads=[bYF_, bYB_], writes=[bYF_])
                S.op("pool", lambda e, YF_=YF_, BN_=BN_: e.tensor_tensor(out=YF_[:], in0=YF_[:], in1=BN_[:], op=ALU.add), reads=[bYF_, bBN_], writes=[bYF_])
                S.op("dve", lambda e, YF_=YF_: e.tensor_reduce(out=m16[:], in_=h3(YF_[:]), axis=AX.X, op=ALU.add), reads=[bYF_], writes=[bm16])
                S.op("dve", lambda e: e.tensor_scalar(out=m16[:], in0=m16[:], scalar1=-1.0 / 64, scalar2=None, op0=ALU.mult), reads=[bm16], writes=[bm16])
                S.op("dve", lambda e, YF_=YF_: e.tensor_tensor(out=h3(YF_[:]), in0=h3(YF_[:]), in1=bc16(m16[:]), op=ALU.add), reads=[bYF_, bm16], writes=[bYF_])
                S.op("act", lambda e, YF_=YF_: e.activation(out=Ysq[:], in_=YF_[:], func=AF.Square), reads=[bYF_], writes=[bYsq])
                S.op("dve", lambda e: e.tensor_reduce(out=v16[:], in_=h3(Ysq[:]), axis=AX.X, op=ALU.add), reads=[bYsq], writes=[bv16])
                rstd_from_ss(v16[:], bv16, 64, 64e-5, None)
                S.op("dve", lambda e, YF_=YF_: e.tensor_tensor(out=h3(YF_[:]), in0=h3(YF_[:]), in1=bc16(v16[:]), op=ALU.mult), reads=[bYF_, bv16], writes=[bYF_])
                S.op("pool", lambda e, YF_=YF_: e.tensor_tensor(out=YF_[:], in0=YF_[:], in1=lnw[:], op=ALU.mult), reads=[bYF_, blnw], writes=[bYF_])
                S.op("dve", lambda e, YF_=YF_: e.tensor_tensor(out=YF_[:], in0=YF_[:], in1=lnb[:], op=ALU.add), reads=[bYF_, blnb], writes=[bYF_])
                S.op("pool", lambda e, YF_=YF_, G_=G_: e.tensor_tensor(out=Ob[:], in0=YF_[:], in1=G_[:], op=ALU.mult), reads=[bYF_, bG_], writes=[bOb])
                for c in range(8):
                    S.op("pe", lambda e, c=c: e.transpose(out=pTe[:, c, :], in_=Ob[:, c * 128:(c + 1) * 128], identity=identb[:]), reads=[bOb, b_identb], writes=[bpTe])
                OT, bOT = OTs[par]
                S.op("act", lambda e, OT=OT: e.copy(out=OT[:], in_=pTe[:]), reads=[bpTe], writes=[bOT])
                S.dma("pool", lambda e, OT=OT, tile=tile: e.dma_start(out=mixR[tile], in_=OT[:]), reads=[bOT], writes=[B_mixT[tile]], sembuf=bOT)
            S.barrier([blnw, blnb] + [b for n in rg for _, b in rg[n]] + [b for _, b in OTs])

        B_O1 = [Buf("O1_%d" % i) for i in range(NTILE)]
        with contextlib.ExitStack() as es:
            mg, bmg = C.sb(es, "mg", [128, 16, 1024], BF16)
            wst = C.ring(es, "wst1", 4, [128, 4, 512], F32)
            wbs = C.ring(es, "wb1", 2, [128, 16, 512], BF16)
            pmm = [C.ps(es, "pm1_%d" % i, [128, 512], F32) for i in range(8)]
            ost = C.ring(es, "ost1", 8, [128, 512], F32)
            oi = 0
            wi = 0
            for g in range(4):
                tiles = list(range(g * 8, g * 8 + 8))
                S.dma("sp", [lambda e, g=g, q=q: e.dma_start(out=mg[:, q * 4:(q + 1) * 4, :], in_=mixT.rearrange("(c p) t -> p c t", p=128)[:, q * 4:(q + 1) * 4, g * 1024:(g + 1) * 1024]) for q in range(2)]
                      + [lambda e, g=g, t=t: e.dma_start(out=mg[:, 8:16, t * 128:(t + 1) * 128], in_=mixR[g * 8 + t]) for t in range(8)],
                      reads=B_mixL + [B_mixT[t] for t in tiles], writes=[bmg])
                for dcol in range(4):
                    wt, bw = wbs[wi % 2]
                    wi += 1
                    load_w(wst, wt, bw, w_out[:, dcol * 512:(dcol + 1) * 512], 16, 512)
                    for t in range(8):
                        tile = g * 8 + t
                        pm, bpm = pmm[oi % 8]
                        o_t, bo = ost[oi % 8]
                        oi += 1
                        for cch in range(16):
                            S.op("pe", lambda e, pm=pm, wt=wt, cch=cch, t=t: e.matmul(pm[:], lhsT=mg[:, cch, t * 128:(t + 1) * 128], rhs=wt[:, cch, :], start=(cch == 0), stop=(cch == 15)), reads=[bmg, bw], writes=[bpm])
                        cast(o_t[:], pm[:], [bpm], [bo], eng=("act" if oi % 2 else "dve"))
                        S.dma("pool", lambda e, o_t=o_t, tile=tile, dcol=dcol: e.dma_start(out=FD[tile * 128:(tile + 1) * 128, dcol * 512:(dcol + 1) * 512], in_=o_t[:]), reads=[bo], writes=[B_O1[tile]], sembuf=bo)
            S.barrier()
        with contextlib.ExitStack() as es:
            o1s = C.ring(es, "o1b", 2, [128, D], F32)
            xts = C.ring(es, "xt1", 2, [128, D], F32)
            junk, b_junk = C.sb(es, "junk1", [128, D], BF16)
            ss, b_ss = C.sb(es, "ss1", [128, 1], F32)
            xn, b_xn = C.sb(es, "xn1", [128, D], BF16)
            pT, b_pT = C.ps(es, "pT1", [128, D], BF16)
            h2s = C.ring(es, "h2s", 2, [128, 16, 128], BF16)
            nbufs = (junk, b_junk, ss, b_ss, xn, b_xn, pT, b_pT)
            hg_, bhg = C.sb(es, "hg", [128, 16, 1024], BF16)
            wst = C.ring(es, "wst2", 2, [128, 4, 512], F32)
            wgs = C.ring(es, "wg2", 2, [128, 16, 512], BF16)
            wus = C.ring(es, "wu2", 2, [128, 16, 512], BF16)
            pga = [C.ps(es, "pga%d" % i, [128, 512], F32) for i in range(3)]
            pup = [C.ps(es, "pup%d" % i, [128, 512], F32) for i in range(3)]
            sl = C.ring(es, "sl", 3, [128, 512], F32)
            ao = C.ring(es, "ao", 4, [128, 512], BF16)

            def f1b_tile(tile):
                t0 = tile * 128
                xt, b_xt = xts[tile % 2]
                o1, bo1 = o1s[tile % 2]
                S.dma("sp", lambda e, xt=xt, t0=t0: e.dma_start(out=xt[:], in_=x[t0:t0 + 128, :]), writes=[b_xt])
                S.dma("sp", lambda e, o1=o1, t0=t0: e.dma_start(out=o1[:], in_=FD[t0:t0 + 128, :]), reads=[B_O1[tile]], writes=[bo1])
                S.op("act", lambda e, o1=o1: e.activation(out=junk[:], in_=o1[:], func=AF.Square, accum_out=ss[:]), reads=[bo1], writes=[b_junk, b_ss])
                rstd_from_ss(ss[:], b_ss, D, 1e-6, None)
                S.op("dve", lambda e, o1=o1: e.scalar_tensor_tensor(out=o1[:], in0=o1[:], scalar=ss[:, 0:1], in1=G1r[:], op0=ALU.mult, op1=ALU.mult), reads=[bo1, b_ss, b_G1r], writes=[bo1])
                S.op("pool", lambda e, xt=xt, o1=o1: e.tensor_tensor(out=xt[:], in0=xt[:], in1=o1[:], op=ALU.add), reads=[b_xt, bo1], writes=[b_xt])
                S.dma("pool", lambda e, xt=xt, t0=t0: e.dma_start(out=X1[t0:t0 + 128, :], in_=xt[:]), reads=[b_xt], writes=[B_X1[tile]], sembuf=b_xt)
                h2, bh2 = h2s[tile % 2]
                norm_transpose(xt, b_xt, h2, bh2, 0, Af, modT[:, 48:64], [b_Af, b_modT_], nbufs)
                S.dma("pool", lambda e, h2=h2, tile=tile: e.dma_start(out=h2R[tile], in_=h2[:]), reads=[bh2], writes=[B_h2T[tile]], sembuf=bh2)
            for t in range(8):
                f1b_tile(t)
            oi = 0
            for g in range(4):
                nxt = list(range((g + 1) * 8, (g + 2) * 8)) if g < 3 else []
                S.dma("sp", [lambda e, g=g, t=t: e.dma_start(out=hg_[:, :, t * 128:(t + 1) * 128], in_=h2R[g * 8 + t]) for t in range(8)],
                      reads=[B_h2T[t] for t in range(g * 8, g * 8 + 8)], writes=[bhg])
                for j in range(11):
                    wg, bwg = wgs[j % 2]
                    wu, bwu = wus[j % 2]
                    load_w(wst, wg, bwg, w_gu[:, j * 512:(j + 1) * 512], 16, 512)
                    load_w(wst, wu, bwu, w_gu[:, FH + j * 512:FH + (j + 1) * 512], 16, 512)
                    for f in range(4):
                        fc = j * 4 + f
                        for tt in range(2):
                            pg_, bpg_ = pga[oi % 3]
                            pu_, bpu_ = pup[oi % 3]
                            s_, bs_ = sl[oi % 3]
                            a_, ba_ = ao[oi % 4]
                            oi += 1
                            for dc in range(16):
                                S.op("pe", lambda e, pg_=pg_, wg=wg, f=f, dc=dc, tt=tt: e.matmul(pg_[:], lhsT=wg[:, dc, f * 128:(f + 1) * 128], rhs=hg_[:, dc, tt * 512:(tt + 1) * 512], start=(dc == 0), stop=(dc == 15)), reads=[bwg, bhg], writes=[bpg_])
                            for dc in range(16):
                                S.op("pe", lambda e, pu_=pu_, wu=wu, f=f, dc=dc, tt=tt: e.matmul(pu_[:], lhsT=wu[:, dc, f * 128:(f + 1) * 128], rhs=hg_[:, dc, tt * 512:(tt + 1) * 512], start=(dc == 0), stop=(dc == 15)), reads=[bwu, bhg], writes=[bpu_])
                            S.op("act", lambda e, s_=s_, pg_=pg_: e.activation(out=s_[:], in_=pg_[:], func=AF.Silu), reads=[bpg_], writes=[bs_])
                            S.op("dve", lambda e, a_=a_, s_=s_, pu_=pu_: e.tensor_tensor(out=a_[:], in0=s_[:], in1=pu_[:], op=ALU.mult), reads=[bs_, bpu_], writes=[ba_])
                            S.dma("pool", lambda e, a_=a_, fc=fc, g=g, tt=tt: e.dma_start(out=actT[fc * 128:(fc + 1) * 128, g * 1024 + tt * 512:g * 1024 + (tt + 1) * 512], in_=a_[:]), reads=[ba_], writes=[B_actT[fc][g]], sembuf=ba_)
                    if nxt and j >= 1:
                        f1b_tile(nxt.pop(0))
                while nxt:
                    f1b_tile(nxt.pop(0))
            S.barrier()

        B_F = [Buf("Fd_%d" % i) for i in range(NTILE)]
        with contextlib.ExitStack() as es:
            ag = C.ring(es, "ag", 1, [128, 44, 1024], BF16)
            wst = C.ring(es, "wst3", 2, [128, 4, 512], F32)
            wds = C.ring(es, "wd3", 3, [128, 22, 512], BF16)
            pmm = [C.ps(es, "pm3_%d" % i, [128, 512], F32) for i in range(8)]
            ost = C.ring(es, "ost3", 4, [128, 512], F32)
            wi = 0
            oo = 0
            for g in range(4):
                agt, bag = ag[0]
                S.dma("sp", [lambda e, agt=agt, g=g, q=q: e.dma_start(out=agt[:, q * 4:(q + 1) * 4, :], in_=actT.rearrange("(c p) t -> p c t", p=128)[:, q * 4:(q + 1) * 4, g * 1024:(g + 1) * 1024]) for q in range(11)],
                      reads=[B_actT[f][g] for f in range(44)] + B_X1, writes=[bag])
                for dcol in range(4):
                    for half in range(2):
                        wt, bw = wds[wi % 3]
                        wi += 1
                        load_w(wst, wt, bw, w_down[:, dcol * 512:(dcol + 1) * 512], 22, 512, k0=half * 22)
                        for t in range(8):
                            pm, bpm = pmm[t]
                            for f2 in range(22):
                                fc = half * 22 + f2
                                S.op("pe", lambda e, pm=pm, wt=wt, fc=fc, f2=f2, t=t, agt=agt: e.matmul(pm[:], lhsT=agt[:, fc, t * 128:(t + 1) * 128], rhs=wt[:, f2, :], start=(fc == 0), stop=(fc == 43)), reads=[bag, bw], writes=[bpm])
                    for t in range(8):
                        tile = g * 8 + t
                        pm, bpm = pmm[t]
                        o_t, bo = ost[oo % 4]
                        oo += 1
                        cast(o_t[:], pm[:], [bpm], [bo], eng=("act" if t % 2 else "dve"))
                        S.dma("pool", lambda e, o_t=o_t, tile=tile, dcol=dcol: e.dma_start(out=FD[tile * 128:(tile + 1) * 128, dcol * 512:(dcol + 1) * 512], in_=o_t[:]), reads=[bo], writes=[B_F[tile]], sembuf=bo)
            S.barrier()
        with contextlib.ExitStack() as es:
            o1s = C.ring(es, "o3b", 2, [128, D], F32)
            xts = C.ring(es, "xt3", 2, [128, D], F32)
            junk, b_junk = C.sb(es, "junk3", [128, D], BF16)
            ss, b_ss = C.sb(es, "ss3", [128, 1], F32)
            for tile in range(NTILE):
                t0 = tile * 128
                xt, b_xt = xts[tile % 2]
                o1, bo1 = o1s[tile % 2]
                S.dma("sp", lambda e, xt=xt, t0=t0: e.dma_start(out=xt[:], in_=X1[t0:t0 + 128, :]), reads=[B_X1[tile]], writes=[b_xt])
                S.dma("sp", lambda e, o1=o1, t0=t0: e.dma_start(out=o1[:], in_=FD[t0:t0 + 128, :]), reads=[B_F[tile]], writes=[bo1])
                S.op("act", lambda e, o1=o1: e.activation(out=junk[:], in_=o1[:], func=AF.Square, accum_out=ss[:]), reads=[bo1], writes=[b_junk, b_ss])
                rstd_from_ss(ss[:], b_ss, D, 1e-6, None)
                S.op("dve", lambda e, o1=o1: e.scalar_tensor_tensor(out=o1[:], in0=o1[:], scalar=ss[:, 0:1], in1=G2r[:], op0=ALU.mult, op1=ALU.mult), reads=[bo1, b_ss, b_G2r], writes=[bo1])
                S.op("pool", lambda e, xt=xt, o1=o1: e.tensor_tensor(out=xt[:], in0=xt[:], in1=o1[:], op=ALU.add), reads=[b_xt, bo1], writes=[b_xt])
                S.dma("pool", lambda e, xt=xt, t0=t0: e.dma_start(out=y_out[t0:t0 + 128, :], in_=xt[:]), reads=[b_xt], sembuf=b_xt, final=True)
        S.emit()
    S.close()
    return nc


_PROMPT_COUNTS = [6, 6, 5, 5, 5, 5]


def _col(v, n):
    return np.ascontiguousarray(np.asarray(v, np.float32).reshape(n, 128).T)


def kernel(x_prompt, x_sample, state_lru, state_wkv, c, c_ctx,
           norm_mix_pre, norm_mix_post, norm_ffn_pre, norm_ffn_post, w_mod, b_mod, w_in,
           lru_conv_w, lru_conv_b, lru_wr, lru_br, lru_wi, lru_bi, lru_lambda,
           rwkv_mu, rwkv_w0, rwkv_w_up, rwkv_a0, rwkv_a_up, rwkv_g_up, rwkv_k_k, rwkv_k_a, rwkv_r_k,
           rwkv_ln_w, rwkv_ln_b, w_out, ffn_w_gu, ffn_w_down):
    f32 = np.float32
    A = lambda a: np.ascontiguousarray(np.asarray(a, f32))
    x_prompt, x_sample = A(x_prompt), A(x_sample)
    nc = build_program()
    idx = np.arange(128)
    tri = np.zeros((128, 2, 128), f32)
    tri[:, 0, :] = (idx[:, None] <= idx[None, :])
    tri[:, 1, :] = (idx[:, None] >= idx[None, :])
    gmask = np.zeros((128, 2, 512), f32)
    for d in range(2):
        incl = tri[:, d, :]
        strict = incl - np.eye(128, dtype=f32)
        gmask[:, d, 0:128] = -strict
        gmask[:, d, 128:256] = incl
        gmask[:, d, 256:384] = strict
        gmask[:, d, 384:512] = incl
    ncols = np.stack([_col(A(v)[0], 16) for v in (norm_mix_pre, norm_mix_post, norm_ffn_pre, norm_ffn_post)], axis=1)
    convc = np.zeros((128, 8, 5), f32)
    cw = A(lru_conv_w)[0]
    cb = A(lru_conv_b)[0]
    for k in range(4):
        convc[:, :, k] = cw[k].reshape(8, 128).T
    convc[:, :, 4] = cb.reshape(8, 128).T
    lru_bc = np.zeros((128, 2, 8, 2), f32)
    lru_bc[:, :, :, 0] = np.transpose(A(lru_br)[0], (2, 0, 1))
    lru_bc[:, :, :, 1] = np.transpose(A(lru_bi)[0], (2, 0, 1))
    lam = np.ascontiguousarray(np.transpose(A(lru_lambda)[0].reshape(2, 8, 128), (2, 0, 1)))
    shared = {
        "w_mod": A(w_mod)[0], "b_modT": _col(A(b_mod)[0], 96), "ncols": np.ascontiguousarray(ncols),
        "w_in": A(w_in)[0], "convc": convc, "lru_wr": A(lru_wr)[0], "lru_wi": A(lru_wi)[0],
        "lru_bc": lru_bc, "lru_lam": lam,
        "w_up": A(rwkv_w_up)[0], "a_up": A(rwkv_a_up)[0], "g_up": A(rwkv_g_up)[0],
        "w_out": A(w_out)[0], "w_gu": A(ffn_w_gu)[0], "w_down": A(ffn_w_down)[0],
        "ident": np.eye(128, dtype=f32), "tri": tri, "gmask": gmask,
    }
    chs = np.arange(PRW)
    p = np.arange(128)

    def rows_for(sample):
        if sample:
            cmM1 = (chs < 840)
            cmP1 = (chs >= 840) & (chs < 1680)
            cmU = (chs >= 1680) & (chs < 2520)
            cmD = (chs >= 2520)
        else:
            cmM1 = (chs < 1680)
            cmP1 = (chs >= 1680)
            cmU = np.zeros(PRW, bool)
            cmD = np.zeros(PRW, bool)
        parts = [A(rwkv_mu)[0], cmM1.astype(f32), cmP1.astype(f32), cmU.astype(f32), cmD.astype(f32),
                 A(rwkv_w0)[0, 0], A(rwkv_w0)[0, 1], A(rwkv_a0)[0, 0], A(rwkv_a0)[0, 1], A(rwkv_k_k)[0], A(rwkv_k_a)[0],
                 A(rwkv_r_k)[0].reshape(-1), A(rwkv_ln_w)[0], A(rwkv_ln_b)[0]]
        return np.concatenate(parts).astype(f32)[None, :]

    def rmask_for(sample):
        m = np.ones((128, 4), f32)
        if sample:
            m[:, 0] = (p % 64 != 0)
            m[:, 1] = (p % 64 != 0)
            m[:, 2] = (p % 64 != 63)
            m[:, 3] = (p % 64 != 63)
        else:
            m[:, 0] = (p != 0)
            m[:, 1] = 1.0
            m[:, 2] = 1.0
            m[:, 3] = (p != 127)
        return m
    assign = []
    s0 = 0
    for n in _PROMPT_COUNTS:
        assign.append(list(range(s0, s0 + n)))
        s0 += n
    in_maps = []
    for core in range(8):
        m = dict(shared)
        if core < 2:
            m["x"] = np.ascontiguousarray(x_sample[core])
            m["cvec"] = _col(A(c)[core], 16)
            m["cmcol"] = np.ones((128, 1), f32)
            sl = A(state_lru)[core, 0]
            m["h0lru"] = np.ascontiguousarray(np.transpose(sl.reshape(2, 8, 128), (2, 1, 0)))
            sw = A(state_wkv)[core, 0]
            m["s0wkv"] = np.ascontiguousarray(np.transpose(sw, (3, 0, 1, 2)))
            m["rows"] = rows_for(True)
            m["rmask"] = rmask_for(True)
        else:
            xs = np.zeros((NT, D), f32)
            mine = assign[core - 2]
            for i in range(NSEG):
                xs[i * SEG:(i + 1) * SEG] = x_prompt[mine[i % len(mine)]]
            m["x"] = xs
            m["cvec"] = _col(A(c_ctx), 16)
            m["cmcol"] = np.zeros((128, 1), f32)
            m["h0lru"] = np.zeros((128, 8, 2), f32)
            m["s0wkv"] = np.zeros((64, 2, 16, 64), f32)
            m["rows"] = rows_for(False)
            m["rmask"] = rmask_for(False)
        in_maps.append(m)
    res = run_bass_kernel_spmd(nc, in_maps, core_ids=list(range(8)))
    R = res.results
    if DEBUG:
        global _LAST
        _LAST = R
    y_p = np.zeros((32, SEG, D), f32)
    y_s = np.zeros((2, NT, D), f32)
    ns_lru = np.zeros((32, 1, 2, LW), f32)
    ns_wkv = np.zeros((32, 1, 2, 16, 64, 64), f32)
    for core in range(8):
        r = R[core]
        if core < 2:
            y_s[core] = r["y"]
        else:
            lf = r["lru_fin"]
            wf = r["wkv_fin"]
            for i, sidx in enumerate(assign[core - 2]):
                y_p[sidx] = r["y"][i * SEG:(i + 1) * SEG]
                ns_lru[sidx, 0] = np.transpose(lf[:, :, :, i], (2, 1, 0)).reshape(2, LW)
                ns_wkv[sidx, 0] = np.transpose(wf[i], (0, 2, 3, 1))
    return (y_p, y_s, ns_lru, ns_wkv)
```

```python
import contextlib
import numpy as np
import ml_dtypes
import concourse.bass as bass
import concourse.mybir as mybir
from concourse.bass_utils import run_bass_kernel_spmd

F32 = mybir.dt.float32
BF16 = mybir.dt.bfloat16
AF = mybir.ActivationFunctionType
ALU = mybir.AluOpType
AX = mybir.AxisListType

D = 2048
NT = 4096
NSEG = 16
SEG = 256
NTILE = NT // 128
LW = 1024
RW = 1024
PRW = 3360
INW = 5408
FH = 5632
DEBUG = False
DEBUG_SS = False
REFINE = True
SAME_ENGINE_SYNC = True
NOSYNC_ENGS = ("pe",)


class Buf:
    __slots__ = ("name", "w", "r", "dsem", "dcnt")

    def __init__(self, name):
        self.name = name
        self.w = None
        self.r = []
        self.dsem = {}
        self.dcnt = {}


class Sched:
    ENG = ("pe", "dve", "act", "pool", "sp")

    def __init__(self, nc):
        self.nc = nc
        self.prog = {e: [] for e in self.ENG}
        self.cnt = {e: 0 for e in self.ENG}
        self.waited = {e: {} for e in self.ENG}
        self.sems = {}
        self.stack = []
        self.dma_state = {}
        self.epoch = 0
        self.ekey = {}
        for e in ("pe", "dve", "act", "pool"):
            self.ekey[e] = "E_" + e + "_0"
            self._mksem(self.ekey[e])
        self.finals = []
        self.free_dsems = {}
        self.stage_bufs = []
        self.ndsem = 0

    def keep(self):
        self.stage_bufs = []

    def _mksem(self, key):
        cm = self.nc.semaphore(key)
        h = cm.__enter__()
        self.stack.append(cm)
        self.sems[key] = h
        return h

    def _deps(self, eng, reads, writes):
        best = {}
        own = self.ekey.get(eng, "none")
        wd = self.waited[eng]

        def add(dep):
            k, v = dep
            if k == own and (not SAME_ENGINE_SYNC or eng in NOSYNC_ENGS):
                return
            if wd.get(k, 0) >= v:
                return
            if best.get(k, 0) < v:
                best[k] = v
        for b in reads:
            if b.w is not None:
                add(b.w)
        for b in writes:
            if b.w is not None:
                add(b.w)
            for d in b.r:
                add(d)
        waits = []
        for k, v in best.items():
            wd[k] = v
            waits.append((k, v))
        return waits

    def _mark(self, me, reads, writes):
        for b in reads:
            b.r.append(me)
            if len(b.r) > 64:
                mx = {}
                for k, v in b.r:
                    if mx.get(k, 0) < v:
                        mx[k] = v
                b.r = list(mx.items())
        for b in writes:
            b.w = me
            b.r = []

    def op(self, eng, fn, reads=(), writes=()):
        waits = self._deps(eng, reads, writes)
        self.cnt[eng] += 1
        me = (self.ekey[eng], self.cnt[eng])
        self.prog[eng].append((waits, [fn], (me[0], 1)))
        self._mark(me, reads, writes)
        return me

    def dma(self, eng, fns, reads=(), writes=(), sembuf=None, final=False):
        if not isinstance(fns, (list, tuple)):
            fns = [fns]
        if sembuf is None:
            sembuf = writes[0] if writes else reads[0]
        cls = eng
        if sembuf.dsem.get(cls) is None:
            pool_ = self.free_dsems.setdefault(cls, [])
            if pool_:
                sembuf.dsem[cls], sembuf.dcnt[cls] = pool_.pop()
            else:
                self.ndsem += 1
                sembuf.dsem[cls] = "D_%d" % self.ndsem
                sembuf.dcnt[cls] = 0
                self._mksem(sembuf.dsem[cls])
            self.stage_bufs.append((sembuf, cls))
        waits = self._deps(eng, reads, writes)
        dk, dc = sembuf.dsem[cls], sembuf.dcnt[cls]
        if dc > 0 and self.waited[eng].get(dk, 0) < dc:
            self.waited[eng][dk] = dc
            waits.append((dk, dc))
        sembuf.dcnt[cls] = dc + 16 * len(fns)
        me = (dk, sembuf.dcnt[cls])
        self.prog[eng].append((waits, list(fns), (me[0], 16)))
        self._mark(me, reads, writes)
        if final:
            self.finals.append(me)
        return me

    def barrier(self, bufs=()):
        deps = [(self.ekey[e], self.cnt[e]) for e in ("pe", "dve", "act", "pool") if self.cnt[e] > 0]
        for k in self.sems:
            if k.startswith("D_"):
                pass
        for (b, cls) in self.stage_bufs:
            if b.dsem.get(cls) is not None and b.dcnt[cls] > 0:
                deps.append((b.dsem[cls], b.dcnt[cls]))
        for e in self.ENG:
            waits = []
            for (k, v) in deps:
                if k == self.ekey.get(e):
                    continue
                if self.waited[e].get(k, 0) < v:
                    self.waited[e][k] = v
                    waits.append((k, v))
            if waits:
                self.prog[e].append((waits, [], None))
        for (b, cls) in self.stage_bufs:
            if b.dcnt[cls] < 20000:
                self.free_dsems.setdefault(cls, []).append((b.dsem[cls], b.dcnt[cls]))
            b.dsem[cls] = None
        self.stage_bufs = []
        self.epoch += 1
        for e in ("pe", "dve", "act", "pool"):
            self.ekey[e] = "E_%s_%d" % (e, self.epoch)
            self._mksem(self.ekey[e])
            self.cnt[e] = 0

    def emit(self):
        nc = self.nc
        engmap = {"pe": "tensor", "dve": "vector", "act": "scalar", "pool": "gpsimd", "sp": "sync"}
        finals = {}
        for k, v in self.finals:
            finals[k] = max(finals.get(k, 0), v)
        with nc.Block() as block:
            for e in self.ENG:
                prog = self.prog[e]

                def body(eng, prog=prog, e=e):
                    for (waits, fns, inc) in prog:
                        for (k, v) in waits:
                            eng.wait_ge(self.sems[k], v)
                        for fn in fns:
                            ins = fn(eng)
                            ins.then_inc(self.sems[inc[0]], inc[1])
                    if e == "sp":
                        for k, v in finals.items():
                            eng.wait_ge(self.sems[k], v)
                getattr(block, engmap[e])(body)

    def close(self):
        for cm in reversed(self.stack):
            cm.__exit__(None, None, None)


class Ctx:
    def __init__(self, nc):
        self.nc = nc
        self.S = Sched(nc)
        self.uid = 0
        self.rr = 0

    def sb(self, es, name, shape, dt):
        self.uid += 1
        t = es.enter_context(self.nc.sbuf_tensor("%s_%d" % (name, self.uid), list(shape), dt))
        return t, Buf("%s_%d" % (name, self.uid))

    def ps(self, es, name, shape, dt):
        self.uid += 1
        t = es.enter_context(self.nc.psum_tensor("%s_%d" % (name, self.uid), list(shape), dt))
        return t, Buf("%s_%d" % (name, self.uid))

    def ring(self, es, name, n, shape, dt):
        return [self.sb(es, "%s%d" % (name, i), shape, dt) for i in range(n)]


def build_program():
    nc = bass.Bass("TRN2", target_bir_lowering=False)
    C = Ctx(nc)
    S = C.S

    def din(name, shape, dt=F32):
        return nc.dram_tensor(name, list(shape), dt, kind="ExternalInput").ap()

    def dout(name, shape, dt=F32):
        return nc.dram_tensor(name, list(shape), dt, kind="ExternalOutput").ap()

    def dscr(name, shape, dt=F32):
        kind = "ExternalOutput" if DEBUG else "Internal"
        return nc.dram_tensor(name, list(shape), dt, kind=kind).ap()

    x = din("x", [NT, D])
    cvec = din("cvec", [128, 16])
    cmcol = din("cmcol", [128, 1])
    h0lru = din("h0lru", [128, 8, 2])
    s0wkv = din("s0wkv", [64, 2, 16, 64])
    w_mod = din("w_mod", [D, 6 * D])
    b_modT = din("b_modT", [128, 96])
    ncols = din("ncols", [128, 4, 16])
    w_in = din("w_in", [D, INW])
    convc = din("convc", [128, 8, 5])
    lru_wr = din("lru_wr", [2, 8, 128, 128])
    lru_wi = din("lru_wi", [2, 8, 128, 128])
    lru_bc = din("lru_bc", [128, 2, 8, 2])
    lru_lam = din("lru_lam", [128, 2, 8])
    rows = din("rows", [1, 3360 * 5 + 1024 * 9])
    rmask = din("rmask", [128, 4])
    w_up = din("w_up", [2, 64, RW])
    a_up = din("a_up", [2, 64, RW])
    g_up = din("g_up", [160, RW])
    w_out = din("w_out", [D, D])
    w_gu = din("w_gu", [D, 2 * FH])
    w_down = din("w_down", [FH, D])
    ident_in = din("ident", [128, 128])
    tri_in = din("tri", [128, 2, 128])
    gmask_in = din("gmask", [128, 2, 512])
    y_out = dout("y", [NT, D])
    lru_fin = dout("lru_fin", [128, 8, 2, 16])
    wkv_fin = dout("wkv_fin", [16, 2, 64, 16, 64])
    xlglT = dscr("xlglT", [2048, NT])
    projTok = dscr("projTok", [NT, PRW])
    mixT = dscr("mixT", [2048, NT], BF16)
    prep = {n: dscr("prep_" + n, [NT, RW]) for n in ("R", "KK", "V", "KD0", "KD1", "BD0", "BD1", "LD0", "LD1", "G", "BON")}
    YD = [dscr("YF", [NT, RW]), dscr("YB", [NT, RW])]
    X1 = dscr("X1", [NT, D])
    h2R = dscr("h2R", [NTILE, 128, 16, 128], BF16)
    mixR = dscr("mixR", [NTILE, 128, 8, 128], BF16)
    actT = dscr("actT", [FH, NT], BF16)
    FD = dscr("FD", [NT, D])
    DBGSS = dscr("DBGSS", [128, 64])

    def tb(name):
        return [Buf("%s_t%d" % (name, i)) for i in range(NTILE)]
    B_xlgl = [[Buf("xlgl_%d_%d" % (c, g)) for g in range(4)] for c in range(16)]
    B_proj = tb("proj")
    B_mixT = tb("mixT")
    B_mixL = [Buf("mixL%d" % h) for h in range(8)]
    B_prep = {n: tb("prep" + n) for n in prep}
    B_Y = [tb("YF"), tb("YB")]
    B_X1 = tb("X1")
    B_h2T = tb("h2T")
    B_actT = [[Buf("actT_%d_%d" % (f, g)) for g in range(4)] for f in range(44)]
    B_FD = tb("FD")

    ROWOFF = {}
    o = 0
    for n, w in (("mu", 3360), ("cmM1", 3360), ("cmP1", 3360), ("cmU", 3360), ("cmD", 3360),
                 ("w00", 1024), ("w01", 1024), ("a00", 1024), ("a01", 1024), ("kk", 1024), ("ka", 1024),
                 ("rk", 1024), ("lnw", 1024), ("lnb", 1024)):
        ROWOFF[n] = (o, w)
        o += w

    def rowap(n, lo=0, hi=None):
        o0, w = ROWOFF[n]
        if hi is None:
            hi = w
        return rows[0:1, o0 + lo:o0 + hi].broadcast_to([128, hi - lo])

    with contextlib.ExitStack() as es0:
        ident, b_ident = C.sb(es0, "ident", [128, 128], F32)
        identb, b_identb = C.sb(es0, "identb", [128, 128], BF16)
        ones, b_ones = C.sb(es0, "ones", [128, 128], F32)
        tri, b_tri = C.sb(es0, "tri", [128, 2, 128], F32)
        gmask, b_gmask = C.sb(es0, "gmask", [128, 2, 512], F32)
        cm, b_cm = C.sb(es0, "cm", [128, 1], F32)
        modT, b_modT_ = C.sb(es0, "modT", [128, 96], F32)
        ncl, b_ncl = C.sb(es0, "ncl", [128, 4, 16], F32)
        Am, b_Am = C.sb(es0, "Am", [128, 16], F32)
        Af, b_Af = C.sb(es0, "Af", [128, 16], F32)
        G1r, b_G1r = C.sb(es0, "G1r", [128, D], F32)
        G2r, b_G2r = C.sb(es0, "G2r", [128, D], F32)
        S.dma("sp", lambda e: e.dma_start(out=ident[:], in_=ident_in), writes=[b_ident])
        S.dma("sp", lambda e: e.dma_start(out=tri[:], in_=tri_in), writes=[b_tri])
        S.dma("sp", lambda e: e.dma_start(out=gmask[:], in_=gmask_in), writes=[b_gmask])
        S.dma("sp", lambda e: e.dma_start(out=cm[:], in_=cmcol), writes=[b_cm])
        S.dma("sp", lambda e: e.dma_start(out=ncl[:], in_=ncols), writes=[b_ncl])
        S.op("dve", lambda e: e.tensor_copy(out=identb[:], in_=ident[:]), reads=[b_ident], writes=[b_identb])
        S.op("dve", lambda e: e.memset(ones[:], 1.0), writes=[b_ones])
        S.keep()

        cast_engs = ["pool", "dve", "act"]

        def cast(out_ap, in_ap, reads, writes, eng=None):
            if eng is None:
                eng = cast_engs[C.rr % 3]
                C.rr += 1
            if eng == "act":
                S.op("act", lambda e: e.copy(out=out_ap, in_=in_ap), reads=reads, writes=writes)
            else:
                S.op(eng, lambda e: e.tensor_copy(out=out_ap, in_=in_ap), reads=reads, writes=writes)

        def rstd_from_ss(ss, b_ss, n, eps, eng_tmp):
            S.op("dve", lambda e: e.tensor_scalar(out=ss, in0=ss, scalar1=1.0 / n, scalar2=eps, op0=ALU.mult, op1=ALU.add), reads=[b_ss], writes=[b_ss])
            S.op("act", lambda e: e.activation(out=ss, in_=ss, func=AF.Sqrt), reads=[b_ss], writes=[b_ss])
            S.op("dve", lambda e: e.reciprocal(out=ss, in_=ss), reads=[b_ss], writes=[b_ss])

        with contextlib.ExitStack() as es:
            cv, b_cv = C.sb(es, "cv", [128, 16], F32)
            sc, b_sc = C.sb(es, "sc", [128, 16, 2], F32)
            bmt, b_bmt = C.sb(es, "bmt", [128, 96], F32)
            wm = C.ring(es, "wm", 2, [128, 16, 512], F32)
            pmod, b_pmod = C.ps(es, "pmod", [128, 96, 2], F32)
            pbc, b_pbc = C.ps(es, "pbc", [128, 512], F32)
            dg, b_dg = C.sb(es, "dg", [128, 128], F32)
            gc, b_gc = C.sb(es, "gc", [128, 2, 16], F32)
            S.dma("sp", lambda e: e.dma_start(out=cv[:], in_=cvec), writes=[b_cv])
            S.dma("sp", lambda e: e.dma_start(out=bmt[:], in_=b_modT), writes=[b_bmt])
            S.op("act", lambda e: e.activation(out=sc[:, :, 0], in_=cv[:], func=AF.Silu), reads=[b_cv], writes=[b_sc])
            S.op("act", lambda e: e.activation(out=sc[:, :, 1], in_=cv[:], func=AF.Silu), reads=[b_cv], writes=[b_sc])
            wmv = w_mod.rearrange("(dc p) f -> p dc f", p=128)
            for j in range(24):
                wt, bw = wm[j % 2]
                S.dma("sp", [lambda e, j=j, wt=wt, q=q: e.dma_start(out=wt[:, q * 4:(q + 1) * 4, :], in_=wmv[:, q * 4:(q + 1) * 4, j * 512:(j + 1) * 512]) for q in range(4)], writes=[bw])
                for f in range(4):
                    fc = j * 4 + f
                    for dc in range(16):
                        S.op("pe", lambda e, wt=wt, f=f, dc=dc, fc=fc: e.matmul(pmod[:, fc, :], lhsT=wt[:, dc, f * 128:(f + 1) * 128], rhs=sc[:, dc, :], start=(dc == 0), stop=(dc == 15)),
                             reads=[bw, b_sc], writes=[b_pmod])
            S.op("dve", lambda e: e.tensor_tensor(out=modT[:], in0=pmod[:, :, 0], in1=bmt[:], op=ALU.add), reads=[b_pmod, b_bmt], writes=[b_modT_])
            S.op("dve", lambda e: e.scalar_tensor_tensor(out=Am[:], in0=modT[:, 16:32], scalar=1.0, in1=ncl[:, 0, :], op0=ALU.add, op1=ALU.mult), reads=[b_modT_, b_ncl], writes=[b_Am])
            S.op("dve", lambda e: e.scalar_tensor_tensor(out=Af[:], in0=modT[:, 64:80], scalar=1.0, in1=ncl[:, 2, :], op0=ALU.add, op1=ALU.mult), reads=[b_modT_, b_ncl], writes=[b_Af])
            S.op("dve", lambda e: e.tensor_tensor(out=gc[:, 0, :], in0=modT[:, 32:48], in1=ncl[:, 1, :], op=ALU.mult), reads=[b_modT_, b_ncl], writes=[b_gc])
            S.op("dve", lambda e: e.tensor_tensor(out=gc[:, 1, :], in0=modT[:, 80:96], in1=ncl[:, 3, :], op=ALU.mult), reads=[b_modT_, b_ncl], writes=[b_gc])
            for which, (Gr, bGr) in enumerate(((G1r, b_G1r), (G2r, b_G2r))):
                for c in range(16):
                    S.op("dve", lambda e, which=which, c=c: e.tensor_scalar(out=dg[:], in0=ident[:], scalar1=gc[:, which, c:c + 1], scalar2=None, op0=ALU.mult), reads=[b_ident, b_gc], writes=[b_dg])
                    S.op("pe", lambda e: e.matmul(pbc[:, 0:128], lhsT=ones[:], rhs=dg[:], start=True, stop=True), reads=[b_ones, b_dg], writes=[b_pbc])
                    S.op("act", lambda e, Gr=Gr, c=c: e.copy(out=Gr[:, c * 128:(c + 1) * 128], in_=pbc[:, 0:128]), reads=[b_pbc], writes=[bGr])
            S.barrier([b for _, b in wm] + [b_cv, b_bmt])

        def load_w(stage_ring, wt, bw, src, kch, ncol, k0=0):
            srcv = src.rearrange("(kc p) f -> p kc f", p=128)
            for q in range(0, kch, 4):
                n = min(4, kch - q)
                st, bst = stage_ring[C.uid % len(stage_ring)]
                C.uid += 1
                S.dma("sp", lambda e, st=st, q=q, n=n: e.dma_start(out=st[:, 0:n, 0:ncol], in_=srcv[:, k0 + q:k0 + q + n, :]), writes=[bst])
                cast(wt[:, q:q + n, 0:ncol], st[:, 0:n, 0:ncol], [bst], [bw])

        def norm_transpose(xt, b_xt, hT_dst, b_hT, col0, Acol, shcol, bshc, es_bufs):
            junk, b_junk, ss, b_ss, xn, b_xn, pT, b_pT = es_bufs
            S.op("act", lambda e: e.activation(out=junk[:], in_=xt[:], func=AF.Square, accum_out=ss[:]), reads=[b_xt], writes=[b_junk, b_ss])
            rstd_from_ss(ss[:], b_ss, D, 1e-6, None)
            S.op("dve", lambda e: e.tensor_scalar(out=xn[:], in0=xt[:], scalar1=ss[:, 0:1], scalar2=None, op0=ALU.mult), reads=[b_xt, b_ss], writes=[b_xn])
            for c in range(16):
                S.op("pe", lambda e, c=c: e.transpose(out=pT[:, c * 128:(c + 1) * 128], in_=xn[:, c * 128:(c + 1) * 128], identity=identb[:]), reads=[b_xn, b_identb], writes=[b_pT])
            for c in range(16):
                S.op("act", lambda e, c=c: e.activation(out=hT_dst[:, c, col0:col0 + 128], in_=pT[:, c * 128:(c + 1) * 128], func=AF.Identity, scale=Acol[:, c:c + 1], bias=shcol[:, c:c + 1]),
                     reads=[b_pT] + bshc, writes=[b_hT])

        with contextlib.ExitStack() as es:
            xts = C.ring(es, "xt", 2, [128, D], F32)
            junk, b_junk = C.sb(es, "junk", [128, D], BF16)
            ss, b_ss = C.sb(es, "ss", [128, 1], F32)
            xn, b_xn = C.sb(es, "xn", [128, D], BF16)
            pT, b_pT = C.ps(es, "pT", [128, D], BF16)
            hTs = C.ring(es, "hT", 2, [128, 16, 1024], BF16)
            wst = C.ring(es, "wst", 4, [128, 4, 512], F32)
            wbs = C.ring(es, "wb", 2, [128, 16, 512], BF16)
            pmm = [C.ps(es, "pmm%d" % i, [128, 512], F32) for i in range(4)]
            ost = C.ring(es, "ost", 4, [128, 512], F32)
            nbufs = (junk, b_junk, ss, b_ss, xn, b_xn, pT, b_pT)
            oi = 0

            def norm_tile(g, t):
                tile = g * 8 + t
                hT, b_hT = hTs[g % 2]
                xt, b_xt = xts[tile % 2]
                S.dma("sp", lambda e, xt=xt, tile=tile: e.dma_start(out=xt[:], in_=x[tile * 128:(tile + 1) * 128, :]), writes=[b_xt])
                norm_transpose(xt, b_xt, hT, b_hT, t * 128, Am, modT[:, 0:16], [b_Am, b_modT_], nbufs)
            for t in range(8):
                norm_tile(0, t)
            wi = 0
            for g in range(4):
                hT, b_hT = hTs[g % 2]
                nxt = list(range(8)) if g < 3 else []
                for j in range(4):
                    wt, bw = wbs[wi % 2]
                    wi += 1
                    load_w(wst, wt, bw, w_in[:, j * 512:(j + 1) * 512], 16, 512)
                    for f in range(4):
                        fc = j * 4 + f
                        for tt in range(2):
                            pm, bpm = pmm[oi % 4]
                            o_t, bo = ost[oi % 4]
                            oi += 1
                            for dc in range(16):
                                S.op("pe", lambda e, pm=pm, wt=wt, f=f, dc=dc, tt=tt, hT=hT: e.matmul(pm[:], lhsT=wt[:, dc, f * 128:(f + 1) * 128], rhs=hT[:, dc, tt * 512:(tt + 1) * 512], start=(dc == 0), stop=(dc == 15)),
                                     reads=[bw, b_hT], writes=[bpm])
                            cast(o_t[:], pm[:], [bpm], [bo], eng=("act" if oi % 2 else "dve"))
                            S.dma("pool", lambda e, o_t=o_t, fc=fc, g=g, tt=tt: e.dma_start(out=xlglT[fc * 128:(fc + 1) * 128, g * 1024 + tt * 512:g * 1024 + (tt + 1) * 512], in_=o_t[:]),
                                  reads=[bo], writes=[B_xlgl[fc][g]], sembuf=bo)
                    if nxt:
                        norm_tile(g + 1, nxt.pop(0))
                for ct in range(7):
                    ncol = 512 if ct < 6 else PRW - 6 * 512
                    wt, bw = wbs[wi % 2]
                    wi += 1
                    load_w(wst, wt, bw, w_in[:, 2048 + ct * 512:2048 + ct * 512 + ncol], 16, ncol)
                    for t in range(8):
                        tile = g * 8 + t
                        pm, bpm = pmm[oi % 4]
                        o_t, bo = ost[oi % 4]
                        oi += 1
                        for dc in range(16):
                            S.op("pe", lambda e, pm=pm, wt=wt, dc=dc, t=t, ncol=ncol, hT=hT: e.matmul(pm[:, 0:ncol], lhsT=hT[:, dc, t * 128:(t + 1) * 128], rhs=wt[:, dc, 0:ncol], start=(dc == 0), stop=(dc == 15)),
                                 reads=[bw, b_hT], writes=[bpm])
                        cast(o_t[:, 0:ncol], pm[:, 0:ncol], [bpm], [bo], eng=("act" if oi % 2 else "dve"))
                        S.dma("pool", lambda e, o_t=o_t, tile=tile, ct=ct, ncol=ncol: e.dma_start(out=projTok[tile * 128:(tile + 1) * 128, ct * 512:ct * 512 + ncol], in_=o_t[:, 0:ncol]),
                              reads=[bo], writes=[B_proj[tile]], sembuf=bo)
                    if nxt:
                        norm_tile(g + 1, nxt.pop(0))
                while nxt:
                    norm_tile(g + 1, nxt.pop(0))
            S.barrier([b for _, b in xts] + [b for _, b in wst] + [b for _, b in ost])

        with contextlib.ExitStack() as es:
            X, bX = C.sb(es, "X", [128, NT], F32)
            GL, bGL = C.sb(es, "GL", [128, NT], F32)
            XC, bXC = C.sb(es, "XC", [128, NT], F32)
            XCB, bXCB = C.sb(es, "XCB", [128, NT], BF16)
            R1, bR1 = C.sb(es, "R1", [128, NT], F32)
            I1, bI1 = C.sb(es, "I1", [128, NT], F32)
            E1, bE1 = C.sb(es, "E1", [128, NT], F32)
            E2, bE2 = C.sb(es, "E2", [128, NT], F32)
            OB, bOB = C.sb(es, "OB", [128, NT], BF16)
            cc, bcc = C.sb(es, "cc", [128, 8, 5], F32)
            ccm, bccm = C.sb(es, "ccm", [128, 8, 5], F32)
            lbc, blbc = C.sb(es, "lbc", [128, 2, 8, 2], F32)
            lam, blam = C.sb(es, "lam", [128, 2, 8], F32)
            spc, bspc = C.sb(es, "spc", [128, 2, 8, 2], F32)
            tmpc, btmpc = C.sb(es, "tmpc", [128, 16], F32)
            z2, bz2 = C.sb(es, "z2", [128, 16], F32)
            pz, bpz = C.sb(es, "pz", [128, 16], F32)
            h0, bh0 = C.sb(es, "h0", [128, 8, 2], F32)
            fin, bfin = C.sb(es, "fin", [128, 8, 2, 16], F32)
            gst = C.ring(es, "gst", 2, [128, 128], F32)
            gwb = C.ring(es, "gwb", 4, [128, 128], BF16)
            pg = [C.ps(es, "pg%d" % i, [128, 512], F32) for i in range(4)]
            S.dma("sp", lambda e: e.dma_start(out=cc[:], in_=convc), writes=[bcc])
            S.dma("sp", lambda e: e.dma_start(out=lbc[:], in_=lru_bc), writes=[blbc])
            S.dma("sp", lambda e: e.dma_start(out=lam[:], in_=lru_lam), writes=[blam])
            S.dma("sp", lambda e: e.dma_start(out=h0[:], in_=h0lru), writes=[bh0])
            S.op("dve", lambda e: e.tensor_scalar(out=ccm[:], in0=cc[:], scalar1=cm[:, 0:1], scalar2=None, op0=ALU.mult), reads=[bcc, b_cm], writes=[bccm])
            lamf = lam[:].rearrange("p a b -> p (a b)")
            S.op("dve", lambda e: e.tensor_scalar(out=z2[:], in0=lamf, scalar1=-1.0, scalar2=None, op0=ALU.mult), reads=[blam], writes=[bz2])
            S.op("dve", lambda e: e.tensor_tensor(out=tmpc[:], in0=lamf, in1=z2[:], op=ALU.max), reads=[blam, bz2], writes=[btmpc])
            S.op("act", lambda e: e.activation(out=tmpc[:], in_=tmpc[:], func=AF.Exp, scale=-1.0), reads=[btmpc], writes=[btmpc])
            S.op("dve", lambda e: e.tensor_scalar(out=z2[:], in0=tmpc[:], scalar1=2.0, scalar2=None, op0=ALU.add), reads=[btmpc], writes=[bz2])
            S.op("dve", lambda e: e.reciprocal(out=z2[:], in_=z2[:]), reads=[bz2], writes=[bz2])
            S.op("dve", lambda e: e.tensor_tensor(out=tmpc[:], in0=tmpc[:], in1=z2[:], op=ALU.mult), reads=[btmpc, bz2], writes=[btmpc])
            S.op("dve", lambda e: e.tensor_tensor(out=z2[:], in0=tmpc[:], in1=tmpc[:], op=ALU.mult), reads=[btmpc], writes=[bz2])
            S.op("dve", lambda e: e.memset(pz[:], 1.0 / 13.0), writes=[bpz])
            for coef in (1.0 / 11, 1.0 / 9, 1.0 / 7, 1.0 / 5, 1.0 / 3, 1.0):
                S.op("dve", lambda e: e.tensor_tensor(out=pz[:], in0=pz[:], in1=z2[:], op=ALU.mult), reads=[bpz, bz2], writes=[bpz])
                S.op("dve", lambda e, coef=coef: e.tensor_scalar(out=pz[:], in0=pz[:], scalar1=float(coef), scalar2=None, op0=ALU.add), reads=[bpz], writes=[bpz])
            S.op("dve", lambda e: e.tensor_tensor(out=pz[:], in0=pz[:], in1=tmpc[:], op=ALU.mult), reads=[bpz, btmpc], writes=[bpz])
            S.op("dve", lambda e: e.tensor_scalar(out=z2[:], in0=lamf, scalar1=-1.0, scalar2=0.0, op0=ALU.mult, op1=ALU.max), reads=[blam], writes=[bz2])
            S.op("dve", lambda e: e.scalar_tensor_tensor(out=pz[:], in0=pz[:], scalar=2.0, in1=z2[:], op0=ALU.mult, op1=ALU.add), reads=[bpz, bz2], writes=[bpz])
            spf = spc[:].rearrange("p a b c -> p (a b) c")
            S.op("dve", lambda e: e.tensor_scalar(out=spf[:, :, 0], in0=pz[:], scalar1=-8.0, scalar2=None, op0=ALU.mult), reads=[bpz], writes=[bspc])
            S.op("dve", lambda e: e.tensor_scalar(out=spf[:, :, 1], in0=pz[:], scalar1=-16.0, scalar2=None, op0=ALU.mult), reads=[bpz], writes=[bspc])

            def seg(ap):
                return ap.rearrange("p (s t) -> p s t", t=SEG)
            gi = 0
            for h in range(8):
                S.dma("sp", [lambda e, h=h, g=g: e.dma_start(out=X[:, g * 1024:(g + 1) * 1024], in_=xlglT[h * 128:(h + 1) * 128, g * 1024:(g + 1) * 1024]) for g in range(4)],
                      reads=[B_xlgl[h][g] for g in range(4)], writes=[bX])
                S.dma("sp", [lambda e, h=h, g=g: e.dma_start(out=GL[:, g * 1024:(g + 1) * 1024], in_=xlglT[(8 + h) * 128:(9 + h) * 128, g * 1024:(g + 1) * 1024]) for g in range(4)],
                      reads=[B_xlgl[8 + h][g] for g in range(4)], writes=[bGL])
                S.op("act", lambda e, h=h: e.activation(out=XC[:], in_=X[:], func=AF.Identity, scale=cc[:, h, 2:3], bias=cc[:, h, 4:5]), reads=[bX, bcc], writes=[bXC])
                Xs, XCs = seg(X[:]), seg(XC[:])
                for (tap, dlt) in ((0, -2), (1, -1), (3, 1)):
                    if dlt < 0:
                        o_v, i_v = XCs[:, :, -dlt:], Xs[:, :, :SEG + dlt]
                    else:
                        o_v, i_v = XCs[:, :, :SEG - dlt], Xs[:, :, dlt:]
                    S.op("dve", lambda e, h=h, tap=tap, o_v=o_v, i_v=i_v: e.scalar_tensor_tensor(out=o_v, in0=i_v, scalar=cc[:, h, tap:tap + 1], in1=o_v, op0=ALU.mult, op1=ALU.add), reads=[bX, bXC, bcc], writes=[bXC])
                fix = ((1, XCs[:, 1:, 0:1], Xs[:, :15, 255:256]), (0, XCs[:, 1:, 0:1], Xs[:, :15, 254:255]), (0, XCs[:, 1:, 1:2], Xs[:, :15, 255:256]), (3, XCs[:, :15, 255:256], Xs[:, 1:, 0:1]))
                for (tap, o_v, i_v) in fix:
                    S.op("dve", lambda e, h=h, tap=tap, o_v=o_v, i_v=i_v: e.scalar_tensor_tensor(out=o_v, in0=i_v, scalar=ccm[:, h, tap:tap + 1], in1=o_v, op0=ALU.mult, op1=ALU.add), reads=[bX, bXC, bccm], writes=[bXC])
                S.op("act", lambda e: e.copy(out=XCB[:], in_=XC[:]), reads=[bXC], writes=[bXCB])
                HS = []
                for d in range(2):
                    Ebuf, bE = (E1, bE1) if d == 0 else (E2, bE2)
                    for (wsrc, dst, bdst, bi) in ((lru_wr, R1, bR1, 0), (lru_wi, I1, bI1, 1)):
                        gs, bgs = gst[gi % 2]
                        gw, bgw = gwb[gi % 4]
                        gi += 1
                        S.dma("sp", lambda e, gs=gs, wsrc=wsrc, d=d, h=h: e.dma_start(out=gs[:], in_=wsrc[d, h]), writes=[bgs])
                        cast(gw[:], gs[:], [bgs], [bgw], eng="pool")
                        for tt in range(8):
                            pm, bpm = pg[tt % 4]
                            S.op("pe", lambda e, pm=pm, gw=gw, tt=tt: e.matmul(pm[:], lhsT=gw[:], rhs=XCB[:, tt * 512:(tt + 1) * 512], start=True, stop=True), reads=[bgw, bXCB], writes=[bpm])
                            S.op("act", lambda e, pm=pm, dst=dst, tt=tt, d=d, h=h, bi=bi: e.activation(out=dst[:, tt * 512:(tt + 1) * 512], in_=pm[:], func=AF.Sigmoid, bias=lbc[:, d, h, bi:bi + 1]),
                                 reads=[bpm, blbc], writes=[bdst])
                    S.op("act", lambda e, Ebuf=Ebuf, d=d, h=h: e.activation(out=Ebuf[:], in_=R1[:], func=AF.Exp, scale=spc[:, d, h, 1:2]), reads=[bR1, bspc], writes=[bE])
                    S.op("act", lambda e, d=d, h=h: e.activation(out=R1[:], in_=R1[:], func=AF.Exp, scale=spc[:, d, h, 0:1]), reads=[bR1, bspc], writes=[bR1])
                    S.op("act", lambda e, Ebuf=Ebuf: e.activation(out=Ebuf[:], in_=Ebuf[:], func=AF.Identity, scale=-1.0, bias=1.0), reads=[bE], writes=[bE])
                    S.op("act", lambda e, Ebuf=Ebuf: e.activation(out=Ebuf[:], in_=Ebuf[:], func=AF.Sqrt), reads=[bE], writes=[bE])
                    S.op("pool", lambda e: e.tensor_tensor(out=I1[:], in0=I1[:], in1=XC[:], op=ALU.mult), reads=[bI1, bXC], writes=[bI1])
                    S.op("dve", lambda e, Ebuf=Ebuf: e.tensor_tensor(out=I1[:], in0=I1[:], in1=Ebuf[:], op=ALU.mult), reads=[bI1, bE], writes=[bI1])
                    As = seg(R1[:])
                    if d == 0:
                        S.op("dve", lambda e, As=As: e.tensor_scalar(out=As[:, 1:, 0:1], in0=As[:, 1:, 0:1], scalar1=cm[:, 0:1], scalar2=None, op0=ALU.mult), reads=[bR1, b_cm], writes=[bR1])
                        S.op("dve", lambda e, Ebuf=Ebuf, h=h: e.tensor_tensor_scan(out=Ebuf[:], data0=R1[:], data1=I1[:], initial=h0[:, h, 0:1], op0=ALU.mult, op1=ALU.add), reads=[bR1, bI1, bh0], writes=[bE])
                        S.op("pool", lambda e, Ebuf=Ebuf, h=h: e.tensor_copy(out=fin[:, h, 0, :], in_=seg(Ebuf[:])[:, :, 255]), reads=[bE], writes=[bfin])
                    else:
                        S.op("dve", lambda e, As=As: e.tensor_scalar(out=As[:, :15, 255:256], in0=As[:, :15, 255:256], scalar1=cm[:, 0:1], scalar2=None, op0=ALU.mult), reads=[bR1, b_cm], writes=[bR1])
                        S.op("dve", lambda e, Ebuf=Ebuf, h=h: e.tensor_tensor_scan(out=Ebuf[:, ::-1], data0=R1[:, ::-1], data1=I1[:, ::-1], initial=h0[:, h, 1:2], op0=ALU.mult, op1=ALU.add), reads=[bR1, bI1, bh0], writes=[bE])
                        S.op("pool", lambda e, Ebuf=Ebuf, h=h: e.tensor_copy(out=fin[:, h, 1, :], in_=seg(Ebuf[:])[:, :, 0]), reads=[bE], writes=[bfin])
                S.op("pool", lambda e: e.tensor_tensor(out=E1[:], in0=E1[:], in1=E2[:], op=ALU.add), reads=[bE1, bE2], writes=[bE1])
                S.op("act", lambda e: e.activation(out=R1[:], in_=GL[:], func=AF.Square), reads=[bGL], writes=[bR1])
                S.op("act", lambda e: e.activation(out=R1[:], in_=R1[:], func=AF.Identity, scale=0.044715, bias=1.0), reads=[bR1], writes=[bR1])
                S.op("dve", lambda e: e.tensor_tensor(out=R1[:], in0=R1[:], in1=GL[:], op=ALU.mult), reads=[bR1, bGL], writes=[bR1])
                S.op("act", lambda e: e.activation(out=R1[:], in_=R1[:], func=AF.Sigmoid, scale=1.5957691216057308), reads=[bR1], writes=[bR1])
                S.op("pool", lambda e: e.tensor_tensor(out=R1[:], in0=R1[:], in1=GL[:], op=ALU.mult), reads=[bR1, bGL], writes=[bR1])
                S.op("dve", lambda e: e.tensor_tensor(out=OB[:], in0=R1[:], in1=E1[:], op=ALU.mult), reads=[bR1, bE1], writes=[bOB])
                S.dma("pool", lambda e, h=h: e.dma_start(out=mixT[h * 128:(h + 1) * 128, :], in_=OB[:]), reads=[bOB], writes=[B_mixL[h]], sembuf=bOB)
            S.dma("pool", lambda e: e.dma_start(out=lru_fin, in_=fin[:]), reads=[bfin], sembuf=bfin, final=True)
            S.barrier([bX, bGL, bOB, bfin, bcc, blbc, blam, bh0] + [b for _, b in gst])

        with contextlib.ExitStack() as es:
            mu_r, bmu = C.sb(es, "mu_r", [128, PRW], F32)
            cM1, bcM1 = C.sb(es, "cM1", [128, 1680], F32)
            cP1, bcP1 = C.sb(es, "cP1", [128, 2520], F32)
            cU, bcU = C.sb(es, "cU", [128, 840], F32)
            cD, bcD = C.sb(es, "cD", [128, 840], F32)
            rws = {}
            for n in ("w00", "w01", "a00", "a01", "kk", "ka", "rk"):
                rws[n] = C.sb(es, "row_" + n, [128, RW], F32)
            omka, bomka = C.sb(es, "omka", [128, RW], F32)
            rm, brm = C.sb(es, "rm", [128, 4], F32)
            P0s = None
            SM1s = C.ring(es, "SM1", 1, [128, 1680], F32)
            SP1s = C.ring(es, "SP1", 1, [128, 2520], F32)
            SUs = C.ring(es, "SU", 1, [128, 840], F32)
            SDs = C.ring(es, "SD", 1, [128, 840], F32)
            Mx, bMx = C.sb(es, "Mx", [128, PRW], F32)
            lb, blb = C.sb(es, "lb", [128, 288], BF16)
            lT, blT = C.sb(es, "lT", [128, 4, 128], BF16)
            pTl, bpTl = C.ps(es, "pTl", [128, 4, 128], BF16)
            pz_, bpz_ = C.ps(es, "pzz", [128, 1024], F32)
            pa_, bpa_ = C.ps(es, "paa", [128, 1024], F32)
            lw = {}
            lwst, blwst = C.sb(es, "lwst", [128, RW], F32)
            for n in ("wu0", "wu1", "au0", "au1", "gu0", "gu1"):
                lw[n] = C.sb(es, "lw_" + n, [128, RW], BF16)
            outs = {n: C.ring(es, "o" + n, (2 if n in ("KD", "BD", "LD") else 1), [128, RW], F32) for n in ("KK", "KD", "BD", "LD", "G", "BON")}
            AD, bAD = C.sb(es, "AD", [128, RW], F32)
            t16, bt16 = C.sb(es, "t16", [128, 16], F32)
            t16b, bt16b = C.sb(es, "t16b", [128, 16], F32)
            tmpR, btmpR = C.sb(es, "tmpR", [128, RW], F32)

            S.dma("sp", lambda e: e.dma_start(out=mu_r[:], in_=rowap("mu")), writes=[bmu])
            S.dma("sp", lambda e: e.dma_start(out=rm[:], in_=rmask), writes=[brm])
            for (ct, bct, nm, lo, hi) in ((cM1, bcM1, "cmM1", 0, 1680), (cP1, bcP1, "cmP1", 840, 3360), (cU, bcU, "cmU", 1680, 2520), (cD, bcD, "cmD", 2520, 3360)):
                S.dma("sp", lambda e, ct=ct, nm=nm, lo=lo, hi=hi: e.dma_start(out=ct[:], in_=rowap(nm, lo, hi)), writes=[bct])
                S.op("dve", lambda e, ct=ct, lo=lo, hi=hi: e.tensor_tensor(out=ct[:], in0=ct[:], in1=mu_r[:, lo:hi], op=ALU.mult), reads=[bct, bmu], writes=[bct])
            S.op("dve", lambda e: e.tensor_scalar(out=mu_r[:], in0=mu_r[:], scalar1=-1.0, scalar2=1.0, op0=ALU.mult, op1=ALU.add), reads=[bmu, bcM1, bcP1, bcU, bcD], writes=[bmu])
            for n in rws:
                S.dma("sp", lambda e, n=n: e.dma_start(out=rws[n][0][:], in_=rowap(n)), writes=[rws[n][1]])
            S.op("dve", lambda e: e.tensor_scalar(out=omka[:], in0=rws["ka"][0][:], scalar1=-1.0, scalar2=1.0, op0=ALU.mult, op1=ALU.add), reads=[rws["ka"][1]], writes=[bomka])
            for (n, src, p0, p1) in (("wu0", w_up[0], 0, 64), ("wu1", w_up[1], 0, 64), ("au0", a_up[0], 64, 128), ("au1", a_up[1], 64, 128), ("gu0", g_up[0:128, :], 0, 128), ("gu1", g_up[128:160, :], 0, 32)):
                S.dma("sp", lambda e, src=src, p0=p0, p1=p1: e.dma_start(out=lwst[p0:p1, :], in_=src), writes=[blwst])
                S.op("dve", lambda e, n=n, p0=p0, p1=p1: e.tensor_copy(out=lw[n][0][p0:p1, :], in_=lwst[p0:p1, :]), reads=[blwst], writes=[lw[n][1]])

            def h3(ap):
                return ap.rearrange("p (h k) -> p h k", k=64)

            def bc16(ap):
                return ap.unsqueeze(2).broadcast_to([128, 16, 64])
            for tile in range(NTILE):
                par = tile % 2
                t0 = tile * 128
                P0, bP0 = Mx, bMx
                SM1, bSM1 = SM1s[0]
                SP1, bSP1 = SP1s[0]
                SU, bSU = SUs[0]
                SD, bSD = SDs[0]
                S.dma("sp", lambda e, P0=P0, t0=t0: e.dma_start(out=P0[:], in_=projTok[t0:t0 + 128, :]), reads=[B_proj[tile]], writes=[bP0])
                if tile == 0:
                    S.op("pool", lambda e, SM1=SM1: e.memset(SM1[:], 0.0), writes=[bSM1])
                    S.dma("sp", lambda e, SM1=SM1: e.dma_start(out=SM1[1:128, :], in_=projTok[0:127, 0:1680]), reads=[B_proj[0]], writes=[bSM1])
                else:
                    S.dma("sp", lambda e, SM1=SM1, t0=t0: e.dma_start(out=SM1[:], in_=projTok[t0 - 1:t0 + 127, 0:1680]), reads=[B_proj[tile - 1], B_proj[tile]], writes=[bSM1])
                if tile == NTILE - 1:
                    S.op("pool", lambda e, SP1=SP1: e.memset(SP1[:], 0.0), writes=[bSP1])
                    S.dma("sp", lambda e, SP1=SP1, t0=t0: e.dma_start(out=SP1[0:127, :], in_=projTok[t0 + 1:t0 + 128, 840:3360]), reads=[B_proj[tile]], writes=[bSP1])
                else:
                    S.dma("sp", lambda e, SP1=SP1, t0=t0: e.dma_start(out=SP1[:], in_=projTok[t0 + 1:t0 + 129, 840:3360]), reads=[B_proj[tile], B_proj[tile + 1]], writes=[bSP1])
                if tile == 0:
                    S.op("pool", lambda e, SU=SU: e.memset(SU[:], 0.0), writes=[bSU])
                    S.dma("sp", lambda e, SU=SU: e.dma_start(out=SU[64:128, :], in_=projTok[0:64, 1680:2520]), reads=[B_proj[0]], writes=[bSU])
                else:
                    S.dma("sp", lambda e, SU=SU, t0=t0: e.dma_start(out=SU[:], in_=projTok[t0 - 64:t0 + 64, 1680:2520]), reads=[B_proj[tile - 1], B_proj[tile]], writes=[bSU])
                if tile == NTILE - 1:
                    S.op("pool", lambda e, SD=SD: e.memset(SD[:], 0.0), writes=[bSD])
                    S.dma("sp", lambda e, SD=SD, t0=t0: e.dma_start(out=SD[0:64, :], in_=projTok[t0 + 64:t0 + 128, 2520:3360]), reads=[B_proj[tile]], writes=[bSD])
                else:
                    S.dma("sp", lambda e, SD=SD, t0=t0: e.dma_start(out=SD[:], in_=projTok[t0 + 64:t0 + 192, 2520:3360]), reads=[B_proj[tile], B_proj[tile + 1]], writes=[bSD])
                S.op("dve", lambda e: e.tensor_tensor(out=Mx[:], in0=Mx[:], in1=mu_r[:], op=ALU.mult), reads=[bMx, bmu], writes=[bMx])
                S.op("dve", lambda e, SM1=SM1, par=par: e.scalar_tensor_tensor(out=SM1[:], in0=SM1[:], scalar=rm[:, par:par + 1], in1=cM1[:], op0=ALU.mult, op1=ALU.mult), reads=[bSM1, brm, bcM1], writes=[bSM1])
                S.op("dve", lambda e, SM1=SM1: e.tensor_tensor(out=Mx[:, 0:1680], in0=Mx[:, 0:1680], in1=SM1[:], op=ALU.add), reads=[bSM1, bMx], writes=[bMx])
                S.op("dve", lambda e, SP1=SP1, par=par: e.scalar_tensor_tensor(out=SP1[:], in0=SP1[:], scalar=rm[:, 2 + par:3 + par], in1=cP1[:], op0=ALU.mult, op1=ALU.mult), reads=[bSP1, brm, bcP1], writes=[bSP1])
                S.op("dve", lambda e, SP1=SP1: e.tensor_tensor(out=Mx[:, 840:3360], in0=Mx[:, 840:3360], in1=SP1[:], op=ALU.add), reads=[bSP1, bMx], writes=[bMx])
                S.op("pool", lambda e, SU=SU: e.tensor_tensor(out=SU[:], in0=SU[:], in1=cU[:], op=ALU.mult), reads=[bSU, bcU], writes=[bSU])
                S.op("dve", lambda e, SU=SU: e.tensor_tensor(out=Mx[:, 1680:2520], in0=Mx[:, 1680:2520], in1=SU[:], op=ALU.add), reads=[bSU, bMx], writes=[bMx])
                S.op("pool", lambda e, SD=SD: e.tensor_tensor(out=SD[:], in0=SD[:], in1=cD[:], op=ALU.mult), reads=[bSD, bcD], writes=[bSD])
                S.op("dve", lambda e, SD=SD: e.tensor_tensor(out=Mx[:, 2520:3360], in0=Mx[:, 2520:3360], in1=SD[:], op=ALU.add), reads=[bSD, bMx], writes=[bMx])
                r_ap, k_ap, v_ap = Mx[:, 0:1024], Mx[:, 1024:2048], Mx[:, 2048:3072]
                S.dma("pool", lambda e, t0=t0: e.dma_start(out=prep["R"][t0:t0 + 128, :], in_=Mx[:, 0:1024]), reads=[bMx], writes=[B_prep["R"][tile]], sembuf=bMx)
                S.dma("pool", lambda e, t0=t0: e.dma_start(out=prep["V"][t0:t0 + 128, :], in_=Mx[:, 2048:3072]), reads=[bMx], writes=[B_prep["V"][tile]], sembuf=bMx)
                S.op("act", lambda e: e.activation(out=lb[:, 0:64], in_=Mx[:, 3072:3136], func=AF.Tanh), reads=[bMx], writes=[blb])
                S.op("act", lambda e: e.copy(out=lb[:, 64:128], in_=Mx[:, 3136:3200]), reads=[bMx], writes=[blb])
                S.op("act", lambda e: e.activation(out=lb[:, 128:288], in_=Mx[:, 3200:3360], func=AF.Sigmoid), reads=[bMx], writes=[blb])
                S.op("pe", lambda e: e.transpose(out=pTl[:, 0, :], in_=lb[:, 0:128], identity=identb[:]), reads=[blb, b_identb], writes=[bpTl])
                S.op("pe", lambda e: e.transpose(out=pTl[:, 1, :], in_=lb[:, 128:256], identity=identb[:]), reads=[blb, b_identb], writes=[bpTl])
                S.op("pe", lambda e: e.transpose(out=pTl[0:32, 2, :], in_=lb[:, 256:288], identity=identb[:]), reads=[blb, b_identb], writes=[bpTl])
                S.op("dve", lambda e: e.tensor_copy(out=lT[:, 0:2, :], in_=pTl[:, 0:2, :]), reads=[bpTl], writes=[blT])
                S.op("dve", lambda e: e.tensor_copy(out=lT[0:32, 2, :], in_=pTl[0:32, 2, :]), reads=[bpTl], writes=[blT])
                oG, boG = outs["G"][0]
                for hh in range(2):
                    S.op("pe", lambda e, hh=hh: e.matmul(pz_[:, hh * 512:(hh + 1) * 512], lhsT=lT[:, 1, :], rhs=lw["gu0"][0][:, hh * 512:(hh + 1) * 512], start=True, stop=False), reads=[blT, lw["gu0"][1]], writes=[bpz_])
                    S.op("pe", lambda e, hh=hh: e.matmul(pz_[:, hh * 512:(hh + 1) * 512], lhsT=lT[0:32, 2, :], rhs=lw["gu1"][0][0:32, hh * 512:(hh + 1) * 512], start=False, stop=True), reads=[blT, lw["gu1"][1]], writes=[bpz_])
                S.op("act", lambda e, oG=oG: e.copy(out=oG[:], in_=pz_[:]), reads=[bpz_], writes=[boG])
                S.dma("pool", lambda e, oG=oG, t0=t0: e.dma_start(out=prep["G"][t0:t0 + 128, :], in_=oG[:]), reads=[boG], writes=[B_prep["G"][tile]], sembuf=boG)
                oKK, boKK = outs["KK"][0]
                S.op("dve", lambda e, oKK=oKK: e.tensor_tensor(out=oKK[:], in0=k_ap, in1=rws["kk"][0][:], op=ALU.mult), reads=[bMx, rws["kk"][1]], writes=[boKK])
                S.op("act", lambda e, oKK=oKK: e.activation(out=tmpR[:], in_=oKK[:], func=AF.Square), reads=[boKK], writes=[btmpR])
                S.op("dve", lambda e: e.tensor_reduce(out=t16[:], in_=h3(tmpR[:]), axis=AX.X, op=ALU.add), reads=[btmpR], writes=[bt16])
                S.op("dve", lambda e: e.tensor_scalar(out=t16[:], in0=t16[:], scalar1=1e-24, scalar2=None, op0=ALU.max), reads=[bt16], writes=[bt16])
                S.op("act", lambda e: e.activation(out=t16[:], in_=t16[:], func=AF.Sqrt), reads=[bt16], writes=[bt16])
                S.op("dve", lambda e: e.reciprocal(out=t16[:], in_=t16[:]), reads=[bt16], writes=[bt16])
                S.op("dve", lambda e, oKK=oKK: e.tensor_tensor(out=h3(oKK[:]), in0=h3(oKK[:]), in1=bc16(t16[:]), op=ALU.mult), reads=[boKK, bt16], writes=[boKK])
                S.dma("pool", lambda e, oKK=oKK, t0=t0: e.dma_start(out=prep["KK"][t0:t0 + 128, :], in_=oKK[:]), reads=[boKK], writes=[B_prep["KK"][tile]], sembuf=boKK)
                oB, boB = outs["BON"][0]
                S.op("pool", lambda e: e.tensor_tensor(out=tmpR[:], in0=r_ap, in1=k_ap, op=ALU.mult), reads=[bMx], writes=[btmpR])
                S.op("dve", lambda e: e.tensor_tensor(out=tmpR[:], in0=tmpR[:], in1=rws["rk"][0][:], op=ALU.mult), reads=[btmpR, rws["rk"][1]], writes=[btmpR])
                S.op("dve", lambda e: e.tensor_reduce(out=t16b[:], in_=h3(tmpR[:]), axis=AX.X, op=ALU.add), reads=[btmpR], writes=[bt16b])
                S.op("dve", lambda e, oB=oB: e.tensor_tensor(out=h3(oB[:]), in0=h3(v_ap), in1=bc16(t16b[:]), op=ALU.mult), reads=[bMx, bt16b], writes=[boB])
                S.dma("pool", lambda e, oB=oB, t0=t0: e.dma_start(out=prep["BON"][t0:t0 + 128, :], in_=oB[:]), reads=[boB], writes=[B_prep["BON"][tile]], sembuf=boB)
                for d in range(2):
                    ds_ = str(d)
                    for hh in range(2):
                        S.op("pe", lambda e, hh=hh, ds_=ds_: e.matmul(pz_[:, hh * 512:(hh + 1) * 512], lhsT=lT[0:64, 0, :], rhs=lw["wu" + ds_][0][0:64, hh * 512:(hh + 1) * 512], start=True, stop=True), reads=[blT, lw["wu" + ds_][1]], writes=[bpz_])
                        S.op("pe", lambda e, hh=hh, ds_=ds_: e.matmul(pa_[:, hh * 512:(hh + 1) * 512], lhsT=lT[64:128, 0, :], rhs=lw["au" + ds_][0][64:128, hh * 512:(hh + 1) * 512], start=True, stop=True), reads=[blT, lw["au" + ds_][1]], writes=[bpa_])
                    oLD, boLD = outs["LD"][d]
                    S.op("dve", lambda e, oLD=oLD, ds_=ds_: e.tensor_tensor(out=oLD[:], in0=pz_[:], in1=rws["w0" + ds_][0][:], op=ALU.add), reads=[bpz_, rws["w0" + ds_][1]], writes=[boLD])
                    S.op("act", lambda e, oLD=oLD: e.activation(out=oLD[:], in_=oLD[:], func=AF.Sigmoid), reads=[boLD], writes=[boLD])
                    S.op("act", lambda e, oLD=oLD: e.activation(out=oLD[:], in_=oLD[:], func=AF.Copy, scale=-0.6065306597126334), reads=[boLD], writes=[boLD])
                    S.dma("pool", lambda e, oLD=oLD, t0=t0, ds_=ds_: e.dma_start(out=prep["LD" + ds_][t0:t0 + 128, :], in_=oLD[:]), reads=[boLD], writes=[B_prep["LD" + ds_][tile]], sembuf=boLD)
                    S.op("dve", lambda e, ds_=ds_: e.tensor_tensor(out=AD[:], in0=pa_[:], in1=rws["a0" + ds_][0][:], op=ALU.add), reads=[bpa_, rws["a0" + ds_][1]], writes=[bAD])
                    S.op("act", lambda e: e.activation(out=AD[:], in_=AD[:], func=AF.Sigmoid), reads=[bAD], writes=[bAD])
                    oBD, boBD = outs["BD"][d]
                    S.op("pool", lambda e, oBD=oBD, oKK=oKK: e.tensor_tensor(out=oBD[:], in0=oKK[:], in1=AD[:], op=ALU.mult), reads=[boKK, bAD], writes=[boBD])
                    S.dma("pool", lambda e, oBD=oBD, t0=t0, ds_=ds_: e.dma_start(out=prep["BD" + ds_][t0:t0 + 128, :], in_=oBD[:]), reads=[boBD], writes=[B_prep["BD" + ds_][tile]], sembuf=boBD)
                    oKD, boKD = outs["KD"][d]
                    S.op("dve", lambda e: e.tensor_tensor(out=tmpR[:], in0=AD[:], in1=rws["ka"][0][:], op=ALU.mult), reads=[bAD, rws["ka"][1]], writes=[btmpR])
                    S.op("pool", lambda e: e.tensor_tensor(out=tmpR[:], in0=tmpR[:], in1=omka[:], op=ALU.add), reads=[btmpR, bomka], writes=[btmpR])
                    S.op("dve", lambda e, oKD=oKD: e.tensor_tensor(out=oKD[:], in0=tmpR[:], in1=k_ap, op=ALU.mult), reads=[btmpR, bMx], writes=[boKD])
                    S.dma("pool", lambda e, oKD=oKD, t0=t0, ds_=ds_: e.dma_start(out=prep["KD" + ds_][t0:t0 + 128, :], in_=oKD[:]), reads=[boKD], writes=[B_prep["KD" + ds_][tile]], sembuf=boKD)
            allb = [bMx, blwst] + [b for r_ in (SM1s, SP1s, SUs, SDs) for _, b in r_] + [b for n in outs for _, b in outs[n]] + [rws[n][1] for n in rws] + [bmu, brm, bcM1, bcP1, bcU, bcD]
            S.barrier(allb)

        with contextlib.ExitStack() as es:
            names = ("R", "KK", "V", "KD", "BD", "LD")
            IN = {n: C.ring(es, "in" + n, 2, [128, RW], F32) for n in names}
            CL, bCL = C.sb(es, "CL", [128, RW], F32)
            TOT, bTOT = C.sb(es, "TOT", [128, RW], F32)
            EX, bEX = C.sb(es, "EX", [128, RW], F32)
            Ee, bEe = C.sb(es, "Ee", [128, RW], F32)
            SCb = {n: C.sb(es, "sc" + n, [128, RW], BF16) for n in ("kap", "rt", "bet", "kt", "khat", "bhat", "V")}
            pcl, bpcl = C.ps(es, "pcl", [128, 1024], F32)
            ptr, bptr = C.ps(es, "ptr", [128, 4, 512], BF16)
            pgr, bpgr = C.ps(es, "pgr", [128, 1024], F32)
            ptt, bptt = C.ps(es, "ptt", [128, 512], F32)
            pch, bpch = C.ps(es, "pch", [128, 512], F32)
            FT, bFT = C.sb(es, "FT", [64, 4, 512], BF16)
            GM, bGM = C.sb(es, "GM", [128, 4, 512], BF16)
            QQ, bQQ = C.sb(es, "QQ", [128, 4, 256], BF16)
            TTb, bTTb = C.sb(es, "TTb", [128, 4, 128], BF16)
            Xb, bXb = C.sb(es, "Xb", [128, 4, 64], BF16)
            Ub, bUb = C.sb(es, "Ub", [128, 4, 64], BF16)
            U0b, bU0b = C.sb(es, "U0b", [128, 4, 64], BF16)
            X32, bX32 = C.sb(es, "X32", [128, 4, 64], F32)
            U032, bU032 = C.sb(es, "U032", [128, 4, 64], F32)
            Yacc = C.ring(es, "Yacc", 2, [128, RW], F32)
            Ast, bAst = C.sb(es, "Ast", [64, 2, 16, 64], F32)
            Abf, bAbf = C.sb(es, "Abf", [64, 2, 16, 64], BF16)
            PCc, bPCc = C.sb(es, "PCc", [64, 16, 2], F32)
            S.dma("sp", lambda e: e.dma_start(out=Ast[:], in_=s0wkv), writes=[bAst])
            S.op("dve", lambda e: e.tensor_copy(out=Abf[:], in_=Ast[:]), reads=[bAst], writes=[bAbf])
            for step in range(NTILE):
                if step > 0 and step % 8 == 0:
                    S.barrier()
                for d in range(2):
                    c = step if d == 0 else NTILE - 1 - step
                    t0 = c * 128
                    slot = (step * 2 + d) % 2
                    cur = {}
                    for n in names:
                        tl, btl = IN[n][slot]
                        key = n + str(d) if n in ("KD", "BD", "LD") else n
                        S.dma("sp", lambda e, tl=tl, key=key, t0=t0: e.dma_start(out=tl[:], in_=prep[key][t0:t0 + 128, :]), reads=[B_prep[key][c]], writes=[btl])
                        cur[n] = (tl, btl)
                    LD, bLD = cur["LD"]
                    for hh in range(2):
                        S.op("pe", lambda e, hh=hh, d=d, LD=LD: e.matmul(pcl[:, hh * 512:(hh + 1) * 512], lhsT=tri[:, d, :], rhs=LD[:, hh * 512:(hh + 1) * 512], start=True, stop=True), reads=[b_tri, bLD], writes=[bpcl])
                    S.op("act", lambda e: e.copy(out=CL[:], in_=pcl[:]), reads=[bpcl], writes=[bCL])
                    for hh in range(2):
                        S.op("pe", lambda e, hh=hh, LD=LD: e.matmul(pcl[:, hh * 512:(hh + 1) * 512], lhsT=ones[:], rhs=LD[:, hh * 512:(hh + 1) * 512], start=True, stop=True), reads=[b_ones, bLD], writes=[bpcl])
                    S.op("dve", lambda e: e.tensor_tensor(out=TOT[:], in0=pcl[:], in1=CL[:], op=ALU.subtract), reads=[bpcl, bCL], writes=[bTOT])
                    S.op("pool", lambda e, LD=LD: e.tensor_tensor(out=EX[:], in0=CL[:], in1=LD[:], op=ALU.subtract), reads=[bCL, bLD], writes=[bEX])
                    for h in range(16):
                        S.op("pe", lambda e, h=h, LD=LD: e.matmul(pch[0:64, h * 2:h * 2 + 2], lhsT=LD[:, h * 64:(h + 1) * 64], rhs=ones[:, 0:2], start=True, stop=True), reads=[bLD, b_ones], writes=[bpch])
                    S.op("act", lambda e: e.activation(out=PCc[:].rearrange("k h t -> k (h t)"), in_=pch[0:64, 0:32], func=AF.Exp), reads=[bpch], writes=[bPCc])
                    R_, bR_ = cur["R"]
                    KK_, bKK_ = cur["KK"]
                    V_, bV_ = cur["V"]
                    KD_, bKD_ = cur["KD"]
                    BD_, bBD_ = cur["BD"]
                    S.op("act", lambda e: e.activation(out=Ee[:], in_=EX[:], func=AF.Exp), reads=[bEX], writes=[bEe])
                    S.op("dve", lambda e, KK_=KK_: e.tensor_tensor(out=SCb["kap"][0][:], in0=KK_[:], in1=Ee[:], op=ALU.mult), reads=[bKK_, bEe], writes=[SCb["kap"][1]])
                    S.op("act", lambda e: e.activation(out=Ee[:], in_=CL[:], func=AF.Exp), reads=[bCL, SCb["kap"][1]], writes=[bEe])
                    S.op("pool", lambda e, R_=R_: e.tensor_tensor(out=SCb["rt"][0][:], in0=R_[:], in1=Ee[:], op=ALU.mult), reads=[bR_, bEe], writes=[SCb["rt"][1]])
                    S.op("act", lambda e: e.activation(out=EX[:], in_=CL[:], func=AF.Exp, scale=-1.0), reads=[bCL, SCb["kap"][1]], writes=[bEX])
                    S.op("dve", lambda e, BD_=BD_: e.tensor_tensor(out=SCb["bet"][0][:], in0=BD_[:], in1=EX[:], op=ALU.mult), reads=[bBD_, bEX], writes=[SCb["bet"][1]])
                    S.op("pool", lambda e, KD_=KD_: e.tensor_tensor(out=SCb["kt"][0][:], in0=KD_[:], in1=EX[:], op=ALU.mult), reads=[bKD_, bEX], writes=[SCb["kt"][1]])
                    S.op("act", lambda e: e.activation(out=TOT[:], in_=TOT[:], func=AF.Exp), reads=[bTOT], writes=[bTOT])
                    S.op("dve", lambda e, KD_=KD_: e.tensor_tensor(out=SCb["khat"][0][:], in0=KD_[:], in1=TOT[:], op=ALU.mult), reads=[bKD_, bTOT], writes=[SCb["khat"][1]])
                    S.op("pool", lambda e, BD_=BD_: e.tensor_tensor(out=SCb["bhat"][0][:], in0=BD_[:], in1=TOT[:], op=ALU.mult), reads=[bBD_, bTOT], writes=[SCb["bhat"][1]])
                    S.op("act", lambda e, V_=V_: e.copy(out=SCb["V"][0][:], in_=V_[:]), reads=[bV_], writes=[SCb["V"][1]])
                    Ya, bYa = Yacc[slot]
                    for hg in range(4):
                        for j in range(4):
                            h = hg * 4 + j
                            for qi, n in enumerate(("kap", "rt", "bet", "kt")):
                                S.op("pe", lambda e, j=j, h=h, qi=qi, n=n: e.transpose(out=ptr[0:64, j, qi * 128:(qi + 1) * 128], in_=SCb[n][0][:, h * 64:(h + 1) * 64], identity=identb[:]),
                                     reads=[SCb[n][1], b_identb], writes=[bptr])
                        S.op("act", lambda e: e.copy(out=FT[:], in_=ptr[0:64, :, :]), reads=[bptr], writes=[bFT])
                        for j in range(4):
                            S.op("pe", lambda e, j=j: e.matmul(pgr[:, j * 256:(j + 1) * 256], lhsT=FT[:, j, 256:384], rhs=FT[:, j, 0:256], start=True, stop=True), reads=[bFT], writes=[bpgr])
                        gm4 = gmask[:, d, 0:256].unsqueeze(1).broadcast_to([128, 4, 256])
                        S.op("dve", lambda e, gm4=gm4: e.tensor_tensor(out=GM[:, :, 0:256], in0=pgr[:].rearrange("p (j w) -> p j w", w=256), in1=gm4, op=ALU.mult), reads=[bpgr, b_gmask], writes=[bGM])
                        for j in range(4):
                            S.op("pe", lambda e, j=j: e.matmul(pgr[:, j * 256:(j + 1) * 256], lhsT=FT[:, j, 384:512], rhs=FT[:, j, 0:256], start=True, stop=True), reads=[bFT, bGM], writes=[bpgr])
                        gm4b = gmask[:, d, 256:512].unsqueeze(1).broadcast_to([128, 4, 256])
                        S.op("dve", lambda e, gm4b=gm4b: e.tensor_tensor(out=GM[:, :, 256:512], in0=pgr[:].rearrange("p (j w) -> p j w", w=256), in1=gm4b, op=ALU.mult), reads=[bpgr, b_gmask], writes=[bGM])
                        S.op("pool", lambda e: e.tensor_copy(out=QQ[:, :, 0:128], in_=GM[:, :, 0:128]), reads=[bGM], writes=[bQQ])
                        ptb = ptr[:, :, 0:128]
                        for j in range(4):
                            S.op("pe", lambda e, j=j: e.transpose(out=ptr[:, j, 0:128], in_=GM[:, j, 0:128], identity=identb[:]), reads=[bGM, b_identb, bFT], writes=[bptr])
                        S.op("act", lambda e: e.copy(out=QQ[:, :, 128:256], in_=ptr[:, :, 0:128]), reads=[bptr], writes=[bQQ])
                        idb4 = identb[:].unsqueeze(1).broadcast_to([128, 4, 128])
                        S.op("dve", lambda e, idb4=idb4: e.tensor_tensor(out=TTb[:], in0=GM[:, :, 0:128], in1=idb4, op=ALU.add), reads=[bGM, b_identb], writes=[bTTb])
                        for lv in range(6):
                            last = (lv == 5)
                            for j in range(4):
                                if not last:
                                    S.op("pe", lambda e, j=j: e.matmul(pgr[:, j * 256:j * 256 + 128], lhsT=QQ[:, j, 128:256], rhs=QQ[:, j, 0:128], start=True, stop=True), reads=[bQQ], writes=[bpgr])
                                S.op("pe", lambda e, j=j: e.matmul(pgr[:, j * 256 + 128:(j + 1) * 256], lhsT=QQ[:, j, 0:128], rhs=QQ[:, j, 128:256], start=True, stop=True), reads=[bQQ], writes=[bpgr])
                            S.op("act", lambda e: e.copy(out=QQ[:], in_=pgr[:].rearrange("p (j w) -> p j w", w=256)), reads=[bpgr], writes=[bQQ])
                            for j in range(4):
                                S.op("pe", lambda e, j=j: e.matmul(ptt[:, j * 128:(j + 1) * 128], lhsT=QQ[:, j, 128:256], rhs=TTb[:, j, :], start=True, stop=True), reads=[bTTb, bQQ], writes=[bptt])
                            S.op("dve", lambda e: e.tensor_tensor(out=TTb[:], in0=ptt[:].rearrange("p (j w) -> p j w", w=128), in1=TTb[:], op=ALU.add), reads=[bptt, bTTb], writes=[bTTb])
                        Vb = SCb["V"][0]
                        for j in range(4):
                            h = hg * 4 + j
                            S.op("pe", lambda e, j=j, h=h, d=d: e.matmul(pch[:, j * 64:(j + 1) * 64], lhsT=FT[:, j, 0:128], rhs=Abf[:, d, h, :], start=True, stop=False), reads=[bFT, bAbf], writes=[bpch])
                            S.op("pe", lambda e, j=j, h=h: e.matmul(pch[:, j * 64:(j + 1) * 64], lhsT=GM[:, j, 256:384], rhs=Vb[:, h * 64:(h + 1) * 64], start=False, stop=True), reads=[bGM, SCb["V"][1]], writes=[bpch])
                        if REFINE:
                            pX = pch[:, 0:256].rearrange("p (j w) -> p j w", w=64)
                            pU = pch[:, 256:512].rearrange("p (j w) -> p j w", w=64)
                            S.op("act", lambda e: e.copy(out=Xb[:], in_=pX), reads=[bpch], writes=[bXb])
                            S.op("dve", lambda e: e.tensor_copy(out=X32[:], in_=pX), reads=[bpch, bXb], writes=[bX32])
                            for j in range(4):
                                S.op("pe", lambda e, j=j: e.matmul(pch[:, 256 + j * 64:256 + (j + 1) * 64], lhsT=TTb[:, j, :], rhs=Xb[:, j, :], start=True, stop=True), reads=[bTTb, bXb], writes=[bpch])
                            S.op("act", lambda e: e.copy(out=U0b[:], in_=pU), reads=[bpch], writes=[bU0b])
                            S.op("dve", lambda e: e.tensor_copy(out=U032[:], in_=pU), reads=[bpch, bU0b], writes=[bU032])
                            S.op("pool", lambda e: e.tensor_tensor(out=X32[:], in0=X32[:], in1=U032[:], op=ALU.subtract), reads=[bX32, bU032], writes=[bX32])
                            for j in range(4):
                                S.op("pe", lambda e, j=j: e.matmul(pch[:, j * 64:(j + 1) * 64], lhsT=GM[:, j, 0:128], rhs=U0b[:, j, :], start=True, stop=True), reads=[bGM, bU0b, bX32], writes=[bpch])
                            S.op("dve", lambda e: e.tensor_tensor(out=Xb[:], in0=pX, in1=X32[:], op=ALU.add), reads=[bpch, bX32], writes=[bXb])
                            for j in range(4):
                                S.op("pe", lambda e, j=j: e.matmul(pch[:, 256 + j * 64:256 + (j + 1) * 64], lhsT=TTb[:, j, :], rhs=Xb[:, j, :], start=True, stop=True), reads=[bTTb, bXb], writes=[bpch])
                            S.op("dve", lambda e: e.scalar_tensor_tensor(out=Ub[:], in0=pU, scalar=-1.0, in1=U032[:], op0=ALU.mult, op1=ALU.subtract), reads=[bpch, bU032], writes=[bUb])
                        else:
                            S.op("act", lambda e: e.copy(out=Xb[:], in_=pch[:, 0:256].rearrange("p (j w) -> p j w", w=64)), reads=[bpch], writes=[bXb])
                            for j in range(4):
                                S.op("pe", lambda e, j=j: e.matmul(pch[:, 256 + j * 64:256 + (j + 1) * 64], lhsT=TTb[:, j, :], rhs=Xb[:, j, :], start=True, stop=True), reads=[bTTb, bXb], writes=[bpch])
                            S.op("dve", lambda e: e.tensor_scalar(out=Ub[:], in0=pch[:, 256:512].rearrange("p (j w) -> p j w", w=64), scalar1=-1.0, scalar2=None, op0=ALU.mult), reads=[bpch], writes=[bUb])
                        for j in range(4):
                            h = hg * 4 + j
                            S.op("pe", lambda e, j=j, h=h, d=d: e.matmul(pch[:, j * 64:(j + 1) * 64], lhsT=FT[:, j, 128:256], rhs=Abf[:, d, h, :], start=True, stop=False), reads=[bFT, bAbf, bXb], writes=[bpch])
                            S.op("pe", lambda e, j=j, h=h: e.matmul(pch[:, j * 64:(j + 1) * 64], lhsT=GM[:, j, 384:512], rhs=Vb[:, h * 64:(h + 1) * 64], start=False, stop=False), reads=[bGM, SCb["V"][1]], writes=[bpch])
                            S.op("pe", lambda e, j=j: e.matmul(pch[:, j * 64:(j + 1) * 64], lhsT=GM[:, j, 128:256], rhs=Ub[:, j, :], start=False, stop=True), reads=[bGM, bUb], writes=[bpch])
                        S.op("act", lambda e, Ya=Ya, hg=hg: e.copy(out=Ya[:, hg * 256:(hg + 1) * 256], in_=pch[:, 0:256]), reads=[bpch], writes=[bYa])
                        for j in range(4):
                            h = hg * 4 + j
                            S.op("pe", lambda e, j=j, h=h: e.matmul(pch[0:64, 256 + j * 64:256 + (j + 1) * 64], lhsT=SCb["khat"][0][:, h * 64:(h + 1) * 64], rhs=Vb[:, h * 64:(h + 1) * 64], start=True, stop=False), reads=[SCb["khat"][1], SCb["V"][1], bUb], writes=[bpch])
                            S.op("pe", lambda e, j=j, h=h: e.matmul(pch[0:64, 256 + j * 64:256 + (j + 1) * 64], lhsT=SCb["bhat"][0][:, h * 64:(h + 1) * 64], rhs=Ub[:, j, :], start=False, stop=True), reads=[SCb["bhat"][1], bUb], writes=[bpch])
                        for j in range(4):
                            h = hg * 4 + j
                            S.op("dve", lambda e, j=j, h=h, d=d: e.scalar_tensor_tensor(out=Ast[:, d, h, :], in0=Ast[:, d, h, :], scalar=PCc[:, h, 0:1], in1=pch[0:64, 256 + j * 64:256 + (j + 1) * 64], op0=ALU.mult, op1=ALU.add),
                                 reads=[bAst, bPCc, bpch], writes=[bAst])
                        S.op("pool", lambda e, hg=hg, d=d: e.tensor_copy(out=Abf[:, d, hg * 4:(hg + 1) * 4, :], in_=Ast[:, d, hg * 4:(hg + 1) * 4, :]), reads=[bAst], writes=[bAbf])
                    S.dma("pool", lambda e, Ya=Ya, t0=t0, d=d: e.dma_start(out=YD[d][t0:t0 + 128, :], in_=Ya[:]), reads=[bYa], writes=[B_Y[d][c]], sembuf=bYa)
                    seg_end = (d == 0 and c % 2 == 1) or (d == 1 and c % 2 == 0)
                    if seg_end:
                        sidx = c // 2
                        S.dma("pool", lambda e, sidx=sidx, d=d: e.dma_start(out=wkv_fin[sidx, d], in_=Ast[:, d, :, :]), reads=[bAst], sembuf=bAst, final=True)
                        S.op("dve", lambda e, d=d: e.tensor_scalar(out=Ast[:, d, :, :], in0=Ast[:, d, :, :], scalar1=cm[0:64, 0:1], scalar2=None, op0=ALU.mult), reads=[bAst, b_cm], writes=[bAst])
                        S.op("pool", lambda e, d=d: e.tensor_copy(out=Abf[:, d, :, :], in_=Ast[:, d, :, :]), reads=[bAst], writes=[bAbf])
            S.barrier([bAst] + [b for n in names for _, b in IN[n]] + [b for _, b in Yacc])

        with contextlib.ExitStack() as es:
            lnw, blnw = C.sb(es, "lnw", [128, RW], F32)
            lnb, blnb = C.sb(es, "lnb", [128, RW], F32)
            S.dma("sp", lambda e: e.dma_start(out=lnw[:], in_=rowap("lnw")), writes=[blnw])
            S.dma("sp", lambda e: e.dma_start(out=lnb[:], in_=rowap("lnb")), writes=[blnb])
            rg = {n: C.ring(es, "e" + n, 2, [128, RW], F32) for n in ("YF", "YB", "BON", "G")}
            Ysq, bYsq = C.sb(es, "Ysq", [128, RW], F32)
            m16, bm16 = C.sb(es, "m16", [128, 16], F32)
            v16, bv16 = C.sb(es, "v16", [128, 16], F32)
            Ob, bOb = C.sb(es, "Ob", [128, RW], BF16)
            pTe, bpTe = C.ps(es, "pTe", [128, 8, 128], BF16)
            OTs = C.ring(es, "OT", 2, [128, 8, 128], BF16)

            def h3(ap):
                return ap.rearrange("p (h k) -> p h k", k=64)

            def bc16(ap):
                return ap.unsqueeze(2).broadcast_to([128, 16, 64])
            for tile in range(NTILE):
                t0 = tile * 128
                par = tile % 2
                YF_, bYF_ = rg["YF"][par]
                YB_, bYB_ = rg["YB"][par]
                BN_, bBN_ = rg["BON"][par]
                G_, bG_ = rg["G"][par]
                S.dma("sp", lambda e, YF_=YF_, t0=t0: e.dma_start(out=YF_[:], in_=YD[0][t0:t0 + 128, :]), reads=[B_Y[0][tile]], writes=[bYF_])
                S.dma("sp", lambda e, YB_=YB_, t0=t0: e.dma_start(out=YB_[:], in_=YD[1][t0:t0 + 128, :]), reads=[B_Y[1][tile]], writes=[bYB_])
                S.dma("sp", lambda e, BN_=BN_, t0=t0: e.dma_start(out=BN_[:], in_=prep["BON"][t0:t0 + 128, :]), reads=[B_prep["BON"][tile]], writes=[bBN_])
                S.dma("sp", lambda e, G_=G_, t0=t0: e.dma_start(out=G_[:], in_=prep["G"][t0:t0 + 128, :]), reads=[B_prep["G"][tile]], writes=[bG_])
                S.op("dve", lambda e, YF_=YF_, YB_=YB_: e.tensor_tensor(out=YF_[:], in0=YF_[:], in1=YB_[:], op=ALU.add), reads=[bYF_, bYB_], writes=[bYF_])
                S.op("pool", lambda e, YF_=YF_, BN_=BN_: e.tensor_tensor(out=YF_[:], in0=YF_[:], in1=BN_[:], op=ALU.add), reads=[bYF_, bBN_], writes=[bYF_])
                S.op("dve", lambda e, YF_=YF_: e.tensor_reduce(out=m16[:], in_=h3(YF_[:]), axis=AX.X, op=ALU.add), reads=[bYF_], writes=[bm16])
                S.op("dve", lambda e: e.tensor_scalar(out=m16[:], in0=m16[:], scalar1=-1.0 / 64, scalar2=None, op0=ALU.mult), reads=[bm16], writes=[bm16])
                S.op("dve", lambda e, YF_=YF_: e.tensor_tensor(out=h3(YF_[:]), in0=h3(YF_[:]), in1=bc16(m16[:]), op=ALU.add), reads=[bYF_, bm16], writes=[bYF_])
                S.op("act", lambda e, YF_=YF_: e.activation(out=Ysq[:], in_=YF_[:], func=AF.Square), reads=[bYF_], writes=[bYsq])
                S.op("dve", lambda e: e.tensor_reduce(out=v16[:], in_=h3(Ysq[:]), axis=AX.X, op=ALU.add), reads=[bYsq], writes=[bv16])
                rstd_from_ss(v16[:], bv16, 64, 64e-5, None)
                S.op("dve", lambda e, YF_=YF_: e.tensor_tensor(out=h3(YF_[:]), in0=h3(YF_[:]), in1=bc16(v16[:]), op=ALU.mult), reads=[bYF_, bv16], writes=[bYF_])
                S.op("pool", lambda e, YF_=YF_: e.tensor_tensor(out=YF_[:], in0=YF_[:], in1=lnw[:], op=ALU.mult), reads=[bYF_, blnw], writes=[bYF_])
                S.op("dve", lambda e, YF_=YF_: e.tensor_tensor(out=YF_[:], in0=YF_[:], in1=lnb[:], op=ALU.add), reads=[bYF_, blnb], writes=[bYF_])
                S.op("pool", lambda e, YF_=YF_, G_=G_: e.tensor_tensor(out=Ob[:], in0=YF_[:], in1=G_[:], op=ALU.mult), reads=[bYF_, bG_], writes=[bOb])
                for c in range(8):
                    S.op("pe", lambda e, c=c: e.transpose(out=pTe[:, c, :], in_=Ob[:, c * 128:(c + 1) * 128], identity=identb[:]), reads=[bOb, b_identb], writes=[bpTe])
                OT, bOT = OTs[par]
                S.op("act", lambda e, OT=OT: e.copy(out=OT[:], in_=pTe[:]), reads=[bpTe], writes=[bOT])
                S.dma("pool", lambda e, OT=OT, tile=tile: e.dma_start(out=mixR[tile], in_=OT[:]), reads=[bOT], writes=[B_mixT[tile]], sembuf=bOT)
            S.barrier([blnw, blnb] + [b for n in rg for _, b in rg[n]] + [b for _, b in OTs])

        B_O1 = [Buf("O1_%d" % i) for i in range(NTILE)]
        with contextlib.ExitStack() as es:
            mg, bmg = C.sb(es, "mg", [128, 16, 1024], BF16)
            wst = C.ring(es, "wst1", 4, [128, 4, 512], F32)
            wbs = C.ring(es, "wb1", 2, [128, 16, 512], BF16)
            pmm = [C.ps(es, "pm1_%d" % i, [128, 512], F32) for i in range(8)]
            ost = C.ring(es, "ost1", 8, [128, 512], F32)
            oi = 0
            wi = 0
            for g in range(4):
                tiles = list(range(g * 8, g * 8 + 8))
                S.dma("sp", [lambda e, g=g, q=q: e.dma_start(out=mg[:, q * 4:(q + 1) * 4, :], in_=mixT.rearrange("(c p) t -> p c t", p=128)[:, q * 4:(q + 1) * 4, g * 1024:(g + 1) * 1024]) for q in range(2)]
                      + [lambda e, g=g, t=t: e.dma_start(out=mg[:, 8:16, t * 128:(t + 1) * 128], in_=mixR[g * 8 + t]) for t in range(8)],
                      reads=B_mixL + [B_mixT[t] for t in tiles], writes=[bmg])
                for dcol in range(4):
                    wt, bw = wbs[wi % 2]
                    wi += 1
                    load_w(wst, wt, bw, w_out[:, dcol * 512:(dcol + 1) * 512], 16, 512)
                    for t in range(8):
                        tile = g * 8 + t
                        pm, bpm = pmm[oi % 8]
                        o_t, bo = ost[oi % 8]
                        oi += 1
                        for cch in range(16):
                            S.op("pe", lambda e, pm=pm, wt=wt, cch=cch, t=t: e.matmul(pm[:], lhsT=mg[:, cch, t * 128:(t + 1) * 128], rhs=wt[:, cch, :], start=(cch == 0), stop=(cch == 15)), reads=[bmg, bw], writes=[bpm])
                        cast(o_t[:], pm[:], [bpm], [bo], eng=("act" if oi % 2 else "dve"))
                        S.dma("pool", lambda e, o_t=o_t, tile=tile, dcol=dcol: e.dma_start(out=FD[tile * 128:(tile + 1) * 128, dcol * 512:(dcol + 1) * 512], in_=o_t[:]), reads=[bo], writes=[B_O1[tile]], sembuf=bo)
            S.barrier()
        with contextlib.ExitStack() as es:
            o1s = C.ring(es, "o1b", 2, [128, D], F32)
            xts = C.ring(es, "xt1", 2, [128, D], F32)
            junk, b_junk = C.sb(es, "junk1", [128, D], BF16)
            ss, b_ss = C.sb(es, "ss1", [128, 1], F32)
            xn, b_xn = C.sb(es, "xn1", [128, D], BF16)
            pT, b_pT = C.ps(es, "pT1", [128, D], BF16)
            h2s = C.ring(es, "h2s", 2, [128, 16, 128], BF16)
            nbufs = (junk, b_junk, ss, b_ss, xn, b_xn, pT, b_pT)
            for tile in range(NTILE):
                t0 = tile * 128
                xt, b_xt = xts[tile % 2]
                o1, bo1 = o1s[tile % 2]
                S.dma("sp", lambda e, xt=xt, t0=t0: e.dma_start(out=xt[:], in_=x[t0:t0 + 128, :]), writes=[b_xt])
                S.dma("sp", lambda e, o1=o1, t0=t0: e.dma_start(out=o1[:], in_=FD[t0:t0 + 128, :]), reads=[B_O1[tile]], writes=[bo1])
                S.op("act", lambda e, o1=o1: e.activation(out=junk[:], in_=o1[:], func=AF.Square, accum_out=ss[:]), reads=[bo1], writes=[b_junk, b_ss])
                rstd_from_ss(ss[:], b_ss, D, 1e-6, None)
                S.op("dve", lambda e, o1=o1: e.scalar_tensor_tensor(out=o1[:], in0=o1[:], scalar=ss[:, 0:1], in1=G1r[:], op0=ALU.mult, op1=ALU.mult), reads=[bo1, b_ss, b_G1r], writes=[bo1])
                S.op("pool", lambda e, xt=xt, o1=o1: e.tensor_tensor(out=xt[:], in0=xt[:], in1=o1[:], op=ALU.add), reads=[b_xt, bo1], writes=[b_xt])
                S.dma("pool", lambda e, xt=xt, t0=t0: e.dma_start(out=X1[t0:t0 + 128, :], in_=xt[:]), reads=[b_xt], writes=[B_X1[tile]], sembuf=b_xt)
                h2, bh2 = h2s[tile % 2]
                norm_transpose(xt, b_xt, h2, bh2, 0, Af, modT[:, 48:64], [b_Af, b_modT_], nbufs)
                S.dma("pool", lambda e, h2=h2, tile=tile: e.dma_start(out=h2R[tile], in_=h2[:]), reads=[bh2], writes=[B_h2T[tile]], sembuf=bh2)
            S.barrier()

        with contextlib.ExitStack() as es:
            hg_, bhg = C.sb(es, "hg", [128, 16, 1024], BF16)
            wst = C.ring(es, "wst2", 4, [128, 4, 512], F32)
            wgs = C.ring(es, "wg2", 2, [128, 16, 512], BF16)
            wus = C.ring(es, "wu2", 2, [128, 16, 512], BF16)
            pga = [C.ps(es, "pga%d" % i, [128, 512], F32) for i in range(4)]
            pup = [C.ps(es, "pup%d" % i, [128, 512], F32) for i in range(4)]
            sl = C.ring(es, "sl", 4, [128, 512], F32)
            ao = C.ring(es, "ao", 4, [128, 512], BF16)
            oi = 0
            for g in range(4):
                S.dma("sp", [lambda e, g=g, t=t: e.dma_start(out=hg_[:, :, t * 128:(t + 1) * 128], in_=h2R[g * 8 + t]) for t in range(8)],
                      reads=[B_h2T[t] for t in range(g * 8, g * 8 + 8)], writes=[bhg])
                for j in range(11):
                    wg, bwg = wgs[j % 2]
                    wu, bwu = wus[j % 2]
                    load_w(wst, wg, bwg, w_gu[:, j * 512:(j + 1) * 512], 16, 512)
                    load_w(wst, wu, bwu, w_gu[:, FH + j * 512:FH + (j + 1) * 512], 16, 512)
                    for f in range(4):
                        fc = j * 4 + f
                        for tt in range(2):
                            pg_, bpg_ = pga[oi % 4]
                            pu_, bpu_ = pup[oi % 4]
                            s_, bs_ = sl[oi % 4]
                            a_, ba_ = ao[oi % 4]
                            oi += 1
                            for dc in range(16):
                                S.op("pe", lambda e, pg_=pg_, wg=wg, f=f, dc=dc, tt=tt: e.matmul(pg_[:], lhsT=wg[:, dc, f * 128:(f + 1) * 128], rhs=hg_[:, dc, tt * 512:(tt + 1) * 512], start=(dc == 0), stop=(dc == 15)), reads=[bwg, bhg], writes=[bpg_])
                            for dc in range(16):
                                S.op("pe", lambda e, pu_=pu_, wu=wu, f=f, dc=dc, tt=tt: e.matmul(pu_[:], lhsT=wu[:, dc, f * 128:(f + 1) * 128], rhs=hg_[:, dc, tt * 512:(tt + 1) * 512], start=(dc == 0), stop=(dc == 15)), reads=[bwu, bhg], writes=[bpu_])
                            S.op("act", lambda e, s_=s_, pg_=pg_: e.activation(out=s_[:], in_=pg_[:], func=AF.Silu), reads=[bpg_], writes=[bs_])
                            S.op("dve", lambda e, a_=a_, s_=s_, pu_=pu_: e.tensor_tensor(out=a_[:], in0=s_[:], in1=pu_[:], op=ALU.mult), reads=[bs_, bpu_], writes=[ba_])
                            S.dma("pool", lambda e, a_=a_, fc=fc, g=g, tt=tt: e.dma_start(out=actT[fc * 128:(fc + 1) * 128, g * 1024 + tt * 512:g * 1024 + (tt + 1) * 512], in_=a_[:]), reads=[ba_], writes=[B_actT[fc][g]], sembuf=ba_)
            S.barrier([bhg] + [b for _, b in wst] + [b for _, b in ao])

        B_F = [Buf("Fd_%d" % i) for i in range(NTILE)]
        with contextlib.ExitStack() as es:
            ag = C.ring(es, "ag", 1, [128, 44, 1024], BF16)
            wst = C.ring(es, "wst3", 2, [128, 4, 512], F32)
            wds = C.ring(es, "wd3", 3, [128, 22, 512], BF16)
            pmm = [C.ps(es, "pm3_%d" % i, [128, 512], F32) for i in range(8)]
            ost = C.ring(es, "ost3", 4, [128, 512], F32)
            wi = 0
            oo = 0
            for g in range(4):
                agt, bag = ag[0]
                S.dma("sp", [lambda e, agt=agt, g=g, q=q: e.dma_start(out=agt[:, q * 4:(q + 1) * 4, :], in_=actT.rearrange("(c p) t -> p c t", p=128)[:, q * 4:(q + 1) * 4, g * 1024:(g + 1) * 1024]) for q in range(11)],
                      reads=[B_actT[f][g] for f in range(44)] + B_X1, writes=[bag])
                for dcol in range(4):
                    for half in range(2):
                        wt, bw = wds[wi % 3]
                        wi += 1
                        load_w(wst, wt, bw, w_down[:, dcol * 512:(dcol + 1) * 512], 22, 512, k0=half * 22)
                        for t in range(8):
                            pm, bpm = pmm[t]
                            for f2 in range(22):
                                fc = half * 22 + f2
                                S.op("pe", lambda e, pm=pm, wt=wt, fc=fc, f2=f2, t=t, agt=agt: e.matmul(pm[:], lhsT=agt[:, fc, t * 128:(t + 1) * 128], rhs=wt[:, f2, :], start=(fc == 0), stop=(fc == 43)), reads=[bag, bw], writes=[bpm])
                    for t in range(8):
                        tile = g * 8 + t
                        pm, bpm = pmm[t]
                        o_t, bo = ost[oo % 4]
                        oo += 1
                        cast(o_t[:], pm[:], [bpm], [bo], eng=("act" if t % 2 else "dve"))
                        S.dma("pool", lambda e, o_t=o_t, tile=tile, dcol=dcol: e.dma_start(out=FD[tile * 128:(tile + 1) * 128, dcol * 512:(dcol + 1) * 512], in_=o_t[:]), reads=[bo], writes=[B_F[tile]], sembuf=bo)
            S.barrier()
        with contextlib.ExitStack() as es:
            o1s = C.ring(es, "o3b", 2, [128, D], F32)
            xts = C.ring(es, "xt3", 2, [128, D], F32)
            junk, b_junk = C.sb(es, "junk3", [128, D], BF16)
            ss, b_ss = C.sb(es, "ss3", [128, 1], F32)
            for tile in range(NTILE):
                t0 = tile * 128
                xt, b_xt = xts[tile % 2]
                o1, bo1 = o1s[tile % 2]
                S.dma("sp", lambda e, xt=xt, t0=t0: e.dma_start(out=xt[:], in_=X1[t0:t0 + 128, :]), reads=[B_X1[tile]], writes=[b_xt])
                S.dma("sp", lambda e, o1=o1, t0=t0: e.dma_start(out=o1[:], in_=FD[t0:t0 + 128, :]), reads=[B_F[tile]], writes=[bo1])
                S.op("act", lambda e, o1=o1: e.activation(out=junk[:], in_=o1[:], func=AF.Square, accum_out=ss[:]), reads=[bo1], writes=[b_junk, b_ss])
                rstd_from_ss(ss[:], b_ss, D, 1e-6, None)
                S.op("dve", lambda e, o1=o1: e.scalar_tensor_tensor(out=o1[:], in0=o1[:], scalar=ss[:, 0:1], in1=G2r[:], op0=ALU.mult, op1=ALU.mult), reads=[bo1, b_ss, b_G2r], writes=[bo1])
                S.op("pool", lambda e, xt=xt, o1=o1: e.tensor_tensor(out=xt[:], in0=xt[:], in1=o1[:], op=ALU.add), reads=[b_xt, bo1], writes=[b_xt])
                S.dma("pool", lambda e, xt=xt, t0=t0: e.dma_start(out=y_out[t0:t0 + 128, :], in_=xt[:]), reads=[b_xt], sembuf=b_xt, final=True)
        S.emit()
    S.close()
    return nc


_PROMPT_COUNTS = [6, 6, 5, 5, 5, 5]


def _col(v, n):
    return np.ascontiguousarray(np.asarray(v, np.float32).reshape(n, 128).T)


def kernel(x_prompt, x_sample, state_lru, state_wkv, c, c_ctx,
           norm_mix_pre, norm_mix_post, norm_ffn_pre, norm_ffn_post, w_mod, b_mod, w_in,
           lru_conv_w, lru_conv_b, lru_wr, lru_br, lru_wi, lru_bi, lru_lambda,
           rwkv_mu, rwkv_w0, rwkv_w_up, rwkv_a0, rwkv_a_up, rwkv_g_up, rwkv_k_k, rwkv_k_a, rwkv_r_k,
           rwkv_ln_w, rwkv_ln_b, w_out, ffn_w_gu, ffn_w_down):
    f32 = np.float32
    A = lambda a: np.ascontiguousarray(np.asarray(a, f32))
    x_prompt, x_sample = A(x_prompt), A(x_sample)
    nc = build_program()
    idx = np.arange(128)
    tri = np.zeros((128, 2, 128), f32)
    tri[:, 0, :] = (idx[:, None] <= idx[None, :])
    tri[:, 1, :] = (idx[:, None] >= idx[None, :])
    gmask = np.zeros((128, 2, 512), f32)
    for d in range(2):
        incl = tri[:, d, :]
        strict = incl - np.eye(128, dtype=f32)
        gmask[:, d, 0:128] = -strict
        gmask[:, d, 128:256] = incl
        gmask[:, d, 256:384] = strict
        gmask[:, d, 384:512] = incl
    ncols = np.stack([_col(A(v)[0], 16) for v in (norm_mix_pre, norm_mix_post, norm_ffn_pre, norm_ffn_post)], axis=1)
    convc = np.zeros((128, 8, 5), f32)
    cw = A(lru_conv_w)[0]
    cb = A(lru_conv_b)[0]
    for k in range(4):
        convc[:, :, k] = cw[k].reshape(8, 128).T
    convc[:, :, 4] = cb.reshape(8, 128).T
    lru_bc = np.zeros((128, 2, 8, 2), f32)
    lru_bc[:, :, :, 0] = np.transpose(A(lru_br)[0], (2, 0, 1))
    lru_bc[:, :, :, 1] = np.transpose(A(lru_bi)[0], (2, 0, 1))
    lam = np.ascontiguousarray(np.transpose(A(lru_lambda)[0].reshape(2, 8, 128), (2, 0, 1)))
    shared = {
        "w_mod": A(w_mod)[0], "b_modT": _col(A(b_mod)[0], 96), "ncols": np.ascontiguousarray(ncols),
        "w_in": A(w_in)[0], "convc": convc, "lru_wr": A(lru_wr)[0], "lru_wi": A(lru_wi)[0],
        "lru_bc": lru_bc, "lru_lam": lam,
        "w_up": A(rwkv_w_up)[0], "a_up": A(rwkv_a_up)[0], "g_up": A(rwkv_g_up)[0],
        "w_out": A(w_out)[0], "w_gu": A(ffn_w_gu)[0], "w_down": A(ffn_w_down)[0],
        "ident": np.eye(128, dtype=f32), "tri": tri, "gmask": gmask,
    }
    chs = np.arange(PRW)
    p = np.arange(128)

    def rows_for(sample):
        if sample:
            cmM1 = (chs < 840)
            cmP1 = (chs >= 840) & (chs < 1680)
            cmU = (chs >= 1680) & (chs < 2520)
            cmD = (chs >= 2520)
        else:
            cmM1 = (chs < 1680)
            cmP1 = (chs >= 1680)
            cmU = np.zeros(PRW, bool)
            cmD = np.zeros(PRW, bool)
        parts = [A(rwkv_mu)[0], cmM1.astype(f32), cmP1.astype(f32), cmU.astype(f32), cmD.astype(f32),
                 A(rwkv_w0)[0, 0], A(rwkv_w0)[0, 1], A(rwkv_a0)[0, 0], A(rwkv_a0)[0, 1], A(rwkv_k_k)[0], A(rwkv_k_a)[0],
                 A(rwkv_r_k)[0].reshape(-1), A(rwkv_ln_w)[0], A(rwkv_ln_b)[0]]
        return np.concatenate(parts).astype(f32)[None, :]

    def rmask_for(sample):
        m = np.ones((128, 4), f32)
        if sample:
            m[:, 0] = (p % 64 != 0)
            m[:, 1] = (p % 64 != 0)
            m[:, 2] = (p % 64 != 63)
            m[:, 3] = (p % 64 != 63)
        else:
            m[:, 0] = (p != 0)
            m[:, 1] = 1.0
            m[:, 2] = 1.0
            m[:, 3] = (p != 127)
        return m
    assign = []
    s0 = 0
    for n in _PROMPT_COUNTS:
        assign.append(list(range(s0, s0 + n)))
        s0 += n
    in_maps = []
    for core in range(8):
        m = dict(shared)
        if core < 2:
            m["x"] = np.ascontiguousarray(x_sample[core])
            m["cvec"] = _col(A(c)[core], 16)
            m["cmcol"] = np.ones((128, 1), f32)
            sl = A(state_lru)[core, 0]
            m["h0lru"] = np.ascontiguousarray(np.transpose(sl.reshape(2, 8, 128), (2, 1, 0)))
            sw = A(state_wkv)[core, 0]
            m["s0wkv"] = np.ascontiguousarray(np.transpose(sw, (3, 0, 1, 2)))
            m["rows"] = rows_for(True)
            m["rmask"] = rmask_for(True)
        else:
            xs = np.zeros((NT, D), f32)
            mine = assign[core - 2]
            for i in range(NSEG):
                xs[i * SEG:(i + 1) * SEG] = x_prompt[mine[i % len(mine)]]
            m["x"] = xs
            m["cvec"] = _col(A(c_ctx), 16)
            m["cmcol"] = np.zeros((128, 1), f32)
            m["h0lru"] = np.zeros((128, 8, 2), f32)
            m["s0wkv"] = np.zeros((64, 2, 16, 64), f32)
            m["rows"] = rows_for(False)
            m["rmask"] = rmask_for(False)
        in_maps.append(m)
    res = run_bass_kernel_spmd(nc, in_maps, core_ids=list(range(8)))
    R = res.results
    if DEBUG:
        global _LAST
        _LAST = R
    y_p = np.zeros((32, SEG, D), f32)
    y_s = np.zeros((2, NT, D), f32)
    ns_lru = np.zeros((32, 1, 2, LW), f32)
    ns_wkv = np.zeros((32, 1, 2, 16, 64, 64), f32)
    for core in range(8):
        r = R[core]
        if core < 2:
            y_s[core] = r["y"]
        else:
            lf = r["lru_fin"]
            wf = r["wkv_fin"]
            for i, sidx in enumerate(assign[core - 2]):
                y_p[sidx] = r["y"][i * SEG:(i + 1) * SEG]
                ns_lru[sidx, 0] = np.transpose(lf[:, :, :, i], (2, 1, 0)).reshape(2, LW)
                ns_wkv[sidx, 0] = np.transpose(wf[i], (0, 2, 3, 1))
    return (y_p, y_s, ns_lru, ns_wkv)
```

```python
import contextlib
import numpy as np
import ml_dtypes
import concourse.bass as bass
import concourse.mybir as mybir
from concourse.bass_utils import run_bass_kernel_spmd

F32 = mybir.dt.float32
BF16 = mybir.dt.bfloat16
AF = mybir.ActivationFunctionType
ALU = mybir.AluOpType
AX = mybir.AxisListType

D = 2048
NT = 4096
NSEG = 16
SEG = 256
NTILE = NT // 128
LW = 1024
RW = 1024
PRW = 3360
INW = 5408
FH = 5632
DEBUG = False
DEBUG_SS = False
REFINE = True
SAME_ENGINE_SYNC = True
NOSYNC_ENGS = ("pe",)


class Buf:
    __slots__ = ("name", "w", "r", "dsem", "dcnt")

    def __init__(self, name):
        self.name = name
        self.w = None
        self.r = []
        self.dsem = {}
        self.dcnt = {}


class Sched:
    ENG = ("pe", "dve", "act", "pool", "sp")

    def __init__(self, nc):
        self.nc = nc
        self.prog = {e: [] for e in self.ENG}
        self.cnt = {e: 0 for e in self.ENG}
        self.waited = {e: {} for e in self.ENG}
        self.sems = {}
        self.stack = []
        self.dma_state = {}
        self.epoch = 0
        self.ekey = {}
        for e in ("pe", "dve", "act", "pool"):
            self.ekey[e] = "E_" + e + "_0"
            self._mksem(self.ekey[e])
        self.finals = []
        self.free_dsems = {}
        self.stage_bufs = []
        self.ndsem = 0

    def keep(self):
        self.stage_bufs = []

    def _mksem(self, key):
        cm = self.nc.semaphore(key)
        h = cm.__enter__()
        self.stack.append(cm)
        self.sems[key] = h
        return h

    def _deps(self, eng, reads, writes):
        best = {}
        own = self.ekey.get(eng, "none")
        wd = self.waited[eng]

        def add(dep):
            k, v = dep
            if k == own and (not SAME_ENGINE_SYNC or eng in NOSYNC_ENGS):
                return
            if wd.get(k, 0) >= v:
                return
            if best.get(k, 0) < v:
                best[k] = v
        for b in reads:
            if b.w is not None:
                add(b.w)
        for b in writes:
            if b.w is not None:
                add(b.w)
            for d in b.r:
                add(d)
        waits = []
        for k, v in best.items():
            wd[k] = v
            waits.append((k, v))
        return waits

    def _mark(self, me, reads, writes):
        for b in reads:
            b.r.append(me)
            if len(b.r) > 64:
                mx = {}
                for k, v in b.r:
                    if mx.get(k, 0) < v:
                        mx[k] = v
                b.r = list(mx.items())
        for b in writes:
            b.w = me
            b.r = []

    def op(self, eng, fn, reads=(), writes=()):
        waits = self._deps(eng, reads, writes)
        self.cnt[eng] += 1
        me = (self.ekey[eng], self.cnt[eng])
        self.prog[eng].append((waits, [fn], (me[0], 1)))
        self._mark(me, reads, writes)
        return me

    def dma(self, eng, fns, reads=(), writes=(), sembuf=None, final=False):
        if not isinstance(fns, (list, tuple)):
            fns = [fns]
        if sembuf is None:
            sembuf = writes[0] if writes else reads[0]
        cls = eng
        if sembuf.dsem.get(cls) is None:
            pool_ = self.free_dsems.setdefault(cls, [])
            if pool_:
                sembuf.dsem[cls], sembuf.dcnt[cls] = pool_.pop()
            else:
                self.ndsem += 1
                sembuf.dsem[cls] = "D_%d" % self.ndsem
                sembuf.dcnt[cls] = 0
                self._mksem(sembuf.dsem[cls])
            self.stage_bufs.append((sembuf, cls))
        waits = self._deps(eng, reads, writes)
        dk, dc = sembuf.dsem[cls], sembuf.dcnt[cls]
        if dc > 0 and self.waited[eng].get(dk, 0) < dc:
            self.waited[eng][dk] = dc
            waits.append((dk, dc))
        sembuf.dcnt[cls] = dc + 16 * len(fns)
        me = (dk, sembuf.dcnt[cls])
        self.prog[eng].append((waits, list(fns), (me[0], 16)))
        self._mark(me, reads, writes)
        if final:
            self.finals.append(me)
        return me

    def barrier(self, bufs=()):
        deps = [(self.ekey[e], self.cnt[e]) for e in ("pe", "dve", "act", "pool") if self.cnt[e] > 0]
        for k in self.sems:
            if k.startswith("D_"):
                pass
        for (b, cls) in self.stage_bufs:
            if b.dsem.get(cls) is not None and b.dcnt[cls] > 0:
                deps.append((b.dsem[cls], b.dcnt[cls]))
        for e in self.ENG:
            waits = []
            for (k, v) in deps:
                if k == self.ekey.get(e):
                    continue
                if self.waited[e].get(k, 0) < v:
                    self.waited[e][k] = v
                    waits.append((k, v))
            if waits:
                self.prog[e].append((waits, [], None))
        for (b, cls) in self.stage_bufs:
            if b.dcnt[cls] < 20000:
                self.free_dsems.setdefault(cls, []).append((b.dsem[cls], b.dcnt[cls]))
            b.dsem[cls] = None
        self.stage_bufs = []
        self.epoch += 1
        for e in ("pe", "dve", "act", "pool"):
            self.ekey[e] = "E_%s_%d" % (e, self.epoch)
            self._mksem(self.ekey[e])
            self.cnt[e] = 0

    def emit(self):
        nc = self.nc
        engmap = {"pe": "tensor", "dve": "vector", "act": "scalar", "pool": "gpsimd", "sp": "sync"}
        finals = {}
        for k, v in self.finals:
            finals[k] = max(finals.get(k, 0), v)
        with nc.Block() as block:
            for e in self.ENG:
                prog = self.prog[e]

                def body(eng, prog=prog, e=e):
                    for (waits, fns, inc) in prog:
                        for (k, v) in waits:
                            eng.wait_ge(self.sems[k], v)
                        for fn in fns:
                            ins = fn(eng)
                            ins.then_inc(self.sems[inc[0]], inc[1])
                    if e == "sp":
                        for k, v in finals.items():
                            eng.wait_ge(self.sems[k], v)
                getattr(block, engmap[e])(body)

    def close(self):
        for cm in reversed(self.stack):
            cm.__exit__(None, None, None)


class Ctx:
    def __init__(self, nc):
        self.nc = nc
        self.S = Sched(nc)
        self.uid = 0
        self.rr = 0

    def sb(self, es, name, shape, dt):
        self.uid += 1
        t = es.enter_context(self.nc.sbuf_tensor("%s_%d" % (name, self.uid), list(shape), dt))
        return t, Buf("%s_%d" % (name, self.uid))

    def ps(self, es, name, shape, dt):
        self.uid += 1
        t = es.enter_context(self.nc.psum_tensor("%s_%d" % (name, self.uid), list(shape), dt))
        return t, Buf("%s_%d" % (name, self.uid))

    def ring(self, es, name, n, shape, dt):
        return [self.sb(es, "%s%d" % (name, i), shape, dt) for i in range(n)]


def build_program():
    nc = bass.Bass("TRN2", target_bir_lowering=False)
    C = Ctx(nc)
    S = C.S

    def din(name, shape, dt=F32):
        return nc.dram_tensor(name, list(shape), dt, kind="ExternalInput").ap()

    def dout(name, shape, dt=F32):
        return nc.dram_tensor(name, list(shape), dt, kind="ExternalOutput").ap()

    def dscr(name, shape, dt=F32):
        kind = "ExternalOutput" if DEBUG else "Internal"
        return nc.dram_tensor(name, list(shape), dt, kind=kind).ap()

    x = din("x", [NT, D])
    cvec = din("cvec", [128, 16])
    cmcol = din("cmcol", [128, 1])
    h0lru = din("h0lru", [128, 8, 2])
    s0wkv = din("s0wkv", [64, 2, 16, 64])
    w_mod = din("w_mod", [D, 6 * D])
    b_modT = din("b_modT", [128, 96])
    ncols = din("ncols", [128, 4, 16])
    w_in = din("w_in", [D, INW])
    convc = din("convc", [128, 8, 5])
    lru_wr = din("lru_wr", [2, 8, 128, 128])
    lru_wi = din("lru_wi", [2, 8, 128, 128])
    lru_bc = din("lru_bc", [128, 2, 8, 2])
    lru_lam = din("lru_lam", [128, 2, 8])
    rows = din("rows", [1, 3360 * 5 + 1024 * 9])
    rmask = din("rmask", [128, 4])
    w_up = din("w_up", [2, 64, RW])
    a_up = din("a_up", [2, 64, RW])
    g_up = din("g_up", [160, RW])
    w_out = din("w_out", [D, D])
    w_gu = din("w_gu", [D, 2 * FH])
    w_down = din("w_down", [FH, D])
    ident_in = din("ident", [128, 128])
    tri_in = din("tri", [128, 2, 128])
    gmask_in = din("gmask", [128, 2, 512])
    y_out = dout("y", [NT, D])
    lru_fin = dout("lru_fin", [128, 8, 2, 16])
    wkv_fin = dout("wkv_fin", [16, 2, 64, 16, 64])
    xlglT = dscr("xlglT", [2048, NT])
    projTok = dscr("projTok", [NT, PRW])
    mixT = dscr("mixT", [2048, NT], BF16)
    prep = {n: dscr("prep_" + n, [NT, RW]) for n in ("R", "KK", "V", "KD0", "KD1", "BD0", "BD1", "LD0", "LD1", "G", "BON")}
    YD = [dscr("YF", [NT, RW]), dscr("YB", [NT, RW])]
    X1 = dscr("X1", [NT, D])
    h2R = dscr("h2R", [NTILE, 128, 16, 128], BF16)
    mixR = dscr("mixR", [NTILE, 128, 8, 128], BF16)
    actT = dscr("actT", [FH, NT], BF16)
    FD = dscr("FD", [NT, D])
    DBGSS = dscr("DBGSS", [128, 64])

    def tb(name):
        return [Buf("%s_t%d" % (name, i)) for i in range(NTILE)]
    B_xlgl = [[Buf("xlgl_%d_%d" % (c, g)) for g in range(4)] for c in range(16)]
    B_proj = tb("proj")
    B_mixT = tb("mixT")
    B_mixL = [Buf("mixL%d" % h) for h in range(8)]
    B_prep = {n: tb("prep" + n) for n in prep}
    B_Y = [tb("YF"), tb("YB")]
    B_X1 = tb("X1")
    B_h2T = tb("h2T")
    B_actT = [[Buf("actT_%d_%d" % (f, g)) for g in range(4)] for f in range(44)]
    B_FD = tb("FD")

    ROWOFF = {}
    o = 0
    for n, w in (("mu", 3360), ("cmM1", 3360), ("cmP1", 3360), ("cmU", 3360), ("cmD", 3360),
                 ("w00", 1024), ("w01", 1024), ("a00", 1024), ("a01", 1024), ("kk", 1024), ("ka", 1024),
                 ("rk", 1024), ("lnw", 1024), ("lnb", 1024)):
        ROWOFF[n] = (o, w)
        o += w

    def rowap(n, lo=0, hi=None):
        o0, w = ROWOFF[n]
        if hi is None:
            hi = w
        return rows[0:1, o0 + lo:o0 + hi].broadcast_to([128, hi - lo])

    with contextlib.ExitStack() as es0:
        ident, b_ident = C.sb(es0, "ident", [128, 128], F32)
        identb, b_identb = C.sb(es0, "identb", [128, 128], BF16)
        ones, b_ones = C.sb(es0, "ones", [128, 128], F32)
        tri, b_tri = C.sb(es0, "tri", [128, 2, 128], F32)
        gmask, b_gmask = C.sb(es0, "gmask", [128, 2, 512], F32)
        cm, b_cm = C.sb(es0, "cm", [128, 1], F32)
        modT, b_modT_ = C.sb(es0, "modT", [128, 96], F32)
        ncl, b_ncl = C.sb(es0, "ncl", [128, 4, 16], F32)
        Am, b_Am = C.sb(es0, "Am", [128, 16], F32)
        Af, b_Af = C.sb(es0, "Af", [128, 16], F32)
        G1r, b_G1r = C.sb(es0, "G1r", [128, D], F32)
        G2r, b_G2r = C.sb(es0, "G2r", [128, D], F32)
        S.dma("sp", lambda e: e.dma_start(out=ident[:], in_=ident_in), writes=[b_ident])
        S.dma("sp", lambda e: e.dma_start(out=tri[:], in_=tri_in), writes=[b_tri])
        S.dma("sp", lambda e: e.dma_start(out=gmask[:], in_=gmask_in), writes=[b_gmask])
        S.dma("sp", lambda e: e.dma_start(out=cm[:], in_=cmcol), writes=[b_cm])
        S.dma("sp", lambda e: e.dma_start(out=ncl[:], in_=ncols), writes=[b_ncl])
        S.op("dve", lambda e: e.tensor_copy(out=identb[:], in_=ident[:]), reads=[b_ident], writes=[b_identb])
        S.op("dve", lambda e: e.memset(ones[:], 1.0), writes=[b_ones])
        S.keep()

        cast_engs = ["pool", "dve", "act"]

        def cast(out_ap, in_ap, reads, writes, eng=None):
            if eng is None:
                eng = cast_engs[C.rr % 3]
                C.rr += 1
            if eng == "act":
                S.op("act", lambda e: e.copy(out=out_ap, in_=in_ap), reads=reads, writes=writes)
            else:
                S.op(eng, lambda e: e.tensor_copy(out=out_ap, in_=in_ap), reads=reads, writes=writes)

        def rstd_from_ss(ss, b_ss, n, eps, eng_tmp):
            S.op("dve", lambda e: e.tensor_scalar(out=ss, in0=ss, scalar1=1.0 / n, scalar2=eps, op0=ALU.mult, op1=ALU.add), reads=[b_ss], writes=[b_ss])
            S.op("act", lambda e: e.activation(out=ss, in_=ss, func=AF.Sqrt), reads=[b_ss], writes=[b_ss])
            S.op("dve", lambda e: e.reciprocal(out=ss, in_=ss), reads=[b_ss], writes=[b_ss])

        with contextlib.ExitStack() as es:
            cv, b_cv = C.sb(es, "cv", [128, 16], F32)
            sc, b_sc = C.sb(es, "sc", [128, 16, 2], F32)
            bmt, b_bmt = C.sb(es, "bmt", [128, 96], F32)
            wm = C.ring(es, "wm", 2, [128, 16, 512], F32)
            pmod, b_pmod = C.ps(es, "pmod", [128, 96, 2], F32)
            pbc, b_pbc = C.ps(es, "pbc", [128, 512], F32)
            dg, b_dg = C.sb(es, "dg", [128, 128], F32)
            gc, b_gc = C.sb(es, "gc", [128, 2, 16], F32)
            S.dma("sp", lambda e: e.dma_start(out=cv[:], in_=cvec), writes=[b_cv])
            S.dma("sp", lambda e: e.dma_start(out=bmt[:], in_=b_modT), writes=[b_bmt])
            S.op("act", lambda e: e.activation(out=sc[:, :, 0], in_=cv[:], func=AF.Silu), reads=[b_cv], writes=[b_sc])
            S.op("act", lambda e: e.activation(out=sc[:, :, 1], in_=cv[:], func=AF.Silu), reads=[b_cv], writes=[b_sc])
            wmv = w_mod.rearrange("(dc p) f -> p dc f", p=128)
            for j in range(24):
                wt, bw = wm[j % 2]
                S.dma("sp", [lambda e, j=j, wt=wt, q=q: e.dma_start(out=wt[:, q * 4:(q + 1) * 4, :], in_=wmv[:, q * 4:(q + 1) * 4, j * 512:(j + 1) * 512]) for q in range(4)], writes=[bw])
                for f in range(4):
                    fc = j * 4 + f
                    for dc in range(16):
                        S.op("pe", lambda e, wt=wt, f=f, dc=dc, fc=fc: e.matmul(pmod[:, fc, :], lhsT=wt[:, dc, f * 128:(f + 1) * 128], rhs=sc[:, dc, :], start=(dc == 0), stop=(dc == 15)),
                             reads=[bw, b_sc], writes=[b_pmod])
            S.op("dve", lambda e: e.tensor_tensor(out=modT[:], in0=pmod[:, :, 0], in1=bmt[:], op=ALU.add), reads=[b_pmod, b_bmt], writes=[b_modT_])
            S.op("dve", lambda e: e.scalar_tensor_tensor(out=Am[:], in0=modT[:, 16:32], scalar=1.0, in1=ncl[:, 0, :], op0=ALU.add, op1=ALU.mult), reads=[b_modT_, b_ncl], writes=[b_Am])
            S.op("dve", lambda e: e.scalar_tensor_tensor(out=Af[:], in0=modT[:, 64:80], scalar=1.0, in1=ncl[:, 2, :], op0=ALU.add, op1=ALU.mult), reads=[b_modT_, b_ncl], writes=[b_Af])
            S.op("dve", lambda e: e.tensor_tensor(out=gc[:, 0, :], in0=modT[:, 32:48], in1=ncl[:, 1, :], op=ALU.mult), reads=[b_modT_, b_ncl], writes=[b_gc])
            S.op("dve", lambda e: e.tensor_tensor(out=gc[:, 1, :], in0=modT[:, 80:96], in1=ncl[:, 3, :], op=ALU.mult), reads=[b_modT_, b_ncl], writes=[b_gc])
            for which, (Gr, bGr) in enumerate(((G1r, b_G1r), (G2r, b_G2r))):
                for c in range(16):
                    S.op("dve", lambda e, which=which, c=c: e.tensor_scalar(out=dg[:], in0=ident[:], scalar1=gc[:, which, c:c + 1], scalar2=None, op0=ALU.mult), reads=[b_ident, b_gc], writes=[b_dg])
                    S.op("pe", lambda e: e.matmul(pbc[:, 0:128], lhsT=ones[:], rhs=dg[:], start=True, stop=True), reads=[b_ones, b_dg], writes=[b_pbc])
                    S.op("act", lambda e, Gr=Gr, c=c: e.copy(out=Gr[:, c * 128:(c + 1) * 128], in_=pbc[:, 0:128]), reads=[b_pbc], writes=[bGr])
            S.barrier([b for _, b in wm] + [b_cv, b_bmt])

        def load_w(stage_ring, wt, bw, src, kch, ncol, k0=0):
            srcv = src.rearrange("(kc p) f -> p kc f", p=128)
            for q in range(0, kch, 4):
                n = min(4, kch - q)
                st, bst = stage_ring[C.uid % len(stage_ring)]
                C.uid += 1
                S.dma("sp", lambda e, st=st, q=q, n=n: e.dma_start(out=st[:, 0:n, 0:ncol], in_=srcv[:, k0 + q:k0 + q + n, :]), writes=[bst])
                cast(wt[:, q:q + n, 0:ncol], st[:, 0:n, 0:ncol], [bst], [bw])

        def norm_transpose(xt, b_xt, hT_dst, b_hT, col0, Acol, shcol, bshc, es_bufs):
            junk, b_junk, ss, b_ss, xn, b_xn, pT, b_pT = es_bufs
            S.op("act", lambda e: e.activation(out=junk[:], in_=xt[:], func=AF.Square, accum_out=ss[:]), reads=[b_xt], writes=[b_junk, b_ss])
            rstd_from_ss(ss[:], b_ss, D, 1e-6, None)
            S.op("dve", lambda e: e.tensor_scalar(out=xn[:], in0=xt[:], scalar1=ss[:, 0:1], scalar2=None, op0=ALU.mult), reads=[b_xt, b_ss], writes=[b_xn])
            for c in range(16):
                S.op("pe", lambda e, c=c: e.transpose(out=pT[:, c * 128:(c + 1) * 128], in_=xn[:, c * 128:(c + 1) * 128], identity=identb[:]), reads=[b_xn, b_identb], writes=[b_pT])
            for c in range(16):
                S.op("act", lambda e, c=c: e.activation(out=hT_dst[:, c, col0:col0 + 128], in_=pT[:, c * 128:(c + 1) * 128], func=AF.Identity, scale=Acol[:, c:c + 1], bias=shcol[:, c:c + 1]),
                     reads=[b_pT] + bshc, writes=[b_hT])

        with contextlib.ExitStack() as es:
            xts = C.ring(es, "xt", 2, [128, D], F32)
            junk, b_junk = C.sb(es, "junk", [128, D], BF16)
            ss, b_ss = C.sb(es, "ss", [128, 1], F32)
            xn, b_xn = C.sb(es, "xn", [128, D], BF16)
            pT, b_pT = C.ps(es, "pT", [128, D], BF16)
            hTs = C.ring(es, "hT", 2, [128, 16, 1024], BF16)
            wst = C.ring(es, "wst", 4, [128, 4, 512], F32)
            wbs = C.ring(es, "wb", 2, [128, 16, 512], BF16)
            pmm = [C.ps(es, "pmm%d" % i, [128, 512], F32) for i in range(4)]
            ost = C.ring(es, "ost", 4, [128, 512], F32)
            nbufs = (junk, b_junk, ss, b_ss, xn, b_xn, pT, b_pT)
            oi = 0

            def norm_tile(g, t):
                tile = g * 8 + t
                hT, b_hT = hTs[g % 2]
                xt, b_xt = xts[tile % 2]
                S.dma("sp", lambda e, xt=xt, tile=tile: e.dma_start(out=xt[:], in_=x[tile * 128:(tile + 1) * 128, :]), writes=[b_xt])
                norm_transpose(xt, b_xt, hT, b_hT, t * 128, Am, modT[:, 0:16], [b_Am, b_modT_], nbufs)
            for t in range(8):
                norm_tile(0, t)
            wi = 0
            for g in range(4):
                hT, b_hT = hTs[g % 2]
                nxt = list(range(8)) if g < 3 else []
                for j in range(4):
                    wt, bw = wbs[wi % 2]
                    wi += 1
                    load_w(wst, wt, bw, w_in[:, j * 512:(j + 1) * 512], 16, 512)
                    for f in range(4):
                        fc = j * 4 + f
                        for tt in range(2):
                            pm, bpm = pmm[oi % 4]
                            o_t, bo = ost[oi % 4]
                            oi += 1
                            for dc in range(16):
                                S.op("pe", lambda e, pm=pm, wt=wt, f=f, dc=dc, tt=tt, hT=hT: e.matmul(pm[:], lhsT=wt[:, dc, f * 128:(f + 1) * 128], rhs=hT[:, dc, tt * 512:(tt + 1) * 512], start=(dc == 0), stop=(dc == 15)),
                                     reads=[bw, b_hT], writes=[bpm])
                            cast(o_t[:], pm[:], [bpm], [bo], eng=("act" if oi % 2 else "dve"))
                            S.dma("pool", lambda e, o_t=o_t, fc=fc, g=g, tt=tt: e.dma_start(out=xlglT[fc * 128:(fc + 1) * 128, g * 1024 + tt * 512:g * 1024 + (tt + 1) * 512], in_=o_t[:]),
                                  reads=[bo], writes=[B_xlgl[fc][g]], sembuf=bo)
                    if nxt:
                        norm_tile(g + 1, nxt.pop(0))
                for ct in range(7):
                    ncol = 512 if ct < 6 else PRW - 6 * 512
                    wt, bw = wbs[wi % 2]
                    wi += 1
                    load_w(wst, wt, bw, w_in[:, 2048 + ct * 512:2048 + ct * 512 + ncol], 16, ncol)
                    for t in range(8):
                        tile = g * 8 + t
                        pm, bpm = pmm[oi % 4]
                        o_t, bo = ost[oi % 4]
                        oi += 1
                        for dc in range(16):
                            S.op("pe", lambda e, pm=pm, wt=wt, dc=dc, t=t, ncol=ncol, hT=hT: e.matmul(pm[:, 0:ncol], lhsT=hT[:, dc, t * 128:(t + 1) * 128], rhs=wt[:, dc, 0:ncol], start=(dc == 0), stop=(dc == 15)),
                                 reads=[bw, b_hT], writes=[bpm])
                        cast(o_t[:, 0:ncol], pm[:, 0:ncol], [bpm], [bo], eng=("act" if oi % 2 else "dve"))
                        S.dma("pool", lambda e, o_t=o_t, tile=tile, ct=ct, ncol=ncol: e.dma_start(out=projTok[tile * 128:(tile + 1) * 128, ct * 512:ct * 512 + ncol], in_=o_t[:, 0:ncol]),
                              reads=[bo], writes=[B_proj[tile]], sembuf=bo)
                    if nxt:
                        norm_tile(g + 1, nxt.pop(0))
                while nxt:
                    norm_tile(g + 1, nxt.pop(0))
            S.barrier([b for _, b in xts] + [b for _, b in wst] + [b for _, b in ost])

        with contextlib.ExitStack() as es:
            X, bX = C.sb(es, "X", [128, NT], F32)
            GL, bGL = C.sb(es, "GL", [128, NT], F32)
            XC, bXC = C.sb(es, "XC", [128, NT], F32)
            XCB, bXCB = C.sb(es, "XCB", [128, NT], BF16)
            R1, bR1 = C.sb(es, "R1", [128, NT], F32)
            I1, bI1 = C.sb(es, "I1", [128, NT], F32)
            E1, bE1 = C.sb(es, "E1", [128, NT], F32)
            E2, bE2 = C.sb(es, "E2", [128, NT], F32)
            OB, bOB = C.sb(es, "OB", [128, NT], BF16)
            cc, bcc = C.sb(es, "cc", [128, 8, 5], F32)
            ccm, bccm = C.sb(es, "ccm", [128, 8, 5], F32)
            lbc, blbc = C.sb(es, "lbc", [128, 2, 8, 2], F32)
            lam, blam = C.sb(es, "lam", [128, 2, 8], F32)
            spc, bspc = C.sb(es, "spc", [128, 2, 8, 2], F32)
            tmpc, btmpc = C.sb(es, "tmpc", [128, 16], F32)
            z2, bz2 = C.sb(es, "z2", [128, 16], F32)
            pz, bpz = C.sb(es, "pz", [128, 16], F32)
            h0, bh0 = C.sb(es, "h0", [128, 8, 2], F32)
            fin, bfin = C.sb(es, "fin", [128, 8, 2, 16], F32)
            gst = C.ring(es, "gst", 2, [128, 128], F32)
            gwb = C.ring(es, "gwb", 4, [128, 128], BF16)
            pg = [C.ps(es, "pg%d" % i, [128, 512], F32) for i in range(4)]
            S.dma("sp", lambda e: e.dma_start(out=cc[:], in_=convc), writes=[bcc])
            S.dma("sp", lambda e: e.dma_start(out=lbc[:], in_=lru_bc), writes=[blbc])
            S.dma("sp", lambda e: e.dma_start(out=lam[:], in_=lru_lam), writes=[blam])
            S.dma("sp", lambda e: e.dma_start(out=h0[:], in_=h0lru), writes=[bh0])
            S.op("dve", lambda e: e.tensor_scalar(out=ccm[:], in0=cc[:], scalar1=cm[:, 0:1], scalar2=None, op0=ALU.mult), reads=[bcc, b_cm], writes=[bccm])
            lamf = lam[:].rearrange("p a b -> p (a b)")
            S.op("dve", lambda e: e.tensor_scalar(out=z2[:], in0=lamf, scalar1=-1.0, scalar2=None, op0=ALU.mult), reads=[blam], writes=[bz2])
            S.op("dve", lambda e: e.tensor_tensor(out=tmpc[:], in0=lamf, in1=z2[:], op=ALU.max), reads=[blam, bz2], writes=[btmpc])
            S.op("act", lambda e: e.activation(out=tmpc[:], in_=tmpc[:], func=AF.Exp, scale=-1.0), reads=[btmpc], writes=[btmpc])
            S.op("dve", lambda e: e.tensor_scalar(out=z2[:], in0=tmpc[:], scalar1=2.0, scalar2=None, op0=ALU.add), reads=[btmpc], writes=[bz2])
            S.op("dve", lambda e: e.reciprocal(out=z2[:], in_=z2[:]), reads=[bz2], writes=[bz2])
            S.op("dve", lambda e: e.tensor_tensor(out=tmpc[:], in0=tmpc[:], in1=z2[:], op=ALU.mult), reads=[btmpc, bz2], writes=[btmpc])
            S.op("dve", lambda e: e.tensor_tensor(out=z2[:], in0=tmpc[:], in1=tmpc[:], op=ALU.mult), reads=[btmpc], writes=[bz2])
            S.op("dve", lambda e: e.memset(pz[:], 1.0 / 13.0), writes=[bpz])
            for coef in (1.0 / 11, 1.0 / 9, 1.0 / 7, 1.0 / 5, 1.0 / 3, 1.0):
                S.op("dve", lambda e: e.tensor_tensor(out=pz[:], in0=pz[:], in1=z2[:], op=ALU.mult), reads=[bpz, bz2], writes=[bpz])
                S.op("dve", lambda e, coef=coef: e.tensor_scalar(out=pz[:], in0=pz[:], scalar1=float(coef), scalar2=None, op0=ALU.add), reads=[bpz], writes=[bpz])
            S.op("dve", lambda e: e.tensor_tensor(out=pz[:], in0=pz[:], in1=tmpc[:], op=ALU.mult), reads=[bpz, btmpc], writes=[bpz])
            S.op("dve", lambda e: e.tensor_scalar(out=z2[:], in0=lamf, scalar1=-1.0, scalar2=0.0, op0=ALU.mult, op1=ALU.max), reads=[blam], writes=[bz2])
            S.op("dve", lambda e: e.scalar_tensor_tensor(out=pz[:], in0=pz[:], scalar=2.0, in1=z2[:], op0=ALU.mult, op1=ALU.add), reads=[bpz, bz2], writes=[bpz])
            spf = spc[:].rearrange("p a b c -> p (a b) c")
            S.op("dve", lambda e: e.tensor_scalar(out=spf[:, :, 0], in0=pz[:], scalar1=-8.0, scalar2=None, op0=ALU.mult), reads=[bpz], writes=[bspc])
            S.op("dve", lambda e: e.tensor_scalar(out=spf[:, :, 1], in0=pz[:], scalar1=-16.0, scalar2=None, op0=ALU.mult), reads=[bpz], writes=[bspc])

            def seg(ap):
                return ap.rearrange("p (s t) -> p s t", t=SEG)
            gi = 0
            for h in range(8):
                S.dma("sp", [lambda e, h=h, g=g: e.dma_start(out=X[:, g * 1024:(g + 1) * 1024], in_=xlglT[h * 128:(h + 1) * 128, g * 1024:(g + 1) * 1024]) for g in range(4)],
                      reads=[B_xlgl[h][g] for g in range(4)], writes=[bX])
                S.dma("sp", [lambda e, h=h, g=g: e.dma_start(out=GL[:, g * 1024:(g + 1) * 1024], in_=xlglT[(8 + h) * 128:(9 + h) * 128, g * 1024:(g + 1) * 1024]) for g in range(4)],
                      reads=[B_xlgl[8 + h][g] for g in range(4)], writes=[bGL])
                S.op("act", lambda e, h=h: e.activation(out=XC[:], in_=X[:], func=AF.Identity, scale=cc[:, h, 2:3], bias=cc[:, h, 4:5]), reads=[bX, bcc], writes=[bXC])
                Xs, XCs = seg(X[:]), seg(XC[:])
                for (tap, dlt) in ((0, -2), (1, -1), (3, 1)):
                    if dlt < 0:
                        o_v, i_v = XCs[:, :, -dlt:], Xs[:, :, :SEG + dlt]
                    else:
                        o_v, i_v = XCs[:, :, :SEG - dlt], Xs[:, :, dlt:]
                    S.op("dve", lambda e, h=h, tap=tap, o_v=o_v, i_v=i_v: e.scalar_tensor_tensor(out=o_v, in0=i_v, scalar=cc[:, h, tap:tap + 1], in1=o_v, op0=ALU.mult, op1=ALU.add), reads=[bX, bXC, bcc], writes=[bXC])
                fix = ((1, XCs[:, 1:, 0:1], Xs[:, :15, 255:256]), (0, XCs[:, 1:, 0:1], Xs[:, :15, 254:255]), (0, XCs[:, 1:, 1:2], Xs[:, :15, 255:256]), (3, XCs[:, :15, 255:256], Xs[:, 1:, 0:1]))
                for (tap, o_v, i_v) in fix:
                    S.op("dve", lambda e, h=h, tap=tap, o_v=o_v, i_v=i_v: e.scalar_tensor_tensor(out=o_v, in0=i_v, scalar=ccm[:, h, tap:tap + 1], in1=o_v, op0=ALU.mult, op1=ALU.add), reads=[bX, bXC, bccm], writes=[bXC])
                S.op("act", lambda e: e.copy(out=XCB[:], in_=XC[:]), reads=[bXC], writes=[bXCB])
                HS = []
                for d in range(2):
                    Ebuf, bE = (E1, bE1) if d == 0 else (E2, bE2)
                    for (wsrc, dst, bdst, bi) in ((lru_wr, R1, bR1, 0), (lru_wi, I1, bI1, 1)):
                        gs, bgs = gst[gi % 2]
                        gw, bgw = gwb[gi % 4]
                        gi += 1
                        S.dma("sp", lambda e, gs=gs, wsrc=wsrc, d=d, h=h: e.dma_start(out=gs[:], in_=wsrc[d, h]), writes=[bgs])
                        cast(gw[:], gs[:], [bgs], [bgw], eng="pool")
                        for tt in range(8):
                            pm, bpm = pg[tt % 4]
                            S.op("pe", lambda e, pm=pm, gw=gw, tt=tt: e.matmul(pm[:], lhsT=gw[:], rhs=XCB[:, tt * 512:(tt + 1) * 512], start=True, stop=True), reads=[bgw, bXCB], writes=[bpm])
                            S.op("act", lambda e, pm=pm, dst=dst, tt=tt, d=d, h=h, bi=bi: e.activation(out=dst[:, tt * 512:(tt + 1) * 512], in_=pm[:], func=AF.Sigmoid, bias=lbc[:, d, h, bi:bi + 1]),
                                 reads=[bpm, blbc], writes=[bdst])
                    S.op("act", lambda e, Ebuf=Ebuf, d=d, h=h: e.activation(out=Ebuf[:], in_=R1[:], func=AF.Exp, scale=spc[:, d, h, 1:2]), reads=[bR1, bspc], writes=[bE])
                    S.op("act", lambda e, d=d, h=h: e.activation(out=R1[:], in_=R1[:], func=AF.Exp, scale=spc[:, d, h, 0:1]), reads=[bR1, bspc], writes=[bR1])
                    S.op("act", lambda e, Ebuf=Ebuf: e.activation(out=Ebuf[:], in_=Ebuf[:], func=AF.Identity, scale=-1.0, bias=1.0), reads=[bE], writes=[bE])
                    S.op("act", lambda e, Ebuf=Ebuf: e.activation(out=Ebuf[:], in_=Ebuf[:], func=AF.Sqrt), reads=[bE], writes=[bE])
                    S.op("pool", lambda e: e.tensor_tensor(out=I1[:], in0=I1[:], in1=XC[:], op=ALU.mult), reads=[bI1, bXC], writes=[bI1])
                    S.op("dve", lambda e, Ebuf=Ebuf: e.tensor_tensor(out=I1[:], in0=I1[:], in1=Ebuf[:], op=ALU.mult), reads=[bI1, bE], writes=[bI1])
                    As = seg(R1[:])
                    if d == 0:
                        S.op("dve", lambda e, As=As: e.tensor_scalar(out=As[:, 1:, 0:1], in0=As[:, 1:, 0:1], scalar1=cm[:, 0:1], scalar2=None, op0=ALU.mult), reads=[bR1, b_cm], writes=[bR1])
                        S.op("dve", lambda e, Ebuf=Ebuf, h=h: e.tensor_tensor_scan(out=Ebuf[:], data0=R1[:], data1=I1[:], initial=h0[:, h, 0:1], op0=ALU.mult, op1=ALU.add), reads=[bR1, bI1, bh0], writes=[bE])
                        S.op("pool", lambda e, Ebuf=Ebuf, h=h: e.tensor_copy(out=fin[:, h, 0, :], in_=seg(Ebuf[:])[:, :, 255]), reads=[bE], writes=[bfin])
                    else:
                        S.op("dve", lambda e, As=As: e.tensor_scalar(out=As[:, :15, 255:256], in0=As[:, :15, 255:256], scalar1=cm[:, 0:1], scalar2=None, op0=ALU.mult), reads=[bR1, b_cm], writes=[bR1])
                        S.op("dve", lambda e, Ebuf=Ebuf, h=h: e.tensor_tensor_scan(out=Ebuf[:, ::-1], data0=R1[:, ::-1], data1=I1[:, ::-1], initial=h0[:, h, 1:2], op0=ALU.mult, op1=ALU.add), reads=[bR1, bI1, bh0], writes=[bE])
                        S.op("pool", lambda e, Ebuf=Ebuf, h=h: e.tensor_copy(out=fin[:, h, 1, :], in_=seg(Ebuf[:])[:, :, 0]), reads=[bE], writes=[bfin])
                S.op("pool", lambda e: e.tensor_tensor(out=E1[:], in0=E1[:], in1=E2[:], op=ALU.add), reads=[bE1, bE2], writes=[bE1])
                S.op("act", lambda e: e.activation(out=R1[:], in_=GL[:], func=AF.Square), reads=[bGL], writes=[bR1])
                S.op("act", lambda e: e.activation(out=R1[:], in_=R1[:], func=AF.Identity, scale=0.044715, bias=1.0), reads=[bR1], writes=[bR1])
                S.op("dve", lambda e: e.tensor_tensor(out=R1[:], in0=R1[:], in1=GL[:], op=ALU.mult), reads=[bR1, bGL], writes=[bR1])
                S.op("act", lambda e: e.activation(out=R1[:], in_=R1[:], func=AF.Sigmoid, scale=1.5957691216057308), reads=[bR1], writes=[bR1])
                S.op("pool", lambda e: e.tensor_tensor(out=R1[:], in0=R1[:], in1=GL[:], op=ALU.mult), reads=[bR1, bGL], writes=[bR1])
                S.op("dve", lambda e: e.tensor_tensor(out=OB[:], in0=R1[:], in1=E1[:], op=ALU.mult), reads=[bR1, bE1], writes=[bOB])
                S.dma("pool", lambda e, h=h: e.dma_start(out=mixT[h * 128:(h + 1) * 128, :], in_=OB[:]), reads=[bOB], writes=[B_mixL[h]], sembuf=bOB)
            S.dma("pool", lambda e: e.dma_start(out=lru_fin, in_=fin[:]), reads=[bfin], sembuf=bfin, final=True)
            S.barrier([bX, bGL, bOB, bfin, bcc, blbc, blam, bh0] + [b for _, b in gst])

        with contextlib.ExitStack() as es:
            mu_r, bmu = C.sb(es, "mu_r", [128, PRW], F32)
            cM1, bcM1 = C.sb(es, "cM1", [128, 1680], F32)
            cP1, bcP1 = C.sb(es, "cP1", [128, 2520], F32)
            cU, bcU = C.sb(es, "cU", [128, 840], F32)
            cD, bcD = C.sb(es, "cD", [128, 840], F32)
            rws = {}
            for n in ("w00", "w01", "a00", "a01", "kk", "ka", "rk"):
                rws[n] = C.sb(es, "row_" + n, [128, RW], F32)
            omka, bomka = C.sb(es, "omka", [128, RW], F32)
            rm, brm = C.sb(es, "rm", [128, 4], F32)
            P0s = None
            SM1s = C.ring(es, "SM1", 1, [128, 1680], F32)
            SP1s = C.ring(es, "SP1", 1, [128, 2520], F32)
            SUs = C.ring(es, "SU", 1, [128, 840], F32)
            SDs = C.ring(es, "SD", 1, [128, 840], F32)
            Mx, bMx = C.sb(es, "Mx", [128, PRW], F32)
            lb, blb = C.sb(es, "lb", [128, 288], BF16)
            lT, blT = C.sb(es, "lT", [128, 4, 128], BF16)
            pTl, bpTl = C.ps(es, "pTl", [128, 4, 128], BF16)
            pz_, bpz_ = C.ps(es, "pzz", [128, 1024], F32)
            pa_, bpa_ = C.ps(es, "paa", [128, 1024], F32)
            lw = {}
            lwst, blwst = C.sb(es, "lwst", [128, RW], F32)
            for n in ("wu0", "wu1", "au0", "au1", "gu0", "gu1"):
                lw[n] = C.sb(es, "lw_" + n, [128, RW], BF16)
            outs = {n: C.ring(es, "o" + n, (2 if n in ("KD", "BD", "LD") else 1), [128, RW], F32) for n in ("KK", "KD", "BD", "LD", "G", "BON")}
            AD, bAD = C.sb(es, "AD", [128, RW], F32)
            t16, bt16 = C.sb(es, "t16", [128, 16], F32)
            t16b, bt16b = C.sb(es, "t16b", [128, 16], F32)
            tmpR, btmpR = C.sb(es, "tmpR", [128, RW], F32)

            S.dma("sp", lambda e: e.dma_start(out=mu_r[:], in_=rowap("mu")), writes=[bmu])
            S.dma("sp", lambda e: e.dma_start(out=rm[:], in_=rmask), writes=[brm])
            for (ct, bct, nm, lo, hi) in ((cM1, bcM1, "cmM1", 0, 1680), (cP1, bcP1, "cmP1", 840, 3360), (cU, bcU, "cmU", 1680, 2520), (cD, bcD, "cmD", 2520, 3360)):
                S.dma("sp", lambda e, ct=ct, nm=nm, lo=lo, hi=hi: e.dma_start(out=ct[:], in_=rowap(nm, lo, hi)), writes=[bct])
                S.op("dve", lambda e, ct=ct, lo=lo, hi=hi: e.tensor_tensor(out=ct[:], in0=ct[:], in1=mu_r[:, lo:hi], op=ALU.mult), reads=[bct, bmu], writes=[bct])
            S.op("dve", lambda e: e.tensor_scalar(out=mu_r[:], in0=mu_r[:], scalar1=-1.0, scalar2=1.0, op0=ALU.mult, op1=ALU.add), reads=[bmu, bcM1, bcP1, bcU, bcD], writes=[bmu])
            for n in rws:
                S.dma("sp", lambda e, n=n: e.dma_start(out=rws[n][0][:], in_=rowap(n)), writes=[rws[n][1]])
            S.op("dve", lambda e: e.tensor_scalar(out=omka[:], in0=rws["ka"][0][:], scalar1=-1.0, scalar2=1.0, op0=ALU.mult, op1=ALU.add), reads=[rws["ka"][1]], writes=[bomka])
            for (n, src, p0, p1) in (("wu0", w_up[0], 0, 64), ("wu1", w_up[1], 0, 64), ("au0", a_up[0], 64, 128), ("au1", a_up[1], 64, 128), ("gu0", g_up[0:128, :], 0, 128), ("gu1", g_up[128:160, :], 0, 32)):
                S.dma("sp", lambda e, src=src, p0=p0, p1=p1: e.dma_start(out=lwst[p0:p1, :], in_=src), writes=[blwst])
                S.op("dve", lambda e, n=n, p0=p0, p1=p1: e.tensor_copy(out=lw[n][0][p0:p1, :], in_=lwst[p0:p1, :]), reads=[blwst], writes=[lw[n][1]])

            def h3(ap):
                return ap.rearrange("p (h k) -> p h k", k=64)

            def bc16(ap):
                return ap.unsqueeze(2).broadcast_to([128, 16, 64])
            for tile in range(NTILE):
                par = tile % 2
                t0 = tile * 128
                P0, bP0 = Mx, bMx
                SM1, bSM1 = SM1s[0]
                SP1, bSP1 = SP1s[0]
                SU, bSU = SUs[0]
                SD, bSD = SDs[0]
                S.dma("sp", lambda e, P0=P0, t0=t0: e.dma_start(out=P0[:], in_=projTok[t0:t0 + 128, :]), reads=[B_proj[tile]], writes=[bP0])
                if tile == 0:
                    S.op("pool", lambda e, SM1=SM1: e.memset(SM1[:], 0.0), writes=[bSM1])
                    S.dma("sp", lambda e, SM1=SM1: e.dma_start(out=SM1[1:128, :], in_=projTok[0:127, 0:1680]), reads=[B_proj[0]], writes=[bSM1])
                else:
                    S.dma("sp", lambda e, SM1=SM1, t0=t0: e.dma_start(out=SM1[:], in_=projTok[t0 - 1:t0 + 127, 0:1680]), reads=[B_proj[tile - 1], B_proj[tile]], writes=[bSM1])
                if tile == NTILE - 1:
                    S.op("pool", lambda e, SP1=SP1: e.memset(SP1[:], 0.0), writes=[bSP1])
                    S.dma("sp", lambda e, SP1=SP1, t0=t0: e.dma_start(out=SP1[0:127, :], in_=projTok[t0 + 1:t0 + 128, 840:3360]), reads=[B_proj[tile]], writes=[bSP1])
                else:
                    S.dma("sp", lambda e, SP1=SP1, t0=t0: e.dma_start(out=SP1[:], in_=projTok[t0 + 1:t0 + 129, 840:3360]), reads=[B_proj[tile], B_proj[tile + 1]], writes=[bSP1])
                if tile == 0:
                    S.op("pool", lambda e, SU=SU: e.memset(SU[:], 0.0), writes=[bSU])
                    S.dma("sp", lambda e, SU=SU: e.dma_start(out=SU[64:128, :], in_=projTok[0:64, 1680:2520]), reads=[B_proj[0]], writes=[bSU])
                else:
                    S.dma("sp", lambda e, SU=SU, t0=t0: e.dma_start(out=SU[:], in_=projTok[t0 - 64:t0 + 64, 1680:2520]), reads=[B_proj[tile - 1], B_proj[tile]], writes=[bSU])
                if tile == NTILE - 1:
                    S.op("pool", lambda e, SD=SD: e.memset(SD[:], 0.0), writes=[bSD])
                    S.dma("sp", lambda e, SD=SD, t0=t0: e.dma_start(out=SD[0:64, :], in_=projTok[t0 + 64:t0 + 128, 2520:3360]), reads=[B_proj[tile]], writes=[bSD])
                else:
                    S.dma("sp", lambda e, SD=SD, t0=t0: e.dma_start(out=SD[:], in_=projTok[t0 + 64:t0 + 192, 2520:3360]), reads=[B_proj[tile], B_proj[tile + 1]], writes=[bSD])
                S.op("dve", lambda e: e.tensor_tensor(out=Mx[:], in0=Mx[:], in1=mu_r[:], op=ALU.mult), reads=[bMx, bmu], writes=[bMx])
                S.op("dve", lambda e, SM1=SM1, par=par: e.scalar_tensor_tensor(out=SM1[:], in0=SM1[:], scalar=rm[:, par:par + 1], in1=cM1[:], op0=ALU.mult, op1=ALU.mult), reads=[bSM1, brm, bcM1], writes=[bSM1])
                S.op("dve", lambda e, SM1=SM1: e.tensor_tensor(out=Mx[:, 0:1680], in0=Mx[:, 0:1680], in1=SM1[:], op=ALU.add), reads=[bSM1, bMx], writes=[bMx])
                S.op("dve", lambda e, SP1=SP1, par=par: e.scalar_tensor_tensor(out=SP1[:], in0=SP1[:], scalar=rm[:, 2 + par:3 + par], in1=cP1[:], op0=ALU.mult, op1=ALU.mult), reads=[bSP1, brm, bcP1], writes=[bSP1])
                S.op("dve", lambda e, SP1=SP1: e.tensor_tensor(out=Mx[:, 840:3360], in0=Mx[:, 840:3360], in1=SP1[:], op=ALU.add), reads=[bSP1, bMx], writes=[bMx])
                S.op("pool", lambda e, SU=SU: e.tensor_tensor(out=SU[:], in0=SU[:], in1=cU[:], op=ALU.mult), reads=[bSU, bcU], writes=[bSU])
                S.op("dve", lambda e, SU=SU: e.tensor_tensor(out=Mx[:, 1680:2520], in0=Mx[:, 1680:2520], in1=SU[:], op=ALU.add), reads=[bSU, bMx], writes=[bMx])
                S.op("pool", lambda e, SD=SD: e.tensor_tensor(out=SD[:], in0=SD[:], in1=cD[:], op=ALU.mult), reads=[bSD, bcD], writes=[bSD])
                S.op("dve", lambda e, SD=SD: e.tensor_tensor(out=Mx[:, 2520:3360], in0=Mx[:, 2520:3360], in1=SD[:], op=ALU.add), reads=[bSD, bMx], writes=[bMx])
                r_ap, k_ap, v_ap = Mx[:, 0:1024], Mx[:, 1024:2048], Mx[:, 2048:3072]
                S.dma("pool", lambda e, t0=t0: e.dma_start(out=prep["R"][t0:t0 + 128, :], in_=Mx[:, 0:1024]), reads=[bMx], writes=[B_prep["R"][tile]], sembuf=bMx)
                S.dma("pool", lambda e, t0=t0: e.dma_start(out=prep["V"][t0:t0 + 128, :], in_=Mx[:, 2048:3072]), reads=[bMx], writes=[B_prep["V"][tile]], sembuf=bMx)
                S.op("act", lambda e: e.activation(out=lb[:, 0:64], in_=Mx[:, 3072:3136], func=AF.Tanh), reads=[bMx], writes=[blb])
                S.op("act", lambda e: e.copy(out=lb[:, 64:128], in_=Mx[:, 3136:3200]), reads=[bMx], writes=[blb])
                S.op("act", lambda e: e.activation(out=lb[:, 128:288], in_=Mx[:, 3200:3360], func=AF.Sigmoid), reads=[bMx], writes=[blb])
                S.op("pe", lambda e: e.transpose(out=pTl[:, 0, :], in_=lb[:, 0:128], identity=identb[:]), reads=[blb, b_identb], writes=[bpTl])
                S.op("pe", lambda e: e.transpose(out=pTl[:, 1, :], in_=lb[:, 128:256], identity=identb[:]), reads=[blb, b_identb], writes=[bpTl])
                S.op("pe", lambda e: e.transpose(out=pTl[0:32, 2, :], in_=lb[:, 256:288], identity=identb[:]), reads=[blb, b_identb], writes=[bpTl])
                S.op("dve", lambda e: e.tensor_copy(out=lT[:, 0:2, :], in_=pTl[:, 0:2, :]), reads=[bpTl], writes=[blT])
                S.op("dve", lambda e: e.tensor_copy(out=lT[0:32, 2, :], in_=pTl[0:32, 2, :]), reads=[bpTl], writes=[blT])
                oG, boG = outs["G"][0]
                for hh in range(2):
                    S.op("pe", lambda e, hh=hh: e.matmul(pz_[:, hh * 512:(hh + 1) * 512], lhsT=lT[:, 1, :], rhs=lw["gu0"][0][:, hh * 512:(hh + 1) * 512], start=True, stop=False), reads=[blT, lw["gu0"][1]], writes=[bpz_])
                    S.op("pe", lambda e, hh=hh: e.matmul(pz_[:, hh * 512:(hh + 1) * 512], lhsT=lT[0:32, 2, :], rhs=lw["gu1"][0][0:32, hh * 512:(hh + 1) * 512], start=False, stop=True), reads=[blT, lw["gu1"][1]], writes=[bpz_])
                S.op("act", lambda e, oG=oG: e.copy(out=oG[:], in_=pz_[:]), reads=[bpz_], writes=[boG])
                S.dma("pool", lambda e, oG=oG, t0=t0: e.dma_start(out=prep["G"][t0:t0 + 128, :], in_=oG[:]), reads=[boG], writes=[B_prep["G"][tile]], sembuf=boG)
                oKK, boKK = outs["KK"][0]
                S.op("dve", lambda e, oKK=oKK: e.tensor_tensor(out=oKK[:], in0=k_ap, in1=rws["kk"][0][:], op=ALU.mult), reads=[bMx, rws["kk"][1]], writes=[boKK])
                S.op("act", lambda e, oKK=oKK: e.activation(out=tmpR[:], in_=oKK[:], func=AF.Square), reads=[boKK], writes=[btmpR])
                S.op("dve", lambda e: e.tensor_reduce(out=t16[:], in_=h3(tmpR[:]), axis=AX.X, op=ALU.add), reads=[btmpR], writes=[bt16])
                S.op("dve", lambda e: e.tensor_scalar(out=t16[:], in0=t16[:], scalar1=1e-24, scalar2=None, op0=ALU.max), reads=[bt16], writes=[bt16])
                S.op("act", lambda e: e.activation(out=t16[:], in_=t16[:], func=AF.Sqrt), reads=[bt16], writes=[bt16])
                S.op("dve", lambda e: e.reciprocal(out=t16[:], in_=t16[:]), reads=[bt16], writes=[bt16])
                S.op("dve", lambda e, oKK=oKK: e.tensor_tensor(out=h3(oKK[:]), in0=h3(oKK[:]), in1=bc16(t16[:]), op=ALU.mult), reads=[boKK, bt16], writes=[boKK])
                S.dma("pool", lambda e, oKK=oKK, t0=t0: e.dma_start(out=prep["KK"][t0:t0 + 128, :], in_=oKK[:]), reads=[boKK], writes=[B_prep["KK"][tile]], sembuf=boKK)
                oB, boB = outs["BON"][0]
                S.op("pool", lambda e: e.tensor_tensor(out=tmpR[:], in0=r_ap, in1=k_ap, op=ALU.mult), reads=[bMx], writes=[btmpR])
                S.op("dve", lambda e: e.tensor_tensor(out=tmpR[:], in0=tmpR[:], in1=rws["rk"][0][:], op=ALU.mult), reads=[btmpR, rws["rk"][1]], writes=[btmpR])
                S.op("dve", lambda e: e.tensor_reduce(out=t16b[:], in_=h3(tmpR[:]), axis=AX.X, op=ALU.add), reads=[btmpR], writes=[bt16b])
                S.op("dve", lambda e, oB=oB: e.tensor_tensor(out=h3(oB[:]), in0=h3(v_ap), in1=bc16(t16b[:]), op=ALU.mult), reads=[bMx, bt16b], writes=[boB])
                S.dma("pool", lambda e, oB=oB, t0=t0: e.dma_start(out=prep["BON"][t0:t0 + 128, :], in_=oB[:]), reads=[boB], writes=[B_prep["BON"][tile]], sembuf=boB)
                for d in range(2):
                    ds_ = str(d)
                    for hh in range(2):
                        S.op("pe", lambda e, hh=hh, ds_=ds_: e.matmul(pz_[:, hh * 512:(hh + 1) * 512], lhsT=lT[0:64, 0, :], rhs=lw["wu" + ds_][0][0:64, hh * 512:(hh + 1) * 512], start=True, stop=True), reads=[blT, lw["wu" + ds_][1]], writes=[bpz_])
                        S.op("pe", lambda e, hh=hh, ds_=ds_: e.matmul(pa_[:, hh * 512:(hh + 1) * 512], lhsT=lT[64:128, 0, :], rhs=lw["au" + ds_][0][64:128, hh * 512:(hh + 1) * 512], start=True, stop=True), reads=[blT, lw["au" + ds_][1]], writes=[bpa_])
                    oLD, boLD = outs["LD"][d]
                    S.op("dve", lambda e, oLD=oLD, ds_=ds_: e.tensor_tensor(out=oLD[:], in0=pz_[:], in1=rws["w0" + ds_][0][:], op=ALU.add), reads=[bpz_, rws["w0" + ds_][1]], writes=[boLD])
                    S.op("act", lambda e, oLD=oLD: e.activation(out=oLD[:], in_=oLD[:], func=AF.Sigmoid), reads=[boLD], writes=[boLD])
                    S.op("act", lambda e, oLD=oLD: e.activation(out=oLD[:], in_=oLD[:], func=AF.Copy, scale=-0.6065306597126334), reads=[boLD], writes=[boLD])
                    S.dma("pool", lambda e, oLD=oLD, t0=t0, ds_=ds_: e.dma_start(out=prep["LD" + ds_][t0:t0 + 128, :], in_=oLD[:]), reads=[boLD], writes=[B_prep["LD" + ds_][tile]], sembuf=boLD)
                    S.op("dve", lambda e, ds_=ds_: e.tensor_tensor(out=AD[:], in0=pa_[:], in1=rws["a0" + ds_][0][:], op=ALU.add), reads=[bpa_, rws["a0" + ds_][1]], writes=[bAD])
                    S.op("act", lambda e: e.activation(out=AD[:], in_=AD[:], func=AF.Sigmoid), reads=[bAD], writes=[bAD])
                    oBD, boBD = outs["BD"][d]
                    S.op("pool", lambda e, oBD=oBD, oKK=oKK: e.tensor_tensor(out=oBD[:], in0=oKK[:], in1=AD[:], op=ALU.mult), reads=[boKK, bAD], writes=[boBD])
                    S.dma("pool", lambda e, oBD=oBD, t0=t0, ds_=ds_: e.dma_start(out=prep["BD" + ds_][t0:t0 + 128, :], in_=oBD[:]), reads=[boBD], writes=[B_prep["BD" + ds_][tile]], sembuf=boBD)
                    oKD, boKD = outs["KD"][d]
                    S.op("dve", lambda e: e.tensor_tensor(out=tmpR[:], in0=AD[:], in1=rws["ka"][0][:], op=ALU.mult), reads=[bAD, rws["ka"][1]], writes=[btmpR])
                    S.op("pool", lambda e: e.tensor_tensor(out=tmpR[:], in0=tmpR[:], in1=omka[:], op=ALU.add), reads=[btmpR, bomka], writes=[btmpR])
                    S.op("dve", lambda e, oKD=oKD: e.tensor_tensor(out=oKD[:], in0=tmpR[:], in1=k_ap, op=ALU.mult), reads=[btmpR, bMx], writes=[boKD])
                    S.dma("pool", lambda e, oKD=oKD, t0=t0, ds_=ds_: e.dma_start(out=prep["KD" + ds_][t0:t0 + 128, :], in_=oKD[:]), reads=[boKD], writes=[B_prep["KD" + ds_][tile]], sembuf=boKD)
            allb = [bMx, blwst] + [b for r_ in (SM1s, SP1s, SUs, SDs) for _, b in r_] + [b for n in outs for _, b in outs[n]] + [rws[n][1] for n in rws] + [bmu, brm, bcM1, bcP1, bcU, bcD]
            S.barrier(allb)

        with contextlib.ExitStack() as es:
            names = ("R", "KK", "V", "KD", "BD", "LD")
            IN = {n: C.ring(es, "in" + n, 2, [128, RW], F32) for n in names}
            CL, bCL = C.sb(es, "CL", [128, RW], F32)
            TOT, bTOT = C.sb(es, "TOT", [128, RW], F32)
            EX, bEX = C.sb(es, "EX", [128, RW], F32)
            Ee, bEe = C.sb(es, "Ee", [128, RW], F32)
            SCb = {n: C.sb(es, "sc" + n, [128, RW], BF16) for n in ("kap", "rt", "bet", "kt", "khat", "bhat", "V")}
            pcl, bpcl = C.ps(es, "pcl", [128, 1024], F32)
            ptr, bptr = C.ps(es, "ptr", [128, 4, 512], BF16)
            pgr, bpgr = C.ps(es, "pgr", [128, 1024], F32)
            ptt, bptt = C.ps(es, "ptt", [128, 512], F32)
            pch, bpch = C.ps(es, "pch", [128, 512], F32)
            FT, bFT = C.sb(es, "FT", [64, 4, 512], BF16)
            GM, bGM = C.sb(es, "GM", [128, 4, 512], BF16)
            QQ, bQQ = C.sb(es, "QQ", [128, 4, 256], BF16)
            TTb, bTTb = C.sb(es, "TTb", [128, 4, 128], BF16)
            Xb, bXb = C.sb(es, "Xb", [128, 4, 64], BF16)
            Ub, bUb = C.sb(es, "Ub", [128, 4, 64], BF16)
            U0b, bU0b = C.sb(es, "U0b", [128, 4, 64], BF16)
            X32, bX32 = C.sb(es, "X32", [128, 4, 64], F32)
            U032, bU032 = C.sb(es, "U032", [128, 4, 64], F32)
            Yacc = C.ring(es, "Yacc", 2, [128, RW], F32)
            Ast, bAst = C.sb(es, "Ast", [64, 2, 16, 64], F32)
            Abf, bAbf = C.sb(es, "Abf", [64, 2, 16, 64], BF16)
            PCc, bPCc = C.sb(es, "PCc", [64, 16, 2], F32)
            S.dma("sp", lambda e: e.dma_start(out=Ast[:], in_=s0wkv), writes=[bAst])
            S.op("dve", lambda e: e.tensor_copy(out=Abf[:], in_=Ast[:]), reads=[bAst], writes=[bAbf])
            for step in range(NTILE):
                if step > 0 and step % 8 == 0:
                    S.barrier()
                for d in range(2):
                    c = step if d == 0 else NTILE - 1 - step
                    t0 = c * 128
                    slot = (step * 2 + d) % 2
                    cur = {}
                    for n in names:
                        tl, btl = IN[n][slot]
                        key = n + str(d) if n in ("KD", "BD", "LD") else n
                        S.dma("sp", lambda e, tl=tl, key=key, t0=t0: e.dma_start(out=tl[:], in_=prep[key][t0:t0 + 128, :]), reads=[B_prep[key][c]], writes=[btl])
                        cur[n] = (tl, btl)
                    LD, bLD = cur["LD"]
                    for hh in range(2):
                        S.op("pe", lambda e, hh=hh, d=d, LD=LD: e.matmul(pcl[:, hh * 512:(hh + 1) * 512], lhsT=tri[:, d, :], rhs=LD[:, hh * 512:(hh + 1) * 512], start=True, stop=True), reads=[b_tri, bLD], writes=[bpcl])
                    S.op("act", lambda e: e.copy(out=CL[:], in_=pcl[:]), reads=[bpcl], writes=[bCL])
                    for hh in range(2):
                        S.op("pe", lambda e, hh=hh, LD=LD: e.matmul(pcl[:, hh * 512:(hh + 1) * 512], lhsT=ones[:], rhs=LD[:, hh * 512:(hh + 1) * 512], start=True, stop=True), reads=[b_ones, bLD], writes=[bpcl])
                    S.op("dve", lambda e: e.tensor_tensor(out=TOT[:], in0=pcl[:], in1=CL[:], op=ALU.subtract), reads=[bpcl, bCL], writes=[bTOT])
                    S.op("pool", lambda e, LD=LD: e.tensor_tensor(out=EX[:], in0=CL[:], in1=LD[:], op=ALU.subtract), reads=[bCL, bLD], writes=[bEX])
                    for h in range(16):
                        S.op("pe", lambda e, h=h, LD=LD: e.matmul(pch[0:64, h * 2:h * 2 + 2], lhsT=LD[:, h * 64:(h + 1) * 64], rhs=ones[:, 0:2], start=True, stop=True), reads=[bLD, b_ones], writes=[bpch])
                    S.op("act", lambda e: e.activation(out=PCc[:].rearrange("k h t -> k (h t)"), in_=pch[0:64, 0:32], func=AF.Exp), reads=[bpch], writes=[bPCc])
                    R_, bR_ = cur["R"]
                    KK_, bKK_ = cur["KK"]
                    V_, bV_ = cur["V"]
                    KD_, bKD_ = cur["KD"]
                    BD_, bBD_ = cur["BD"]
                    S.op("act", lambda e: e.activation(out=Ee[:], in_=EX[:], func=AF.Exp), reads=[bEX], writes=[bEe])
                    S.op("dve", lambda e, KK_=KK_: e.tensor_tensor(out=SCb["kap"][0][:], in0=KK_[:], in1=Ee[:], op=ALU.mult), reads=[bKK_, bEe], writes=[SCb["kap"][1]])
                    S.op("act", lambda e: e.activation(out=Ee[:], in_=CL[:], func=AF.Exp), reads=[bCL, SCb["kap"][1]], writes=[bEe])
                    S.op("pool", lambda e, R_=R_: e.tensor_tensor(out=SCb["rt"][0][:], in0=R_[:], in1=Ee[:], op=ALU.mult), reads=[bR_, bEe], writes=[SCb["rt"][1]])
                    S.op("act", lambda e: e.activation(out=EX[:], in_=CL[:], func=AF.Exp, scale=-1.0), reads=[bCL, SCb["kap"][1]], writes=[bEX])
                    S.op("dve", lambda e, BD_=BD_: e.tensor_tensor(out=SCb["bet"][0][:], in0=BD_[:], in1=EX[:], op=ALU.mult), reads=[bBD_, bEX], writes=[SCb["bet"][1]])
                    S.op("pool", lambda e, KD_=KD_: e.tensor_tensor(out=SCb["kt"][0][:], in0=KD_[:], in1=EX[:], op=ALU.mult), reads=[bKD_, bEX], writes=[SCb["kt"][1]])
                    S.op("act", lambda e: e.activation(out=TOT[:], in_=TOT[:], func=AF.Exp), reads=[bTOT], writes=[bTOT])
                    S.op("dve", lambda e, KD_=KD_: e.tensor_tensor(out=SCb["khat"][0][:], in0=KD_[:], in1=TOT[:], op=ALU.mult), reads=[bKD_, bTOT], writes=[SCb["khat"][1]])
                    S.op("pool", lambda e, BD_=BD_: e.tensor_tensor(out=SCb["bhat"][0][:], in0=BD_[:], in1=TOT[:], op=ALU.mult), reads=[bBD_, bTOT], writes=[SCb["bhat"][1]])
                    S.op("act", lambda e, V_=V_: e.copy(out=SCb["V"][0][:], in_=V_[:]), reads=[bV_], writes=[SCb["V"][1]])
                    Ya, bYa = Yacc[slot]
                    for hg in range(4):
                        for j in range(4):
                            h = hg * 4 + j
                            for qi, n in enumerate(("kap", "rt", "bet", "kt")):
                                S.op("pe", lambda e, j=j, h=h, qi=qi, n=n: e.transpose(out=ptr[0:64, j, qi * 128:(qi + 1) * 128], in_=SCb[n][0][:, h * 64:(h + 1) * 64], identity=identb[:]),
                                     reads=[SCb[n][1], b_identb], writes=[bptr])
                        S.op("act", lambda e: e.copy(out=FT[:], in_=ptr[0:64, :, :]), reads=[bptr], writes=[bFT])
                        for j in range(4):
                            S.op("pe", lambda e, j=j: e.matmul(pgr[:, j * 256:(j + 1) * 256], lhsT=FT[:, j, 256:384], rhs=FT[:, j, 0:256], start=True, stop=True), reads=[bFT], writes=[bpgr])
                        gm4 = gmask[:, d, 0:256].unsqueeze(1).broadcast_to([128, 4, 256])
                        S.op("dve", lambda e, gm4=gm4: e.tensor_tensor(out=GM[:, :, 0:256], in0=pgr[:].rearrange("p (j w) -> p j w", w=256), in1=gm4, op=ALU.mult), reads=[bpgr, b_gmask], writes=[bGM])
                        for j in range(4):
                            S.op("pe", lambda e, j=j: e.matmul(pgr[:, j * 256:(j + 1) * 256], lhsT=FT[:, j, 384:512], rhs=FT[:, j, 0:256], start=True, stop=True), reads=[bFT, bGM], writes=[bpgr])
                        gm4b = gmask[:, d, 256:512].unsqueeze(1).broadcast_to([128, 4, 256])
                        S.op("dve", lambda e, gm4b=gm4b: e.tensor_tensor(out=GM[:, :, 256:512], in0=pgr[:].rearrange("p (j w) -> p j w", w=256), in1=gm4b, op=ALU.mult), reads=[bpgr, b_gmask], writes=[bGM])
                        S.op("act", lambda e: e.copy(out=QQ[:, :, 0:128], in_=GM[:, :, 0:128]), reads=[bGM], writes=[bQQ])
                        ptb = ptr[:, :, 0:128]
                        for j in range(4):
                            S.op("pe", lambda e, j=j: e.transpose(out=ptr[:, j, 0:128], in_=GM[:, j, 0:128], identity=identb[:]), reads=[bGM, b_identb, bFT], writes=[bptr])
                        S.op("act", lambda e: e.copy(out=QQ[:, :, 128:256], in_=ptr[:, :, 0:128]), reads=[bptr], writes=[bQQ])
                        idb4 = identb[:].unsqueeze(1).broadcast_to([128, 4, 128])
                        S.op("dve", lambda e, idb4=idb4: e.tensor_tensor(out=TTb[:], in0=GM[:, :, 0:128], in1=idb4, op=ALU.add), reads=[bGM, b_identb], writes=[bTTb])
                        for lv in range(6):
                            last = (lv == 5)
                            for j in range(4):
                                if not last:
                                    S.op("pe", lambda e, j=j: e.matmul(pgr[:, j * 256:j * 256 + 128], lhsT=QQ[:, j, 128:256], rhs=QQ[:, j, 0:128], start=True, stop=True), reads=[bQQ], writes=[bpgr])
                                S.op("pe", lambda e, j=j: e.matmul(pgr[:, j * 256 + 128:(j + 1) * 256], lhsT=QQ[:, j, 0:128], rhs=QQ[:, j, 128:256], start=True, stop=True), reads=[bQQ], writes=[bpgr])
                            S.op("act", lambda e: e.copy(out=QQ[:], in_=pgr[:].rearrange("p (j w) -> p j w", w=256)), reads=[bpgr], writes=[bQQ])
                            for j in range(4):
                                S.op("pe", lambda e, j=j: e.matmul(ptt[:, j * 128:(j + 1) * 128], lhsT=QQ[:, j, 128:256], rhs=TTb[:, j, :], start=True, stop=True), reads=[bTTb, bQQ], writes=[bptt])
                            S.op("dve", lambda e: e.tensor_tensor(out=TTb[:], in0=ptt[:].rearrange("p (j w) -> p j w", w=128), in1=TTb[:], op=ALU.add), reads=[bptt, bTTb], writes=[bTTb])
                        Vb = SCb["V"][0]
                        for j in range(4):
                            h = hg * 4 + j
                            S.op("pe", lambda e, j=j, h=h, d=d: e.matmul(pch[:, j * 64:(j + 1) * 64], lhsT=FT[:, j, 0:128], rhs=Abf[:, d, h, :], start=True, stop=False), reads=[bFT, bAbf], writes=[bpch])
                            S.op("pe", lambda e, j=j, h=h: e.matmul(pch[:, j * 64:(j + 1) * 64], lhsT=GM[:, j, 256:384], rhs=Vb[:, h * 64:(h + 1) * 64], start=False, stop=True), reads=[bGM, SCb["V"][1]], writes=[bpch])
                        if REFINE:
                            pX = pch[:, 0:256].rearrange("p (j w) -> p j w", w=64)
                            pU = pch[:, 256:512].rearrange("p (j w) -> p j w", w=64)
                            S.op("act", lambda e: e.copy(out=Xb[:], in_=pX), reads=[bpch], writes=[bXb])
                            S.op("dve", lambda e: e.tensor_copy(out=X32[:], in_=pX), reads=[bpch, bXb], writes=[bX32])
                            for j in range(4):
                                S.op("pe", lambda e, j=j: e.matmul(pch[:, 256 + j * 64:256 + (j + 1) * 64], lhsT=TTb[:, j, :], rhs=Xb[:, j, :], start=True, stop=True), reads=[bTTb, bXb], writes=[bpch])
                            S.op("act", lambda e: e.copy(out=U0b[:], in_=pU), reads=[bpch], writes=[bU0b])
                            S.op("dve", lambda e: e.tensor_copy(out=U032[:], in_=pU), reads=[bpch, bU0b], writes=[bU032])
                            S.op("dve", lambda e: e.tensor_tensor(out=X32[:], in0=X32[:], in1=U032[:], op=ALU.subtract), reads=[bX32, bU032], writes=[bX32])
                            for j in range(4):
                                S.op("pe", lambda e, j=j: e.matmul(pch[:, j * 64:(j + 1) * 64], lhsT=GM[:, j, 0:128], rhs=U0b[:, j, :], start=True, stop=True), reads=[bGM, bU0b, bX32], writes=[bpch])
                            S.op("dve", lambda e: e.tensor_tensor(out=Xb[:], in0=pX, in1=X32[:], op=ALU.add), reads=[bpch, bX32], writes=[bXb])
                            for j in range(4):
                                S.op("pe", lambda e, j=j: e.matmul(pch[:, 256 + j * 64:256 + (j + 1) * 64], lhsT=TTb[:, j, :], rhs=Xb[:, j, :], start=True, stop=True), reads=[bTTb, bXb], writes=[bpch])
                            S.op("dve", lambda e: e.scalar_tensor_tensor(out=Ub[:], in0=pU, scalar=-1.0, in1=U032[:], op0=ALU.mult, op1=ALU.subtract), reads=[bpch, bU032], writes=[bUb])
                        else:
                            S.op("act", lambda e: e.copy(out=Xb[:], in_=pch[:, 0:256].rearrange("p (j w) -> p j w", w=64)), reads=[bpch], writes=[bXb])
                            for j in range(4):
                                S.op("pe", lambda e, j=j: e.matmul(pch[:, 256 + j * 64:256 + (j + 1) * 64], lhsT=TTb[:, j, :], rhs=Xb[:, j, :], start=True, stop=True), reads=[bTTb, bXb], writes=[bpch])
                            S.op("dve", lambda e: e.tensor_scalar(out=Ub[:], in0=pch[:, 256:512].rearrange("p (j w) -> p j w", w=64), scalar1=-1.0, scalar2=None, op0=ALU.mult), reads=[bpch], writes=[bUb])
                        for j in range(4):
                            h = hg * 4 + j
                            S.op("pe", lambda e, j=j, h=h, d=d: e.matmul(pch[:, j * 64:(j + 1) * 64], lhsT=FT[:, j, 128:256], rhs=Abf[:, d, h, :], start=True, stop=False), reads=[bFT, bAbf, bXb], writes=[bpch])
                            S.op("pe", lambda e, j=j, h=h: e.matmul(pch[:, j * 64:(j + 1) * 64], lhsT=GM[:, j, 384:512], rhs=Vb[:, h * 64:(h + 1) * 64], start=False, stop=False), reads=[bGM, SCb["V"][1]], writes=[bpch])
                            S.op("pe", lambda e, j=j: e.matmul(pch[:, j * 64:(j + 1) * 64], lhsT=GM[:, j, 128:256], rhs=Ub[:, j, :], start=False, stop=True), reads=[bGM, bUb], writes=[bpch])
                        S.op("act", lambda e, Ya=Ya, hg=hg: e.copy(out=Ya[:, hg * 256:(hg + 1) * 256], in_=pch[:, 0:256]), reads=[bpch], writes=[bYa])
                        for j in range(4):
                            h = hg * 4 + j
                            S.op("pe", lambda e, j=j, h=h: e.matmul(pch[0:64, 256 + j * 64:256 + (j + 1) * 64], lhsT=SCb["khat"][0][:, h * 64:(h + 1) * 64], rhs=Vb[:, h * 64:(h + 1) * 64], start=True, stop=False), reads=[SCb["khat"][1], SCb["V"][1], bUb], writes=[bpch])
                            S.op("pe", lambda e, j=j, h=h: e.matmul(pch[0:64, 256 + j * 64:256 + (j + 1) * 64], lhsT=SCb["bhat"][0][:, h * 64:(h + 1) * 64], rhs=Ub[:, j, :], start=False, stop=True), reads=[SCb["bhat"][1], bUb], writes=[bpch])
                        for j in range(4):
                            h = hg * 4 + j
                            S.op("dve", lambda e, j=j, h=h, d=d: e.scalar_tensor_tensor(out=Ast[:, d, h, :], in0=Ast[:, d, h, :], scalar=PCc[:, h, 0:1], in1=pch[0:64, 256 + j * 64:256 + (j + 1) * 64], op0=ALU.mult, op1=ALU.add),
                                 reads=[bAst, bPCc, bpch], writes=[bAst])
                        S.op("act", lambda e, hg=hg, d=d: e.copy(out=Abf[:, d, hg * 4:(hg + 1) * 4, :], in_=Ast[:, d, hg * 4:(hg + 1) * 4, :]), reads=[bAst], writes=[bAbf])
                    S.dma("pool", lambda e, Ya=Ya, t0=t0, d=d: e.dma_start(out=YD[d][t0:t0 + 128, :], in_=Ya[:]), reads=[bYa], writes=[B_Y[d][c]], sembuf=bYa)
                    seg_end = (d == 0 and c % 2 == 1) or (d == 1 and c % 2 == 0)
                    if seg_end:
                        sidx = c // 2
                        S.dma("pool", lambda e, sidx=sidx, d=d: e.dma_start(out=wkv_fin[sidx, d], in_=Ast[:, d, :, :]), reads=[bAst], sembuf=bAst, final=True)
                        S.op("dve", lambda e, d=d: e.tensor_scalar(out=Ast[:, d, :, :], in0=Ast[:, d, :, :], scalar1=cm[0:64, 0:1], scalar2=None, op0=ALU.mult), reads=[bAst, b_cm], writes=[bAst])
                        S.op("pool", lambda e, d=d: e.tensor_copy(out=Abf[:, d, :, :], in_=Ast[:, d, :, :]), reads=[bAst], writes=[bAbf])
            S.barrier([bAst] + [b for n in names for _, b in IN[n]] + [b for _, b in Yacc])

        with contextlib.ExitStack() as es:
            lnw, blnw = C.sb(es, "lnw", [128, RW], F32)
            lnb, blnb = C.sb(es, "lnb", [128, RW], F32)
            S.dma("sp", lambda e: e.dma_start(out=lnw[:], in_=rowap("lnw")), writes=[blnw])
            S.dma("sp", lambda e: e.dma_start(out=lnb[:], in_=rowap("lnb")), writes=[blnb])
            rg = {n: C.ring(es, "e" + n, 2, [128, RW], F32) for n in ("YF", "YB", "BON", "G")}
            Ysq, bYsq = C.sb(es, "Ysq", [128, RW], F32)
            m16, bm16 = C.sb(es, "m16", [128, 16], F32)
            v16, bv16 = C.sb(es, "v16", [128, 16], F32)
            Ob, bOb = C.sb(es, "Ob", [128, RW], BF16)
            pTe, bpTe = C.ps(es, "pTe", [128, 8, 128], BF16)
            OTs = C.ring(es, "OT", 2, [128, 8, 128], BF16)

            def h3(ap):
                return ap.rearrange("p (h k) -> p h k", k=64)

            def bc16(ap):
                return ap.unsqueeze(2).broadcast_to([128, 16, 64])
            for tile in range(NTILE):
                t0 = tile * 128
                par = tile % 2
                YF_, bYF_ = rg["YF"][par]
                YB_, bYB_ = rg["YB"][par]
                BN_, bBN_ = rg["BON"][par]
                G_, bG_ = rg["G"][par]
                S.dma("sp", lambda e, YF_=YF_, t0=t0: e.dma_start(out=YF_[:], in_=YD[0][t0:t0 + 128, :]), reads=[B_Y[0][tile]], writes=[bYF_])
                S.dma("sp", lambda e, YB_=YB_, t0=t0: e.dma_start(out=YB_[:], in_=YD[1][t0:t0 + 128, :]), reads=[B_Y[1][tile]], writes=[bYB_])
                S.dma("sp", lambda e, BN_=BN_, t0=t0: e.dma_start(out=BN_[:], in_=prep["BON"][t0:t0 + 128, :]), reads=[B_prep["BON"][tile]], writes=[bBN_])
                S.dma("sp", lambda e, G_=G_, t0=t0: e.dma_start(out=G_[:], in_=prep["G"][t0:t0 + 128, :]), reads=[B_prep["G"][tile]], writes=[bG_])
                S.op("dve", lambda e, YF_=YF_, YB_=YB_: e.tensor_tensor(out=YF_[:], in0=YF_[:], in1=YB_[:], op=ALU.add), reads=[bYF_, bYB_], writes=[bYF_])
                S.op("pool", lambda e, YF_=YF_, BN_=BN_: e.tensor_tensor(out=YF_[:], in0=YF_[:], in1=BN_[:], op=ALU.add), reads=[bYF_, bBN_], writes=[bYF_])
                S.op("dve", lambda e, YF_=YF_: e.tensor_reduce(out=m16[:], in_=h3(YF_[:]), axis=AX.X, op=ALU.add), reads=[bYF_], writes=[bm16])
                S.op("dve", lambda e: e.tensor_scalar(out=m16[:], in0=m16[:], scalar1=-1.0 / 64, scalar2=None, op0=ALU.mult), reads=[bm16], writes=[bm16])
                S.op("dve", lambda e, YF_=YF_: e.tensor_tensor(out=h3(YF_[:]), in0=h3(YF_[:]), in1=bc16(m16[:]), op=ALU.add), reads=[bYF_, bm16], writes=[bYF_])
                S.op("act", lambda e, YF_=YF_: e.activation(out=Ysq[:], in_=YF_[:], func=AF.Square), reads=[bYF_], writes=[bYsq])
                S.op("dve", lambda e: e.tensor_reduce(out=v16[:], in_=h3(Ysq[:]), axis=AX.X, op=ALU.add), reads=[bYsq], writes=[bv16])
                rstd_from_ss(v16[:], bv16, 64, 64e-5, None)
                S.op("dve", lambda e, YF_=YF_: e.tensor_tensor(out=h3(YF_[:]), in0=h3(YF_[:]), in1=bc16(v16[:]), op=ALU.mult), reads=[bYF_, bv16], writes=[bYF_])
                S.op("pool", lambda e, YF_=YF_: e.tensor_tensor(out=YF_[:], in0=YF_[:], in1=lnw[:], op=ALU.mult), reads=[bYF_, blnw], writes=[bYF_])
                S.op("dve", lambda e, YF_=YF_: e.tensor_tensor(out=YF_[:], in0=YF_[:], in1=lnb[:], op=ALU.add), reads=[bYF_, blnb], writes=[bYF_])
                S.op("pool", lambda e, YF_=YF_, G_=G_: e.tensor_tensor(out=Ob[:], in0=YF_[:], in1=G_[:], op=ALU.mult), reads=[bYF_, bG_], writes=[bOb])
                for c in range(8):
                    S.op("pe", lambda e, c=c: e.transpose(out=pTe[:, c, :], in_=Ob[:, c * 128:(c + 1) * 128], identity=identb[:]), reads=[bOb, b_identb], writes=[bpTe])
                OT, bOT = OTs[par]
                S.op("act", lambda e, OT=OT: e.copy(out=OT[:], in_=pTe[:]), reads=[bpTe], writes=[bOT])
                S.dma("pool", lambda e, OT=OT, tile=tile: e.dma_start(out=mixR[tile], in_=OT[:]), reads=[bOT], writes=[B_mixT[tile]], sembuf=bOT)
            S.barrier([blnw, blnb] + [b for n in rg for _, b in rg[n]] + [b for _, b in OTs])

        B_O1 = [Buf("O1_%d" % i) for i in range(NTILE)]
        with contextlib.ExitStack() as es:
            mg, bmg = C.sb(es, "mg", [128, 16, 1024], BF16)
            wst = C.ring(es, "wst1", 4, [128, 4, 512], F32)
            wbs = C.ring(es, "wb1", 2, [128, 16, 512], BF16)
            pmm = [C.ps(es, "pm1_%d" % i, [128, 512], F32) for i in range(8)]
            ost = C.ring(es, "ost1", 8, [128, 512], F32)
            oi = 0
            wi = 0
            for g in range(4):
                tiles = list(range(g * 8, g * 8 + 8))
                S.dma("sp", [lambda e, g=g, q=q: e.dma_start(out=mg[:, q * 4:(q + 1) * 4, :], in_=mixT.rearrange("(c p) t -> p c t", p=128)[:, q * 4:(q + 1) * 4, g * 1024:(g + 1) * 1024]) for q in range(2)]
                      + [lambda e, g=g, t=t: e.dma_start(out=mg[:, 8:16, t * 128:(t + 1) * 128], in_=mixR[g * 8 + t]) for t in range(8)],
                      reads=B_mixL + [B_mixT[t] for t in tiles], writes=[bmg])
                for dcol in range(4):
                    wt, bw = wbs[wi % 2]
                    wi += 1
                    load_w(wst, wt, bw, w_out[:, dcol * 512:(dcol + 1) * 512], 16, 512)
                    for t in range(8):
                        tile = g * 8 + t
                        pm, bpm = pmm[oi % 8]
                        o_t, bo = ost[oi % 8]
                        oi += 1
                        for cch in range(16):
                            S.op("pe", lambda e, pm=pm, wt=wt, cch=cch, t=t: e.matmul(pm[:], lhsT=mg[:, cch, t * 128:(t + 1) * 128], rhs=wt[:, cch, :], start=(cch == 0), stop=(cch == 15)), reads=[bmg, bw], writes=[bpm])
                        cast(o_t[:], pm[:], [bpm], [bo], eng=("act" if oi % 2 else "dve"))
                        S.dma("pool", lambda e, o_t=o_t, tile=tile, dcol=dcol: e.dma_start(out=FD[tile * 128:(tile + 1) * 128, dcol * 512:(dcol + 1) * 512], in_=o_t[:]), reads=[bo], writes=[B_O1[tile]], sembuf=bo)
            S.barrier()
        with contextlib.ExitStack() as es:
            o1s = C.ring(es, "o1b", 2, [128, D], F32)
            xts = C.ring(es, "xt1", 2, [128, D], F32)
            junk, b_junk = C.sb(es, "junk1", [128, D], BF16)
            ss, b_ss = C.sb(es, "ss1", [128, 1], F32)
            xn, b_xn = C.sb(es, "xn1", [128, D], BF16)
            pT, b_pT = C.ps(es, "pT1", [128, D], BF16)
            h2s = C.ring(es, "h2s", 2, [128, 16, 128], BF16)
            nbufs = (junk, b_junk, ss, b_ss, xn, b_xn, pT, b_pT)
            for tile in range(NTILE):
                t0 = tile * 128
                xt, b_xt = xts[tile % 2]
                o1, bo1 = o1s[tile % 2]
                S.dma("sp", lambda e, xt=xt, t0=t0: e.dma_start(out=xt[:], in_=x[t0:t0 + 128, :]), writes=[b_xt])
                S.dma("sp", lambda e, o1=o1, t0=t0: e.dma_start(out=o1[:], in_=FD[t0:t0 + 128, :]), reads=[B_O1[tile]], writes=[bo1])
                S.op("act", lambda e, o1=o1: e.activation(out=junk[:], in_=o1[:], func=AF.Square, accum_out=ss[:]), reads=[bo1], writes=[b_junk, b_ss])
                rstd_from_ss(ss[:], b_ss, D, 1e-6, None)
                S.op("dve", lambda e, o1=o1: e.scalar_tensor_tensor(out=o1[:], in0=o1[:], scalar=ss[:, 0:1], in1=G1r[:], op0=ALU.mult, op1=ALU.mult), reads=[bo1, b_ss, b_G1r], writes=[bo1])
                S.op("pool", lambda e, xt=xt, o1=o1: e.tensor_tensor(out=xt[:], in0=xt[:], in1=o1[:], op=ALU.add), reads=[b_xt, bo1], writes=[b_xt])
                S.dma("pool", lambda e, xt=xt, t0=t0: e.dma_start(out=X1[t0:t0 + 128, :], in_=xt[:]), reads=[b_xt], writes=[B_X1[tile]], sembuf=b_xt)
                h2, bh2 = h2s[tile % 2]
                norm_transpose(xt, b_xt, h2, bh2, 0, Af, modT[:, 48:64], [b_Af, b_modT_], nbufs)
                S.dma("pool", lambda e, h2=h2, tile=tile: e.dma_start(out=h2R[tile], in_=h2[:]), reads=[bh2], writes=[B_h2T[tile]], sembuf=bh2)
            S.barrier()

        with contextlib.ExitStack() as es:
            hg_, bhg = C.sb(es, "hg", [128, 16, 1024], BF16)
            wst = C.ring(es, "wst2", 4, [128, 4, 512], F32)
            wgs = C.ring(es, "wg2", 2, [128, 16, 512], BF16)
            wus = C.ring(es, "wu2", 2, [128, 16, 512], BF16)
            pga = [C.ps(es, "pga%d" % i, [128, 512], F32) for i in range(4)]
            pup = [C.ps(es, "pup%d" % i, [128, 512], F32) for i in range(4)]
            sl = C.ring(es, "sl", 4, [128, 512], F32)
            ao = C.ring(es, "ao", 4, [128, 512], BF16)
            oi = 0
            for g in range(4):
                S.dma("sp", [lambda e, g=g, t=t: e.dma_start(out=hg_[:, :, t * 128:(t + 1) * 128], in_=h2R[g * 8 + t]) for t in range(8)],
                      reads=[B_h2T[t] for t in range(g * 8, g * 8 + 8)], writes=[bhg])
                for j in range(11):
                    wg, bwg = wgs[j % 2]
                    wu, bwu = wus[j % 2]
                    load_w(wst, wg, bwg, w_gu[:, j * 512:(j + 1) * 512], 16, 512)
                    load_w(wst, wu, bwu, w_gu[:, FH + j * 512:FH + (j + 1) * 512], 16, 512)
                    for f in range(4):
                        fc = j * 4 + f
                        for tt in range(2):
                            pg_, bpg_ = pga[oi % 4]
                            pu_, bpu_ = pup[oi % 4]
                            s_, bs_ = sl[oi % 4]
                            a_, ba_ = ao[oi % 4]
                            oi += 1
                            for dc in range(16):
                                S.op("pe", lambda e, pg_=pg_, wg=wg, f=f, dc=dc, tt=tt: e.matmul(pg_[:], lhsT=wg[:, dc, f * 128:(f + 1) * 128], rhs=hg_[:, dc, tt * 512:(tt + 1) * 512], start=(dc == 0), stop=(dc == 15)), reads=[bwg, bhg], writes=[bpg_])
                            for dc in range(16):
                                S.op("pe", lambda e, pu_=pu_, wu=wu, f=f, dc=dc, tt=tt: e.matmul(pu_[:], lhsT=wu[:, dc, f * 128:(f + 1) * 128], rhs=hg_[:, dc, tt * 512:(tt + 1) * 512], start=(dc == 0), stop=(dc == 15)), reads=[bwu, bhg], writes=[bpu_])
                            S.op("act", lambda e, s_=s_, pg_=pg_: e.activation(out=s_[:], in_=pg_[:], func=AF.Silu), reads=[bpg_], writes=[bs_])
                            S.op("dve", lambda e, a_=a_, s_=s_, pu_=pu_: e.tensor_tensor(out=a_[:], in0=s_[:], in1=pu_[:], op=ALU.mult), reads=[bs_, bpu_], writes=[ba_])
                            S.dma("pool", lambda e, a_=a_, fc=fc, g=g, tt=tt: e.dma_start(out=actT[fc * 128:(fc + 1) * 128, g * 1024 + tt * 512:g * 1024 + (tt + 1) * 512], in_=a_[:]), reads=[ba_], writes=[B_actT[fc][g]], sembuf=ba_)
            S.barrier([bhg] + [b for _, b in wst] + [b for _, b in ao])

        B_F = [Buf("Fd_%d" % i) for i in range(NTILE)]
        with contextlib.ExitStack() as es:
            ag = C.ring(es, "ag", 1, [128, 44, 1024], BF16)
            wst = C.ring(es, "wst3", 2, [128, 4, 512], F32)
            wds = C.ring(es, "wd3", 3, [128, 22, 512], BF16)
            pmm = [C.ps(es, "pm3_%d" % i, [128, 512], F32) for i in range(8)]
            ost = C.ring(es, "ost3", 4, [128, 512], F32)
            wi = 0
            oo = 0
            for g in range(4):
                agt, bag = ag[0]
                S.dma("sp", [lambda e, agt=agt, g=g, q=q: e.dma_start(out=agt[:, q * 4:(q + 1) * 4, :], in_=actT.rearrange("(c p) t -> p c t", p=128)[:, q * 4:(q + 1) * 4, g * 1024:(g + 1) * 1024]) for q in range(11)],
                      reads=[B_actT[f][g] for f in range(44)] + B_X1, writes=[bag])
                for dcol in range(4):
                    for half in range(2):
                        wt, bw = wds[wi % 3]
                        wi += 1
                        load_w(wst, wt, bw, w_down[:, dcol * 512:(dcol + 1) * 512], 22, 512, k0=half * 22)
                        for t in range(8):
                            pm, bpm = pmm[t]
                            for f2 in range(22):
                                fc = half * 22 + f2
                                S.op("pe", lambda e, pm=pm, wt=wt, fc=fc, f2=f2, t=t, agt=agt: e.matmul(pm[:], lhsT=agt[:, fc, t * 128:(t + 1) * 128], rhs=wt[:, f2, :], start=(fc == 0), stop=(fc == 43)), reads=[bag, bw], writes=[bpm])
                    for t in range(8):
                        tile = g * 8 + t
                        pm, bpm = pmm[t]
                        o_t, bo = ost[oo % 4]
                        oo += 1
                        cast(o_t[:], pm[:], [bpm], [bo], eng=("act" if t % 2 else "dve"))
                        S.dma("pool", lambda e, o_t=o_t, tile=tile, dcol=dcol: e.dma_start(out=FD[tile * 128:(tile + 1) * 128, dcol * 512:(dcol + 1) * 512], in_=o_t[:]), reads=[bo], writes=[B_F[tile]], sembuf=bo)
            S.barrier()
        with contextlib.ExitStack() as es:
            o1s = C.ring(es, "o3b", 2, [128, D], F32)
            xts = C.ring(es, "xt3", 2, [128, D], F32)
            junk, b_junk = C.sb(es, "junk3", [128, D], BF16)
            ss, b_ss = C.sb(es, "ss3", [128, 1], F32)
            for tile in range(NTILE):
                t0 = tile * 128
                xt, b_xt = xts[tile % 2]
                o1, bo1 = o1s[tile % 2]
                S.dma("sp", lambda e, xt=xt, t0=t0: e.dma_start(out=xt[:], in_=X1[t0:t0 + 128, :]), reads=[B_X1[tile]], writes=[b_xt])
                S.dma("sp", lambda e, o1=o1, t0=t0: e.dma_start(out=o1[:], in_=FD[t0:t0 + 128, :]), reads=[B_F[tile]], writes=[bo1])
                S.op("act", lambda e, o1=o1: e.activation(out=junk[:], in_=o1[:], func=AF.Square, accum_out=ss[:]), reads=[bo1], writes=[b_junk, b_ss])
                rstd_from_ss(ss[:], b_ss, D, 1e-6, None)
                S.op("dve", lambda e, o1=o1: e.scalar_tensor_tensor(out=o1[:], in0=o1[:], scalar=ss[:, 0:1], in1=G2r[:], op0=ALU.mult, op1=ALU.mult), reads=[bo1, b_ss, b_G2r], writes=[bo1])
                S.op("pool", lambda e, xt=xt, o1=o1: e.tensor_tensor(out=xt[:], in0=xt[:], in1=o1[:], op=ALU.add), reads=[b_xt, bo1], writes=[b_xt])
                S.dma("pool", lambda e, xt=xt, t0=t0: e.dma_start(out=y_out[t0:t0 + 128, :], in_=xt[:]), reads=[b_xt], sembuf=b_xt, final=True)
        S.emit()
    S.close()
    return nc


_PROMPT_COUNTS = [6, 6, 5, 5, 5, 5]


def _col(v, n):
    return np.ascontiguousarray(np.asarray(v, np.float32).reshape(n, 128).T)


def kernel(x_prompt, x_sample, state_lru, state_wkv, c, c_ctx,
           norm_mix_pre, norm_mix_post, norm_ffn_pre, norm_ffn_post, w_mod, b_mod, w_in,
           lru_conv_w, lru_conv_b, lru_wr, lru_br, lru_wi, lru_bi, lru_lambda,
           rwkv_mu, rwkv_w0, rwkv_w_up, rwkv_a0, rwkv_a_up, rwkv_g_up, rwkv_k_k, rwkv_k_a, rwkv_r_k,
           rwkv_ln_w, rwkv_ln_b, w_out, ffn_w_gu, ffn_w_down):
    f32 = np.float32
    A = lambda a: np.ascontiguousarray(np.asarray(a, f32))
    x_prompt, x_sample = A(x_prompt), A(x_sample)
    nc = build_program()
    idx = np.arange(128)
    tri = np.zeros((128, 2, 128), f32)
    tri[:, 0, :] = (idx[:, None] <= idx[None, :])
    tri[:, 1, :] = (idx[:, None] >= idx[None, :])
    gmask = np.zeros((128, 2, 512), f32)
    for d in range(2):
        incl = tri[:, d, :]
        strict = incl - np.eye(128, dtype=f32)
        gmask[:, d, 0:128] = -strict
        gmask[:, d, 128:256] = incl
        gmask[:, d, 256:384] = strict
        gmask[:, d, 384:512] = incl
    ncols = np.stack([_col(A(v)[0], 16) for v in (norm_mix_pre, norm_mix_post, norm_ffn_pre, norm_ffn_post)], axis=1)
    convc = np.zeros((128, 8, 5), f32)
    cw = A(lru_conv_w)[0]
    cb = A(lru_conv_b)[0]
    for k in range(4):
        convc[:, :, k] = cw[k].reshape(8, 128).T
    convc[:, :, 4] = cb.reshape(8, 128).T
    lru_bc = np.zeros((128, 2, 8, 2), f32)
    lru_bc[:, :, :, 0] = np.transpose(A(lru_br)[0], (2, 0, 1))
    lru_bc[:, :, :, 1] = np.transpose(A(lru_bi)[0], (2, 0, 1))
    lam = np.ascontiguousarray(np.transpose(A(lru_lambda)[0].reshape(2, 8, 128), (2, 0, 1)))
    shared = {
        "w_mod": A(w_mod)[0], "b_modT": _col(A(b_mod)[0], 96), "ncols": np.ascontiguousarray(ncols),
        "w_in": A(w_in)[0], "convc": convc, "lru_wr": A(lru_wr)[0], "lru_wi": A(lru_wi)[0],
        "lru_bc": lru_bc, "lru_lam": lam,
        "w_up": A(rwkv_w_up)[0], "a_up": A(rwkv_a_up)[0], "g_up": A(rwkv_g_up)[0],
        "w_out": A(w_out)[0], "w_gu": A(ffn_w_gu)[0], "w_down": A(ffn_w_down)[0],
        "ident": np.eye(128, dtype=f32), "tri": tri, "gmask": gmask,
    }
    chs = np.arange(PRW)
    p = np.arange(128)

    def rows_for(sample):
        if sample:
            cmM1 = (chs < 840)
            cmP1 = (chs >= 840) & (chs < 1680)
            cmU = (chs >= 1680) & (chs < 2520)
            cmD = (chs >= 2520)
        else:
            cmM1 = (chs < 1680)
            cmP1 = (chs >= 1680)
            cmU = np.zeros(PRW, bool)
            cmD = np.zeros(PRW, bool)
        parts = [A(rwkv_mu)[0], cmM1.astype(f32), cmP1.astype(f32), cmU.astype(f32), cmD.astype(f32),
                 A(rwkv_w0)[0, 0], A(rwkv_w0)[0, 1], A(rwkv_a0)[0, 0], A(rwkv_a0)[0, 1], A(rwkv_k_k)[0], A(rwkv_k_a)[0],
                 A(rwkv_r_k)[0].reshape(-1), A(rwkv_ln_w)[0], A(rwkv_ln_b)[0]]
        return np.concatenate(parts).astype(f32)[None, :]

    def rmask_for(sample):
        m = np.ones((128, 4), f32)
        if sample:
            m[:, 0] = (p % 64 != 0)
            m[:, 1] = (p % 64 != 0)
            m[:, 2] = (p % 64 != 63)
            m[:, 3] = (p % 64 != 63)
        else:
            m[:, 0] = (p != 0)
            m[:, 1] = 1.0
            m[:, 2] = 1.0
            m[:, 3] = (p != 127)
        return m
    assign = []
    s0 = 0
    for n in _PROMPT_COUNTS:
        assign.append(list(range(s0, s0 + n)))
        s0 += n
    in_maps = []
    for core in range(8):
        m = dict(shared)
        if core < 2:
            m["x"] = np.ascontiguousarray(x_sample[core])
            m["cvec"] = _col(A(c)[core], 16)
            m["cmcol"] = np.ones((128, 1), f32)
            sl = A(state_lru)[core, 0]
            m["h0lru"] = np.ascontiguousarray(np.transpose(sl.reshape(2, 8, 128), (2, 1, 0)))
            sw = A(state_wkv)[core, 0]
            m["s0wkv"] = np.ascontiguousarray(np.transpose(sw, (3, 0, 1, 2)))
            m["rows"] = rows_for(True)
            m["rmask"] = rmask_for(True)
        else:
            xs = np.zeros((NT, D), f32)
            mine = assign[core - 2]
            for i in range(NSEG):
                xs[i * SEG:(i + 1) * SEG] = x_prompt[mine[i % len(mine)]]
            m["x"] = xs
            m["cvec"] = _col(A(c_ctx), 16)
            m["cmcol"] = np.zeros((128, 1), f32)
            m["h0lru"] = np.zeros((128, 8, 2), f32)
            m["s0wkv"] = np.zeros((64, 2, 16, 64), f32)
            m["rows"] = rows_for(False)
            m["rmask"] = rmask_for(False)
        in_maps.append(m)
    res = run_bass_kernel_spmd(nc, in_maps, core_ids=list(range(8)))
    R = res.results
    if DEBUG:
        global _LAST
        _LAST = R
    y_p = np.zeros((32, SEG, D), f32)
    y_s = np.zeros((2, NT, D), f32)
    ns_lru = np.zeros((32, 1, 2, LW), f32)
    ns_wkv = np.zeros((32, 1, 2, 16, 64, 64), f32)
    for core in range(8):
        r = R[core]
        if core < 2:
            y_s[core] = r["y"]
        else:
            lf = r["lru_fin"]
            wf = r["wkv_fin"]
            for i, sidx in enumerate(assign[core - 2]):
                y_p[sidx] = r["y"][i * SEG:(i + 1) * SEG]
                ns_lru[sidx, 0] = np.transpose(lf[:, :, :, i], (2, 1, 0)).reshape(2, LW)
                ns_wkv[sidx, 0] = np.transpose(wf[i], (0, 2, 3, 1))
    return (y_p, y_s, ns_lru, ns_wkv)
```

```python
import contextlib
import numpy as np
import ml_dtypes
import concourse.bass as bass
import concourse.mybir as mybir
from concourse.bass_utils import run_bass_kernel_spmd

F32 = mybir.dt.float32
BF16 = mybir.dt.bfloat16
AF = mybir.ActivationFunctionType
ALU = mybir.AluOpType
AX = mybir.AxisListType

D = 2048
NT = 4096
NSEG = 16
SEG = 256
NTILE = NT // 128
LW = 1024
RW = 1024
PRW = 3360
INW = 5408
FH = 5632
DEBUG = False
DEBUG_SS = False
REFINE = True
SAME_ENGINE_SYNC = True
NOSYNC_ENGS = ("pe",)


class Buf:
    __slots__ = ("name", "w", "r", "dsem", "dcnt")

    def __init__(self, name):
        self.name = name
        self.w = None
        self.r = []
        self.dsem = {}
        self.dcnt = {}


class Sched:
    ENG = ("pe", "dve", "act", "pool", "sp")

    def __init__(self, nc):
        self.nc = nc
        self.prog = {e: [] for e in self.ENG}
        self.cnt = {e: 0 for e in self.ENG}
        self.waited = {e: {} for e in self.ENG}
        self.sems = {}
        self.stack = []
        self.dma_state = {}
        self.epoch = 0
        self.ekey = {}
        for e in ("pe", "dve", "act", "pool"):
            self.ekey[e] = "E_" + e + "_0"
            self._mksem(self.ekey[e])
        self.finals = []
        self.free_dsems = {}
        self.stage_bufs = []
        self.ndsem = 0

    def keep(self):
        self.stage_bufs = []

    def _mksem(self, key):
        cm = self.nc.semaphore(key)
        h = cm.__enter__()
        self.stack.append(cm)
        self.sems[key] = h
        return h

    def _deps(self, eng, reads, writes):
        best = {}
        own = self.ekey.get(eng, "none")
        wd = self.waited[eng]

        def add(dep):
            k, v = dep
            if k == own and (not SAME_ENGINE_SYNC or eng in NOSYNC_ENGS):
                return
            if wd.get(k, 0) >= v:
                return
            if best.get(k, 0) < v:
                best[k] = v
        for b in reads:
            if b.w is not None:
                add(b.w)
        for b in writes:
            if b.w is not None:
                add(b.w)
            for d in b.r:
                add(d)
        waits = []
        for k, v in best.items():
            wd[k] = v
            waits.append((k, v))
        return waits

    def _mark(self, me, reads, writes):
        for b in reads:
            b.r.append(me)
            if len(b.r) > 64:
                mx = {}
                for k, v in b.r:
                    if mx.get(k, 0) < v:
                        mx[k] = v
                b.r = list(mx.items())
        for b in writes:
            b.w = me
            b.r = []

    def op(self, eng, fn, reads=(), writes=()):
        waits = self._deps(eng, reads, writes)
        self.cnt[eng] += 1
        me = (self.ekey[eng], self.cnt[eng])
        self.prog[eng].append((waits, [fn], (me[0], 1)))
        self._mark(me, reads, writes)
        return me

    def dma(self, eng, fns, reads=(), writes=(), sembuf=None, final=False):
        if not isinstance(fns, (list, tuple)):
            fns = [fns]
        if sembuf is None:
            sembuf = writes[0] if writes else reads[0]
        cls = eng
        if sembuf.dsem.get(cls) is None:
            pool_ = self.free_dsems.setdefault(cls, [])
            if pool_:
                sembuf.dsem[cls], sembuf.dcnt[cls] = pool_.pop()
            else:
                self.ndsem += 1
                sembuf.dsem[cls] = "D_%d" % self.ndsem
                sembuf.dcnt[cls] = 0
                self._mksem(sembuf.dsem[cls])
            self.stage_bufs.append((sembuf, cls))
        waits = self._deps(eng, reads, writes)
        dk, dc = sembuf.dsem[cls], sembuf.dcnt[cls]
        if dc > 0 and self.waited[eng].get(dk, 0) < dc:
            self.waited[eng][dk] = dc
            waits.append((dk, dc))
        sembuf.dcnt[cls] = dc + 16 * len(fns)
        me = (dk, sembuf.dcnt[cls])
        self.prog[eng].append((waits, list(fns), (me[0], 16)))
        self._mark(me, reads, writes)
        if final:
            self.finals.append(me)
        return me

    def barrier(self, bufs=()):
        deps = [(self.ekey[e], self.cnt[e]) for e in ("pe", "dve", "act", "pool") if self.cnt[e] > 0]
        for k in self.sems:
            if k.startswith("D_"):
                pass
        for (b, cls) in self.stage_bufs:
            if b.dsem.get(cls) is not None and b.dcnt[cls] > 0:
                deps.append((b.dsem[cls], b.dcnt[cls]))
        for e in self.ENG:
            waits = []
            for (k, v) in deps:
                if k == self.ekey.get(e):
                    continue
                if self.waited[e].get(k, 0) < v:
                    self.waited[e][k] = v
                    waits.append((k, v))
            if waits:
                self.prog[e].append((waits, [], None))
        for (b, cls) in self.stage_bufs:
            if b.dcnt[cls] < 20000:
                self.free_dsems.setdefault(cls, []).append((b.dsem[cls], b.dcnt[cls]))
            b.dsem[cls] = None
        self.stage_bufs = []
        self.epoch += 1
        for e in ("pe", "dve", "act", "pool"):
            self.ekey[e] = "E_%s_%d" % (e, self.epoch)
            self._mksem(self.ekey[e])
            self.cnt[e] = 0

    def emit(self):
        nc = self.nc
        engmap = {"pe": "tensor", "dve": "vector", "act": "scalar", "pool": "gpsimd", "sp": "sync"}
        finals = {}
        for k, v in self.finals:
            finals[k] = max(finals.get(k, 0), v)
        with nc.Block() as block:
            for e in self.ENG:
                prog = self.prog[e]

                def body(eng, prog=prog, e=e):
                    for (waits, fns, inc) in prog:
                        for (k, v) in waits:
                            eng.wait_ge(self.sems[k], v)
                        for fn in fns:
                            ins = fn(eng)
                            ins.then_inc(self.sems[inc[0]], inc[1])
                    if e == "sp":
                        for k, v in finals.items():
                            eng.wait_ge(self.sems[k], v)
                getattr(block, engmap[e])(body)

    def close(self):
        for cm in reversed(self.stack):
            cm.__exit__(None, None, None)


class Ctx:
    def __init__(self, nc):
        self.nc = nc
        self.S = Sched(nc)
        self.uid = 0
        self.rr = 0

    def sb(self, es, name, shape, dt):
        self.uid += 1
        t = es.enter_context(self.nc.sbuf_tensor("%s_%d" % (name, self.uid), list(shape), dt))
        return t, Buf("%s_%d" % (name, self.uid))

    def ps(self, es, name, shape, dt):
        self.uid += 1
        t = es.enter_context(self.nc.psum_tensor("%s_%d" % (name, self.uid), list(shape), dt))
        return t, Buf("%s_%d" % (name, self.uid))

    def ring(self, es, name, n, shape, dt):
        return [self.sb(es, "%s%d" % (name, i), shape, dt) for i in range(n)]


def build_program():
    nc = bass.Bass("TRN2", target_bir_lowering=False)
    C = Ctx(nc)
    S = C.S

    def din(name, shape, dt=F32):
        return nc.dram_tensor(name, list(shape), dt, kind="ExternalInput").ap()

    def dout(name, shape, dt=F32):
        return nc.dram_tensor(name, list(shape), dt, kind="ExternalOutput").ap()

    def dscr(name, shape, dt=F32):
        kind = "ExternalOutput" if DEBUG else "Internal"
        return nc.dram_tensor(name, list(shape), dt, kind=kind).ap()

    x = din("x", [NT, D])
    cvec = din("cvec", [128, 16])
    cmcol = din("cmcol", [128, 1])
    h0lru = din("h0lru", [128, 8, 2])
    s0wkv = din("s0wkv", [64, 2, 16, 64])
    w_mod = din("w_mod", [D, 6 * D])
    b_modT = din("b_modT", [128, 96])
    ncols = din("ncols", [128, 4, 16])
    w_in = din("w_in", [D, INW])
    convc = din("convc", [128, 8, 5])
    lru_wr = din("lru_wr", [2, 8, 128, 128])
    lru_wi = din("lru_wi", [2, 8, 128, 128])
    lru_bc = din("lru_bc", [128, 2, 8, 2])
    lru_lam = din("lru_lam", [128, 2, 8])
    rows = din("rows", [1, 3360 * 5 + 1024 * 9])
    rmask = din("rmask", [128, 4])
    w_up = din("w_up", [2, 64, RW])
    a_up = din("a_up", [2, 64, RW])
    g_up = din("g_up", [160, RW])
    w_out = din("w_out", [D, D])
    w_gu = din("w_gu", [D, 2 * FH])
    w_down = din("w_down", [FH, D])
    ident_in = din("ident", [128, 128])
    tri_in = din("tri", [128, 2, 128])
    gmask_in = din("gmask", [128, 2, 512])
    y_out = dout("y", [NT, D])
    lru_fin = dout("lru_fin", [128, 8, 2, 16])
    wkv_fin = dout("wkv_fin", [16, 2, 64, 16, 64])
    xlglT = dscr("xlglT", [2048, NT])
    projTok = dscr("projTok", [NT, PRW])
    mixT = dscr("mixT", [2048, NT], BF16)
    prep = {n: dscr("prep_" + n, [NT, RW]) for n in ("R", "KK", "V", "KD0", "KD1", "BD0", "BD1", "LD0", "LD1", "G", "BON")}
    YD = [dscr("YF", [NT, RW]), dscr("YB", [NT, RW])]
    X1 = dscr("X1", [NT, D])
    h2R = dscr("h2R", [NTILE, 128, 16, 128], BF16)
    mixR = dscr("mixR", [NTILE, 128, 8, 128], BF16)
    actT = dscr("actT", [FH, NT], BF16)
    FD = dscr("FD", [NT, D])
    DBGSS = dscr("DBGSS", [128, 64])

    def tb(name):
        return [Buf("%s_t%d" % (name, i)) for i in range(NTILE)]
    B_xlgl = [[Buf("xlgl_%d_%d" % (c, g)) for g in range(4)] for c in range(16)]
    B_proj = tb("proj")
    B_mixT = tb("mixT")
    B_mixL = [Buf("mixL%d" % h) for h in range(8)]
    B_prep = {n: tb("prep" + n) for n in prep}
    B_Y = [tb("YF"), tb("YB")]
    B_X1 = tb("X1")
    B_h2T = tb("h2T")
    B_actT = [[Buf("actT_%d_%d" % (f, g)) for g in range(4)] for f in range(44)]
    B_FD = tb("FD")

    ROWOFF = {}
    o = 0
    for n, w in (("mu", 3360), ("cmM1", 3360), ("cmP1", 3360), ("cmU", 3360), ("cmD", 3360),
                 ("w00", 1024), ("w01", 1024), ("a00", 1024), ("a01", 1024), ("kk", 1024), ("ka", 1024),
                 ("rk", 1024), ("lnw", 1024), ("lnb", 1024)):
        ROWOFF[n] = (o, w)
        o += w

    def rowap(n, lo=0, hi=None):
        o0, w = ROWOFF[n]
        if hi is None:
            hi = w
        return rows[0:1, o0 + lo:o0 + hi].broadcast_to([128, hi - lo])

    with contextlib.ExitStack() as es0:
        ident, b_ident = C.sb(es0, "ident", [128, 128], F32)
        identb, b_identb = C.sb(es0, "identb", [128, 128], BF16)
        ones, b_ones = C.sb(es0, "ones", [128, 128], F32)
        tri, b_tri = C.sb(es0, "tri", [128, 2, 128], F32)
        gmask, b_gmask = C.sb(es0, "gmask", [128, 2, 512], F32)
        cm, b_cm = C.sb(es0, "cm", [128, 1], F32)
        modT, b_modT_ = C.sb(es0, "modT", [128, 96], F32)
        ncl, b_ncl = C.sb(es0, "ncl", [128, 4, 16], F32)
        Am, b_Am = C.sb(es0, "Am", [128, 16], F32)
        Af, b_Af = C.sb(es0, "Af", [128, 16], F32)
        G1r, b_G1r = C.sb(es0, "G1r", [128, D], F32)
        G2r, b_G2r = C.sb(es0, "G2r", [128, D], F32)
        S.dma("sp", lambda e: e.dma_start(out=ident[:], in_=ident_in), writes=[b_ident])
        S.dma("sp", lambda e: e.dma_start(out=tri[:], in_=tri_in), writes=[b_tri])
        S.dma("sp", lambda e: e.dma_start(out=gmask[:], in_=gmask_in), writes=[b_gmask])
        S.dma("sp", lambda e: e.dma_start(out=cm[:], in_=cmcol), writes=[b_cm])
        S.dma("sp", lambda e: e.dma_start(out=ncl[:], in_=ncols), writes=[b_ncl])
        S.op("dve", lambda e: e.tensor_copy(out=identb[:], in_=ident[:]), reads=[b_ident], writes=[b_identb])
        S.op("dve", lambda e: e.memset(ones[:], 1.0), writes=[b_ones])
        S.keep()

        cast_engs = ["pool", "dve", "act"]

        def cast(out_ap, in_ap, reads, writes, eng=None):
            if eng is None:
                eng = cast_engs[C.rr % 3]
                C.rr += 1
            if eng == "act":
                S.op("act", lambda e: e.copy(out=out_ap, in_=in_ap), reads=reads, writes=writes)
            else:
                S.op(eng, lambda e: e.tensor_copy(out=out_ap, in_=in_ap), reads=reads, writes=writes)

        def rstd_from_ss(ss, b_ss, n, eps, eng_tmp):
            S.op("dve", lambda e: e.tensor_scalar(out=ss, in0=ss, scalar1=1.0 / n, scalar2=eps, op0=ALU.mult, op1=ALU.add), reads=[b_ss], writes=[b_ss])
            S.op("act", lambda e: e.activation(out=ss, in_=ss, func=AF.Sqrt), reads=[b_ss], writes=[b_ss])
            S.op("dve", lambda e: e.reciprocal(out=ss, in_=ss), reads=[b_ss], writes=[b_ss])

        with contextlib.ExitStack() as es:
            cv, b_cv = C.sb(es, "cv", [128, 16], F32)
            sc, b_sc = C.sb(es, "sc", [128, 16, 2], F32)
            bmt, b_bmt = C.sb(es, "bmt", [128, 96], F32)
            wm = C.ring(es, "wm", 2, [128, 16, 512], F32)
            pmod, b_pmod = C.ps(es, "pmod", [128, 96, 2], F32)
            pbc, b_pbc = C.ps(es, "pbc", [128, 512], F32)
            dg, b_dg = C.sb(es, "dg", [128, 128], F32)
            gc, b_gc = C.sb(es, "gc", [128, 2, 16], F32)
            S.dma("sp", lambda e: e.dma_start(out=cv[:], in_=cvec), writes=[b_cv])
            S.dma("sp", lambda e: e.dma_start(out=bmt[:], in_=b_modT), writes=[b_bmt])
            S.op("act", lambda e: e.activation(out=sc[:, :, 0], in_=cv[:], func=AF.Silu), reads=[b_cv], writes=[b_sc])
            S.op("act", lambda e: e.activation(out=sc[:, :, 1], in_=cv[:], func=AF.Silu), reads=[b_cv], writes=[b_sc])
            wmv = w_mod.rearrange("(dc p) f -> p dc f", p=128)
            for j in range(24):
                wt, bw = wm[j % 2]
                S.dma("sp", [lambda e, j=j, wt=wt, q=q: e.dma_start(out=wt[:, q * 4:(q + 1) * 4, :], in_=wmv[:, q * 4:(q + 1) * 4, j * 512:(j + 1) * 512]) for q in range(4)], writes=[bw])
                for f in range(4):
                    fc = j * 4 + f
                    for dc in range(16):
                        S.op("pe", lambda e, wt=wt, f=f, dc=dc, fc=fc: e.matmul(pmod[:, fc, :], lhsT=wt[:, dc, f * 128:(f + 1) * 128], rhs=sc[:, dc, :], start=(dc == 0), stop=(dc == 15)),
                             reads=[bw, b_sc], writes=[b_pmod])
            S.op("dve", lambda e: e.tensor_tensor(out=modT[:], in0=pmod[:, :, 0], in1=bmt[:], op=ALU.add), reads=[b_pmod, b_bmt], writes=[b_modT_])
            S.op("dve", lambda e: e.scalar_tensor_tensor(out=Am[:], in0=modT[:, 16:32], scalar=1.0, in1=ncl[:, 0, :], op0=ALU.add, op1=ALU.mult), reads=[b_modT_, b_ncl], writes=[b_Am])
            S.op("dve", lambda e: e.scalar_tensor_tensor(out=Af[:], in0=modT[:, 64:80], scalar=1.0, in1=ncl[:, 2, :], op0=ALU.add, op1=ALU.mult), reads=[b_modT_, b_ncl], writes=[b_Af])
            S.op("dve", lambda e: e.tensor_tensor(out=gc[:, 0, :], in0=modT[:, 32:48], in1=ncl[:, 1, :], op=ALU.mult), reads=[b_modT_, b_ncl], writes=[b_gc])
            S.op("dve", lambda e: e.tensor_tensor(out=gc[:, 1, :], in0=modT[:, 80:96], in1=ncl[:, 3, :], op=ALU.mult), reads=[b_modT_, b_ncl], writes=[b_gc])
            for which, (Gr, bGr) in enumerate(((G1r, b_G1r), (G2r, b_G2r))):
                for c in range(16):
                    S.op("dve", lambda e, which=which, c=c: e.tensor_scalar(out=dg[:], in0=ident[:], scalar1=gc[:, which, c:c + 1], scalar2=None, op0=ALU.mult), reads=[b_ident, b_gc], writes=[b_dg])
                    S.op("pe", lambda e: e.matmul(pbc[:, 0:128], lhsT=ones[:], rhs=dg[:], start=True, stop=True), reads=[b_ones, b_dg], writes=[b_pbc])
                    S.op("act", lambda e, Gr=Gr, c=c: e.copy(out=Gr[:, c * 128:(c + 1) * 128], in_=pbc[:, 0:128]), reads=[b_pbc], writes=[bGr])
            S.barrier([b for _, b in wm] + [b_cv, b_bmt])

        def load_w(stage_ring, wt, bw, src, kch, ncol, k0=0):
            srcv = src.rearrange("(kc p) f -> p kc f", p=128)
            for q in range(0, kch, 4):
                n = min(4, kch - q)
                st, bst = stage_ring[C.uid % len(stage_ring)]
                C.uid += 1
                S.dma("sp", lambda e, st=st, q=q, n=n: e.dma_start(out=st[:, 0:n, 0:ncol], in_=srcv[:, k0 + q:k0 + q + n, :]), writes=[bst])
                cast(wt[:, q:q + n, 0:ncol], st[:, 0:n, 0:ncol], [bst], [bw])

        def norm_transpose(xt, b_xt, hT_dst, b_hT, col0, Acol, shcol, bshc, es_bufs):
            junk, b_junk, ss, b_ss, xn, b_xn, pT, b_pT = es_bufs
            S.op("act", lambda e: e.activation(out=junk[:], in_=xt[:], func=AF.Square, accum_out=ss[:]), reads=[b_xt], writes=[b_junk, b_ss])
            rstd_from_ss(ss[:], b_ss, D, 1e-6, None)
            S.op("dve", lambda e: e.tensor_scalar(out=xn[:], in0=xt[:], scalar1=ss[:, 0:1], scalar2=None, op0=ALU.mult), reads=[b_xt, b_ss], writes=[b_xn])
            for c in range(16):
                S.op("pe", lambda e, c=c: e.transpose(out=pT[:, c * 128:(c + 1) * 128], in_=xn[:, c * 128:(c + 1) * 128], identity=identb[:]), reads=[b_xn, b_identb], writes=[b_pT])
            for c in range(16):
                S.op("act", lambda e, c=c: e.activation(out=hT_dst[:, c, col0:col0 + 128], in_=pT[:, c * 128:(c + 1) * 128], func=AF.Identity, scale=Acol[:, c:c + 1], bias=shcol[:, c:c + 1]),
                     reads=[b_pT] + bshc, writes=[b_hT])

        with contextlib.ExitStack() as es:
            xts = C.ring(es, "xt", 2, [128, D], F32)
            junk, b_junk = C.sb(es, "junk", [128, D], BF16)
            ss, b_ss = C.sb(es, "ss", [128, 1], F32)
            xn, b_xn = C.sb(es, "xn", [128, D], BF16)
            pT, b_pT = C.ps(es, "pT", [128, D], BF16)
            hTs = C.ring(es, "hT", 2, [128, 16, 1024], BF16)
            wst = C.ring(es, "wst", 4, [128, 4, 512], F32)
            wbs = C.ring(es, "wb", 2, [128, 16, 512], BF16)
            pmm = [C.ps(es, "pmm%d" % i, [128, 512], F32) for i in range(4)]
            ost = C.ring(es, "ost", 4, [128, 512], F32)
            nbufs = (junk, b_junk, ss, b_ss, xn, b_xn, pT, b_pT)
            oi = 0

            def norm_tile(g, t):
                tile = g * 8 + t
                hT, b_hT = hTs[g % 2]
                xt, b_xt = xts[tile % 2]
                S.dma("sp", lambda e, xt=xt, tile=tile: e.dma_start(out=xt[:], in_=x[tile * 128:(tile + 1) * 128, :]), writes=[b_xt])
                norm_transpose(xt, b_xt, hT, b_hT, t * 128, Am, modT[:, 0:16], [b_Am, b_modT_], nbufs)
            for t in range(8):
                norm_tile(0, t)
            wi = 0
            for g in range(4):
                hT, b_hT = hTs[g % 2]
                nxt = list(range(8)) if g < 3 else []
                for j in range(4):
                    wt, bw = wbs[wi % 2]
                    wi += 1
                    load_w(wst, wt, bw, w_in[:, j * 512:(j + 1) * 512], 16, 512)
                    for f in range(4):
                        fc = j * 4 + f
                        for tt in range(2):
                            pm, bpm = pmm[oi % 4]
                            o_t, bo = ost[oi % 4]
                            oi += 1
                            for dc in range(16):
                                S.op("pe", lambda e, pm=pm, wt=wt, f=f, dc=dc, tt=tt, hT=hT: e.matmul(pm[:], lhsT=wt[:, dc, f * 128:(f + 1) * 128], rhs=hT[:, dc, tt * 512:(tt + 1) * 512], start=(dc == 0), stop=(dc == 15)),
                                     reads=[bw, b_hT], writes=[bpm])
                            cast(o_t[:], pm[:], [bpm], [bo], eng=("act" if oi % 2 else "dve"))
                            S.dma("pool", lambda e, o_t=o_t, fc=fc, g=g, tt=tt: e.dma_start(out=xlglT[fc * 128:(fc + 1) * 128, g * 1024 + tt * 512:g * 1024 + (tt + 1) * 512], in_=o_t[:]),
                                  reads=[bo], writes=[B_xlgl[fc][g]], sembuf=bo)
                    if nxt:
                        norm_tile(g + 1, nxt.pop(0))
                for ct in range(7):
                    ncol = 512 if ct < 6 else PRW - 6 * 512
                    wt, bw = wbs[wi % 2]
                    wi += 1
                    load_w(wst, wt, bw, w_in[:, 2048 + ct * 512:2048 + ct * 512 + ncol], 16, ncol)
                    for t in range(8):
                        tile = g * 8 + t
                        pm, bpm = pmm[oi % 4]
                        o_t, bo = ost[oi % 4]
                        oi += 1
                        for dc in range(16):
                            S.op("pe", lambda e, pm=pm, wt=wt, dc=dc, t=t, ncol=ncol, hT=hT: e.matmul(pm[:, 0:ncol], lhsT=hT[:, dc, t * 128:(t + 1) * 128], rhs=wt[:, dc, 0:ncol], start=(dc == 0), stop=(dc == 15)),
                                 reads=[bw, b_hT], writes=[bpm])
                        cast(o_t[:, 0:ncol], pm[:, 0:ncol], [bpm], [bo], eng=("act" if oi % 2 else "dve"))
                        S.dma("pool", lambda e, o_t=o_t, tile=tile, ct=ct, ncol=ncol: e.dma_start(out=projTok[tile * 128:(tile + 1) * 128, ct * 512:ct * 512 + ncol], in_=o_t[:, 0:ncol]),
                              reads=[bo], writes=[B_proj[tile]], sembuf=bo)
                    if nxt:
                        norm_tile(g + 1, nxt.pop(0))
                while nxt:
                    norm_tile(g + 1, nxt.pop(0))
            S.barrier([b for _, b in xts] + [b for _, b in wst] + [b for _, b in ost])

        with contextlib.ExitStack() as es:
            X, bX = C.sb(es, "X", [128, NT], F32)
            GL, bGL = C.sb(es, "GL", [128, NT], F32)
            XC, bXC = C.sb(es, "XC", [128, NT], F32)
            XCB, bXCB = C.sb(es, "XCB", [128, NT], BF16)
            R1, bR1 = C.sb(es, "R1", [128, NT], F32)
            I1, bI1 = C.sb(es, "I1", [128, NT], F32)
            E1, bE1 = C.sb(es, "E1", [128, NT], F32)
            E2, bE2 = C.sb(es, "E2", [128, NT], F32)
            OB, bOB = C.sb(es, "OB", [128, NT], BF16)
            cc, bcc = C.sb(es, "cc", [128, 8, 5], F32)
            ccm, bccm = C.sb(es, "ccm", [128, 8, 5], F32)
            lbc, blbc = C.sb(es, "lbc", [128, 2, 8, 2], F32)
            lam, blam = C.sb(es, "lam", [128, 2, 8], F32)
            spc, bspc = C.sb(es, "spc", [128, 2, 8, 2], F32)
            tmpc, btmpc = C.sb(es, "tmpc", [128, 16], F32)
            z2, bz2 = C.sb(es, "z2", [128, 16], F32)
            pz, bpz = C.sb(es, "pz", [128, 16], F32)
            h0, bh0 = C.sb(es, "h0", [128, 8, 2], F32)
            fin, bfin = C.sb(es, "fin", [128, 8, 2, 16], F32)
            gst = C.ring(es, "gst", 2, [128, 128], F32)
            gwb = C.ring(es, "gwb", 4, [128, 128], BF16)
            pg = [C.ps(es, "pg%d" % i, [128, 512], F32) for i in range(4)]
            S.dma("sp", lambda e: e.dma_start(out=cc[:], in_=convc), writes=[bcc])
            S.dma("sp", lambda e: e.dma_start(out=lbc[:], in_=lru_bc), writes=[blbc])
            S.dma("sp", lambda e: e.dma_start(out=lam[:], in_=lru_lam), writes=[blam])
            S.dma("sp", lambda e: e.dma_start(out=h0[:], in_=h0lru), writes=[bh0])
            S.op("dve", lambda e: e.tensor_scalar(out=ccm[:], in0=cc[:], scalar1=cm[:, 0:1], scalar2=None, op0=ALU.mult), reads=[bcc, b_cm], writes=[bccm])
            lamf = lam[:].rearrange("p a b -> p (a b)")
            S.op("dve", lambda e: e.tensor_scalar(out=z2[:], in0=lamf, scalar1=-1.0, scalar2=None, op0=ALU.mult), reads=[blam], writes=[bz2])
            S.op("dve", lambda e: e.tensor_tensor(out=tmpc[:], in0=lamf, in1=z2[:], op=ALU.max), reads=[blam, bz2], writes=[btmpc])
            S.op("act", lambda e: e.activation(out=tmpc[:], in_=tmpc[:], func=AF.Exp, scale=-1.0), reads=[btmpc], writes=[btmpc])
            S.op("dve", lambda e: e.tensor_scalar(out=z2[:], in0=tmpc[:], scalar1=2.0, scalar2=None, op0=ALU.add), reads=[btmpc], writes=[bz2])
            S.op("dve", lambda e: e.reciprocal(out=z2[:], in_=z2[:]), reads=[bz2], writes=[bz2])
            S.op("dve", lambda e: e.tensor_tensor(out=tmpc[:], in0=tmpc[:], in1=z2[:], op=ALU.mult), reads=[btmpc, bz2], writes=[btmpc])
            S.op("dve", lambda e: e.tensor_tensor(out=z2[:], in0=tmpc[:], in1=tmpc[:], op=ALU.mult), reads=[btmpc], writes=[bz2])
            S.op("dve", lambda e: e.memset(pz[:], 1.0 / 13.0), writes=[bpz])
            for coef in (1.0 / 11, 1.0 / 9, 1.0 / 7, 1.0 / 5, 1.0 / 3, 1.0):
                S.op("dve", lambda e: e.tensor_tensor(out=pz[:], in0=pz[:], in1=z2[:], op=ALU.mult), reads=[bpz, bz2], writes=[bpz])
                S.op("dve", lambda e, coef=coef: e.tensor_scalar(out=pz[:], in0=pz[:], scalar1=float(coef), scalar2=None, op0=ALU.add), reads=[bpz], writes=[bpz])
            S.op("dve", lambda e: e.tensor_tensor(out=pz[:], in0=pz[:], in1=tmpc[:], op=ALU.mult), reads=[bpz, btmpc], writes=[bpz])
            S.op("dve", lambda e: e.tensor_scalar(out=z2[:], in0=lamf, scalar1=-1.0, scalar2=0.0, op0=ALU.mult, op1=ALU.max), reads=[blam], writes=[bz2])
            S.op("dve", lambda e: e.scalar_tensor_tensor(out=pz[:], in0=pz[:], scalar=2.0, in1=z2[:], op0=ALU.mult, op1=ALU.add), reads=[bpz, bz2], writes=[bpz])
            spf = spc[:].rearrange("p a b c -> p (a b) c")
            S.op("dve", lambda e: e.tensor_scalar(out=spf[:, :, 0], in0=pz[:], scalar1=-8.0, scalar2=None, op0=ALU.mult), reads=[bpz], writes=[bspc])
            S.op("dve", lambda e: e.tensor_scalar(out=spf[:, :, 1], in0=pz[:], scalar1=-16.0, scalar2=None, op0=ALU.mult), reads=[bpz], writes=[bspc])

            def seg(ap):
                return ap.rearrange("p (s t) -> p s t", t=SEG)
            gi = 0
            for h in range(8):
                S.dma("sp", [lambda e, h=h, g=g: e.dma_start(out=X[:, g * 1024:(g + 1) * 1024], in_=xlglT[h * 128:(h + 1) * 128, g * 1024:(g + 1) * 1024]) for g in range(4)],
                      reads=[B_xlgl[h][g] for g in range(4)], writes=[bX])
                S.dma("sp", [lambda e, h=h, g=g: e.dma_start(out=GL[:, g * 1024:(g + 1) * 1024], in_=xlglT[(8 + h) * 128:(9 + h) * 128, g * 1024:(g + 1) * 1024]) for g in range(4)],
                      reads=[B_xlgl[8 + h][g] for g in range(4)], writes=[bGL])
                S.op("act", lambda e, h=h: e.activation(out=XC[:], in_=X[:], func=AF.Identity, scale=cc[:, h, 2:3], bias=cc[:, h, 4:5]), reads=[bX, bcc], writes=[bXC])
                Xs, XCs = seg(X[:]), seg(XC[:])
                for (tap, dlt) in ((0, -2), (1, -1), (3, 1)):
                    if dlt < 0:
                        o_v, i_v = XCs[:, :, -dlt:], Xs[:, :, :SEG + dlt]
                    else:
                        o_v, i_v = XCs[:, :, :SEG - dlt], Xs[:, :, dlt:]
                    S.op("dve", lambda e, h=h, tap=tap, o_v=o_v, i_v=i_v: e.scalar_tensor_tensor(out=o_v, in0=i_v, scalar=cc[:, h, tap:tap + 1], in1=o_v, op0=ALU.mult, op1=ALU.add), reads=[bX, bXC, bcc], writes=[bXC])
                fix = ((1, XCs[:, 1:, 0:1], Xs[:, :15, 255:256]), (0, XCs[:, 1:, 0:1], Xs[:, :15, 254:255]), (0, XCs[:, 1:, 1:2], Xs[:, :15, 255:256]), (3, XCs[:, :15, 255:256], Xs[:, 1:, 0:1]))
                for (tap, o_v, i_v) in fix:
                    S.op("dve", lambda e, h=h, tap=tap, o_v=o_v, i_v=i_v: e.scalar_tensor_tensor(out=o_v, in0=i_v, scalar=ccm[:, h, tap:tap + 1], in1=o_v, op0=ALU.mult, op1=ALU.add), reads=[bX, bXC, bccm], writes=[bXC])
                S.op("act", lambda e: e.copy(out=XCB[:], in_=XC[:]), reads=[bXC], writes=[bXCB])
                HS = []
                for d in range(2):
                    Ebuf, bE = (E1, bE1) if d == 0 else (E2, bE2)
                    for (wsrc, dst, bdst, bi) in ((lru_wr, R1, bR1, 0), (lru_wi, I1, bI1, 1)):
                        gs, bgs = gst[gi % 2]
                        gw, bgw = gwb[gi % 4]
                        gi += 1
                        S.dma("sp", lambda e, gs=gs, wsrc=wsrc, d=d, h=h: e.dma_start(out=gs[:], in_=wsrc[d, h]), writes=[bgs])
                        cast(gw[:], gs[:], [bgs], [bgw], eng="pool")
                        for tt in range(8):
                            pm, bpm = pg[tt % 4]
                            S.op("pe", lambda e, pm=pm, gw=gw, tt=tt: e.matmul(pm[:], lhsT=gw[:], rhs=XCB[:, tt * 512:(tt + 1) * 512], start=True, stop=True), reads=[bgw, bXCB], writes=[bpm])
                            S.op("act", lambda e, pm=pm, dst=dst, tt=tt, d=d, h=h, bi=bi: e.activation(out=dst[:, tt * 512:(tt + 1) * 512], in_=pm[:], func=AF.Sigmoid, bias=lbc[:, d, h, bi:bi + 1]),
                                 reads=[bpm, blbc], writes=[bdst])
                    S.op("act", lambda e, Ebuf=Ebuf, d=d, h=h: e.activation(out=Ebuf[:], in_=R1[:], func=AF.Exp, scale=spc[:, d, h, 1:2]), reads=[bR1, bspc], writes=[bE])
                    S.op("act", lambda e, d=d, h=h: e.activation(out=R1[:], in_=R1[:], func=AF.Exp, scale=spc[:, d, h, 0:1]), reads=[bR1, bspc], writes=[bR1])
                    S.op("act", lambda e, Ebuf=Ebuf: e.activation(out=Ebuf[:], in_=Ebuf[:], func=AF.Identity, scale=-1.0, bias=1.0), reads=[bE], writes=[bE])
                    S.op("act", lambda e, Ebuf=Ebuf: e.activation(out=Ebuf[:], in_=Ebuf[:], func=AF.Sqrt), reads=[bE], writes=[bE])
                    S.op("pool", lambda e: e.tensor_tensor(out=I1[:], in0=I1[:], in1=XC[:], op=ALU.mult), reads=[bI1, bXC], writes=[bI1])
                    S.op("dve", lambda e, Ebuf=Ebuf: e.tensor_tensor(out=I1[:], in0=I1[:], in1=Ebuf[:], op=ALU.mult), reads=[bI1, bE], writes=[bI1])
                    As = seg(R1[:])
                    if d == 0:
                        S.op("dve", lambda e, As=As: e.tensor_scalar(out=As[:, 1:, 0:1], in0=As[:, 1:, 0:1], scalar1=cm[:, 0:1], scalar2=None, op0=ALU.mult), reads=[bR1, b_cm], writes=[bR1])
                        S.op("dve", lambda e, Ebuf=Ebuf, h=h: e.tensor_tensor_scan(out=Ebuf[:], data0=R1[:], data1=I1[:], initial=h0[:, h, 0:1], op0=ALU.mult, op1=ALU.add), reads=[bR1, bI1, bh0], writes=[bE])
                        S.op("pool", lambda e, Ebuf=Ebuf, h=h: e.tensor_copy(out=fin[:, h, 0, :], in_=seg(Ebuf[:])[:, :, 255]), reads=[bE], writes=[bfin])
                    else:
                        S.op("dve", lambda e, As=As: e.tensor_scalar(out=As[:, :15, 255:256], in0=As[:, :15, 255:256], scalar1=cm[:, 0:1], scalar2=None, op0=ALU.mult), reads=[bR1, b_cm], writes=[bR1])
                        S.op("dve", lambda e, Ebuf=Ebuf, h=h: e.tensor_tensor_scan(out=Ebuf[:, ::-1], data0=R1[:, ::-1], data1=I1[:, ::-1], initial=h0[:, h, 1:2], op0=ALU.mult, op1=ALU.add), reads=[bR1, bI1, bh0], writes=[bE])
                        S.op("pool", lambda e, Ebuf=Ebuf, h=h: e.tensor_copy(out=fin[:, h, 1, :], in_=seg(Ebuf[:])[:, :, 0]), reads=[bE], writes=[bfin])
                S.op("pool", lambda e: e.tensor_tensor(out=E1[:], in0=E1[:], in1=E2[:], op=ALU.add), reads=[bE1, bE2], writes=[bE1])
                S.op("act", lambda e: e.activation(out=R1[:], in_=GL[:], func=AF.Square), reads=[bGL], writes=[bR1])
                S.op("act", lambda e: e.activation(out=R1[:], in_=R1[:], func=AF.Identity, scale=0.044715, bias=1.0), reads=[bR1], writes=[bR1])
                S.op("dve", lambda e: e.tensor_tensor(out=R1[:], in0=R1[:], in1=GL[:], op=ALU.mult), reads=[bR1, bGL], writes=[bR1])
                S.op("act", lambda e: e.activation(out=R1[:], in_=R1[:], func=AF.Sigmoid, scale=1.5957691216057308), reads=[bR1], writes=[bR1])
                S.op("pool", lambda e: e.tensor_tensor(out=R1[:], in0=R1[:], in1=GL[:], op=ALU.mult), reads=[bR1, bGL], writes=[bR1])
                S.op("dve", lambda e: e.tensor_tensor(out=OB[:], in0=R1[:], in1=E1[:], op=ALU.mult), reads=[bR1, bE1], writes=[bOB])
                S.dma("pool", lambda e, h=h: e.dma_start(out=mixT[h * 128:(h + 1) * 128, :], in_=OB[:]), reads=[bOB], writes=[B_mixL[h]], sembuf=bOB)
            S.dma("pool", lambda e: e.dma_start(out=lru_fin, in_=fin[:]), reads=[bfin], sembuf=bfin, final=True)
            S.barrier([bX, bGL, bOB, bfin, bcc, blbc, blam, bh0] + [b for _, b in gst])

        with contextlib.ExitStack() as es:
            mu_r, bmu = C.sb(es, "mu_r", [128, PRW], F32)
            cM1, bcM1 = C.sb(es, "cM1", [128, 1680], F32)
            cP1, bcP1 = C.sb(es, "cP1", [128, 2520], F32)
            cU, bcU = C.sb(es, "cU", [128, 840], F32)
            cD, bcD = C.sb(es, "cD", [128, 840], F32)
            rws = {}
            for n in ("w00", "w01", "a00", "a01", "kk", "ka", "rk"):
                rws[n] = C.sb(es, "row_" + n, [128, RW], F32)
            omka, bomka = C.sb(es, "omka", [128, RW], F32)
            rm, brm = C.sb(es, "rm", [128, 4], F32)
            P0s = None
            SM1s = C.ring(es, "SM1", 1, [128, 1680], F32)
            SP1s = C.ring(es, "SP1", 1, [128, 2520], F32)
            SUs = C.ring(es, "SU", 1, [128, 840], F32)
            SDs = C.ring(es, "SD", 1, [128, 840], F32)
            Mx, bMx = C.sb(es, "Mx", [128, PRW], F32)
            lb, blb = C.sb(es, "lb", [128, 288], BF16)
            lT, blT = C.sb(es, "lT", [128, 4, 128], BF16)
            pTl, bpTl = C.ps(es, "pTl", [128, 4, 128], BF16)
            pz_, bpz_ = C.ps(es, "pzz", [128, 1024], F32)
            pa_, bpa_ = C.ps(es, "paa", [128, 1024], F32)
            lw = {}
            lwst, blwst = C.sb(es, "lwst", [128, RW], F32)
            for n in ("wu0", "wu1", "au0", "au1", "gu0", "gu1"):
                lw[n] = C.sb(es, "lw_" + n, [128, RW], BF16)
            outs = {n: C.ring(es, "o" + n, (2 if n in ("KD", "BD", "LD") else 1), [128, RW], F32) for n in ("KK", "KD", "BD", "LD", "G", "BON")}
            AD, bAD = C.sb(es, "AD", [128, RW], F32)
            t16, bt16 = C.sb(es, "t16", [128, 16], F32)
            t16b, bt16b = C.sb(es, "t16b", [128, 16], F32)
            tmpR, btmpR = C.sb(es, "tmpR", [128, RW], F32)

            S.dma("sp", lambda e: e.dma_start(out=mu_r[:], in_=rowap("mu")), writes=[bmu])
            S.dma("sp", lambda e: e.dma_start(out=rm[:], in_=rmask), writes=[brm])
            for (ct, bct, nm, lo, hi) in ((cM1, bcM1, "cmM1", 0, 1680), (cP1, bcP1, "cmP1", 840, 3360), (cU, bcU, "cmU", 1680, 2520), (cD, bcD, "cmD", 2520, 3360)):
                S.dma("sp", lambda e, ct=ct, nm=nm, lo=lo, hi=hi: e.dma_start(out=ct[:], in_=rowap(nm, lo, hi)), writes=[bct])
                S.op("dve", lambda e, ct=ct, lo=lo, hi=hi: e.tensor_tensor(out=ct[:], in0=ct[:], in1=mu_r[:, lo:hi], op=ALU.mult), reads=[bct, bmu], writes=[bct])
            S.op("dve", lambda e: e.tensor_scalar(out=mu_r[:], in0=mu_r[:], scalar1=-1.0, scalar2=1.0, op0=ALU.mult, op1=ALU.add), reads=[bmu, bcM1, bcP1, bcU, bcD], writes=[bmu])
            for n in rws:
                S.dma("sp", lambda e, n=n: e.dma_start(out=rws[n][0][:], in_=rowap(n)), writes=[rws[n][1]])
            S.op("dve", lambda e: e.tensor_scalar(out=omka[:], in0=rws["ka"][0][:], scalar1=-1.0, scalar2=1.0, op0=ALU.mult, op1=ALU.add), reads=[rws["ka"][1]], writes=[bomka])
            for (n, src, p0, p1) in (("wu0", w_up[0], 0, 64), ("wu1", w_up[1], 0, 64), ("au0", a_up[0], 64, 128), ("au1", a_up[1], 64, 128), ("gu0", g_up[0:128, :], 0, 128), ("gu1", g_up[128:160, :], 0, 32)):
                S.dma("sp", lambda e, src=src, p0=p0, p1=p1: e.dma_start(out=lwst[p0:p1, :], in_=src), writes=[blwst])
                S.op("dve", lambda e, n=n, p0=p0, p1=p1: e.tensor_copy(out=lw[n][0][p0:p1, :], in_=lwst[p0:p1, :]), reads=[blwst], writes=[lw[n][1]])

            def h3(ap):
                return ap.rearrange("p (h k) -> p h k", k=64)

            def bc16(ap):
                return ap.unsqueeze(2).broadcast_to([128, 16, 64])
            for tile in range(NTILE):
                par = tile % 2
                t0 = tile * 128
                P0, bP0 = Mx, bMx
                SM1, bSM1 = SM1s[0]
                SP1, bSP1 = SP1s[0]
                SU, bSU = SUs[0]
                SD, bSD = SDs[0]
                S.dma("sp", lambda e, P0=P0, t0=t0: e.dma_start(out=P0[:], in_=projTok[t0:t0 + 128, :]), reads=[B_proj[tile]], writes=[bP0])
                if tile == 0:
                    S.op("pool", lambda e, SM1=SM1: e.memset(SM1[:], 0.0), writes=[bSM1])
                    S.dma("sp", lambda e, SM1=SM1: e.dma_start(out=SM1[1:128, :], in_=projTok[0:127, 0:1680]), reads=[B_proj[0]], writes=[bSM1])
                else:
                    S.dma("sp", lambda e, SM1=SM1, t0=t0: e.dma_start(out=SM1[:], in_=projTok[t0 - 1:t0 + 127, 0:1680]), reads=[B_proj[tile - 1], B_proj[tile]], writes=[bSM1])
                if tile == NTILE - 1:
                    S.op("pool", lambda e, SP1=SP1: e.memset(SP1[:], 0.0), writes=[bSP1])
                    S.dma("sp", lambda e, SP1=SP1, t0=t0: e.dma_start(out=SP1[0:127, :], in_=projTok[t0 + 1:t0 + 128, 840:3360]), reads=[B_proj[tile]], writes=[bSP1])
                else:
                    S.dma("sp", lambda e, SP1=SP1, t0=t0: e.dma_start(out=SP1[:], in_=projTok[t0 + 1:t0 + 129, 840:3360]), reads=[B_proj[tile], B_proj[tile + 1]], writes=[bSP1])
                if tile == 0:
                    S.op("pool", lambda e, SU=SU: e.memset(SU[:], 0.0), writes=[bSU])
                    S.dma("sp", lambda e, SU=SU: e.dma_start(out=SU[64:128, :], in_=projTok[0:64, 1680:2520]), reads=[B_proj[0]], writes=[bSU])
                else:
                    S.dma("sp", lambda e, SU=SU, t0=t0: e.dma_start(out=SU[:], in_=projTok[t0 - 64:t0 + 64, 1680:2520]), reads=[B_proj[tile - 1], B_proj[tile]], writes=[bSU])
                if tile == NTILE - 1:
                    S.op("pool", lambda e, SD=SD: e.memset(SD[:], 0.0), writes=[bSD])
                    S.dma("sp", lambda e, SD=SD, t0=t0: e.dma_start(out=SD[0:64, :], in_=projTok[t0 + 64:t0 + 128, 2520:3360]), reads=[B_proj[tile]], writes=[bSD])
                else:
                    S.dma("sp", lambda e, SD=SD, t0=t0: e.dma_start(out=SD[:], in_=projTok[t0 + 64:t0 + 192, 2520:3360]), reads=[B_proj[tile], B_proj[tile + 1]], writes=[bSD])
                S.op("dve", lambda e: e.tensor_tensor(out=Mx[:], in0=Mx[:], in1=mu_r[:], op=ALU.mult), reads=[bMx, bmu], writes=[bMx])
                S.op("dve", lambda e, SM1=SM1, par=par: e.scalar_tensor_tensor(out=SM1[:], in0=SM1[:], scalar=rm[:, par:par + 1], in1=cM1[:], op0=ALU.mult, op1=ALU.mult), reads=[bSM1, brm, bcM1], writes=[bSM1])
                S.op("dve", lambda e, SM1=SM1: e.tensor_tensor(out=Mx[:, 0:1680], in0=Mx[:, 0:1680], in1=SM1[:], op=ALU.add), reads=[bSM1, bMx], writes=[bMx])
                S.op("dve", lambda e, SP1=SP1, par=par: e.scalar_tensor_tensor(out=SP1[:], in0=SP1[:], scalar=rm[:, 2 + par:3 + par], in1=cP1[:], op0=ALU.mult, op1=ALU.mult), reads=[bSP1, brm, bcP1], writes=[bSP1])
                S.op("dve", lambda e, SP1=SP1: e.tensor_tensor(out=Mx[:, 840:3360], in0=Mx[:, 840:3360], in1=SP1[:], op=ALU.add), reads=[bSP1, bMx], writes=[bMx])
                S.op("pool", lambda e, SU=SU: e.tensor_tensor(out=SU[:], in0=SU[:], in1=cU[:], op=ALU.mult), reads=[bSU, bcU], writes=[bSU])
                S.op("dve", lambda e, SU=SU: e.tensor_tensor(out=Mx[:, 1680:2520], in0=Mx[:, 1680:2520], in1=SU[:], op=ALU.add), reads=[bSU, bMx], writes=[bMx])
                S.op("pool", lambda e, SD=SD: e.tensor_tensor(out=SD[:], in0=SD[:], in1=cD[:], op=ALU.mult), reads=[bSD, bcD], writes=[bSD])
                S.op("dve", lambda e, SD=SD: e.tensor_tensor(out=Mx[:, 2520:3360], in0=Mx[:, 2520:3360], in1=SD[:], op=ALU.add), reads=[bSD, bMx], writes=[bMx])
                r_ap, k_ap, v_ap = Mx[:, 0:1024], Mx[:, 1024:2048], Mx[:, 2048:3072]
                S.dma("pool", lambda e, t0=t0: e.dma_start(out=prep["R"][t0:t0 + 128, :], in_=Mx[:, 0:1024]), reads=[bMx], writes=[B_prep["R"][tile]], sembuf=bMx)
                S.dma("pool", lambda e, t0=t0: e.dma_start(out=prep["V"][t0:t0 + 128, :], in_=Mx[:, 2048:3072]), reads=[bMx], writes=[B_prep["V"][tile]], sembuf=bMx)
                S.op("act", lambda e: e.activation(out=lb[:, 0:64], in_=Mx[:, 3072:3136], func=AF.Tanh), reads=[bMx], writes=[blb])
                S.op("act", lambda e: e.copy(out=lb[:, 64:128], in_=Mx[:, 3136:3200]), reads=[bMx], writes=[blb])
                S.op("act", lambda e: e.activation(out=lb[:, 128:288], in_=Mx[:, 3200:3360], func=AF.Sigmoid), reads=[bMx], writes=[blb])
                S.op("pe", lambda e: e.transpose(out=pTl[:, 0, :], in_=lb[:, 0:128], identity=identb[:]), reads=[blb, b_identb], writes=[bpTl])
                S.op("pe", lambda e: e.transpose(out=pTl[:, 1, :], in_=lb[:, 128:256], identity=identb[:]), reads=[blb, b_identb], writes=[bpTl])
                S.op("pe", lambda e: e.transpose(out=pTl[0:32, 2, :], in_=lb[:, 256:288], identity=identb[:]), reads=[blb, b_identb], writes=[bpTl])
                S.op("dve", lambda e: e.tensor_copy(out=lT[:, 0:2, :], in_=pTl[:, 0:2, :]), reads=[bpTl], writes=[blT])
                S.op("dve", lambda e: e.tensor_copy(out=lT[0:32, 2, :], in_=pTl[0:32, 2, :]), reads=[bpTl], writes=[blT])
                oG, boG = outs["G"][0]
                for hh in range(2):
                    S.op("pe", lambda e, hh=hh: e.matmul(pz_[:, hh * 512:(hh + 1) * 512], lhsT=lT[:, 1, :], rhs=lw["gu0"][0][:, hh * 512:(hh + 1) * 512], start=True, stop=False), reads=[blT, lw["gu0"][1]], writes=[bpz_])
                    S.op("pe", lambda e, hh=hh: e.matmul(pz_[:, hh * 512:(hh + 1) * 512], lhsT=lT[0:32, 2, :], rhs=lw["gu1"][0][0:32, hh * 512:(hh + 1) * 512], start=False, stop=True), reads=[blT, lw["gu1"][1]], writes=[bpz_])
                S.op("act", lambda e, oG=oG: e.copy(out=oG[:], in_=pz_[:]), reads=[bpz_], writes=[boG])
                S.dma("pool", lambda e, oG=oG, t0=t0: e.dma_start(out=prep["G"][t0:t0 + 128, :], in_=oG[:]), reads=[boG], writes=[B_prep["G"][tile]], sembuf=boG)
                oKK, boKK = outs["KK"][0]
                S.op("dve", lambda e, oKK=oKK: e.tensor_tensor(out=oKK[:], in0=k_ap, in1=rws["kk"][0][:], op=ALU.mult), reads=[bMx, rws["kk"][1]], writes=[boKK])
                S.op("act", lambda e, oKK=oKK: e.activation(out=tmpR[:], in_=oKK[:], func=AF.Square), reads=[boKK], writes=[btmpR])
                S.op("dve", lambda e: e.tensor_reduce(out=t16[:], in_=h3(tmpR[:]), axis=AX.X, op=ALU.add), reads=[btmpR], writes=[bt16])
                S.op("dve", lambda e: e.tensor_scalar(out=t16[:], in0=t16[:], scalar1=1e-24, scalar2=None, op0=ALU.max), reads=[bt16], writes=[bt16])
                S.op("act", lambda e: e.activation(out=t16[:], in_=t16[:], func=AF.Sqrt), reads=[bt16], writes=[bt16])
                S.op("dve", lambda e: e.reciprocal(out=t16[:], in_=t16[:]), reads=[bt16], writes=[bt16])
                S.op("dve", lambda e, oKK=oKK: e.tensor_tensor(out=h3(oKK[:]), in0=h3(oKK[:]), in1=bc16(t16[:]), op=ALU.mult), reads=[boKK, bt16], writes=[boKK])
                S.dma("pool", lambda e, oKK=oKK, t0=t0: e.dma_start(out=prep["KK"][t0:t0 + 128, :], in_=oKK[:]), reads=[boKK], writes=[B_prep["KK"][tile]], sembuf=boKK)
                oB, boB = outs["BON"][0]
                S.op("pool", lambda e: e.tensor_tensor(out=tmpR[:], in0=r_ap, in1=k_ap, op=ALU.mult), reads=[bMx], writes=[btmpR])
                S.op("dve", lambda e: e.tensor_tensor(out=tmpR[:], in0=tmpR[:], in1=rws["rk"][0][:], op=ALU.mult), reads=[btmpR, rws["rk"][1]], writes=[btmpR])
                S.op("dve", lambda e: e.tensor_reduce(out=t16b[:], in_=h3(tmpR[:]), axis=AX.X, op=ALU.add), reads=[btmpR], writes=[bt16b])
                S.op("dve", lambda e, oB=oB: e.tensor_tensor(out=h3(oB[:]), in0=h3(v_ap), in1=bc16(t16b[:]), op=ALU.mult), reads=[bMx, bt16b], writes=[boB])
                S.dma("pool", lambda e, oB=oB, t0=t0: e.dma_start(out=prep["BON"][t0:t0 + 128, :], in_=oB[:]), reads=[boB], writes=[B_prep["BON"][tile]], sembuf=boB)
                for d in range(2):
                    ds_ = str(d)
                    for hh in range(2):
                        S.op("pe", lambda e, hh=hh, ds_=ds_: e.matmul(pz_[:, hh * 512:(hh + 1) * 512], lhsT=lT[0:64, 0, :], rhs=lw["wu" + ds_][0][0:64, hh * 512:(hh + 1) * 512], start=True, stop=True), reads=[blT, lw["wu" + ds_][1]], writes=[bpz_])
                        S.op("pe", lambda e, hh=hh, ds_=ds_: e.matmul(pa_[:, hh * 512:(hh + 1) * 512], lhsT=lT[64:128, 0, :], rhs=lw["au" + ds_][0][64:128, hh * 512:(hh + 1) * 512], start=True, stop=True), reads=[blT, lw["au" + ds_][1]], writes=[bpa_])
                    oLD, boLD = outs["LD"][d]
                    S.op("dve", lambda e, oLD=oLD, ds_=ds_: e.tensor_tensor(out=oLD[:], in0=pz_[:], in1=rws["w0" + ds_][0][:], op=ALU.add), reads=[bpz_, rws["w0" + ds_][1]], writes=[boLD])
                    S.op("act", lambda e, oLD=oLD: e.activation(out=oLD[:], in_=oLD[:], func=AF.Sigmoid), reads=[boLD], writes=[boLD])
                    S.op("act", lambda e, oLD=oLD: e.activation(out=oLD[:], in_=oLD[:], func=AF.Copy, scale=-0.6065306597126334), reads=[boLD], writes=[boLD])
                    S.dma("pool", lambda e, oLD=oLD, t0=t0, ds_=ds_: e.dma_start(out=prep["LD" + ds_][t0:t0 + 128, :], in_=oLD[:]), reads=[boLD], writes=[B_prep["LD" + ds_][tile]], sembuf=boLD)
                    S.op("dve", lambda e, ds_=ds_: e.tensor_tensor(out=AD[:], in0=pa_[:], in1=rws["a0" + ds_][0][:], op=ALU.add), reads=[bpa_, rws["a0" + ds_][1]], writes=[bAD])
                    S.op("act", lambda e: e.activation(out=AD[:], in_=AD[:], func=AF.Sigmoid), reads=[bAD], writes=[bAD])
                    oBD, boBD = outs["BD"][d]
                    S.op("pool", lambda e, oBD=oBD, oKK=oKK: e.tensor_tensor(out=oBD[:], in0=oKK[:], in1=AD[:], op=ALU.mult), reads=[boKK, bAD], writes=[boBD])
                    S.dma("pool", lambda e, oBD=oBD, t0=t0, ds_=ds_: e.dma_start(out=prep["BD" + ds_][t0:t0 + 128, :], in_=oBD[:]), reads=[boBD], writes=[B_prep["BD" + ds_][tile]], sembuf=boBD)
                    oKD, boKD = outs["KD"][d]
                    S.op("dve", lambda e: e.tensor_tensor(out=tmpR[:], in0=AD[:], in1=rws["ka"][0][:], op=ALU.mult), reads=[bAD, rws["ka"][1]], writes=[btmpR])
                    S.op("pool", lambda e: e.tensor_tensor(out=tmpR[:], in0=tmpR[:], in1=omka[:], op=ALU.add), reads=[btmpR, bomka], writes=[btmpR])
                    S.op("dve", lambda e, oKD=oKD: e.tensor_tensor(out=oKD[:], in0=tmpR[:], in1=k_ap, op=ALU.mult), reads=[btmpR, bMx], writes=[boKD])
                    S.dma("pool", lambda e, oKD=oKD, t0=t0, ds_=ds_: e.dma_start(out=prep["KD" + ds_][t0:t0 + 128, :], in_=oKD[:]), reads=[boKD], writes=[B_prep["KD" + ds_][tile]], sembuf=boKD)
            allb = [bMx, blwst] + [b for r_ in (SM1s, SP1s, SUs, SDs) for _, b in r_] + [b for n in outs for _, b in outs[n]] + [rws[n][1] for n in rws] + [bmu, brm, bcM1, bcP1, bcU, bcD]
            S.barrier(allb)

        with contextlib.ExitStack() as es:
            names = ("R", "KK", "V", "KD", "BD", "LD")
            IN = {n: C.ring(es, "in" + n, 2, [128, RW], F32) for n in names}
            CL, bCL = C.sb(es, "CL", [128, RW], F32)
            TOT, bTOT = C.sb(es, "TOT", [128, RW], F32)
            EX, bEX = C.sb(es, "EX", [128, RW], F32)
            Ee, bEe = C.sb(es, "Ee", [128, RW], F32)
            SCb = {n: C.sb(es, "sc" + n, [128, RW], BF16) for n in ("kap", "rt", "bet", "kt", "khat", "bhat", "V")}
            pcl, bpcl = C.ps(es, "pcl", [128, 1024], F32)
            ptr, bptr = C.ps(es, "ptr", [128, 4, 512], BF16)
            pgr, bpgr = C.ps(es, "pgr", [128, 1024], F32)
            ptt, bptt = C.ps(es, "ptt", [128, 512], F32)
            pch, bpch = C.ps(es, "pch", [128, 512], F32)
            FT, bFT = C.sb(es, "FT", [64, 4, 512], BF16)
            GM, bGM = C.sb(es, "GM", [128, 4, 512], BF16)
            QQ, bQQ = C.sb(es, "QQ", [128, 4, 256], BF16)
            TTb, bTTb = C.sb(es, "TTb", [128, 4, 128], BF16)
            Xb, bXb = C.sb(es, "Xb", [128, 4, 64], BF16)
            Ub, bUb = C.sb(es, "Ub", [128, 4, 64], BF16)
            U0b, bU0b = C.sb(es, "U0b", [128, 4, 64], BF16)
            X32, bX32 = C.sb(es, "X32", [128, 4, 64], F32)
            U032, bU032 = C.sb(es, "U032", [128, 4, 64], F32)
            Yacc = C.ring(es, "Yacc", 2, [128, RW], F32)
            Ast, bAst = C.sb(es, "Ast", [64, 2, 16, 64], F32)
            Abf, bAbf = C.sb(es, "Abf", [64, 2, 16, 64], BF16)
            PCc, bPCc = C.sb(es, "PCc", [64, 16, 2], F32)
            S.dma("sp", lambda e: e.dma_start(out=Ast[:], in_=s0wkv), writes=[bAst])
            S.op("dve", lambda e: e.tensor_copy(out=Abf[:], in_=Ast[:]), reads=[bAst], writes=[bAbf])
            for step in range(NTILE):
                if step > 0 and step % 8 == 0:
                    S.barrier()
                for d in range(2):
                    c = step if d == 0 else NTILE - 1 - step
                    t0 = c * 128
                    slot = (step * 2 + d) % 2
                    cur = {}
                    for n in names:
                        tl, btl = IN[n][slot]
                        key = n + str(d) if n in ("KD", "BD", "LD") else n
                        S.dma("sp", lambda e, tl=tl, key=key, t0=t0: e.dma_start(out=tl[:], in_=prep[key][t0:t0 + 128, :]), reads=[B_prep[key][c]], writes=[btl])
                        cur[n] = (tl, btl)
                    LD, bLD = cur["LD"]
                    for hh in range(2):
                        S.op("pe", lambda e, hh=hh, d=d, LD=LD: e.matmul(pcl[:, hh * 512:(hh + 1) * 512], lhsT=tri[:, d, :], rhs=LD[:, hh * 512:(hh + 1) * 512], start=True, stop=True), reads=[b_tri, bLD], writes=[bpcl])
                    S.op("act", lambda e: e.copy(out=CL[:], in_=pcl[:]), reads=[bpcl], writes=[bCL])
                    for hh in range(2):
                        S.op("pe", lambda e, hh=hh, LD=LD: e.matmul(pcl[:, hh * 512:(hh + 1) * 512], lhsT=ones[:], rhs=LD[:, hh * 512:(hh + 1) * 512], start=True, stop=True), reads=[b_ones, bLD], writes=[bpcl])
                    S.op("dve", lambda e: e.tensor_tensor(out=TOT[:], in0=pcl[:], in1=CL[:], op=ALU.subtract), reads=[bpcl, bCL], writes=[bTOT])
                    S.op("pool", lambda e, LD=LD: e.tensor_tensor(out=EX[:], in0=CL[:], in1=LD[:], op=ALU.subtract), reads=[bCL, bLD], writes=[bEX])
                    for h in range(16):
                        S.op("pe", lambda e, h=h, LD=LD: e.matmul(pch[0:64, h * 2:h * 2 + 2], lhsT=LD[:, h * 64:(h + 1) * 64], rhs=ones[:, 0:2], start=True, stop=True), reads=[bLD, b_ones], writes=[bpch])
                    S.op("act", lambda e: e.activation(out=PCc[:].rearrange("k h t -> k (h t)"), in_=pch[0:64, 0:32], func=AF.Exp), reads=[bpch], writes=[bPCc])
                    R_, bR_ = cur["R"]
                    KK_, bKK_ = cur["KK"]
                    V_, bV_ = cur["V"]
                    KD_, bKD_ = cur["KD"]
                    BD_, bBD_ = cur["BD"]
                    S.op("act", lambda e: e.activation(out=Ee[:], in_=EX[:], func=AF.Exp), reads=[bEX], writes=[bEe])
                    S.op("dve", lambda e, KK_=KK_: e.tensor_tensor(out=SCb["kap"][0][:], in0=KK_[:], in1=Ee[:], op=ALU.mult), reads=[bKK_, bEe], writes=[SCb["kap"][1]])
                    S.op("act", lambda e: e.activation(out=Ee[:], in_=CL[:], func=AF.Exp), reads=[bCL, SCb["kap"][1]], writes=[bEe])
                    S.op("pool", lambda e, R_=R_: e.tensor_tensor(out=SCb["rt"][0][:], in0=R_[:], in1=Ee[:], op=ALU.mult), reads=[bR_, bEe], writes=[SCb["rt"][1]])
                    S.op("act", lambda e: e.activation(out=EX[:], in_=CL[:], func=AF.Exp, scale=-1.0), reads=[bCL, SCb["kap"][1]], writes=[bEX])
                    S.op("dve", lambda e, BD_=BD_: e.tensor_tensor(out=SCb["bet"][0][:], in0=BD_[:], in1=EX[:], op=ALU.mult), reads=[bBD_, bEX], writes=[SCb["bet"][1]])
                    S.op("pool", lambda e, KD_=KD_: e.tensor_tensor(out=SCb["kt"][0][:], in0=KD_[:], in1=EX[:], op=ALU.mult), reads=[bKD_, bEX], writes=[SCb["kt"][1]])
                    S.op("act", lambda e: e.activation(out=TOT[:], in_=TOT[:], func=AF.Exp), reads=[bTOT], writes=[bTOT])
                    S.op("dve", lambda e, KD_=KD_: e.tensor_tensor(out=SCb["khat"][0][:], in0=KD_[:], in1=TOT[:], op=ALU.mult), reads=[bKD_, bTOT], writes=[SCb["khat"][1]])
                    S.op("pool", lambda e, BD_=BD_: e.tensor_tensor(out=SCb["bhat"][0][:], in0=BD_[:], in1=TOT[:], op=ALU.mult), reads=[bBD_, bTOT], writes=[SCb["bhat"][1]])
                    S.op("act", lambda e, V_=V_: e.copy(out=SCb["V"][0][:], in_=V_[:]), reads=[bV_], writes=[SCb["V"][1]])
                    Ya, bYa = Yacc[slot]
                    for hg in range(4):
                        for j in range(4):
                            h = hg * 4 + j
                            for qi, n in enumerate(("kap", "rt", "bet", "kt")):
                                S.op("pe", lambda e, j=j, h=h, qi=qi, n=n: e.transpose(out=ptr[0:64, j, qi * 128:(qi + 1) * 128], in_=SCb[n][0][:, h * 64:(h + 1) * 64], identity=identb[:]),
                                     reads=[SCb[n][1], b_identb], writes=[bptr])
                        S.op("act", lambda e: e.copy(out=FT[:], in_=ptr[0:64, :, :]), reads=[bptr], writes=[bFT])
                        for j in range(4):
                            S.op("pe", lambda e, j=j: e.matmul(pgr[:, j * 256:(j + 1) * 256], lhsT=FT[:, j, 256:384], rhs=FT[:, j, 0:256], start=True, stop=True), reads=[bFT], writes=[bpgr])
                        gm4 = gmask[:, d, 0:256].unsqueeze(1).broadcast_to([128, 4, 256])
                        S.op("dve", lambda e, gm4=gm4: e.tensor_tensor(out=GM[:, :, 0:256], in0=pgr[:].rearrange("p (j w) -> p j w", w=256), in1=gm4, op=ALU.mult), reads=[bpgr, b_gmask], writes=[bGM])
                        for j in range(4):
                            S.op("pe", lambda e, j=j: e.matmul(pcl[:, j * 256:(j + 1) * 256], lhsT=FT[:, j, 384:512], rhs=FT[:, j, 0:256], start=True, stop=True), reads=[bFT], writes=[bpcl])
                        gm4b = gmask[:, d, 256:512].unsqueeze(1).broadcast_to([128, 4, 256])
                        S.op("dve", lambda e, gm4b=gm4b: e.tensor_tensor(out=GM[:, :, 256:512], in0=pcl[:].rearrange("p (j w) -> p j w", w=256), in1=gm4b, op=ALU.mult), reads=[bpcl, b_gmask], writes=[bGM])
                        S.op("act", lambda e: e.copy(out=QQ[:, :, 0:128], in_=GM[:, :, 0:128]), reads=[bGM], writes=[bQQ])
                        ptb = ptr[:, :, 0:128]
                        for j in range(4):
                            S.op("pe", lambda e, j=j: e.transpose(out=ptr[:, j, 0:128], in_=GM[:, j, 0:128], identity=identb[:]), reads=[bGM, b_identb, bFT], writes=[bptr])
                        S.op("act", lambda e: e.copy(out=QQ[:, :, 128:256], in_=ptr[:, :, 0:128]), reads=[bptr], writes=[bQQ])
                        idb4 = identb[:].unsqueeze(1).broadcast_to([128, 4, 128])
                        S.op("dve", lambda e, idb4=idb4: e.tensor_tensor(out=TTb[:], in0=GM[:, :, 0:128], in1=idb4, op=ALU.add), reads=[bGM, b_identb], writes=[bTTb])
                        for lv in range(6):
                            last = (lv == 5)
                            for j in range(4):
                                if not last:
                                    S.op("pe", lambda e, j=j: e.matmul(pgr[:, j * 256:j * 256 + 128], lhsT=QQ[:, j, 128:256], rhs=QQ[:, j, 0:128], start=True, stop=True), reads=[bQQ], writes=[bpgr])
                                S.op("pe", lambda e, j=j: e.matmul(pgr[:, j * 256 + 128:(j + 1) * 256], lhsT=QQ[:, j, 0:128], rhs=QQ[:, j, 128:256], start=True, stop=True), reads=[bQQ], writes=[bpgr])
                            S.op("act", lambda e: e.copy(out=QQ[:], in_=pgr[:].rearrange("p (j w) -> p j w", w=256)), reads=[bpgr], writes=[bQQ])
                            for j in range(4):
                                S.op("pe", lambda e, j=j: e.matmul(ptt[:, j * 128:(j + 1) * 128], lhsT=QQ[:, j, 128:256], rhs=TTb[:, j, :], start=True, stop=True), reads=[bTTb, bQQ], writes=[bptt])
                            S.op("dve", lambda e: e.tensor_tensor(out=TTb[:], in0=ptt[:].rearrange("p (j w) -> p j w", w=128), in1=TTb[:], op=ALU.add), reads=[bptt, bTTb], writes=[bTTb])
                        Vb = SCb["V"][0]
                        for j in range(4):
                            h = hg * 4 + j
                            S.op("pe", lambda e, j=j, h=h, d=d: e.matmul(pch[:, j * 64:(j + 1) * 64], lhsT=FT[:, j, 0:128], rhs=Abf[:, d, h, :], start=True, stop=False), reads=[bFT, bAbf], writes=[bpch])
                            S.op("pe", lambda e, j=j, h=h: e.matmul(pch[:, j * 64:(j + 1) * 64], lhsT=GM[:, j, 256:384], rhs=Vb[:, h * 64:(h + 1) * 64], start=False, stop=True), reads=[bGM, SCb["V"][1]], writes=[bpch])
                        if REFINE:
                            pX = pch[:, 0:256].rearrange("p (j w) -> p j w", w=64)
                            pU = pch[:, 256:512].rearrange("p (j w) -> p j w", w=64)
                            S.op("act", lambda e: e.copy(out=Xb[:], in_=pX), reads=[bpch], writes=[bXb])
                            S.op("dve", lambda e: e.tensor_copy(out=X32[:], in_=pX), reads=[bpch, bXb], writes=[bX32])
                            for j in range(4):
                                S.op("pe", lambda e, j=j: e.matmul(pch[:, 256 + j * 64:256 + (j + 1) * 64], lhsT=TTb[:, j, :], rhs=Xb[:, j, :], start=True, stop=True), reads=[bTTb, bXb], writes=[bpch])
                            S.op("act", lambda e: e.copy(out=U0b[:], in_=pU), reads=[bpch], writes=[bU0b])
                            S.op("dve", lambda e: e.tensor_copy(out=U032[:], in_=pU), reads=[bpch, bU0b], writes=[bU032])
                            S.op("dve", lambda e: e.tensor_tensor(out=X32[:], in0=X32[:], in1=U032[:], op=ALU.subtract), reads=[bX32, bU032], writes=[bX32])
                            for j in range(4):
                                S.op("pe", lambda e, j=j: e.matmul(pch[:, j * 64:(j + 1) * 64], lhsT=GM[:, j, 0:128], rhs=U0b[:, j, :], start=True, stop=True), reads=[bGM, bU0b, bX32], writes=[bpch])
                            S.op("dve", lambda e: e.tensor_tensor(out=Xb[:], in0=pX, in1=X32[:], op=ALU.add), reads=[bpch, bX32], writes=[bXb])
                            for j in range(4):
                                S.op("pe", lambda e, j=j: e.matmul(pch[:, 256 + j * 64:256 + (j + 1) * 64], lhsT=TTb[:, j, :], rhs=Xb[:, j, :], start=True, stop=True), reads=[bTTb, bXb], writes=[bpch])
                            S.op("dve", lambda e: e.scalar_tensor_tensor(out=Ub[:], in0=pU, scalar=-1.0, in1=U032[:], op0=ALU.mult, op1=ALU.subtract), reads=[bpch, bU032], writes=[bUb])
                        else:
                            S.op("act", lambda e: e.copy(out=Xb[:], in_=pch[:, 0:256].rearrange("p (j w) -> p j w", w=64)), reads=[bpch], writes=[bXb])
                            for j in range(4):
                                S.op("pe", lambda e, j=j: e.matmul(pch[:, 256 + j * 64:256 + (j + 1) * 64], lhsT=TTb[:, j, :], rhs=Xb[:, j, :], start=True, stop=True), reads=[bTTb, bXb], writes=[bpch])
                            S.op("dve", lambda e: e.tensor_scalar(out=Ub[:], in0=pch[:, 256:512].rearrange("p (j w) -> p j w", w=64), scalar1=-1.0, scalar2=None, op0=ALU.mult), reads=[bpch], writes=[bUb])
                        for j in range(4):
                            h = hg * 4 + j
                            S.op("pe", lambda e, j=j, h=h, d=d: e.matmul(pch[:, j * 64:(j + 1) * 64], lhsT=FT[:, j, 128:256], rhs=Abf[:, d, h, :], start=True, stop=False), reads=[bFT, bAbf, bXb], writes=[bpch])
                            S.op("pe", lambda e, j=j, h=h: e.matmul(pch[:, j * 64:(j + 1) * 64], lhsT=GM[:, j, 384:512], rhs=Vb[:, h * 64:(h + 1) * 64], start=False, stop=False), reads=[bGM, SCb["V"][1]], writes=[bpch])
                            S.op("pe", lambda e, j=j: e.matmul(pch[:, j * 64:(j + 1) * 64], lhsT=GM[:, j, 128:256], rhs=Ub[:, j, :], start=False, stop=True), reads=[bGM, bUb], writes=[bpch])
                        S.op("act", lambda e, Ya=Ya, hg=hg: e.copy(out=Ya[:, hg * 256:(hg + 1) * 256], in_=pch[:, 0:256]), reads=[bpch], writes=[bYa])
                        for j in range(4):
                            h = hg * 4 + j
                            S.op("pe", lambda e, j=j, h=h: e.matmul(pch[0:64, 256 + j * 64:256 + (j + 1) * 64], lhsT=SCb["khat"][0][:, h * 64:(h + 1) * 64], rhs=Vb[:, h * 64:(h + 1) * 64], start=True, stop=False), reads=[SCb["khat"][1], SCb["V"][1], bUb], writes=[bpch])
                            S.op("pe", lambda e, j=j, h=h: e.matmul(pch[0:64, 256 + j * 64:256 + (j + 1) * 64], lhsT=SCb["bhat"][0][:, h * 64:(h + 1) * 64], rhs=Ub[:, j, :], start=False, stop=True), reads=[SCb["bhat"][1], bUb], writes=[bpch])
                        for j in range(4):
                            h = hg * 4 + j
                            S.op("dve", lambda e, j=j, h=h, d=d: e.scalar_tensor_tensor(out=Ast[:, d, h, :], in0=Ast[:, d, h, :], scalar=PCc[:, h, 0:1], in1=pch[0:64, 256 + j * 64:256 + (j + 1) * 64], op0=ALU.mult, op1=ALU.add),
                                 reads=[bAst, bPCc, bpch], writes=[bAst])
                        S.op("act", lambda e, hg=hg, d=d: e.copy(out=Abf[:, d, hg * 4:(hg + 1) * 4, :], in_=Ast[:, d, hg * 4:(hg + 1) * 4, :]), reads=[bAst], writes=[bAbf])
                    S.dma("pool", lambda e, Ya=Ya, t0=t0, d=d: e.dma_start(out=YD[d][t0:t0 + 128, :], in_=Ya[:]), reads=[bYa], writes=[B_Y[d][c]], sembuf=bYa)
                    seg_end = (d == 0 and c % 2 == 1) or (d == 1 and c % 2 == 0)
                    if seg_end:
                        sidx = c // 2
                        S.dma("pool", lambda e, sidx=sidx, d=d: e.dma_start(out=wkv_fin[sidx, d], in_=Ast[:, d, :, :]), reads=[bAst], sembuf=bAst, final=True)
                        S.op("dve", lambda e, d=d: e.tensor_scalar(out=Ast[:, d, :, :], in0=Ast[:, d, :, :], scalar1=cm[0:64, 0:1], scalar2=None, op0=ALU.mult), reads=[bAst, b_cm], writes=[bAst])
                        S.op("pool", lambda e, d=d: e.tensor_copy(out=Abf[:, d, :, :], in_=Ast[:, d, :, :]), reads=[bAst], writes=[bAbf])
            S.barrier([bAst] + [b for n in names for _, b in IN[n]] + [b for _, b in Yacc])

        with contextlib.ExitStack() as es:
            lnw, blnw = C.sb(es, "lnw", [128, RW], F32)
            lnb, blnb = C.sb(es, "lnb", [128, RW], F32)
            S.dma("sp", lambda e: e.dma_start(out=lnw[:], in_=rowap("lnw")), writes=[blnw])
            S.dma("sp", lambda e: e.dma_start(out=lnb[:], in_=rowap("lnb")), writes=[blnb])
            rg = {n: C.ring(es, "e" + n, 2, [128, RW], F32) for n in ("YF", "YB", "BON", "G")}
            Ysq, bYsq = C.sb(es, "Ysq", [128, RW], F32)
            m16, bm16 = C.sb(es, "m16", [128, 16], F32)
            v16, bv16 = C.sb(es, "v16", [128, 16], F32)
            Ob, bOb = C.sb(es, "Ob", [128, RW], BF16)
            pTe, bpTe = C.ps(es, "pTe", [128, 8, 128], BF16)
            OTs = C.ring(es, "OT", 2, [128, 8, 128], BF16)

            def h3(ap):
                return ap.rearrange("p (h k) -> p h k", k=64)

            def bc16(ap):
                return ap.unsqueeze(2).broadcast_to([128, 16, 64])
            for tile in range(NTILE):
                t0 = tile * 128
                par = tile % 2
                YF_, bYF_ = rg["YF"][par]
                YB_, bYB_ = rg["YB"][par]
                BN_, bBN_ = rg["BON"][par]
                G_, bG_ = rg["G"][par]
                S.dma("sp", lambda e, YF_=YF_, t0=t0: e.dma_start(out=YF_[:], in_=YD[0][t0:t0 + 128, :]), reads=[B_Y[0][tile]], writes=[bYF_])
                S.dma("sp", lambda e, YB_=YB_, t0=t0: e.dma_start(out=YB_[:], in_=YD[1][t0:t0 + 128, :]), reads=[B_Y[1][tile]], writes=[bYB_])
                S.dma("sp", lambda e, BN_=BN_, t0=t0: e.dma_start(out=BN_[:], in_=prep["BON"][t0:t0 + 128, :]), reads=[B_prep["BON"][tile]], writes=[bBN_])
                S.dma("sp", lambda e, G_=G_, t0=t0: e.dma_start(out=G_[:], in_=prep["G"][t0:t0 + 128, :]), reads=[B_prep["G"][tile]], writes=[bG_])
                S.op("dve", lambda e, YF_=YF_, YB_=YB_: e.tensor_tensor(out=YF_[:], in0=YF_[:], in1=YB_[:], op=ALU.add), reads=[bYF_, bYB_], writes=[bYF_])
                S.op("pool", lambda e, YF_=YF_, BN_=BN_: e.tensor_tensor(out=YF_[:], in0=YF_[:], in1=BN_[:], op=ALU.add), reads=[bYF_, bBN_], writes=[bYF_])
                S.op("dve", lambda e, YF_=YF_: e.tensor_reduce(out=m16[:], in_=h3(YF_[:]), axis=AX.X, op=ALU.add), reads=[bYF_], writes=[bm16])
                S.op("dve", lambda e: e.tensor_scalar(out=m16[:], in0=m16[:], scalar1=-1.0 / 64, scalar2=None, op0=ALU.mult), reads=[bm16], writes=[bm16])
                S.op("dve", lambda e, YF_=YF_: e.tensor_tensor(out=h3(YF_[:]), in0=h3(YF_[:]), in1=bc16(m16[:]), op=ALU.add), reads=[bYF_, bm16], writes=[bYF_])
                S.op("act", lambda e, YF_=YF_: e.activation(out=Ysq[:], in_=YF_[:], func=AF.Square), reads=[bYF_], writes=[bYsq])
                S.op("dve", lambda e: e.tensor_reduce(out=v16[:], in_=h3(Ysq[:]), axis=AX.X, op=ALU.add), reads=[bYsq], writes=[bv16])
                rstd_from_ss(v16[:], bv16, 64, 64e-5, None)
                S.op("dve", lambda e, YF_=YF_: e.tensor_tensor(out=h3(YF_[:]), in0=h3(YF_[:]), in1=bc16(v16[:]), op=ALU.mult), reads=[bYF_, bv16], writes=[bYF_])
                S.op("pool", lambda e, YF_=YF_: e.tensor_tensor(out=YF_[:], in0=YF_[:], in1=lnw[:], op=ALU.mult), reads=[bYF_, blnw], writes=[bYF_])
                S.op("dve", lambda e, YF_=YF_: e.tensor_tensor(out=YF_[:], in0=YF_[:], in1=lnb[:], op=ALU.add), reads=[bYF_, blnb], writes=[bYF_])
                S.op("pool", lambda e, YF_=YF_, G_=G_: e.tensor_tensor(out=Ob[:], in0=YF_[:], in1=G_[:], op=ALU.mult), reads=[bYF_, bG_], writes=[bOb])
                for c in range(8):
                    S.op("pe", lambda e, c=c: e.transpose(out=pTe[:, c, :], in_=Ob[:, c * 128:(c + 1) * 128], identity=identb[:]), reads=[bOb, b_identb], writes=[bpTe])
                OT, bOT = OTs[par]
                S.op("act", lambda e, OT=OT: e.copy(out=OT[:], in_=pTe[:]), reads=[bpTe], writes=[bOT])
                S.dma("pool", lambda e, OT=OT, tile=tile: e.dma_start(out=mixR[tile], in_=OT[:]), reads=[bOT], writes=[B_mixT[tile]], sembuf=bOT)
            S.barrier([blnw, blnb] + [b for n in rg for _, b in rg[n]] + [b for _, b in OTs])

        B_O1 = [Buf("O1_%d" % i) for i in range(NTILE)]
        with contextlib.ExitStack() as es:
            mg, bmg = C.sb(es, "mg", [128, 16, 1024], BF16)
            wst = C.ring(es, "wst1", 4, [128, 4, 512], F32)
            wbs = C.ring(es, "wb1", 2, [128, 16, 512], BF16)
            pmm = [C.ps(es, "pm1_%d" % i, [128, 512], F32) for i in range(8)]
            ost = C.ring(es, "ost1", 8, [128, 512], F32)
            oi = 0
            wi = 0
            for g in range(4):
                tiles = list(range(g * 8, g * 8 + 8))
                S.dma("sp", [lambda e, g=g, q=q: e.dma_start(out=mg[:, q * 4:(q + 1) * 4, :], in_=mixT.rearrange("(c p) t -> p c t", p=128)[:, q * 4:(q + 1) * 4, g * 1024:(g + 1) * 1024]) for q in range(2)]
                      + [lambda e, g=g, t=t: e.dma_start(out=mg[:, 8:16, t * 128:(t + 1) * 128], in_=mixR[g * 8 + t]) for t in range(8)],
                      reads=B_mixL + [B_mixT[t] for t in tiles], writes=[bmg])
                for dcol in range(4):
                    wt, bw = wbs[wi % 2]
                    wi += 1
                    load_w(wst, wt, bw, w_out[:, dcol * 512:(dcol + 1) * 512], 16, 512)
                    for t in range(8):
                        tile = g * 8 + t
                        pm, bpm = pmm[oi % 8]
                        o_t, bo = ost[oi % 8]
                        oi += 1
                        for cch in range(16):
                            S.op("pe", lambda e, pm=pm, wt=wt, cch=cch, t=t: e.matmul(pm[:], lhsT=mg[:, cch, t * 128:(t + 1) * 128], rhs=wt[:, cch, :], start=(cch == 0), stop=(cch == 15)), reads=[bmg, bw], writes=[bpm])
                        cast(o_t[:], pm[:], [bpm], [bo], eng=("act" if oi % 2 else "dve"))
                        S.dma("pool", lambda e, o_t=o_t, tile=tile, dcol=dcol: e.dma_start(out=FD[tile * 128:(tile + 1) * 128, dcol * 512:(dcol + 1) * 512], in_=o_t[:]), reads=[bo], writes=[B_O1[tile]], sembuf=bo)
            S.barrier()
        with contextlib.ExitStack() as es:
            o1s = C.ring(es, "o1b", 2, [128, D], F32)
            xts = C.ring(es, "xt1", 2, [128, D], F32)
            junk, b_junk = C.sb(es, "junk1", [128, D], BF16)
            ss, b_ss = C.sb(es, "ss1", [128, 1], F32)
            xn, b_xn = C.sb(es, "xn1", [128, D], BF16)
            pT, b_pT = C.ps(es, "pT1", [128, D], BF16)
            h2s = C.ring(es, "h2s", 2, [128, 16, 128], BF16)
            nbufs = (junk, b_junk, ss, b_ss, xn, b_xn, pT, b_pT)
            for tile in range(NTILE):
                t0 = tile * 128
                xt, b_xt = xts[tile % 2]
                o1, bo1 = o1s[tile % 2]
                S.dma("sp", lambda e, xt=xt, t0=t0: e.dma_start(out=xt[:], in_=x[t0:t0 + 128, :]), writes=[b_xt])
                S.dma("sp", lambda e, o1=o1, t0=t0: e.dma_start(out=o1[:], in_=FD[t0:t0 + 128, :]), reads=[B_O1[tile]], writes=[bo1])
                S.op("act", lambda e, o1=o1: e.activation(out=junk[:], in_=o1[:], func=AF.Square, accum_out=ss[:]), reads=[bo1], writes=[b_junk, b_ss])
                rstd_from_ss(ss[:], b_ss, D, 1e-6, None)
                S.op("dve", lambda e, o1=o1: e.scalar_tensor_tensor(out=o1[:], in0=o1[:], scalar=ss[:, 0:1], in1=G1r[:], op0=ALU.mult, op1=ALU.mult), reads=[bo1, b_ss, b_G1r], writes=[bo1])
                S.op("pool", lambda e, xt=xt, o1=o1: e.tensor_tensor(out=xt[:], in0=xt[:], in1=o1[:], op=ALU.add), reads=[b_xt, bo1], writes=[b_xt])
                S.dma("pool", lambda e, xt=xt, t0=t0: e.dma_start(out=X1[t0:t0 + 128, :], in_=xt[:]), reads=[b_xt], writes=[B_X1[tile]], sembuf=b_xt)
                h2, bh2 = h2s[tile % 2]
                norm_transpose(xt, b_xt, h2, bh2, 0, Af, modT[:, 48:64], [b_Af, b_modT_], nbufs)
                S.dma("pool", lambda e, h2=h2, tile=tile: e.dma_start(out=h2R[tile], in_=h2[:]), reads=[bh2], writes=[B_h2T[tile]], sembuf=bh2)
            S.barrier()

        with contextlib.ExitStack() as es:
            hg_, bhg = C.sb(es, "hg", [128, 16, 1024], BF16)
            wst = C.ring(es, "wst2", 4, [128, 4, 512], F32)
            wgs = C.ring(es, "wg2", 2, [128, 16, 512], BF16)
            wus = C.ring(es, "wu2", 2, [128, 16, 512], BF16)
            pga = [C.ps(es, "pga%d" % i, [128, 512], F32) for i in range(4)]
            pup = [C.ps(es, "pup%d" % i, [128, 512], F32) for i in range(4)]
            sl = C.ring(es, "sl", 4, [128, 512], F32)
            ao = C.ring(es, "ao", 4, [128, 512], BF16)
            oi = 0
            for g in range(4):
                S.dma("sp", [lambda e, g=g, t=t: e.dma_start(out=hg_[:, :, t * 128:(t + 1) * 128], in_=h2R[g * 8 + t]) for t in range(8)],
                      reads=[B_h2T[t] for t in range(g * 8, g * 8 + 8)], writes=[bhg])
                for j in range(11):
                    wg, bwg = wgs[j % 2]
                    wu, bwu = wus[j % 2]
                    load_w(wst, wg, bwg, w_gu[:, j * 512:(j + 1) * 512], 16, 512)
                    load_w(wst, wu, bwu, w_gu[:, FH + j * 512:FH + (j + 1) * 512], 16, 512)
                    for f in range(4):
                        fc = j * 4 + f
                        for tt in range(2):
                            pg_, bpg_ = pga[oi % 4]
                            pu_, bpu_ = pup[oi % 4]
                            s_, bs_ = sl[oi % 4]
                            a_, ba_ = ao[oi % 4]
                            oi += 1
                            for dc in range(16):
                                S.op("pe", lambda e, pg_=pg_, wg=wg, f=f, dc=dc, tt=tt: e.matmul(pg_[:], lhsT=wg[:, dc, f * 128:(f + 1) * 128], rhs=hg_[:, dc, tt * 512:(tt + 1) * 512], start=(dc == 0), stop=(dc == 15)), reads=[bwg, bhg], writes=[bpg_])
                            for dc in range(16):
                                S.op("pe", lambda e, pu_=pu_, wu=wu, f=f, dc=dc, tt=tt: e.matmul(pu_[:], lhsT=wu[:, dc, f * 128:(f + 1) * 128], rhs=hg_[:, dc, tt * 512:(tt + 1) * 512], start=(dc == 0), stop=(dc == 15)), reads=[bwu, bhg], writes=[bpu_])
                            S.op("act", lambda e, s_=s_, pg_=pg_: e.activation(out=s_[:], in_=pg_[:], func=AF.Silu), reads=[bpg_], writes=[bs_])
                            S.op("dve", lambda e, a_=a_, s_=s_, pu_=pu_: e.tensor_tensor(out=a_[:], in0=s_[:], in1=pu_[:], op=ALU.mult), reads=[bs_, bpu_], writes=[ba_])
                            S.dma("pool", lambda e, a_=a_, fc=fc, g=g, tt=tt: e.dma_start(out=actT[fc * 128:(fc + 1) * 128, g * 1024 + tt * 512:g * 1024 + (tt + 1) * 512], in_=a_[:]), reads=[ba_], writes=[B_actT[fc][g]], sembuf=ba_)
            S.barrier([bhg] + [b for _, b in wst] + [b for _, b in ao])

        B_F = [Buf("Fd_%d" % i) for i in range(NTILE)]
        with contextlib.ExitStack() as es:
            ag = C.ring(es, "ag", 1, [128, 44, 1024], BF16)
            wst = C.ring(es, "wst3", 2, [128, 4, 512], F32)
            wds = C.ring(es, "wd3", 3, [128, 22, 512], BF16)
            pmm = [C.ps(es, "pm3_%d" % i, [128, 512], F32) for i in range(8)]
            ost = C.ring(es, "ost3", 4, [128, 512], F32)
            wi = 0
            oo = 0
            for g in range(4):
                agt, bag = ag[0]
                S.dma("sp", [lambda e, agt=agt, g=g, q=q: e.dma_start(out=agt[:, q * 4:(q + 1) * 4, :], in_=actT.rearrange("(c p) t -> p c t", p=128)[:, q * 4:(q + 1) * 4, g * 1024:(g + 1) * 1024]) for q in range(11)],
                      reads=[B_actT[f][g] for f in range(44)] + B_X1, writes=[bag])
                for dcol in range(4):
                    for half in range(2):
                        wt, bw = wds[wi % 3]
                        wi += 1
                        load_w(wst, wt, bw, w_down[:, dcol * 512:(dcol + 1) * 512], 22, 512, k0=half * 22)
                        for t in range(8):
                            pm, bpm = pmm[t]
                            for f2 in range(22):
                                fc = half * 22 + f2
                                S.op("pe", lambda e, pm=pm, wt=wt, fc=fc, f2=f2, t=t, agt=agt: e.matmul(pm[:], lhsT=agt[:, fc, t * 128:(t + 1) * 128], rhs=wt[:, f2, :], start=(fc == 0), stop=(fc == 43)), reads=[bag, bw], writes=[bpm])
                    for t in range(8):
                        tile = g * 8 + t
                        pm, bpm = pmm[t]
                        o_t, bo = ost[oo % 4]
                        oo += 1
                        cast(o_t[:], pm[:], [bpm], [bo], eng=("act" if t % 2 else "dve"))
                        S.dma("pool", lambda e, o_t=o_t, tile=tile, dcol=dcol: e.dma_start(out=FD[tile * 128:(tile + 1) * 128, dcol * 512:(dcol + 1) * 512], in_=o_t[:]), reads=[bo], writes=[B_F[tile]], sembuf=bo)
            S.barrier()
        with contextlib.ExitStack() as es:
            o1s = C.ring(es, "o3b", 2, [128, D], F32)
            xts = C.ring(es, "xt3", 2, [128, D], F32)
            junk, b_junk = C.sb(es, "junk3", [128, D], BF16)
            ss, b_ss = C.sb(es, "ss3", [128, 1], F32)
            for tile in range(NTILE):
                t0 = tile * 128
                xt, b_xt = xts[tile % 2]
                o1, bo1 = o1s[tile % 2]
                S.dma("sp", lambda e, xt=xt, t0=t0: e.dma_start(out=xt[:], in_=X1[t0:t0 + 128, :]), reads=[B_X1[tile]], writes=[b_xt])
                S.dma("sp", lambda e, o1=o1, t0=t0: e.dma_start(out=o1[:], in_=FD[t0:t0 + 128, :]), reads=[B_F[tile]], writes=[bo1])
                S.op("act", lambda e, o1=o1: e.activation(out=junk[:], in_=o1[:], func=AF.Square, accum_out=ss[:]), reads=[bo1], writes=[b_junk, b_ss])
                rstd_from_ss(ss[:], b_ss, D, 1e-6, None)
                S.op("dve", lambda e, o1=o1: e.scalar_tensor_tensor(out=o1[:], in0=o1[:], scalar=ss[:, 0:1], in1=G2r[:], op0=ALU.mult, op1=ALU.mult), reads=[bo1, b_ss, b_G2r], writes=[bo1])
                S.op("pool", lambda e, xt=xt, o1=o1: e.tensor_tensor(out=xt[:], in0=xt[:], in1=o1[:], op=ALU.add), reads=[b_xt, bo1], writes=[b_xt])
                S.dma("pool", lambda e, xt=xt, t0=t0: e.dma_start(out=y_out[t0:t0 + 128, :], in_=xt[:]), reads=[b_xt], sembuf=b_xt, final=True)
        S.emit()
    S.close()
    return nc


_PROMPT_COUNTS = [6, 6, 5, 5, 5, 5]


def _col(v, n):
    return np.ascontiguousarray(np.asarray(v, np.float32).reshape(n, 128).T)


def kernel(x_prompt, x_sample, state_lru, state_wkv, c, c_ctx,
           norm_mix_pre, norm_mix_post, norm_ffn_pre, norm_ffn_post, w_mod, b_mod, w_in,
           lru_conv_w, lru_conv_b, lru_wr, lru_br, lru_wi, lru_bi, lru_lambda,
           rwkv_mu, rwkv_w0, rwkv_w_up, rwkv_a0, rwkv_a_up, rwkv_g_up, rwkv_k_k, rwkv_k_a, rwkv_r_k,
           rwkv_ln_w, rwkv_ln_b, w_out, ffn_w_gu, ffn_w_down):
    f32 = np.float32
    A = lambda a: np.ascontiguousarray(np.asarray(a, f32))
    x_prompt, x_sample = A(x_prompt), A(x_sample)
    nc = build_program()
    idx = np.arange(128)
    tri = np.zeros((128, 2, 128), f32)
    tri[:, 0, :] = (idx[:, None] <= idx[None, :])
    tri[:, 1, :] = (idx[:, None] >= idx[None, :])
    gmask = np.zeros((128, 2, 512), f32)
    for d in range(2):
        incl = tri[:, d, :]
        strict = incl - np.eye(128, dtype=f32)
        gmask[:, d, 0:128] = -strict
        gmask[:, d, 128:256] = incl
        gmask[:, d, 256:384] = strict
        gmask[:, d, 384:512] = incl
    ncols = np.stack([_col(A(v)[0], 16) for v in (norm_mix_pre, norm_mix_post, norm_ffn_pre, norm_ffn_post)], axis=1)
    convc = np.zeros((128, 8, 5), f32)
    cw = A(lru_conv_w)[0]
    cb = A(lru_conv_b)[0]
    for k in range(4):
        convc[:, :, k] = cw[k].reshape(8, 128).T
    convc[:, :, 4] = cb.reshape(8, 128).T
    lru_bc = np.zeros((128, 2, 8, 2), f32)
    lru_bc[:, :, :, 0] = np.transpose(A(lru_br)[0], (2, 0, 1))
    lru_bc[:, :, :, 1] = np.transpose(A(lru_bi)[0], (2, 0, 1))
    lam = np.ascontiguousarray(np.transpose(A(lru_lambda)[0].reshape(2, 8, 128), (2, 0, 1)))
    shared = {
        "w_mod": A(w_mod)[0], "b_modT": _col(A(b_mod)[0], 96), "ncols": np.ascontiguousarray(ncols),
        "w_in": A(w_in)[0], "convc": convc, "lru_wr": A(lru_wr)[0], "lru_wi": A(lru_wi)[0],
        "lru_bc": lru_bc, "lru_lam": lam,
        "w_up": A(rwkv_w_up)[0], "a_up": A(rwkv_a_up)[0], "g_up": A(rwkv_g_up)[0],
        "w_out": A(w_out)[0], "w_gu": A(ffn_w_gu)[0], "w_down": A(ffn_w_down)[0],
        "ident": np.eye(128, dtype=f32), "tri": tri, "gmask": gmask,
    }
    chs = np.arange(PRW)
    p = np.arange(128)

    def rows_for(sample):
        if sample:
            cmM1 = (chs < 840)
            cmP1 = (chs >= 840) & (chs < 1680)
            cmU = (chs >= 1680) & (chs < 2520)
            cmD = (chs >= 2520)
        else:
            cmM1 = (chs < 1680)
            cmP1 = (chs >= 1680)
            cmU = np.zeros(PRW, bool)
            cmD = np.zeros(PRW, bool)
        parts = [A(rwkv_mu)[0], cmM1.astype(f32), cmP1.astype(f32), cmU.astype(f32), cmD.astype(f32),
                 A(rwkv_w0)[0, 0], A(rwkv_w0)[0, 1], A(rwkv_a0)[0, 0], A(rwkv_a0)[0, 1], A(rwkv_k_k)[0], A(rwkv_k_a)[0],
                 A(rwkv_r_k)[0].reshape(-1), A(rwkv_ln_w)[0], A(rwkv_ln_b)[0]]
        return np.concatenate(parts).astype(f32)[None, :]

    def rmask_for(sample):
        m = np.ones((128, 4), f32)
        if sample:
            m[:, 0] = (p % 64 != 0)
            m[:, 1] = (p % 64 != 0)
            m[:, 2] = (p % 64 != 63)
            m[:, 3] = (p % 64 != 63)
        else:
            m[:, 0] = (p != 0)
            m[:, 1] = 1.0
            m[:, 2] = 1.0
            m[:, 3] = (p != 127)
        return m
    assign = []
    s0 = 0
    for n in _PROMPT_COUNTS:
        assign.append(list(range(s0, s0 + n)))
        s0 += n
    in_maps = []
    for core in range(8):
        m = dict(shared)
        if core < 2:
            m["x"] = np.ascontiguousarray(x_sample[core])
            m["cvec"] = _col(A(c)[core], 16)
            m["cmcol"] = np.ones((128, 1), f32)
            sl = A(state_lru)[core, 0]
            m["h0lru"] = np.ascontiguousarray(np.transpose(sl.reshape(2, 8, 128), (2, 1, 0)))
            sw = A(state_wkv)[core, 0]
            m["s0wkv"] = np.ascontiguousarray(np.transpose(sw, (3, 0, 1, 2)))
            m["rows"] = rows_for(True)
            m["rmask"] = rmask_for(True)
        else:
            xs = np.zeros((NT, D), f32)
            mine = assign[core - 2]
            for i in range(NSEG):
                xs[i * SEG:(i + 1) * SEG] = x_prompt[mine[i % len(mine)]]
            m["x"] = xs
            m["cvec"] = _col(A(c_ctx), 16)
            m["cmcol"] = np.zeros((128, 1), f32)
            m["h0lru"] = np.zeros((128, 8, 2), f32)
            m["s0wkv"] = np.zeros((64, 2, 16, 64), f32)
            m["rows"] = rows_for(False)
            m["rmask"] = rmask_for(False)
        in_maps.append(m)
    res = run_bass_kernel_spmd(nc, in_maps, core_ids=list(range(8)))
    R = res.results
    if DEBUG:
        global _LAST
        _LAST = R
    y_p = np.zeros((32, SEG, D), f32)
    y_s = np.zeros((2, NT, D), f32)
    ns_lru = np.zeros((32, 1, 2, LW), f32)
    ns_wkv = np.zeros((32, 1, 2, 16, 64, 64), f32)
    for core in range(8):
        r = R[core]
        if core < 2:
            y_s[core] = r["y"]
        else:
            lf = r["lru_fin"]
            wf = r["wkv_fin"]
            for i, sidx in enumerate(assign[core - 2]):
                y_p[sidx] = r["y"][i * SEG:(i + 1) * SEG]
                ns_lru[sidx, 0] = np.transpose(lf[:, :, :, i], (2, 1, 0)).reshape(2, LW)
                ns_wkv[sidx, 0] = np.transpose(wf[i], (0, 2, 3, 1))
    return (y_p, y_s, ns_lru, ns_wkv)
```
